# Optimizing a Trainium2 kernel written in Bass

```python
import math
import jax, jax.numpy as jnp
from jax import lax
import numpy as np

D_MODEL = 2048
BATCH = 4
SEQ = 4096
DEPTH = 4

HEAD_DIM = 128
BR_WIDTH = 1024
N_BRANCH = 3
ROPE_THETA = 500000.0
ROPE_DIM = HEAD_DIM // 4
Q_BLOCK = 128
EPS = 1e-6
NEG_INF = -1e30
FORCE_SCORE = 1e6

DIFF_HEADS = BR_WIDTH // (2 * HEAD_DIM)
DIFF_NORM_EPS = 1e-5
NSA_HEADS = BR_WIDTH // HEAD_DIM
NSA_GROUPS = 2
NSA_REP = NSA_HEADS // NSA_GROUPS
CMP_LEN = 32
CMP_STRIDE = 16
SLC_LEN = 64
SLC_TOPK = 16
WINDOW = 512
SLC_Q_CHUNK = 64
SB_HEADS = BR_WIDTH // HEAD_DIM

NSA_KV = NSA_GROUPS * HEAD_DIM
IN_SPLITS = (
    2 * DIFF_HEADS * HEAD_DIM, 2 * DIFF_HEADS * HEAD_DIM, BR_WIDTH, BR_WIDTH,
    BR_WIDTH, NSA_KV, NSA_KV, NSA_KV, NSA_KV, NSA_KV, NSA_KV, 3 * NSA_HEADS, BR_WIDTH,
    BR_WIDTH, BR_WIDTH, BR_WIDTH, BR_WIDTH,
    N_BRANCH * D_MODEL,
)
N_IN = sum(IN_SPLITS)

kernel_name = "hybrid_diff_nsa_stickbreak_block"


def rmsnorm(x, g, eps=EPS):
    xf = x.astype(jnp.float32)
    y = xf * lax.rsqrt(jnp.mean(xf * xf, axis=-1, keepdims=True) + eps)
    return (y * g.astype(jnp.float32)).astype(x.dtype)


def rope_tables(S):
    pos = jnp.arange(S, dtype=jnp.float32)
    inv = ROPE_THETA ** (-jnp.arange(0, ROPE_DIM, 2, dtype=jnp.float32) / ROPE_DIM)
    ang = pos[:, None] * inv[None, :]
    return jnp.cos(ang), jnp.sin(ang)


def partial_rope(x, cos, sin):
    half = ROPE_DIM // 2
    shape = (1, x.shape[1]) + (1,) * (x.ndim - 3) + (half,)
    cs = cos.reshape(shape).astype(x.dtype)
    sn = sin.reshape(shape).astype(x.dtype)
    x1, x2, xp = x[..., :half], x[..., half:ROPE_DIM], x[..., ROPE_DIM:]
    return jnp.concatenate([x1 * cs - x2 * sn, x2 * cs + x1 * sn, xp], axis=-1)


def diff_attention(q, k, v, lq1, lk1, lq2, lk2, norm_g, layer_idx):
    B, S, H = q.shape[0], q.shape[1], q.shape[2]
    nb = S // Q_BLOCK
    lam_init = 0.8 - 0.6 * math.exp(-0.3 * layer_idx)
    f32 = jnp.float32
    lam = (jnp.exp(jnp.sum(lq1.astype(f32) * lk1.astype(f32)))
           - jnp.exp(jnp.sum(lq2.astype(f32) * lk2.astype(f32))) + lam_init)
    scale = HEAD_DIM ** -0.5
    kpos = jnp.arange(S)
    qb = q.reshape(B, nb, Q_BLOCK, H, 2, HEAD_DIM).transpose(1, 0, 2, 3, 4, 5)

    def block(args):
        qblk, b = args
        s = jnp.einsum('bqhcd,bkhcd->bhcqk', qblk, k).astype(f32) * scale
        qpos = b * Q_BLOCK + jnp.arange(Q_BLOCK)
        s = jnp.where(kpos[None, :] <= qpos[:, None], s, NEG_INF)
        p = jax.nn.softmax(s, axis=-1)
        a = p[:, :, 0] - lam * p[:, :, 1]
        return jnp.einsum('bhqk,bkhe->bqhe', a.astype(v.dtype), v)

    o = lax.map(block, (qb, jnp.arange(nb)))
    o = o.transpose(1, 0, 2, 3, 4).reshape(B, S, H, 2 * HEAD_DIM)
    o = rmsnorm(o, norm_g, eps=DIFF_NORM_EPS) * (1.0 - lam_init)
    return o.reshape(B, S, H * 2 * HEAD_DIM)


def stick_breaking(q, k, v):
    B, S, H = q.shape[0], q.shape[1], q.shape[2]
    nb = S // Q_BLOCK
    scale = HEAD_DIM ** -0.5
    kpos = jnp.arange(S)
    qb = q.reshape(B, nb, Q_BLOCK, H, HEAD_DIM).transpose(1, 0, 2, 3, 4)

    def block(args):
        qblk, b = args
        z = jnp.einsum('bqhd,bkhd->bhqk', qblk, k).astype(jnp.float32) * scale
        qpos = b * Q_BLOCK + jnp.arange(Q_BLOCK)
        strict = kpos[None, :] < qpos[:, None]
        log_1m = jnp.where(strict, jax.nn.log_sigmoid(-z), 0.0)
        after = lax.cumsum(log_1m, axis=3, reverse=True) - log_1m
        a = jnp.where(strict, jnp.exp(jax.nn.log_sigmoid(z) + after), 0.0)
        return jnp.einsum('bhqk,bkhd->bqhd', a.astype(v.dtype), v)

    o = lax.map(block, (qb, jnp.arange(nb)))
    return o.transpose(1, 0, 2, 3, 4).reshape(B, S, H * HEAD_DIM)


def compress(x, pe, w1, w2):
    B, S, G, d = x.shape
    n_cmp = (S - CMP_LEN) // CMP_STRIDE + 1
    idx = CMP_STRIDE * jnp.arange(n_cmp)[:, None] + jnp.arange(CMP_LEN)[None, :]
    blk = x[:, idx] + pe[None, None, :, None, :]
    flat = blk.transpose(0, 1, 3, 2, 4).reshape(B, n_cmp, G, CMP_LEN * d)
    return jax.nn.silu(flat @ w1) @ w2


def nsa_attention(q, kc, vc, ks, vs, kw, vw, gate_logits, pe_k, w1_k, w2_k, pe_v, w1_v, w2_v):
    B, S = q.shape[0], q.shape[1]
    G, R, d = NSA_GROUPS, NSA_REP, HEAD_DIM
    f32 = jnp.float32
    scale = d ** -0.5
    qg = q.reshape(B, S, G, R, d)
    t = jnp.arange(S)

    kcmp = compress(kc, pe_k, w1_k, w2_k)
    vcmp = compress(vc, pe_v, w1_v, w2_v)
    n_cmp = kcmp.shape[1]
    cend = CMP_STRIDE * jnp.arange(n_cmp) + CMP_LEN - 1
    cvalid = cend[None, :] <= t[:, None]
    s = jnp.einsum('btgrd,bngd->bgrtn', qg, kcmp).astype(f32) * scale
    p_cmp = jnp.where(cvalid, jax.nn.softmax(jnp.where(cvalid, s, NEG_INF), axis=-1), 0.0)
    o_cmp = jnp.einsum('bgrtn,bngd->btgrd', p_cmp.astype(vcmp.dtype), vcmp)

    n_slc = S // SLC_LEN
    cstart = CMP_STRIDE * jnp.arange(n_cmp)
    sstart = SLC_LEN * jnp.arange(n_slc)
    overlap = ((cstart[:, None] < sstart[None, :] + SLC_LEN)
               & (cstart[:, None] + CMP_LEN > sstart[None, :])).astype(f32)
    imp = jnp.einsum('bgrtn,nj->bgtj', p_cmp, overlap)
    jj = jnp.arange(n_slc)
    tblk = t // SLC_LEN
    forced = (jj[None, :] == tblk[:, None]) | (jj[None, :] == 0)
    imp = jnp.where(forced, FORCE_SCORE, imp)
    imp = jnp.where(jj[None, :] <= tblk[:, None], imp, NEG_INF)
    n_sel = min(SLC_TOPK, n_slc)
    top_val, top_idx = lax.top_k(imp, n_sel)
    sel_ok = top_val > 0.5 * NEG_INF

    ks_blk = ks.reshape(B, n_slc, SLC_LEN, G, d).transpose(0, 3, 1, 2, 4)
    vs_blk = vs.reshape(B, n_slc, SLC_LEN, G, d).transpose(0, 3, 1, 2, 4)
    C = SLC_Q_CHUNK
    nc = S // C
    q_ch = qg.reshape(B, nc, C, G, R, d).transpose(1, 0, 3, 4, 2, 5)
    i_ch = top_idx.reshape(B, G, nc, C, n_sel).transpose(2, 0, 1, 3, 4)
    ok_ch = sel_ok.reshape(B, G, nc, C, n_sel).transpose(2, 0, 1, 3, 4)
    bi = jnp.arange(B)[:, None, None, None]
    gi = jnp.arange(G)[None, :, None, None]

    def sel_block(args):
        qc, ic, okc, cb = args
        kg = ks_blk[bi, gi, ic]
        vg = vs_blk[bi, gi, ic]
        sc = jnp.einsum('bgrcd,bgcnld->bgrcnl', qc, kg).astype(f32) * scale
        tpos = cb * C + jnp.arange(C)
        kpos = ic[..., None] * SLC_LEN + jnp.arange(SLC_LEN)
        ok = okc[..., None] & (kpos <= tpos[None, None, :, None, None])
        sc = jnp.where(ok[:, :, None], sc, NEG_INF)
        p = jax.nn.softmax(sc.reshape(B, G, R, C, n_sel * SLC_LEN), axis=-1)
        p = p.reshape(B, G, R, C, n_sel, SLC_LEN)
        return jnp.einsum('bgrcnl,bgcnld->bgrcd', p.astype(vg.dtype), vg)

    o_slc = lax.map(sel_block, (q_ch, i_ch, ok_ch, jnp.arange(nc)))
    o_slc = o_slc.transpose(1, 0, 4, 2, 3, 5).reshape(B, S, G, R, d)

    nb = S // Q_BLOCK
    span = WINDOW + Q_BLOCK
    kp = jnp.pad(kw, ((0, 0), (WINDOW, 0), (0, 0), (0, 0)))
    vp = jnp.pad(vw, ((0, 0), (WINDOW, 0), (0, 0), (0, 0)))
    widx = Q_BLOCK * jnp.arange(nb)[:, None] + jnp.arange(span)[None, :]
    kb, vb = kp[:, widx], vp[:, widx]
    qb = qg.reshape(B, nb, Q_BLOCK, G, R, d)
    sw = jnp.einsum('bnqgrd,bnkgd->bgrnqk', qb, kb).astype(f32) * scale
    tpos = Q_BLOCK * jnp.arange(nb)[:, None] + jnp.arange(Q_BLOCK)[None, :]
    kpos = (widx - WINDOW)[:, None, :]
    wok = (kpos <= tpos[:, :, None]) & (kpos > tpos[:, :, None] - WINDOW) & (kpos >= 0)
    pw = jax.nn.softmax(jnp.where(wok, sw, NEG_INF), axis=-1)
    o_win = jnp.einsum('bgrnqk,bnkgd->bnqgrd', pw.astype(vb.dtype), vb).reshape(B, S, G, R, d)

    g = jax.nn.sigmoid(gate_logits.reshape(B, S, G, R, 3))
    o = g[..., 0:1] * o_cmp + g[..., 1:2] * o_slc + g[..., 2:3] * o_win
    return o.reshape(B, S, NSA_HEADS * d)


def setup_inputs(seed: int = 0) -> dict:
    key = jax.random.key(seed)
    ks = jax.random.split(key, 20)
    f32 = jnp.float32

    def nrm(k, shape, s):
        return jax.random.normal(k, shape, f32) * s

    D = D_MODEL
    return {
        "x": nrm(ks[0], (BATCH, SEQ, D), 1.0),
        "c": nrm(ks[1], (BATCH, D), 1.0),
        "norm_pre_g": 1.0 + nrm(ks[2], (DEPTH, D), 0.05),
        "norm_post_g": 1.0 + nrm(ks[3], (DEPTH, D), 0.05),
        "w_ada": nrm(ks[4], (DEPTH, D, 3 * D), 0.5 * D ** -0.5),
        "b_ada": nrm(ks[5], (DEPTH, 3 * D), 0.01),
        "w_in": nrm(ks[6], (DEPTH, D, N_IN), D ** -0.5),
        "lambda_q1": nrm(ks[7], (DEPTH, HEAD_DIM), 0.1),
        "lambda_k1": nrm(ks[8], (DEPTH, HEAD_DIM), 0.1),
        "lambda_q2": nrm(ks[9], (DEPTH, HEAD_DIM), 0.1),
        "lambda_k2": nrm(ks[10], (DEPTH, HEAD_DIM), 0.1),
        "diff_norm_g": 1.0 + nrm(ks[11], (DEPTH, 2 * HEAD_DIM), 0.05),
        "cmp_pe_k": nrm(ks[12], (DEPTH, CMP_LEN, HEAD_DIM), 0.02),
        "cmp_w1_k": nrm(ks[13], (DEPTH, CMP_LEN * HEAD_DIM, HEAD_DIM), (CMP_LEN * HEAD_DIM) ** -0.5),
        "cmp_w2_k": nrm(ks[14], (DEPTH, HEAD_DIM, HEAD_DIM), HEAD_DIM ** -0.5),
        "cmp_pe_v": nrm(ks[15], (DEPTH, CMP_LEN, HEAD_DIM), 0.02),
        "cmp_w1_v": nrm(ks[16], (DEPTH, CMP_LEN * HEAD_DIM, HEAD_DIM), (CMP_LEN * HEAD_DIM) ** -0.5),
        "cmp_w2_v": nrm(ks[17], (DEPTH, HEAD_DIM, HEAD_DIM), HEAD_DIM ** -0.5),
        "w_branch": nrm(ks[18], (DEPTH, N_BRANCH, BR_WIDTH, D), BR_WIDTH ** -0.5),
        "w_out": nrm(ks[19], (DEPTH, D, D), D ** -0.5),
    }


def reference(x, c, norm_pre_g, norm_post_g, w_ada, b_ada, w_in, lambda_q1, lambda_k1,
              lambda_q2, lambda_k2, diff_norm_g, cmp_pe_k, cmp_w1_k, cmp_w2_k,
              cmp_pe_v, cmp_w1_v, cmp_w2_v, w_branch, w_out):
    B, S, D = x.shape
    cos, sin = rope_tables(S)
    split_points = [int(v) for v in np.cumsum(IN_SPLITS)[:-1]]
    for l in range(DEPTH):
        mod = jax.nn.silu(c) @ w_ada[l] + b_ada[l]
        shift, scale, gate = jnp.split(mod, 3, axis=-1)
        h = rmsnorm(x, norm_pre_g[l]) * (1.0 + scale[:, None, :]) + shift[:, None, :]
        (a_q, a_k, a_v, a_z,
         n_q, n_kc, n_vc, n_ks, n_vs, n_kw, n_vw, n_g, n_z,
         sb_q, sb_k, sb_v, sb_z, m_g) = jnp.split(h @ w_in[l], split_points, axis=-1)

        a_q = partial_rope(a_q.reshape(B, S, DIFF_HEADS, 2, HEAD_DIM), cos, sin)
        a_k = partial_rope(a_k.reshape(B, S, DIFF_HEADS, 2, HEAD_DIM), cos, sin)
        a_v = a_v.reshape(B, S, DIFF_HEADS, 2 * HEAD_DIM)
        y_a = diff_attention(a_q, a_k, a_v, lambda_q1[l], lambda_k1[l], lambda_q2[l],
                             lambda_k2[l], diff_norm_g[l], l)

        kvs = lambda t_: t_.reshape(B, S, NSA_GROUPS, HEAD_DIM)
        y_b = nsa_attention(
            partial_rope(n_q.reshape(B, S, NSA_HEADS, HEAD_DIM), cos, sin),
            partial_rope(kvs(n_kc), cos, sin), kvs(n_vc),
            partial_rope(kvs(n_ks), cos, sin), kvs(n_vs),
            partial_rope(kvs(n_kw), cos, sin), kvs(n_vw),
            n_g, cmp_pe_k[l], cmp_w1_k[l], cmp_w2_k[l], cmp_pe_v[l], cmp_w1_v[l], cmp_w2_v[l])

        hs = lambda t_: t_.reshape(B, S, SB_HEADS, HEAD_DIM)
        y_c = stick_breaking(hs(sb_q), hs(sb_k), hs(sb_v))

        ys = jnp.stack([y_a * jax.nn.silu(a_z), y_b * jax.nn.silu(n_z),
                        y_c * jax.nn.silu(sb_z)], axis=0)
        yproj = jnp.einsum('nbsw,nwd->nbsd', ys, w_branch[l])
        mg = jax.nn.sigmoid(m_g.reshape(B, S, N_BRANCH, D))
        merged = jnp.einsum('nbsd,bsnd->bsd', yproj, mg)
        out = rmsnorm(merged @ w_out[l], norm_post_g[l])
        x = x + gate[:, None, :] * out
    return x
```

```python
import math
from contextlib import ExitStack

import numpy as np
import concourse.bass as bass
import concourse.mybir as mybir
from concourse.bass_utils import run_bass_kernel_spmd

F32 = mybir.dt.float32
BF16 = mybir.dt.bfloat16
AF = mybir.ActivationFunctionType
ALU = mybir.AluOpType

D = 2048
SEQ = 4096
DEPTH = 4
HD = 128
N_IN = 17944
NQT = SEQ // 512
NKB = SEQ // 128
SCALE = HD ** -0.5
NCMP = 255
BIG = 30000.0

COL = dict(aq=0, ak=1024, av=2048, az=3072, bq=4096, bkc=5120, bvc=5376, bks=5632, bvs=5888,
           bkw=6144, bvw=6400, bg=6656, bz=6680, cq=7704, ck=8728, cv=9752, cz=10776, mg=11800)
FMROWS = {}
_r = 0
for _n, _w in (("aq", 1024), ("ak", 1024), ("az", 1024), ("bq", 1024), ("bkc", 256), ("bvc", 256),
               ("bks", 256), ("bkw", 256), ("bg", 128), ("bz", 1024), ("cq", 1024), ("ck", 1024),
               ("cz", 1024), ("mg", 6144)):
    FMROWS[_n] = _r
    _r += _w
NFM = _r
VTCOL = dict(av=0, bvs=1024, bvw=1280, cv=1536)
NVT = 2560
GROUPS = []
for _n, _k in (("aq", "rope"), ("ak", "rope")):
    GROUPS += [(_n, 0, 512, _k), (_n, 512, 512, _k)]
GROUPS += [("av", 0, 512, "v"), ("av", 512, 512, "v")]
GROUPS += [("az", 0, 512, "silu"), ("az", 512, 512, "silu")]
GROUPS += [("bq", 0, 512, "rope"), ("bq", 512, 512, "rope")]
GROUPS += [("bkc", 0, 256, "rope"), ("bvc", 0, 256, "fm"), ("bks", 0, 256, "rope"), ("bvs", 0, 256, "v"),
           ("bkw", 0, 256, "rope"), ("bvw", 0, 256, "v"), ("bg", 0, 24, "sig")]
GROUPS += [("bz", 0, 512, "silu"), ("bz", 512, 512, "silu")]
GROUPS += [("cq", 0, 512, "fm"), ("cq", 512, 512, "fm"), ("ck", 0, 512, "fm"), ("ck", 512, 512, "fm")]
GROUPS += [("cv", 0, 512, "v"), ("cv", 512, 512, "v")]
GROUPS += [("cz", 0, 512, "silu"), ("cz", 512, 512, "silu")]
GROUPS += [("mg", 512 * _i, 512, "sig") for _i in range(12)]


class T:
    def __init__(self, ap, name=""):
        self.ap = ap
        self.name = name
        self.lw = None
        self.rd = []

    def __getitem__(self, idx):
        return self.ap[idx]


class Sched:
    ENG = ("pe", "act", "dve", "pool", "sp")

    def __init__(self, nc, es, n_dma_sems=4):
        self.nc = nc
        self.es = es
        self.eng = {"pe": nc.tensor, "act": nc.scalar, "dve": nc.vector, "pool": nc.gpsimd, "sp": nc.sync}
        self.sem = {}
        self.cnt = {}
        for e in self.ENG:
            self.sem[e] = es.enter_context(nc.semaphore("s_" + e))
            self.cnt[e] = 0
        self.dsem = {}
        for q in ("sp", "pool", "act"):
            lst = []
            for i in range(n_dma_sems):
                k = "d_%s%d" % (q, i)
                self.sem[k] = es.enter_context(nc.semaphore(k))
                self.cnt[k] = 0
                lst.append(k)
            self.dsem[q] = [lst, 0]
        self.waited = {}
        self.n_ins = 0

    def sb(self, es, name, shape, dt):
        self.uid = getattr(self, "uid", 0) + 1
        name = "%s_u%d" % (name, self.uid)
        return T(es.enter_context(self.nc.sbuf_tensor(name, list(shape), dt)), name)

    def ps(self, es, name, shape, dt=F32):
        return T(es.enter_context(self.nc.psum_tensor(name, list(shape), dt)), name)

    def _wait(self, e, key, val):
        if val <= 0 or self.waited.get((e, key), 0) >= val:
            return
        self.eng[e].wait_ge(self.sem[key], val)
        self.waited[(e, key)] = val

    def _deps(self, e, reads, writes):
        for r in reads:
            if r.lw is not None:
                self._wait(e, *r.lw)
        for w in writes:
            if w.lw is not None and w.lw[0] != e:
                self._wait(e, *w.lw)
            for (k, v) in w.rd:
                if k != e:
                    self._wait(e, k, v)

    def _mark(self, key, val, reads, writes):
        for w in writes:
            w.lw = (key, val)
            w.rd = []
        for r in reads:
            if r in writes:
                continue
            r.rd.append((key, val))
            if len(r.rd) > 16:
                d = {}
                for (k, v) in r.rd:
                    d[k] = max(d.get(k, 0), v)
                r.rd = list(d.items())

    def op(self, e, fn, reads=(), writes=()):
        self._deps(e, reads, writes)
        ins = fn(self.eng[e])
        self.cnt[e] += 1
        ins.then_inc(self.sem[e], 1)
        self._mark(e, self.cnt[e], reads, writes)
        self.n_ins += 1
        return ins

    def dma(self, q, out, in_, reads=(), writes=(), **kw):
        lst, i = self.dsem[q]
        k = lst[i % len(lst)]
        self.dsem[q][1] = i + 1
        self._wait(q, k, self.cnt[k])
        self._deps(q, reads, writes)
        ins = self.eng[q].dma_start(out=out, in_=in_, **kw)
        self.cnt[k] += 16
        ins.then_inc(self.sem[k], 16)
        self._mark(k, self.cnt[k], reads, writes)
        self.n_ins += 1
        return ins

    def barrier(self):
        for e in self.ENG:
            for k in self.sem:
                if k != e:
                    self._wait(e, k, self.cnt[k])


class Ring:
    def __init__(self, items):
        self.items = items
        self.i = 0

    def next(self):
        t = self.items[self.i % len(self.items)]
        self.i += 1
        return t


def _constants():
    c = {}
    c["ident"] = np.eye(128, dtype=np.float32)
    prot = np.zeros((128, 128), np.float32)
    for i in range(16):
        prot[i + 16, i] = -1.0
        prot[i, i + 16] = 1.0
    c["prot"] = prot
    pos = np.arange(SEQ, dtype=np.float32)
    inv = (np.float32(500000.0) ** (-np.arange(0, 32, 2, dtype=np.float32) / np.float32(32))).astype(np.float32)
    ang = (pos[None, :] * inv[:, None]).astype(np.float32)
    ct = np.ones((128, SEQ), np.float32)
    st = np.zeros((128, SEQ), np.float32)
    ct[0:16] = np.cos(ang); ct[16:32] = np.cos(ang)
    st[0:16] = np.sin(ang); st[16:32] = np.sin(ang)
    c["ropec"] = ct
    c["ropes"] = st
    n = np.arange(256)[:, None]
    j = np.arange(64)[None, :]
    ov = ((16 * n < 64 * j + 64) & (16 * n + 32 > 64 * j) & (n < NCMP)).astype(np.float32)
    c["ovl"] = ov.reshape(2, 128, 64).transpose(1, 0, 2).copy()
    k = np.arange(SEQ)[None, :]
    c["eall"] = (np.arange(64)[:, None] == (k // 64)).astype(np.float32)
    return c


class Builder:
    def __init__(self, n_layers=DEPTH, halves=(0, 1), headsets=(0, 1), dbg=False):
        self.n_layers = n_layers
        self.halves = halves
        self.headsets = headsets
        self.dbg = dbg
        self.nc = bass.Bass("TRN2", target_bir_lowering=False)
        self.es = ExitStack()
        self.s = Sched(self.nc, self.es)

    def declare(self):
        nc = self.nc
        ein = lambda name, shape: nc.dram_tensor(name, list(shape), F32, kind="ExternalInput").ap()
        self.x_in = ein("x", [SEQ, D])
        self.c_in = ein("c", [128, 16])
        self.norm_pre_g = ein("norm_pre_g", [DEPTH, D])
        self.norm_post_g = ein("norm_post_g", [DEPTH, D])
        self.w_ada = ein("w_ada", [DEPTH, D, 3 * D])
        self.b_ada = ein("b_ada", [DEPTH, 3 * D])
        self.w_in = ein("w_in", [DEPTH, D, N_IN])
        self.lam_in = ein("lam", [DEPTH, 4, 128])
        self.diff_norm_g = ein("diff_norm_g", [DEPTH, 256])
        self.cmp_pe = [ein("cmp_pe_k", [DEPTH, 32, 128]), ein("cmp_pe_v", [DEPTH, 32, 128])]
        self.cmp_w1 = [ein("cmp_w1_k", [DEPTH, 4096, 128]), ein("cmp_w1_v", [DEPTH, 4096, 128])]
        self.cmp_w2 = [ein("cmp_w2_k", [DEPTH, 128, 128]), ein("cmp_w2_v", [DEPTH, 128, 128])]
        self.w_branch = ein("w_branch", [DEPTH, 3, 1024, D])
        self.w_out = ein("w_out", [DEPTH, D, D])
        self.k_ident = ein("k_ident", [128, 128])
        self.k_prot = ein("k_prot", [128, 128])
        self.k_ropec = ein("k_ropec", [128, SEQ])
        self.k_ropes = ein("k_ropes", [128, SEQ])
        self.k_ovl = ein("k_ovl", [128, 2, 64])
        self.k_eall = ein("k_eall", [64, SEQ])
        self.out = nc.dram_tensor("out", [SEQ, D], F32, kind="ExternalOutput").ap()
        kind = "ExternalOutput" if self.dbg else "Internal"
        self.FM = nc.dram_tensor("fm", [NFM, SEQ], BF16, kind=kind).ap()
        self.VT = nc.dram_tensor("vt", [SEQ, NVT], BF16, kind=kind).ap()
        self.YS = nc.dram_tensor("ys", [3 * 1024, SEQ], BF16, kind=kind).ap()
        self.XS = [nc.dram_tensor("xs%d" % i, [SEQ, D], F32, kind="Internal").ap() for i in range(2)]
        self.GG = nc.dram_tensor("gg", [DEPTH, D], F32, kind="Internal").ap()

    def setup(self):
        s, es = self.s, self.es
        self.ident = s.sb(es, "ident", [128, 128], F32)
        self.prot = s.sb(es, "prot", [128, 128], F32)
        self.ones_b = s.sb(es, "ones_b", [128, 128], BF16)
        self.ones_f = s.sb(es, "ones_f", [128, 128], F32)
        self.ustr = s.sb(es, "ustr", [128, 128], BF16)
        self.ovl = s.sb(es, "ovl", [128, 2, 64], F32)
        self.eall = s.sb(es, "eall", [64, SEQ], BF16)
        self.modp = s.sb(es, "modp", [128, DEPTH, 4, 16], F32)
        self.psum = [s.ps(es, "pb%d" % i, [128, 512], F32) for i in range(8)]
        s.dma("sp", self.ident[:], self.k_ident, writes=[self.ident])
        s.dma("sp", self.prot[:], self.k_prot, writes=[self.prot])
        s.dma("sp", self.ovl[:], self.k_ovl, writes=[self.ovl])
        s.dma("pool", self.eall[:], self.k_eall, writes=[self.eall])
        s.op("dve", lambda e: e.memset(self.ones_b[:], 1.0), writes=[self.ones_b])
        s.op("dve", lambda e: e.memset(self.ones_f[:], 1.0), writes=[self.ones_f])
        s.op("pool", lambda e: e.memset(self.ustr[:], 1.0), writes=[self.ustr])
        s.op("pool", lambda e: e.affine_select(self.ustr[:], self.ustr[:], [[-1, 128]], ALU.is_gt, 0.0,
                                               base=0, channel_multiplier=1),
             reads=[self.ustr], writes=[self.ustr])

    def phase0(self):
        s = self.s
        with ExitStack() as es:
            cs = s.sb(es, "cs", [128, 16], F32)
            wts = Ring([s.sb(es, "wada%d" % i, [128, 16, 512], F32) for i in range(2)])
            tmp = s.sb(es, "p0tmp", [128, 48], F32)
            bada = s.sb(es, "bada", [128, 48], F32)
            gpre = s.sb(es, "gpre", [128, 16], F32)
            gpost = s.sb(es, "gpost", [128, 16], F32)
            s.dma("sp", cs[:], self.c_in, writes=[cs])
            s.op("act", lambda e: e.activation(cs[:], cs[:], AF.Silu), reads=[cs], writes=[cs])
            pm = self.psum[0]
            for l in range(self.n_layers):
                s.dma("sp", bada[:], self.b_ada[l].rearrange("(j p) -> p j", p=128), writes=[bada],
                      allow_slow_non_contiguous=True)
                s.dma("sp", gpre[:], self.norm_pre_g[l].rearrange("(j p) -> p j", p=128), writes=[gpre],
                      allow_slow_non_contiguous=True)
                s.dma("sp", gpost[:], self.norm_post_g[l].rearrange("(j p) -> p j", p=128), writes=[gpost],
                      allow_slow_non_contiguous=True)
                for g in range(12):
                    wt = wts.next()
                    s.dma("sp", wt[:], self.w_ada[l, :, g * 512:(g + 1) * 512].rearrange("(j p) c -> p j c", p=128),
                          writes=[wt])
                    for cb in range(4):
                        col = g * 4 + cb
                        for j in range(16):
                            s.op("pe", lambda e, wt=wt, cb=cb, j=j, col=col: e.matmul(
                                pm[:, col:col + 1], wt[:, j, cb * 128:(cb + 1) * 128], cs[:, j:j + 1],
                                start=(j == 0), stop=(j == 15)), reads=[wt, cs], writes=[pm])
                s.op("dve", lambda e: e.tensor_tensor(tmp[:], pm[:, 0:48], bada[:], ALU.add),
                     reads=[pm, bada], writes=[tmp])
                mp = self.modp
                s.op("dve", lambda e, l=l: e.scalar_tensor_tensor(mp[:, l, 0, :], tmp[:, 16:32], 1.0, gpre[:],
                                                                    ALU.add, ALU.mult),
                     reads=[tmp, gpre], writes=[mp])
                s.op("dve", lambda e, l=l: e.tensor_copy(mp[:, l, 1, :], tmp[:, 0:16]), reads=[tmp], writes=[mp])
                s.op("dve", lambda e, l=l: e.tensor_tensor(mp[:, l, 2, :], tmp[:, 32:48], gpost[:], ALU.mult),
                     reads=[tmp, gpost], writes=[mp])
                s.dma("sp", self.GG[l].rearrange("(j p) -> p j", p=128), mp[:, l, 2, :], reads=[mp],
                      allow_slow_non_contiguous=True)
            s.barrier()

    def phase12(self, l, half, xsrc):
        s = self.s
        T0 = half * 2048
        with ExitStack() as es:
            hT = [s.sb(es, "hT%d" % i, [128, 16, 512], BF16) for i in range(4)]
            xt = s.sb(es, "xt", [128, 4, 2048], F32)
            junk = s.sb(es, "junk", [128, 2048], BF16)
            ss = s.sb(es, "ss", [128, 4], F32)
            rstd = s.sb(es, "rstd", [128, 4], F32)
            ropec = s.sb(es, "ropec", [128, 2048], F32)
            ropes = s.sb(es, "ropes", [128, 2048], F32)
            wts = Ring([s.sb(es, "wt%d" % i, [128, 16, 512], BF16) for i in range(3)])
            qf = Ring([s.sb(es, "qf%d" % i, [128, 512], F32) for i in range(2)])
            t1 = Ring([s.sb(es, "t1%d" % i, [128, 512], F32) for i in range(2)])
            t2 = Ring([s.sb(es, "t2%d" % i, [128, 512], F32) for i in range(2)])
            stg = Ring([s.sb(es, "stg%d" % i, [128, 512], BF16) for i in range(4)])
            pacc = Ring(self.psum[0:4])
            prot_ps = Ring(self.psum[4:6])
            ptr = Ring(self.psum[6:8])
            mp = self.modp
            s.dma("sp", ropec[:], self.k_ropec[:, T0:T0 + 2048], writes=[ropec])
            s.dma("sp", ropes[:], self.k_ropes[:, T0:T0 + 2048], writes=[ropes])
            for ti in range(4):
                t0 = T0 + ti * 512
                s.dma("sp", xt[:], xsrc[t0:t0 + 512, :].rearrange("(b p) d -> p b d", p=128), writes=[xt])
                for b in range(4):
                    s.op("act", lambda e, b=b: e.activation(junk[:], xt[:, b, :], AF.Square,
                                                            accum_out=ss[:, b:b + 1]),
                         reads=[xt], writes=[junk, ss])
                s.op("act", lambda e: e.activation(rstd[:], ss[:], AF.Ln, scale=1.0 / D, bias=1e-6),
                     reads=[ss], writes=[rstd])
                s.op("act", lambda e: e.activation(rstd[:], rstd[:], AF.Exp, scale=-0.5),
                     reads=[rstd], writes=[rstd])
                for b in range(4):
                    s.op("dve", lambda e, b=b: e.tensor_scalar(xt[:, b, :], xt[:, b, :], rstd[:, b:b + 1], None,
                                                               ALU.mult),
                         reads=[xt, rstd], writes=[xt])
                for j in range(16):
                    pt = ptr.next()
                    for b in range(4):
                        s.op("pe", lambda e, b=b, j=j, pt=pt: e.transpose(
                            pt[:, b * 128:(b + 1) * 128], xt[:, b, j * 128:(j + 1) * 128], self.ident[:]),
                            reads=[xt, self.ident], writes=[pt])
                    s.op("dve", lambda e, j=j, pt=pt, ti=ti: e.tensor_scalar(
                        hT[ti][:, j, :], pt[:], mp[:, l, 0, j:j + 1], mp[:, l, 1, j:j + 1], ALU.mult, ALU.add),
                        reads=[pt, mp], writes=[hT[ti]])
            for (name, off, gw, kind) in GROUPS:
                c0 = COL[name] + off
                wt = wts.next()
                s.dma("pool", wt[:, :, 0:gw], self.w_in[l, :, c0:c0 + gw].rearrange("(j p) c -> p j c", p=128),
                      writes=[wt])
                if kind == "v":
                    vc0 = VTCOL[name] + off
                    for tb in range(16):
                        pa = pacc.next()
                        ti, bb = tb // 4, tb % 4
                        for j in range(16):
                            s.op("pe", lambda e, pa=pa, ti=ti, bb=bb, j=j, wt=wt: e.matmul(
                                pa[:, 0:gw], hT[ti][:, j, bb * 128:(bb + 1) * 128], wt[:, j, 0:gw],
                                start=(j == 0), stop=(j == 15)), reads=[hT[ti], wt], writes=[pa])
                        st = stg.next()
                        eng = "act" if tb % 2 == 0 else "dve"
                        if eng == "act":
                            s.op("act", lambda e, st=st, pa=pa: e.copy(st[:, 0:gw], pa[:, 0:gw]),
                                 reads=[pa], writes=[st])
                        else:
                            s.op("dve", lambda e, st=st, pa=pa: e.tensor_copy(st[:, 0:gw], pa[:, 0:gw]),
                                 reads=[pa], writes=[st])
                        tt = T0 + tb * 128
                        s.dma("sp", self.VT[tt:tt + 128, vc0:vc0 + gw], st[:, 0:gw], reads=[st])
                    continue
                nblk = (gw + 127) // 128
                for blk in range(nblk):
                    bw = min(128, gw - blk * 128)
                    r0 = FMROWS[name] + off + blk * 128
                    for ti in range(4):
                        t0 = T0 + ti * 512
                        tl = ti * 512
                        pa = pacc.next()
                        for j in range(16):
                            s.op("pe", lambda e, pa=pa, ti=ti, j=j, wt=wt, blk=blk, bw=bw: e.matmul(
                                pa[0:bw, :], wt[:, j, blk * 128:blk * 128 + bw], hT[ti][:, j, :],
                                start=(j == 0), stop=(j == 15)), reads=[hT[ti], wt], writes=[pa])
                        st = stg.next()
                        if kind == "fm":
                            s.op("dve", lambda e, st=st, pa=pa: e.tensor_copy(st[:], pa[:]), reads=[pa], writes=[st])
                        elif kind == "silu":
                            s.op("act", lambda e, st=st, pa=pa: e.activation(st[:], pa[:], AF.Silu),
                                 reads=[pa], writes=[st])
                        elif kind == "sig":
                            s.op("act", lambda e, st=st, pa=pa, bw=bw: e.activation(st[0:bw, :], pa[0:bw, :], AF.Sigmoid),
                                 reads=[pa], writes=[st])
                        elif kind == "rope":
                            q = qf.next()
                            pr = prot_ps.next()
                            a1 = t1.next()
                            a2 = t2.next()
                            s.op("act", lambda e, q=q, pa=pa: e.copy(q[:], pa[:]), reads=[pa], writes=[q])
                            s.op("pe", lambda e, q=q, pr=pr: e.matmul(pr[:], self.prot[:], q[:], start=True, stop=True),
                                 reads=[q, self.prot], writes=[pr])
                            s.op("pool", lambda e, q=q, a1=a1, tl=tl: e.tensor_tensor(
                                a1[:], q[:], ropec[:, tl:tl + 512], ALU.mult), reads=[q, ropec], writes=[a1])
                            s.op("dve", lambda e, pr=pr, a2=a2, tl=tl: e.tensor_tensor(
                                a2[:], pr[:], ropes[:, tl:tl + 512], ALU.mult), reads=[pr, ropes], writes=[a2])
                            s.op("pool", lambda e, a1=a1, a2=a2, st=st: e.tensor_tensor(st[:], a1[:], a2[:], ALU.add),
                                 reads=[a1, a2], writes=[st])
                        s.dma("sp", self.FM[r0:r0 + bw, t0:t0 + 512], st[0:bw, :], reads=[st])
            s.barrier()

    def _causal(self, t, k0, t0, npart=128):
        self.s.op("pool", lambda e: e.affine_select(t[0:npart, :], t[0:npart, :], [[1, 512]], ALU.is_ge, 0.0,
                                                    base=t0 - k0, channel_multiplier=-1),
                  reads=[t], writes=[t])

    def _softmax_attn(self, qT, blocks, p_ring, ps_ring, psum_sum, psum_o, scale=SCALE):
        s = self.s
        nb = len(blocks)
        for bi, blk in enumerate(blocks):
            ps = ps_ring.next()
            kT_t, kT_ap = blk["kT"]
            bias = blk.get("bias")
            s.op("pe", lambda e, ps=ps, kT_ap=kT_ap: e.matmul(ps[:], kT_ap, qT[:], start=True, stop=(bias is None)),
                 reads=[kT_t, qT], writes=[ps])
            if bias is not None:
                bl_t, bl_ap, br_t, br_ap = bias
                s.op("pe", lambda e, ps=ps, bl_ap=bl_ap, br_ap=br_ap: e.matmul(ps[:], bl_ap, br_ap, start=False, stop=True),
                     reads=[bl_t, br_t], writes=[ps])
            p = p_ring.next()
            s.op("act", lambda e, p=p, ps=ps: e.activation(p[:], ps[:], AF.Exp, scale=scale), reads=[ps], writes=[p])
            if blk.get("mask") is not None:
                blk["mask"](p)
            s.op("pe", lambda e, p=p: e.matmul(psum_sum[:], self.ones_b[:], p[:], start=(bi == 0), stop=(bi == nb - 1)),
                 reads=[self.ones_b, p], writes=[psum_sum])
            for oi, (v_t, v_ap) in enumerate(blk["v"]):
                po = psum_o[oi]
                s.op("pe", lambda e, p=p, po=po, v_ap=v_ap: e.matmul(po[:], v_ap, p[:], start=(bi == 0), stop=(bi == nb - 1)),
                     reads=[v_t, p], writes=[po])

    def phase3_diff(self, l, hs):
        s = self.s
        lam_init = 0.8 - 0.6 * math.exp(-0.3 * l)
        with ExitStack() as es:
            kT = [Ring([s.sb(es, "akT%d_%d" % (c, i), [128, SEQ], BF16) for i in range(2)]) for c in range(2)]
            vv = Ring([s.sb(es, "avv%d" % i, [128, NKB, 256], BF16) for i in range(2)])
            qTs = Ring([s.sb(es, "aq%d" % i, [128, 512], BF16) for i in range(4)])
            szs = Ring([s.sb(es, "asz%d" % i, [128, 512], BF16) for i in range(4)])
            p_ring = Ring([s.sb(es, "ap%d" % i, [128, 512], BF16) for i in range(3)])
            oc = [[s.sb(es, "aoc%d%d" % (c, h), [128, 512], F32) for h in range(2)] for c in range(2)]
            rs = s.sb(es, "ars", [128, 512], F32)
            sq = [s.sb(es, "asq%d" % h, [128, 512], F32) for h in range(2)]
            rstd = s.sb(es, "arstd", [128, 512], F32)
            yst = Ring([s.sb(es, "ayst%d" % i, [128, 512], BF16) for i in range(2)])
            lamt = s.sb(es, "lamt", [128, 4], F32)
            lam2 = s.sb(es, "lam2", [128, 2], F32)
            neglam = s.sb(es, "neglam", [128, 1], F32)
            gco = s.sb(es, "gco", [128, 2], F32)
            ps_ring = Ring(self.psum[0:2])
            psum_sum = self.psum[2]
            psum_o = self.psum[4:6]
            pmisc = self.psum[6]
            s.dma("sp", lamt[:], self.lam_in[l].rearrange("k p -> p k"), writes=[lamt], allow_slow_non_contiguous=True)
            s.op("dve", lambda e: e.tensor_tensor(lam2[:, 0:1], lamt[:, 0:1], lamt[:, 1:2], ALU.mult), reads=[lamt], writes=[lam2])
            s.op("dve", lambda e: e.tensor_tensor(lam2[:, 1:2], lamt[:, 2:3], lamt[:, 3:4], ALU.mult), reads=[lamt], writes=[lam2])
            s.op("pe", lambda e: e.matmul(pmisc[:, 0:2], self.ones_f[:], lam2[:], start=True, stop=True),
                 reads=[self.ones_f, lam2], writes=[pmisc])
            s.op("act", lambda e: e.activation(lam2[:], pmisc[:, 0:2], AF.Exp), reads=[pmisc], writes=[lam2])
            s.op("dve", lambda e: e.scalar_tensor_tensor(neglam[:], lam2[:, 1:2], -lam_init, lam2[:, 0:1], ALU.add, ALU.subtract),
                 reads=[lam2], writes=[neglam])
            s.dma("sp", gco[:], self.diff_norm_g[l].rearrange("(h p) -> p h", p=128), writes=[gco], allow_slow_non_contiguous=True)
            s.op("dve", lambda e: e.tensor_scalar(gco[:], gco[:], 1.0 - lam_init, None, ALU.mult), reads=[gco], writes=[gco])
            for h in (2 * hs, 2 * hs + 1):
                kts = []
                for c in range(2):
                    kt = kT[c].next()
                    r0 = FMROWS["ak"] + h * 256 + c * 128
                    s.dma("sp", kt[:], self.FM[r0:r0 + 128, :], writes=[kt])
                    kts.append(kt)
                v = vv.next()
                vc = VTCOL["av"] + h * 256
                s.dma("sp", v[:], self.VT[:, vc:vc + 256].rearrange("(kb p) e -> p kb e", p=128), writes=[v])
                for i in range(NQT):
                    t0 = i * 512
                    for c in range(2):
                        q = qTs.next()
                        r0 = FMROWS["aq"] + h * 256 + c * 128
                        s.dma("sp", q[:], self.FM[r0:r0 + 128, t0:t0 + 512], writes=[q])
                        blocks = []
                        for kb in range(4 * i + 4):
                            k0 = kb * 128
                            blk = dict(kT=(kts[c], kts[c][:, k0:k0 + 128]),
                                       v=[(v, v[:, kb, 0:128]), (v, v[:, kb, 128:256])])
                            if kb >= 4 * i:
                                blk["mask"] = (lambda p, k0=k0, t0=t0: self._causal(p, k0, t0))
                            blocks.append(blk)
                        self._softmax_attn(q, blocks, p_ring, ps_ring, psum_sum, psum_o)
                        s.op("dve", lambda e: e.reciprocal(rs[:], psum_sum[:]), reads=[psum_sum], writes=[rs])
                        for hf in range(2):
                            s.op("dve", lambda e, c=c, hf=hf: e.tensor_tensor(oc[c][hf][:], psum_o[hf][:], rs[:], ALU.mult),
                                 reads=[psum_o[hf], rs], writes=[oc[c][hf]])
                    for hf in range(2):
                        s.op("dve", lambda e, hf=hf: e.scalar_tensor_tensor(
                            oc[0][hf][:], oc[1][hf][:], neglam[:, 0:1], oc[0][hf][:], ALU.mult, ALU.add),
                            reads=[oc[1][hf], neglam, oc[0][hf]], writes=[oc[0][hf]])
                        s.op("pool", lambda e, hf=hf: e.tensor_tensor(sq[hf][:], oc[0][hf][:], oc[0][hf][:], ALU.mult),
                             reads=[oc[0][hf]], writes=[sq[hf]])
                    for hf in range(2):
                        s.op("pe", lambda e, hf=hf: e.matmul(pmisc[:], self.ones_f[:], sq[hf][:], start=(hf == 0), stop=(hf == 1)),
                             reads=[self.ones_f, sq[hf]], writes=[pmisc])
                    s.op("act", lambda e: e.activation(rstd[:], pmisc[:], AF.Ln, scale=1.0 / 256, bias=1e-5),
                         reads=[pmisc], writes=[rstd])
                    s.op("act", lambda e: e.activation(rstd[:], rstd[:], AF.Exp, scale=-0.5), reads=[rstd], writes=[rstd])
                    for hf in range(2):
                        sz = szs.next()
                        rz = FMROWS["az"] + h * 256 + hf * 128
                        s.dma("sp", sz[:], self.FM[rz:rz + 128, t0:t0 + 512], writes=[sz])
                        s.op("dve", lambda e, hf=hf: e.scalar_tensor_tensor(
                            oc[0][hf][:], oc[0][hf][:], gco[:, hf:hf + 1], rstd[:], ALU.mult, ALU.mult),
                            reads=[oc[0][hf], gco, rstd], writes=[oc[0][hf]])
                        y = yst.next()
                        s.op("pool", lambda e, hf=hf, y=y, sz=sz: e.tensor_tensor(y[:], oc[0][hf][:], sz[:], ALU.mult),
                             reads=[oc[0][hf], sz], writes=[y])
                        ry = 0 * 1024 + h * 256 + hf * 128
                        s.dma("sp", self.YS[ry:ry + 128, t0:t0 + 512], y[:], reads=[y])
            s.barrier()

    def phase3_sb(self, l, hs):
        s = self.s
        with ExitStack() as es:
            kTr = Ring([s.sb(es, "ckT%d" % i, [128, SEQ], BF16) for i in range(2)])
            vvr = Ring([s.sb(es, "cvv%d" % i, [128, NKB, 128], BF16) for i in range(2)])
            qTs = Ring([s.sb(es, "cq%d" % i, [128, 512], BF16) for i in range(3)])
            szs = Ring([s.sb(es, "csz%d" % i, [128, 512], BF16) for i in range(3)])
            er = Ring([s.sb(es, "ce%d" % i, [128, 512], F32) for i in range(3)])
            lfr = Ring([s.sb(es, "clf%d" % i, [128, 512], F32) for i in range(3)])
            lbr = Ring([s.sb(es, "clb%d" % i, [128, 512], BF16) for i in range(3)])
            argr = Ring([s.sb(es, "carg%d" % i, [128, 512], F32) for i in range(3)])
            wr = Ring([s.sb(es, "cw%d" % i, [128, 512], F32) for i in range(3)])
            ar = Ring([s.sb(es, "ca%d" % i, [128, 512], BF16) for i in range(3)])
            R = s.sb(es, "cR", [128, 512], F32)
            yst = Ring([s.sb(es, "cyst%d" % i, [128, 512], BF16) for i in range(2)])
            zr = Ring(self.psum[0:2])
            c1r = Ring(self.psum[2:4])
            c2r = Ring(self.psum[4:6])
            po = self.psum[6]
            for h in range(4 * hs, 4 * hs + 4):
                kt = kTr.next()
                r0 = FMROWS["ck"] + h * 128
                s.dma("sp", kt[:], self.FM[r0:r0 + 128, :], writes=[kt])
                v = vvr.next()
                vc = VTCOL["cv"] + h * 128
                s.dma("sp", v[:], self.VT[:, vc:vc + 128].rearrange("(kb p) e -> p kb e", p=128), writes=[v])
                for i in range(NQT):
                    t0 = i * 512
                    q = qTs.next()
                    rq = FMROWS["cq"] + h * 128
                    s.dma("sp", q[:], self.FM[rq:rq + 128, t0:t0 + 512], writes=[q])
                    sz = szs.next()
                    rz = FMROWS["cz"] + h * 128
                    s.dma("sp", sz[:], self.FM[rz:rz + 128, t0:t0 + 512], writes=[sz])
                    s.op("pool", lambda e: e.memset(R[:], 0.0), writes=[R])
                    kbs = list(range(4 * i + 3, -1, -1))
                    for bi, kb in enumerate(kbs):
                        k0 = kb * 128
                        zp = zr.next()
                        s.op("pe", lambda e, zp=zp, k0=k0, q=q, kt=kt: e.matmul(zp[:], kt[:, k0:k0 + 128], q[:], start=True, stop=True),
                             reads=[kt, q], writes=[zp])
                        ee = er.next()
                        s.op("act", lambda e, ee=ee, zp=zp: e.activation(ee[:], zp[:], AF.Exp, scale=SCALE), reads=[zp], writes=[ee])
                        if kb >= 4 * i:
                            s.op("pool", lambda e, ee=ee, k0=k0, t0=t0: e.affine_select(
                                ee[:], ee[:], [[1, 512]], ALU.is_gt, 0.0, base=t0 - k0, channel_multiplier=-1),
                                reads=[ee], writes=[ee])
                        lf = lfr.next()
                        s.op("act", lambda e, lf=lf, ee=ee: e.activation(lf[:], ee[:], AF.Ln, bias=1.0), reads=[ee], writes=[lf])
                        lb = lbr.next()
                        s.op("pool", lambda e, lb=lb, lf=lf: e.tensor_copy(lb[:], lf[:]), reads=[lf], writes=[lb])
                        c1 = c1r.next()
                        c2 = c2r.next()
                        s.op("pe", lambda e, c1=c1, lb=lb: e.matmul(c1[:], self.ustr[:], lb[:], start=True, stop=True),
                             reads=[self.ustr, lb], writes=[c1])
                        s.op("pe", lambda e, c2=c2, lb=lb: e.matmul(c2[:], self.ones_b[:], lb[:], start=True, stop=True),
                             reads=[self.ones_b, lb], writes=[c2])
                        arg = argr.next()
                        s.op("dve", lambda e, arg=arg, c1=c1: e.tensor_tensor(arg[:], c1[:], R[:], ALU.add),
                             reads=[c1, R], writes=[arg])
                        s.op("dve", lambda e, c2=c2: e.tensor_tensor(R[:], c2[:], R[:], ALU.add), reads=[c2, R], writes=[R])
                        s.op("pool", lambda e, arg=arg, lf=lf: e.tensor_tensor(arg[:], arg[:], lf[:], ALU.add),
                             reads=[arg, lf], writes=[arg])
                        w = wr.next()
                        s.op("act", lambda e, w=w, arg=arg: e.activation(w[:], arg[:], AF.Exp, scale=-1.0), reads=[arg], writes=[w])
                        a = ar.next()
                        s.op("pool", lambda e, a=a, ee=ee, w=w: e.tensor_tensor(a[:], ee[:], w[:], ALU.mult),
                             reads=[ee, w], writes=[a])
                        s.op("pe", lambda e, a=a, kb=kb, bi=bi, v=v: e.matmul(po[:], v[:, kb, :], a[:], start=(bi == 0),
                                                                                stop=(bi == len(kbs) - 1)),
                             reads=[v, a], writes=[po])
                    y = yst.next()
                    s.op("dve", lambda e, y=y, sz=sz: e.tensor_tensor(y[:], po[:], sz[:], ALU.mult), reads=[po, sz], writes=[y])
                    ry = 2 * 1024 + h * 128
                    s.dma("sp", self.YS[ry:ry + 128, t0:t0 + 512], y[:], reads=[y])
            s.barrier()

    def phase3_nsa(self, l, g):
        s = self.s
        with ExitStack() as es:
            big = [s.sb(es, "bbig%d" % i, [128, SEQ], BF16) for i in range(4)]
            vs = s.sb(es, "bvs", [128, NKB, 128], BF16)
            vw = s.sb(es, "bvw", [128, NKB, 128], BF16)
            w1 = s.sb(es, "bw1", [128, 32, 128], BF16)
            w2 = s.sb(es, "bw2", [128, 128], BF16)
            pe_sb = s.sb(es, "bpe", [32, 128], F32)
            peT = s.sb(es, "bpeT", [128, 32], BF16)
            c1 = s.sb(es, "bc1", [128, 1], F32)
            hsl = s.sb(es, "bhsl", [128, 256], BF16)
            kcmpT = s.sb(es, "bkcmpT", [128, 256], BF16)
            vcmp = s.sb(es, "bvcmp", [128, 2, 128], BF16)
            qTs = [s.sb(es, "bq%d" % i, [128, 512], BF16) for i in range(4)]
            gts = Ring([s.sb(es, "bgt%d" % i, [128, 3, 512], BF16) for i in range(2)])
            szs = Ring([s.sb(es, "bsz%d" % i, [128, 512], BF16) for i in range(2)])
            pf = [s.sb(es, "bpf%d" % i, [128, 512], F32) for i in range(2)]
            pnb = Ring([s.sb(es, "bpnb%d" % i, [128, 512], BF16) for i in range(2)])
            rs = s.sb(es, "brs", [128, 512], F32)
            ocmp = [s.sb(es, "bocmp%d" % i, [128, 512], F32) for i in range(4)]
            impS = s.sb(es, "bimp", [128, 4, 64], F32)
            m8 = s.sb(es, "bm8", [128, 16], F32)
            wk = s.sb(es, "bwk", [128, 64], F32)
            sel = s.sb(es, "bsel", [128, 64], F32)
            negT = s.sb(es, "bnegT", [64, 512], BF16)
            p_ring = Ring([s.sb(es, "bp%d" % i, [128, 512], BF16) for i in range(3)])
            acc = s.sb(es, "bacc", [128, 512], F32)
            tmp = s.sb(es, "btmp", [128, 512], F32)
            yst = Ring([s.sb(es, "byst%d" % i, [128, 512], BF16) for i in range(2)])
            ps_ring = Ring(self.psum[0:2])
            psum_sum = self.psum[2]
            pimp = self.psum[3]
            psum_o = [self.psum[4]]
            pmisc = self.psum[5]
            pmisc2 = self.psum[6]
            kcT, vcT, ksT, kwT = big
            for t_, nm in ((kcT, "bkc"), (vcT, "bvc"), (ksT, "bks"), (kwT, "bkw")):
                r0 = FMROWS[nm] + g * 128
                s.dma("sp", t_[:], self.FM[r0:r0 + 128, :], writes=[t_])
            for t_, nm in ((vs, "bvs"), (vw, "bvw")):
                vc = VTCOL[nm] + g * 128
                s.dma("sp", t_[:], self.VT[:, vc:vc + 128].rearrange("(kb p) e -> p kb e", p=128), writes=[t_])
            for kv in range(2):
                src = kcT if kv == 0 else vcT
                s.dma("pool", w1[:], self.cmp_w1[kv][l].rearrange("(l d) f -> d l f", d=128), writes=[w1])
                s.dma("pool", w2[:], self.cmp_w2[kv][l], writes=[w2])
                s.dma("sp", pe_sb[:], self.cmp_pe[kv][l], writes=[pe_sb])
                s.op("pe", lambda e: e.transpose(pmisc[:, 0:32], pe_sb[:], self.ident[0:32, 0:32]),
                     reads=[pe_sb, self.ident], writes=[pmisc])
                s.op("dve", lambda e: e.tensor_copy(peT[:], pmisc[:, 0:32]), reads=[pmisc], writes=[peT])
                for li in range(32):
                    s.op("pe", lambda e, li=li: e.matmul(pmisc2[:, 0:1], w1[:, li, :], peT[:, li:li + 1],
                                                         start=(li == 0), stop=(li == 31)),
                         reads=[w1, peT], writes=[pmisc2])
                s.op("dve", lambda e: e.tensor_copy(c1[:], pmisc2[:, 0:1]), reads=[pmisc2], writes=[c1])
                for li in range(32):
                    s.op("pe", lambda e, li=li, src=src: e.matmul(pmisc[:, 0:NCMP], w1[:, li, :],
                                                                   src[:, li:li + 16 * (NCMP - 1) + 1:16],
                                                                   start=(li == 0), stop=(li == 31)),
                         reads=[w1, src], writes=[pmisc])
                s.op("dve", lambda e: e.memset(hsl[:], 0.0), writes=[hsl])
                s.op("act", lambda e: e.activation(hsl[:, 0:NCMP], pmisc[:, 0:NCMP], AF.Silu, bias=c1[:, 0:1]),
                     reads=[pmisc, c1], writes=[hsl])
                if kv == 0:
                    s.op("pe", lambda e: e.matmul(pmisc2[:, 0:256], w2[:], hsl[:], start=True, stop=True),
                         reads=[w2, hsl], writes=[pmisc2])
                    s.op("dve", lambda e: e.tensor_copy(kcmpT[:], pmisc2[:, 0:256]), reads=[pmisc2], writes=[kcmpT])
                else:
                    for nb in range(2):
                        s.op("pe", lambda e, nb=nb: e.matmul(pmisc2[:, nb * 128:(nb + 1) * 128], hsl[:, nb * 128:(nb + 1) * 128],
                                                             w2[:], start=True, stop=True),
                             reads=[w2, hsl], writes=[pmisc2])
                    s.op("dve", lambda e: e.tensor_copy(vcmp[:], pmisc2[:, 0:256].rearrange("p (n d) -> p n d", n=2)),
                         reads=[pmisc2], writes=[vcmp])
            for i in range(NQT):
                t0 = i * 512
                nbs = [nb for nb in range(2) if 16 * nb * 128 + 31 <= t0 + 511]
                for r in range(4):
                    h = g * 4 + r
                    rq = FMROWS["bq"] + h * 128
                    s.dma("sp", qTs[r][:], self.FM[rq:rq + 128, t0:t0 + 512], writes=[qTs[r]])
                for r in range(4):
                    q = qTs[r]
                    for nb in nbs:
                        ps = ps_ring.next()
                        s.op("pe", lambda e, ps=ps, nb=nb, q=q: e.matmul(ps[:], kcmpT[:, nb * 128:(nb + 1) * 128], q[:],
                                                                          start=True, stop=True),
                             reads=[kcmpT, q], writes=[ps])
                        s.op("act", lambda e, ps=ps, nb=nb: e.activation(pf[nb][:], ps[:], AF.Exp, scale=SCALE),
                             reads=[ps], writes=[pf[nb]])
                        s.op("pool", lambda e, nb=nb, t0=t0: e.affine_select(
                            pf[nb][:], pf[nb][:], [[1, 512]], ALU.is_ge, 0.0,
                            base=t0 - 16 * nb * 128 - 31, channel_multiplier=-16), reads=[pf[nb]], writes=[pf[nb]])
                        s.op("pe", lambda e, nb=nb: e.matmul(psum_sum[:], self.ones_f[:], pf[nb][:], start=(nb == nbs[0]),
                                                             stop=(nb == nbs[-1])),
                             reads=[self.ones_f, pf[nb]], writes=[psum_sum])
                    s.op("dve", lambda e: e.tensor_scalar(rs[:], psum_sum[:], 1e-30, None, ALU.max), reads=[psum_sum], writes=[rs])
                    s.op("dve", lambda e: e.reciprocal(rs[:], rs[:]), reads=[rs], writes=[rs])
                    for nb in nbs:
                        s.op("dve", lambda e, nb=nb: e.tensor_tensor(pf[nb][:], pf[nb][:], rs[:], ALU.mult),
                             reads=[pf[nb], rs], writes=[pf[nb]])
                        for tb in range(4):
                            first = (r == 0 and nb == nbs[0])
                            last = (r == 3 and nb == nbs[-1])
                            s.op("pe", lambda e, nb=nb, tb=tb, first=first, last=last: e.matmul(
                                pimp[:, tb * 64:(tb + 1) * 64], pf[nb][:, tb * 128:(tb + 1) * 128], self.ovl[:, nb, :],
                                start=first, stop=last), reads=[pf[nb], self.ovl], writes=[pimp])
                        pb = pnb.next()
                        s.op("act", lambda e, pb=pb, nb=nb: e.copy(pb[:], pf[nb][:]), reads=[pf[nb]], writes=[pb])
                        s.op("pe", lambda e, pb=pb, nb=nb: e.matmul(psum_o[0][:], vcmp[:, nb, :], pb[:], start=(nb == nbs[0]),
                                                                     stop=(nb == nbs[-1])),
                             reads=[vcmp, pb], writes=[psum_o[0]])
                    s.op("dve", lambda e, r=r: e.tensor_copy(ocmp[r][:], psum_o[0][:]), reads=[psum_o[0]], writes=[ocmp[r]])
                s.op("dve", lambda e: e.tensor_copy(impS[:], pimp[:, 0:256].rearrange("p (a b) -> p a b", a=4)),
                     reads=[pimp], writes=[impS])
                for tb in range(4):
                    for hh in range(2):
                        tblk = 8 * i + 2 * tb + hh
                        p0 = hh * 64
                        if tblk < 63:
                            s.op("pool", lambda e, tb=tb, p0=p0, tblk=tblk: e.memset(impS[p0:p0 + 64, tb, tblk + 1:64], -1e30),
                                 reads=[impS], writes=[impS])
                        s.op("pool", lambda e, tb=tb, p0=p0, tblk=tblk: e.memset(impS[p0:p0 + 64, tb, tblk:tblk + 1], 1e6),
                             reads=[impS], writes=[impS])
                        s.op("pool", lambda e, tb=tb, p0=p0: e.memset(impS[p0:p0 + 64, tb, 0:1], 2e6),
                             reads=[impS], writes=[impS])
                for tb in range(4):
                    s.op("dve", lambda e, tb=tb: e.max(out=m8[:, 0:8], in_=impS[:, tb, :]), reads=[impS], writes=[m8])
                    s.op("dve", lambda e, tb=tb: e.match_replace(out=wk[:], in_to_replace=m8[:, 0:8], in_values=impS[:, tb, :],
                                                                 imm_value=-3e30), reads=[impS, m8], writes=[wk])
                    s.op("dve", lambda e: e.max(out=m8[:, 8:16], in_=wk[:]), reads=[wk], writes=[m8])
                    s.op("dve", lambda e, tb=tb: e.tensor_scalar(sel[:], impS[:, tb, :], m8[:, 15:16], None, ALU.is_ge),
                         reads=[impS, m8], writes=[sel])
                    s.op("dve", lambda e: e.tensor_scalar(sel[:], sel[:], -1.0, BIG, ALU.add, ALU.mult), reads=[sel], writes=[sel])
                    s.op("pe", lambda e: e.transpose(pmisc[0:64, 0:128], sel[:], self.ident[:]),
                         reads=[sel, self.ident], writes=[pmisc])
                    s.op("act", lambda e, tb=tb: e.copy(negT[:, tb * 128:(tb + 1) * 128], pmisc[0:64, 0:128]),
                         reads=[pmisc], writes=[negT])
                for r in range(4):
                    h = g * 4 + r
                    q = qTs[r]
                    gt = gts.next()
                    for k3 in range(3):
                        rg = FMROWS["bg"] + h * 3 + k3
                        s.dma("sp", gt[:, k3, :], self.FM[rg:rg + 1, t0:t0 + 512].to_broadcast([128, 512]), writes=[gt])
                    sz = szs.next()
                    rz = FMROWS["bz"] + h * 128
                    s.dma("sp", sz[:], self.FM[rz:rz + 128, t0:t0 + 512], writes=[sz])
                    s.op("pool", lambda e, r=r, gt=gt: e.tensor_tensor(acc[:], ocmp[r][:], gt[:, 0, :], ALU.mult),
                         reads=[ocmp[r], gt], writes=[acc])
                    blocks = []
                    for kb in range(4 * i + 4):
                        k0 = kb * 128
                        blk = dict(kT=(ksT, ksT[:, k0:k0 + 128]), v=[(vs, vs[:, kb, :])],
                                   bias=(self.eall, self.eall[:, k0:k0 + 128], negT, negT[:]))
                        if kb >= 4 * i:
                            blk["mask"] = (lambda p, k0=k0, t0=t0: self._causal(p, k0, t0))
                        blocks.append(blk)
                    self._softmax_attn(q, blocks, p_ring, ps_ring, psum_sum, psum_o)
                    s.op("dve", lambda e: e.reciprocal(rs[:], psum_sum[:]), reads=[psum_sum], writes=[rs])
                    s.op("dve", lambda e: e.tensor_tensor(tmp[:], psum_o[0][:], rs[:], ALU.mult), reads=[psum_o[0], rs], writes=[tmp])
                    s.op("pool", lambda e, gt=gt: e.tensor_tensor(tmp[:], tmp[:], gt[:, 1, :], ALU.mult), reads=[tmp, gt], writes=[tmp])
                    s.op("pool", lambda e: e.tensor_tensor(acc[:], acc[:], tmp[:], ALU.add), reads=[acc, tmp], writes=[acc])
                    blocks = []
                    for kb in range(max(0, 4 * i - 4), 4 * i + 4):
                        k0 = kb * 128
                        blk = dict(kT=(kwT, kwT[:, k0:k0 + 128]), v=[(vw, vw[:, kb, :])])
                        if kb >= 4 * i:
                            blk["mask"] = (lambda p, k0=k0, t0=t0: self._causal(p, k0, t0))
                        else:
                            blk["mask"] = (lambda p, k0=k0, t0=t0: s.op("pool", lambda e: e.affine_select(
                                p[:], p[:], [[-1, 512]], ALU.is_gt, 0.0, base=k0 - t0 + 512, channel_multiplier=1),
                                reads=[p], writes=[p]))
                        blocks.append(blk)
                    self._softmax_attn(q, blocks, p_ring, ps_ring, psum_sum, psum_o)
                    s.op("dve", lambda e: e.reciprocal(rs[:], psum_sum[:]), reads=[psum_sum], writes=[rs])
                    s.op("dve", lambda e: e.tensor_tensor(tmp[:], psum_o[0][:], rs[:], ALU.mult), reads=[psum_o[0], rs], writes=[tmp])
                    s.op("pool", lambda e, gt=gt: e.tensor_tensor(tmp[:], tmp[:], gt[:, 2, :], ALU.mult), reads=[tmp, gt], writes=[tmp])
                    s.op("pool", lambda e: e.tensor_tensor(acc[:], acc[:], tmp[:], ALU.add), reads=[acc, tmp], writes=[acc])
                    y = yst.next()
                    s.op("pool", lambda e, y=y, sz=sz: e.tensor_tensor(y[:], acc[:], sz[:], ALU.mult), reads=[acc, sz], writes=[y])
                    ry = 1 * 1024 + h * 128
                    s.dma("sp", self.YS[ry:ry + 128, t0:t0 + 512], y[:], reads=[y])
            s.barrier()

    def phase4(self, l, half, xsrc, xdst):
        s = self.s
        T0 = half * 2048
        with ExitStack() as es:
            mT = [s.sb(es, "mT%d" % i, [128, 16, 512], BF16) for i in range(4)]
            for tp in range(2):
              with ExitStack() as es2:
                ys = [s.sb(es2, "ysb%d" % n, [128, 8, 1024], BF16) for n in range(3)]
                wbr = Ring([s.sb(es2, "wb%d" % i, [128, 8, 512], BF16) for i in range(3)])
                mgr = Ring([s.sb(es2, "mg%d" % i, [128, 1024], BF16) for i in range(3)])
                macc = [s.sb(es2, "macc%d" % i, [128, 512], F32) for i in range(2)]
                tm = Ring([s.sb(es2, "tm%d" % i, [128, 512], F32) for i in range(2)])
                pacc = Ring(self.psum[0:8])
                TP = T0 + tp * 1024
                for n in range(3):
                    s.dma("sp", ys[n][:], self.YS[n * 1024:(n + 1) * 1024, TP:TP + 1024].rearrange("(k p) t -> p k t", p=128),
                          writes=[ys[n]])
                for cg in range(4):
                    wbs = []
                    for n in range(3):
                        wb = wbr.next()
                        s.dma("pool", wb[:], self.w_branch[l, n, :, cg * 512:(cg + 1) * 512].rearrange("(k p) c -> p k c", p=128),
                              writes=[wb])
                        wbs.append(wb)
                    for cb in range(4):
                        cc = cg * 4 + cb
                        for n in range(3):
                            mg = mgr.next()
                            rm = FMROWS["mg"] + n * 2048 + cc * 128
                            s.dma("sp", mg[:], self.FM[rm:rm + 128, TP:TP + 1024], writes=[mg])
                            for tl in range(2):
                                ti = tp * 2 + tl
                                pa = pacc.next()
                                for k in range(8):
                                    s.op("pe", lambda e, pa=pa, n=n, k=k, tl=tl, cb=cb: e.matmul(
                                        pa[:], wbs[n][:, k, cb * 128:(cb + 1) * 128], ys[n][:, k, tl * 512:(tl + 1) * 512],
                                        start=(k == 0), stop=(k == 7)), reads=[wbs[n], ys[n]], writes=[pa])
                                if n == 0:
                                    s.op("dve", lambda e, pa=pa, tl=tl, mg=mg: e.tensor_tensor(
                                        macc[tl][:], pa[:], mg[:, tl * 512:(tl + 1) * 512], ALU.mult),
                                        reads=[pa, mg], writes=[macc[tl]])
                                else:
                                    t_ = tm.next()
                                    s.op("dve", lambda e, pa=pa, tl=tl, mg=mg, t_=t_: e.tensor_tensor(
                                        t_[:], pa[:], mg[:, tl * 512:(tl + 1) * 512], ALU.mult),
                                        reads=[pa, mg], writes=[t_])
                                    if n == 1:
                                        s.op("pool", lambda e, tl=tl, t_=t_: e.tensor_tensor(macc[tl][:], macc[tl][:], t_[:], ALU.add),
                                             reads=[macc[tl], t_], writes=[macc[tl]])
                                    else:
                                        s.op("pool", lambda e, tl=tl, ti=ti, t_=t_, cc=cc: e.tensor_tensor(
                                            mT[ti][:, cc, :], macc[tl][:], t_[:], ALU.add),
                                            reads=[macc[tl], t_], writes=[mT[ti]])
                s.barrier()
            with ExitStack() as es3:
                wo = s.sb(es3, "wo", [128, 16, 2048], BF16)
                ggr = s.sb(es3, "ggr", [128, 2048], F32)
                xr = Ring([s.sb(es3, "xr%d" % i, [128, 2048], F32) for i in range(2)])
                yr = Ring([s.sb(es3, "yr%d" % i, [128, 2048], F32) for i in range(2)])
                junk = s.sb(es3, "junk4", [128, 512], BF16)
                ss = Ring([s.sb(es3, "ss4%d" % i, [128, 4], F32) for i in range(2)])
                rstd = Ring([s.sb(es3, "rstd4%d" % i, [128, 1], F32) for i in range(2)])
                for k4 in range(4):
                    s.dma("pool", wo[:, k4 * 4:(k4 + 1) * 4, :],
                          self.w_out[l, k4 * 512:(k4 + 1) * 512, :].rearrange("(k p) c -> p k c", p=128), writes=[wo])
                s.dma("sp", ggr[:], self.GG[l:l + 1, :].to_broadcast([128, 2048]), writes=[ggr])
                for tb in range(16):
                    ti, bb = tb // 4, tb % 4
                    tt = T0 + tb * 128
                    x = xr.next()
                    s.dma("sp", x[:], xsrc[tt:tt + 128, :], writes=[x])
                    half_banks = self.psum[0:4] if tb % 2 == 0 else self.psum[4:8]
                    sst = ss.next()
                    for cb in range(4):
                        pa = half_banks[cb]
                        for k in range(16):
                            s.op("pe", lambda e, pa=pa, k=k, ti=ti, bb=bb, cb=cb: e.matmul(
                                pa[:], mT[ti][:, k, bb * 128:(bb + 1) * 128], wo[:, k, cb * 512:(cb + 1) * 512],
                                start=(k == 0), stop=(k == 15)), reads=[mT[ti], wo], writes=[pa])
                        s.op("act", lambda e, pa=pa, cb=cb, sst=sst: e.activation(junk[:], pa[:], AF.Square,
                                                                                 accum_out=sst[:, cb:cb + 1]),
                             reads=[pa], writes=[junk, sst])
                    rt = rstd.next()
                    s.op("dve", lambda e, sst=sst, rt=rt: e.tensor_reduce(rt[:], sst[:], mybir.AxisListType.X, ALU.add),
                         reads=[sst], writes=[rt])
                    s.op("act", lambda e, rt=rt: e.activation(rt[:], rt[:], AF.Ln, scale=1.0 / D, bias=1e-6), reads=[rt], writes=[rt])
                    s.op("act", lambda e, rt=rt: e.activation(rt[:], rt[:], AF.Exp, scale=-0.5), reads=[rt], writes=[rt])
                    y = yr.next()
                    for cb in range(4):
                        pa = half_banks[cb]
                        s.op("dve", lambda e, pa=pa, cb=cb, y=y, rt=rt: e.scalar_tensor_tensor(
                            y[:, cb * 512:(cb + 1) * 512], pa[:], rt[:, 0:1], ggr[:, cb * 512:(cb + 1) * 512], ALU.mult, ALU.mult),
                            reads=[pa, rt, ggr], writes=[y])
                    s.op("pool", lambda e, y=y, x=x: e.tensor_tensor(y[:], y[:], x[:], ALU.add), reads=[y, x], writes=[y])
                    s.dma("sp", xdst[tt:tt + 128, :], y[:], reads=[y])
                s.barrier()

    def build(self, phases=("0", "12", "3a", "3b", "3c", "4")):
        self.declare()
        self.setup()
        if "0" in phases:
            self.phase0()
        for l in range(self.n_layers):
            xsrc = self.x_in if l == 0 else self.XS[(l - 1) % 2]
            xdst = self.out if l == self.n_layers - 1 else self.XS[l % 2]
            if "12" in phases:
                for half in self.halves:
                    self.phase12(l, half, xsrc)
            for hs in self.headsets:
                if "3a" in phases:
                    self.phase3_diff(l, hs)
                if "3b" in phases:
                    self.phase3_nsa(l, hs)
                if "3c" in phases:
                    self.phase3_sb(l, hs)
            if "4" in phases:
                for half in self.halves:
                    self.phase4(l, half, xsrc, xdst)
        self.s.barrier()
        self.es.close()
        return self.nc


def make_in_maps(inputs, n_cores):
    k = _constants()
    f = lambda a: np.ascontiguousarray(np.asarray(a, dtype=np.float32))
    lam = np.stack([f(inputs["lambda_q1"]), f(inputs["lambda_k1"]), f(inputs["lambda_q2"]), f(inputs["lambda_k2"])], axis=1)
    shared = dict(
        norm_pre_g=f(inputs["norm_pre_g"]), norm_post_g=f(inputs["norm_post_g"]), w_ada=f(inputs["w_ada"]),
        b_ada=f(inputs["b_ada"]), w_in=f(inputs["w_in"]), lam=np.ascontiguousarray(lam),
        diff_norm_g=f(inputs["diff_norm_g"]), cmp_pe_k=f(inputs["cmp_pe_k"]), cmp_pe_v=f(inputs["cmp_pe_v"]),
        cmp_w1_k=f(inputs["cmp_w1_k"]), cmp_w1_v=f(inputs["cmp_w1_v"]), cmp_w2_k=f(inputs["cmp_w2_k"]),
        cmp_w2_v=f(inputs["cmp_w2_v"]), w_branch=f(inputs["w_branch"]), w_out=f(inputs["w_out"]),
        k_ident=k["ident"], k_prot=k["prot"], k_ropec=k["ropec"], k_ropes=k["ropes"], k_ovl=k["ovl"], k_eall=k["eall"])
    x = f(inputs["x"])
    c = f(inputs["c"])
    maps = []
    for b in range(n_cores):
        m = dict(shared)
        m["x"] = np.ascontiguousarray(x[b])
        m["c"] = np.ascontiguousarray(c[b].reshape(16, 128).T)
        maps.append(m)
    return maps


def kernel(**inputs):
    n_cores = 4
    nc = Builder().build()
    maps = make_in_maps(inputs, n_cores)
    res = run_bass_kernel_spmd(nc, maps, core_ids=list(range(n_cores)))
    return np.stack([np.asarray(r["out"]) for r in res.results], axis=0).astype(np.float32)
```

```python
import math
from contextlib import ExitStack

import numpy as np
import concourse.bass as bass
import concourse.mybir as mybir
from concourse.bass_utils import run_bass_kernel_spmd

F32 = mybir.dt.float32
BF16 = mybir.dt.bfloat16
AF = mybir.ActivationFunctionType
ALU = mybir.AluOpType

D = 2048
SEQ = 4096
DEPTH = 4
HD = 128
N_IN = 17944
NQT = SEQ // 512
NKB = SEQ // 128
SCALE = HD ** -0.5
NCMP = 255
BIG = 30000.0

COLG = dict(aq=0, ak=1024, av=2048, az=3072, bq=4096, bkc=5120, bvc=5376, bks=5632, bvs=5888,
            bkw=6144, bvw=6400, bg=6656, bz=6680, cq=7704, ck=8728, cv=9752, cz=10776, mg=11800)
LOCAL = (("aq", 512), ("ak", 512), ("av", 512), ("az", 512), ("bq", 512), ("bkc", 128), ("bvc", 128),
         ("bks", 128), ("bvs", 128), ("bkw", 128), ("bvw", 128), ("bg", 12), ("bz", 512),
         ("cq", 512), ("ck", 512), ("cv", 512), ("cz", 512), ("mg", 6144))
COL = {}
_c = 0
for _n, _w in LOCAL:
    COL[_n] = _c
    _c += _w
NLOC = _c


def local_cols(hs):
    idx = []
    for n, w in LOCAL:
        g0 = COLG[n] + (0 if n == "mg" else hs * w)
        idx.append(np.arange(g0, g0 + w))
    return np.concatenate(idx)


FMROWS = {}
_r = 0
for _n, _w in (("aq", 512), ("ak", 512), ("az", 512), ("bq", 512), ("bkc", 128), ("bvc", 128),
               ("bks", 128), ("bkw", 128), ("bg", 128), ("bz", 512), ("cq", 512), ("ck", 512),
               ("cz", 512), ("mg", 6144)):
    FMROWS[_n] = _r
    _r += _w
NFM = _r
VTCOL = dict(av=0, bvs=512, bvw=640, cv=768)
NVT = 1280
NYS = 1536
YSROW = dict(a=0, b=512, c=1024)
GROUPS = [("aq", 0, 512, "rope"), ("ak", 0, 512, "rope"), ("av", 0, 512, "v"), ("az", 0, 512, "silu"),
          ("bq", 0, 512, "rope"), ("bkc", 0, 128, "rope"), ("bvc", 0, 128, "fm"), ("bks", 0, 128, "rope"),
          ("bvs", 0, 128, "v"), ("bkw", 0, 128, "rope"), ("bvw", 0, 128, "v"), ("bg", 0, 12, "sig"),
          ("bz", 0, 512, "silu"), ("cq", 0, 512, "fm"), ("ck", 0, 512, "fm"), ("cv", 0, 512, "v"),
          ("cz", 0, 512, "silu")]
GROUPS += [("mg", 512 * _i, 512, "sig") for _i in range(12)]


class T:
    def __init__(self, ap, name=""):
        self.ap = ap
        self.name = name
        self.lw = None
        self.rd = []

    def __getitem__(self, idx):
        return self.ap[idx]


class Sched:
    ENG = ("pe", "act", "dve", "pool", "sp")

    def __init__(self, nc, es, n_dma_sems=4):
        self.nc = nc
        self.es = es
        self.eng = {"pe": nc.tensor, "act": nc.scalar, "dve": nc.vector, "pool": nc.gpsimd, "sp": nc.sync}
        self.sem = {}
        self.cnt = {}
        for e in self.ENG:
            self.sem[e] = es.enter_context(nc.semaphore("s_" + e))
            self.cnt[e] = 0
        self.dsem = {}
        for q in ("sp", "pool", "act"):
            lst = []
            for i in range(n_dma_sems):
                k = "d_%s%d" % (q, i)
                self.sem[k] = es.enter_context(nc.semaphore(k))
                self.cnt[k] = 0
                lst.append(k)
            self.dsem[q] = [lst, 0]
        self.waited = {}
        self.n_ins = 0

    def sb(self, es, name, shape, dt):
        self.uid = getattr(self, "uid", 0) + 1
        name = "%s_u%d" % (name, self.uid)
        return T(es.enter_context(self.nc.sbuf_tensor(name, list(shape), dt)), name)

    def ps(self, es, name, shape, dt=F32):
        return T(es.enter_context(self.nc.psum_tensor(name, list(shape), dt)), name)

    def _wait(self, e, key, val):
        if val <= 0 or self.waited.get((e, key), 0) >= val:
            return
        self.eng[e].wait_ge(self.sem[key], val)
        self.waited[(e, key)] = val

    def _deps(self, e, reads, writes):
        for r in reads:
            if r.lw is not None:
                self._wait(e, *r.lw)
        for w in writes:
            if w.lw is not None and w.lw[0] != e:
                self._wait(e, *w.lw)
            for (k, v) in w.rd:
                if k != e:
                    self._wait(e, k, v)

    def _mark(self, key, val, reads, writes):
        for w in writes:
            w.lw = (key, val)
            w.rd = []
        for r in reads:
            if r in writes:
                continue
            r.rd.append((key, val))
            if len(r.rd) > 16:
                d = {}
                for (k, v) in r.rd:
                    d[k] = max(d.get(k, 0), v)
                r.rd = list(d.items())

    def op(self, e, fn, reads=(), writes=()):
        self._deps(e, reads, writes)
        ins = fn(self.eng[e])
        self.cnt[e] += 1
        ins.then_inc(self.sem[e], 1)
        self._mark(e, self.cnt[e], reads, writes)
        self.n_ins += 1
        return ins

    def dma(self, q, out, in_, reads=(), writes=(), **kw):
        lst, i = self.dsem[q]
        k = lst[i % len(lst)]
        self.dsem[q][1] = i + 1
        self._wait(q, k, self.cnt[k])
        self._deps(q, reads, writes)
        ins = self.eng[q].dma_start(out=out, in_=in_, **kw)
        self.cnt[k] += 16
        ins.then_inc(self.sem[k], 16)
        self._mark(k, self.cnt[k], reads, writes)
        self.n_ins += 1
        return ins

    def coll(self, kind, groups, src, dst, op=None):
        if "cc" not in self.sem:
            self.sem["cc"] = self.es.enter_context(self.nc.semaphore("s_cc"))
            self.cnt["cc"] = 0
        ins = self.nc.gpsimd.collective_compute(kind, op if op is not None else ALU.bypass, replica_groups=groups,
                                                ins=[src.opt()], outs=[dst.opt()])
        self.cnt["cc"] += 1
        ins.then_inc(self.sem["cc"])
        self.n_ins += 1
        return ins

    def barrier(self):
        for e in self.ENG:
            for k in self.sem:
                if k != e:
                    self._wait(e, k, self.cnt[k])


class Ring:
    def __init__(self, items):
        self.items = items
        self.i = 0

    def next(self):
        t = self.items[self.i % len(self.items)]
        self.i += 1
        return t


def _constants():
    c = {}
    c["ident"] = np.eye(128, dtype=np.float32)
    prot = np.zeros((128, 128), np.float32)
    for i in range(16):
        prot[i + 16, i] = -1.0
        prot[i, i + 16] = 1.0
    c["prot"] = prot
    pos = np.arange(SEQ, dtype=np.float32)
    inv = (np.float32(500000.0) ** (-np.arange(0, 32, 2, dtype=np.float32) / np.float32(32))).astype(np.float32)
    ang = (pos[None, :] * inv[:, None]).astype(np.float32)
    ct = np.ones((128, SEQ), np.float32)
    st = np.zeros((128, SEQ), np.float32)
    ct[0:16] = np.cos(ang); ct[16:32] = np.cos(ang)
    st[0:16] = np.sin(ang); st[16:32] = np.sin(ang)
    c["ropec"] = ct
    c["ropes"] = st
    n = np.arange(256)[:, None]
    j = np.arange(64)[None, :]
    ov = ((16 * n < 64 * j + 64) & (16 * n + 32 > 64 * j) & (n < NCMP)).astype(np.float32)
    c["ovl"] = ov.reshape(2, 128, 64).transpose(1, 0, 2).copy()
    k = np.arange(SEQ)[None, :]
    c["eall"] = (np.arange(64)[:, None] == (k // 64)).astype(np.float32)
    return c


class Builder:
    def __init__(self, n_layers=DEPTH, halves=(0, 1), headsets=(0, 1), dbg=False):
        self.n_layers = n_layers
        self.halves = halves
        self.headsets = headsets
        self.dbg = dbg
        self.nc = bass.Bass("TRN2", target_bir_lowering=False)
        self.es = ExitStack()
        self.s = Sched(self.nc, self.es)

    def declare(self):
        nc = self.nc
        ein = lambda name, shape: nc.dram_tensor(name, list(shape), F32, kind="ExternalInput").ap()
        self.x_in = ein("x", [SEQ, D])
        self.c_in = ein("c", [128, 16])
        self.norm_pre_g = ein("norm_pre_g", [DEPTH, D])
        self.norm_post_g = ein("norm_post_g", [DEPTH, D])
        self.w_ada = ein("w_ada", [DEPTH, D, 3 * D])
        self.b_ada = ein("b_ada", [DEPTH, 3 * D])
        self.w_in = ein("w_in", [DEPTH, D, NLOC])
        self.lam_in = ein("lam", [DEPTH, 4, 128])
        self.diff_norm_g = ein("diff_norm_g", [DEPTH, 256])
        self.cmp_pe = [ein("cmp_pe_k", [DEPTH, 32, 128]), ein("cmp_pe_v", [DEPTH, 32, 128])]
        self.cmp_w1 = [ein("cmp_w1_k", [DEPTH, 4096, 128]), ein("cmp_w1_v", [DEPTH, 4096, 128])]
        self.cmp_w2 = [ein("cmp_w2_k", [DEPTH, 128, 128]), ein("cmp_w2_v", [DEPTH, 128, 128])]
        self.w_branch = ein("w_branch", [DEPTH, 3, 512, D])
        self.w_out = ein("w_out", [DEPTH, D, D])
        self.k_ident = ein("k_ident", [128, 128])
        self.k_prot = ein("k_prot", [128, 128])
        self.k_ropec = ein("k_ropec", [128, SEQ])
        self.k_ropes = ein("k_ropes", [128, SEQ])
        self.k_ovl = ein("k_ovl", [128, 2, 64])
        self.k_eall = ein("k_eall", [64, SEQ])
        self.out = nc.dram_tensor("out", [SEQ, D], F32, kind="ExternalOutput").ap()
        kind = "ExternalOutput" if self.dbg else "Internal"
        self.FM = nc.dram_tensor("fm", [NFM, SEQ], BF16, kind=kind).ap()
        self.VT = nc.dram_tensor("vt", [SEQ, NVT], BF16, kind=kind).ap()
        self.YS = nc.dram_tensor("ys", [NYS, SEQ], BF16, kind=kind).ap()
        self.MP = [nc.dram_tensor("mp%d" % i, [256, SEQ], BF16, kind="Internal").ap() for i in range(8)]
        self.MG = [nc.dram_tensor("mgath%d" % i, [512, SEQ], BF16, kind="Internal").ap() for i in range(8)]
        self.XS = [nc.dram_tensor("xs%d" % i, [SEQ, D], F32, kind="Internal").ap() for i in range(2)]
        self.GG = nc.dram_tensor("gg", [DEPTH, D], F32, kind="Internal").ap()

    def setup(self):
        s, es = self.s, self.es
        self.ident = s.sb(es, "ident", [128, 128], F32)
        self.prot = s.sb(es, "prot", [128, 128], F32)
        self.ones_b = s.sb(es, "ones_b", [128, 128], BF16)
        self.ones_f = s.sb(es, "ones_f", [128, 128], F32)
        self.ustr = s.sb(es, "ustr", [128, 128], BF16)
        self.ovl = s.sb(es, "ovl", [128, 2, 64], F32)
        self.eall = s.sb(es, "eall", [64, SEQ], BF16)
        self.modp = s.sb(es, "modp", [128, DEPTH, 4, 16], F32)
        self.psum = [s.ps(es, "pb%d" % i, [128, 512], F32) for i in range(8)]
        s.dma("sp", self.ident[:], self.k_ident, writes=[self.ident])
        s.dma("sp", self.prot[:], self.k_prot, writes=[self.prot])
        s.dma("sp", self.ovl[:], self.k_ovl, writes=[self.ovl])
        s.dma("pool", self.eall[:], self.k_eall, writes=[self.eall])
        s.op("dve", lambda e: e.memset(self.ones_b[:], 1.0), writes=[self.ones_b])
        s.op("dve", lambda e: e.memset(self.ones_f[:], 1.0), writes=[self.ones_f])
        s.op("pool", lambda e: e.memset(self.ustr[:], 1.0), writes=[self.ustr])
        s.op("pool", lambda e: e.affine_select(self.ustr[:], self.ustr[:], [[-1, 128]], ALU.is_gt, 0.0,
                                               base=0, channel_multiplier=1),
             reads=[self.ustr], writes=[self.ustr])

    def phase0(self):
        s = self.s
        with ExitStack() as es:
            cs = s.sb(es, "cs", [128, 16], F32)
            wts = Ring([s.sb(es, "wada%d" % i, [128, 16, 512], F32) for i in range(2)])
            tmp = s.sb(es, "p0tmp", [128, 48], F32)
            bada = s.sb(es, "bada", [128, 48], F32)
            gpre = s.sb(es, "gpre", [128, 16], F32)
            gpost = s.sb(es, "gpost", [128, 16], F32)
            s.dma("sp", cs[:], self.c_in, writes=[cs])
            s.op("act", lambda e: e.activation(cs[:], cs[:], AF.Silu), reads=[cs], writes=[cs])
            pm = self.psum[0]
            for l in range(self.n_layers):
                s.dma("sp", bada[:], self.b_ada[l].rearrange("(j p) -> p j", p=128), writes=[bada],
                      allow_slow_non_contiguous=True)
                s.dma("sp", gpre[:], self.norm_pre_g[l].rearrange("(j p) -> p j", p=128), writes=[gpre],
                      allow_slow_non_contiguous=True)
                s.dma("sp", gpost[:], self.norm_post_g[l].rearrange("(j p) -> p j", p=128), writes=[gpost],
                      allow_slow_non_contiguous=True)
                for g in range(12):
                    wt = wts.next()
                    s.dma("sp", wt[:], self.w_ada[l, :, g * 512:(g + 1) * 512].rearrange("(j p) c -> p j c", p=128),
                          writes=[wt])
                    for cb in range(4):
                        col = g * 4 + cb
                        for j in range(16):
                            s.op("pe", lambda e, wt=wt, cb=cb, j=j, col=col: e.matmul(
                                pm[:, col:col + 1], wt[:, j, cb * 128:(cb + 1) * 128], cs[:, j:j + 1],
                                start=(j == 0), stop=(j == 15)), reads=[wt, cs], writes=[pm])
                s.op("dve", lambda e: e.tensor_tensor(tmp[:], pm[:, 0:48], bada[:], ALU.add),
                     reads=[pm, bada], writes=[tmp])
                mp = self.modp
                s.op("dve", lambda e, l=l: e.scalar_tensor_tensor(mp[:, l, 0, :], tmp[:, 16:32], 1.0, gpre[:],
                                                                    ALU.add, ALU.mult),
                     reads=[tmp, gpre], writes=[mp])
                s.op("dve", lambda e, l=l: e.tensor_copy(mp[:, l, 1, :], tmp[:, 0:16]), reads=[tmp], writes=[mp])
                s.op("dve", lambda e, l=l: e.tensor_tensor(mp[:, l, 2, :], tmp[:, 32:48], gpost[:], ALU.mult),
                     reads=[tmp, gpost], writes=[mp])
                s.dma("sp", self.GG[l].rearrange("(j p) -> p j", p=128), mp[:, l, 2, :], reads=[mp],
                      allow_slow_non_contiguous=True)
            s.barrier()

    def phase12(self, l, half, xsrc):
        s = self.s
        T0 = half * 2048
        with ExitStack() as es:
            hT = [s.sb(es, "hT%d" % i, [128, 16, 512], BF16) for i in range(4)]
            xt = s.sb(es, "xt", [128, 4, 2048], F32)
            junk = s.sb(es, "junk", [128, 2048], BF16)
            ss = s.sb(es, "ss", [128, 4], F32)
            rstd = s.sb(es, "rstd", [128, 4], F32)
            ropec = s.sb(es, "ropec", [128, 2048], F32)
            ropes = s.sb(es, "ropes", [128, 2048], F32)
            wts = Ring([s.sb(es, "wt%d" % i, [128, 16, 512], BF16) for i in range(3)])
            qf = Ring([s.sb(es, "qf%d" % i, [128, 512], F32) for i in range(2)])
            t1 = Ring([s.sb(es, "t1%d" % i, [128, 512], F32) for i in range(2)])
            t2 = Ring([s.sb(es, "t2%d" % i, [128, 512], F32) for i in range(2)])
            stg = Ring([s.sb(es, "stg%d" % i, [128, 512], BF16) for i in range(4)])
            pacc = Ring(self.psum[0:4])
            prot_ps = Ring(self.psum[4:6])
            ptr = Ring(self.psum[6:8])
            mp = self.modp
            s.dma("sp", ropec[:], self.k_ropec[:, T0:T0 + 2048], writes=[ropec])
            s.dma("sp", ropes[:], self.k_ropes[:, T0:T0 + 2048], writes=[ropes])
            for ti in range(4):
                t0 = T0 + ti * 512
                s.dma("sp", xt[:], xsrc[t0:t0 + 512, :].rearrange("(b p) d -> p b d", p=128), writes=[xt])
                for b in range(4):
                    s.op("act", lambda e, b=b: e.activation(junk[:], xt[:, b, :], AF.Square,
                                                            accum_out=ss[:, b:b + 1]),
                         reads=[xt], writes=[junk, ss])
                s.op("act", lambda e: e.activation(rstd[:], ss[:], AF.Ln, scale=1.0 / D, bias=1e-6),
                     reads=[ss], writes=[rstd])
                s.op("act", lambda e: e.activation(rstd[:], rstd[:], AF.Exp, scale=-0.5),
                     reads=[rstd], writes=[rstd])
                for b in range(4):
                    s.op("dve", lambda e, b=b: e.tensor_scalar(xt[:, b, :], xt[:, b, :], rstd[:, b:b + 1], None,
                                                               ALU.mult),
                         reads=[xt, rstd], writes=[xt])
                for j in range(16):
                    pt = ptr.next()
                    for b in range(4):
                        s.op("pe", lambda e, b=b, j=j, pt=pt: e.transpose(
                            pt[:, b * 128:(b + 1) * 128], xt[:, b, j * 128:(j + 1) * 128], self.ident[:]),
                            reads=[xt, self.ident], writes=[pt])
                    s.op("dve", lambda e, j=j, pt=pt, ti=ti: e.tensor_scalar(
                        hT[ti][:, j, :], pt[:], mp[:, l, 0, j:j + 1], mp[:, l, 1, j:j + 1], ALU.mult, ALU.add),
                        reads=[pt, mp], writes=[hT[ti]])
            for (name, off, gw, kind) in GROUPS:
                c0 = COL[name] + off
                wt = wts.next()
                s.dma("pool", wt[:, :, 0:gw], self.w_in[l, :, c0:c0 + gw].rearrange("(j p) c -> p j c", p=128),
                      writes=[wt])
                if kind == "v":
                    vc0 = VTCOL[name] + off
                    for tb in range(16):
                        pa = pacc.next()
                        ti, bb = tb // 4, tb % 4
                        for j in range(16):
                            s.op("pe", lambda e, pa=pa, ti=ti, bb=bb, j=j, wt=wt: e.matmul(
                                pa[:, 0:gw], hT[ti][:, j, bb * 128:(bb + 1) * 128], wt[:, j, 0:gw],
                                start=(j == 0), stop=(j == 15)), reads=[hT[ti], wt], writes=[pa])
                        st = stg.next()
                        eng = "act" if tb % 2 == 0 else "dve"
                        if eng == "act":
                            s.op("act", lambda e, st=st, pa=pa: e.copy(st[:, 0:gw], pa[:, 0:gw]),
                                 reads=[pa], writes=[st])
                        else:
                            s.op("dve", lambda e, st=st, pa=pa: e.tensor_copy(st[:, 0:gw], pa[:, 0:gw]),
                                 reads=[pa], writes=[st])
                        tt = T0 + tb * 128
                        s.dma("sp", self.VT[tt:tt + 128, vc0:vc0 + gw], st[:, 0:gw], reads=[st])
                    continue
                nblk = (gw + 127) // 128
                for blk in range(nblk):
                    bw = min(128, gw - blk * 128)
                    r0 = FMROWS[name] + off + blk * 128
                    for ti in range(4):
                        t0 = T0 + ti * 512
                        tl = ti * 512
                        pa = pacc.next()
                        for j in range(16):
                            s.op("pe", lambda e, pa=pa, ti=ti, j=j, wt=wt, blk=blk, bw=bw: e.matmul(
                                pa[0:bw, :], wt[:, j, blk * 128:blk * 128 + bw], hT[ti][:, j, :],
                                start=(j == 0), stop=(j == 15)), reads=[hT[ti], wt], writes=[pa])
                        st = stg.next()
                        if kind == "fm":
                            s.op("dve", lambda e, st=st, pa=pa: e.tensor_copy(st[:], pa[:]), reads=[pa], writes=[st])
                        elif kind == "silu":
                            s.op("act", lambda e, st=st, pa=pa: e.activation(st[:], pa[:], AF.Silu),
                                 reads=[pa], writes=[st])
                        elif kind == "sig":
                            s.op("act", lambda e, st=st, pa=pa, bw=bw: e.activation(st[0:bw, :], pa[0:bw, :], AF.Sigmoid),
                                 reads=[pa], writes=[st])
                        elif kind == "rope":
                            q = qf.next()
                            pr = prot_ps.next()
                            a1 = t1.next()
                            a2 = t2.next()
                            s.op("act", lambda e, q=q, pa=pa: e.copy(q[:], pa[:]), reads=[pa], writes=[q])
                            s.op("pe", lambda e, q=q, pr=pr: e.matmul(pr[:], self.prot[:], q[:], start=True, stop=True),
                                 reads=[q, self.prot], writes=[pr])
                            s.op("pool", lambda e, q=q, a1=a1, tl=tl: e.tensor_tensor(
                                a1[:], q[:], ropec[:, tl:tl + 512], ALU.mult), reads=[q, ropec], writes=[a1])
                            s.op("dve", lambda e, pr=pr, a2=a2, tl=tl: e.tensor_tensor(
                                a2[:], pr[:], ropes[:, tl:tl + 512], ALU.mult), reads=[pr, ropes], writes=[a2])
                            s.op("pool", lambda e, a1=a1, a2=a2, st=st: e.tensor_tensor(st[:], a1[:], a2[:], ALU.add),
                                 reads=[a1, a2], writes=[st])
                        s.dma("sp", self.FM[r0:r0 + bw, t0:t0 + 512], st[0:bw, :], reads=[st])
            s.barrier()

    def _causal(self, t, k0, t0, npart=128):
        self.s.op("pool", lambda e: e.affine_select(t[0:npart, :], t[0:npart, :], [[1, 512]], ALU.is_ge, 0.0,
                                                    base=t0 - k0, channel_multiplier=-1),
                  reads=[t], writes=[t])

    def _softmax_attn(self, qT, blocks, p_ring, ps_ring, psum_sum, psum_o, scale=SCALE):
        s = self.s
        nb = len(blocks)
        for bi, blk in enumerate(blocks):
            ps = ps_ring.next()
            kT_t, kT_ap = blk["kT"]
            bias = blk.get("bias")
            s.op("pe", lambda e, ps=ps, kT_ap=kT_ap: e.matmul(ps[:], kT_ap, qT[:], start=True, stop=(bias is None)),
                 reads=[kT_t, qT], writes=[ps])
            if bias is not None:
                bl_t, bl_ap, br_t, br_ap = bias
                s.op("pe", lambda e, ps=ps, bl_ap=bl_ap, br_ap=br_ap: e.matmul(ps[:], bl_ap, br_ap, start=False, stop=True),
                     reads=[bl_t, br_t], writes=[ps])
            p = p_ring.next()
            s.op("act", lambda e, p=p, ps=ps: e.activation(p[:], ps[:], AF.Exp, scale=scale), reads=[ps], writes=[p])
            if blk.get("mask") is not None:
                blk["mask"](p)
            s.op("pe", lambda e, p=p: e.matmul(psum_sum[:], self.ones_b[:], p[:], start=(bi == 0), stop=(bi == nb - 1)),
                 reads=[self.ones_b, p], writes=[psum_sum])
            for oi, (v_t, v_ap) in enumerate(blk["v"]):
                po = psum_o[oi]
                s.op("pe", lambda e, p=p, po=po, v_ap=v_ap: e.matmul(po[:], v_ap, p[:], start=(bi == 0), stop=(bi == nb - 1)),
                     reads=[v_t, p], writes=[po])

    def phase3_diff(self, l, hs):
        s = self.s
        lam_init = 0.8 - 0.6 * math.exp(-0.3 * l)
        with ExitStack() as es:
            kT = [Ring([s.sb(es, "akT%d_%d" % (c, i), [128, SEQ], BF16) for i in range(2)]) for c in range(2)]
            vv = Ring([s.sb(es, "avv%d" % i, [128, NKB, 256], BF16) for i in range(2)])
            qTs = Ring([s.sb(es, "aq%d" % i, [128, 512], BF16) for i in range(4)])
            szs = Ring([s.sb(es, "asz%d" % i, [128, 512], BF16) for i in range(4)])
            p_ring = Ring([s.sb(es, "ap%d" % i, [128, 512], BF16) for i in range(3)])
            oc = [[s.sb(es, "aoc%d%d" % (c, h), [128, 512], F32) for h in range(2)] for c in range(2)]
            rs = s.sb(es, "ars", [128, 512], F32)
            sq = [s.sb(es, "asq%d" % h, [128, 512], F32) for h in range(2)]
            rstd = s.sb(es, "arstd", [128, 512], F32)
            yst = Ring([s.sb(es, "ayst%d" % i, [128, 512], BF16) for i in range(2)])
            lamt = s.sb(es, "lamt", [128, 4], F32)
            lam2 = s.sb(es, "lam2", [128, 2], F32)
            neglam = s.sb(es, "neglam", [128, 1], F32)
            gco = s.sb(es, "gco", [128, 2], F32)
            ps_ring = Ring(self.psum[0:2])
            psum_sum = self.psum[2]
            psum_o = self.psum[4:6]
            pmisc = self.psum[6]
            s.dma("sp", lamt[:], self.lam_in[l].rearrange("k p -> p k"), writes=[lamt], allow_slow_non_contiguous=True)
            s.op("dve", lambda e: e.tensor_tensor(lam2[:, 0:1], lamt[:, 0:1], lamt[:, 1:2], ALU.mult), reads=[lamt], writes=[lam2])
            s.op("dve", lambda e: e.tensor_tensor(lam2[:, 1:2], lamt[:, 2:3], lamt[:, 3:4], ALU.mult), reads=[lamt], writes=[lam2])
            s.op("pe", lambda e: e.matmul(pmisc[:, 0:2], self.ones_f[:], lam2[:], start=True, stop=True),
                 reads=[self.ones_f, lam2], writes=[pmisc])
            s.op("act", lambda e: e.activation(lam2[:], pmisc[:, 0:2], AF.Exp), reads=[pmisc], writes=[lam2])
            s.op("dve", lambda e: e.scalar_tensor_tensor(neglam[:], lam2[:, 1:2], -lam_init, lam2[:, 0:1], ALU.add, ALU.subtract),
                 reads=[lam2], writes=[neglam])
            s.dma("sp", gco[:], self.diff_norm_g[l].rearrange("(h p) -> p h", p=128), writes=[gco], allow_slow_non_contiguous=True)
            s.op("dve", lambda e: e.tensor_scalar(gco[:], gco[:], 1.0 - lam_init, None, ALU.mult), reads=[gco], writes=[gco])
            for h in range(2):
                kts = []
                for c in range(2):
                    kt = kT[c].next()
                    r0 = FMROWS["ak"] + h * 256 + c * 128
                    s.dma("sp", kt[:], self.FM[r0:r0 + 128, :], writes=[kt])
                    kts.append(kt)
                v = vv.next()
                vc = VTCOL["av"] + h * 256
                s.dma("sp", v[:], self.VT[:, vc:vc + 256].rearrange("(kb p) e -> p kb e", p=128), writes=[v])
                for i in range(NQT):
                    t0 = i * 512
                    for c in range(2):
                        q = qTs.next()
                        r0 = FMROWS["aq"] + h * 256 + c * 128
                        s.dma("sp", q[:], self.FM[r0:r0 + 128, t0:t0 + 512], writes=[q])
                        blocks = []
                        for kb in range(4 * i + 4):
                            k0 = kb * 128
                            blk = dict(kT=(kts[c], kts[c][:, k0:k0 + 128]),
                                       v=[(v, v[:, kb, 0:128]), (v, v[:, kb, 128:256])])
                            if kb >= 4 * i:
                                blk["mask"] = (lambda p, k0=k0, t0=t0: self._causal(p, k0, t0))
                            blocks.append(blk)
                        self._softmax_attn(q, blocks, p_ring, ps_ring, psum_sum, psum_o)
                        s.op("dve", lambda e: e.reciprocal(rs[:], psum_sum[:]), reads=[psum_sum], writes=[rs])
                        for hf in range(2):
                            s.op("dve", lambda e, c=c, hf=hf: e.tensor_tensor(oc[c][hf][:], psum_o[hf][:], rs[:], ALU.mult),
                                 reads=[psum_o[hf], rs], writes=[oc[c][hf]])
                    for hf in range(2):
                        s.op("dve", lambda e, hf=hf: e.scalar_tensor_tensor(
                            oc[0][hf][:], oc[1][hf][:], neglam[:, 0:1], oc[0][hf][:], ALU.mult, ALU.add),
                            reads=[oc[1][hf], neglam, oc[0][hf]], writes=[oc[0][hf]])
                        s.op("pool", lambda e, hf=hf: e.tensor_tensor(sq[hf][:], oc[0][hf][:], oc[0][hf][:], ALU.mult),
                             reads=[oc[0][hf]], writes=[sq[hf]])
                    for hf in range(2):
                        s.op("pe", lambda e, hf=hf: e.matmul(pmisc[:], self.ones_f[:], sq[hf][:], start=(hf == 0), stop=(hf == 1)),
                             reads=[self.ones_f, sq[hf]], writes=[pmisc])
                    s.op("act", lambda e: e.activation(rstd[:], pmisc[:], AF.Ln, scale=1.0 / 256, bias=1e-5),
                         reads=[pmisc], writes=[rstd])
                    s.op("act", lambda e: e.activation(rstd[:], rstd[:], AF.Exp, scale=-0.5), reads=[rstd], writes=[rstd])
                    for hf in range(2):
                        sz = szs.next()
                        rz = FMROWS["az"] + h * 256 + hf * 128
                        s.dma("sp", sz[:], self.FM[rz:rz + 128, t0:t0 + 512], writes=[sz])
                        s.op("dve", lambda e, hf=hf: e.scalar_tensor_tensor(
                            oc[0][hf][:], oc[0][hf][:], gco[:, hf:hf + 1], rstd[:], ALU.mult, ALU.mult),
                            reads=[oc[0][hf], gco, rstd], writes=[oc[0][hf]])
                        y = yst.next()
                        s.op("pool", lambda e, hf=hf, y=y, sz=sz: e.tensor_tensor(y[:], oc[0][hf][:], sz[:], ALU.mult),
                             reads=[oc[0][hf], sz], writes=[y])
                        ry = YSROW["a"] + h * 256 + hf * 128
                        s.dma("sp", self.YS[ry:ry + 128, t0:t0 + 512], y[:], reads=[y])
            s.barrier()

    def phase3_sb(self, l, hs):
        s = self.s
        with ExitStack() as es:
            kTr = Ring([s.sb(es, "ckT%d" % i, [128, SEQ], BF16) for i in range(2)])
            vvr = Ring([s.sb(es, "cvv%d" % i, [128, NKB, 128], BF16) for i in range(2)])
            qTs = Ring([s.sb(es, "cq%d" % i, [128, 512], BF16) for i in range(3)])
            szs = Ring([s.sb(es, "csz%d" % i, [128, 512], BF16) for i in range(3)])
            er = Ring([s.sb(es, "ce%d" % i, [128, 512], F32) for i in range(3)])
            lfr = Ring([s.sb(es, "clf%d" % i, [128, 512], F32) for i in range(3)])
            lbr = Ring([s.sb(es, "clb%d" % i, [128, 512], BF16) for i in range(3)])
            argr = Ring([s.sb(es, "carg%d" % i, [128, 512], F32) for i in range(3)])
            wr = Ring([s.sb(es, "cw%d" % i, [128, 512], F32) for i in range(3)])
            ar = Ring([s.sb(es, "ca%d" % i, [128, 512], BF16) for i in range(3)])
            R = s.sb(es, "cR", [128, 512], F32)
            yst = Ring([s.sb(es, "cyst%d" % i, [128, 512], BF16) for i in range(2)])
            zr = Ring(self.psum[0:2])
            c1r = Ring(self.psum[2:4])
            c2r = Ring(self.psum[4:6])
            po = self.psum[6]
            for h in range(4):
                kt = kTr.next()
                r0 = FMROWS["ck"] + h * 128
                s.dma("sp", kt[:], self.FM[r0:r0 + 128, :], writes=[kt])
                v = vvr.next()
                vc = VTCOL["cv"] + h * 128
                s.dma("sp", v[:], self.VT[:, vc:vc + 128].rearrange("(kb p) e -> p kb e", p=128), writes=[v])
                for i in range(NQT):
                    t0 = i * 512
                    q = qTs.next()
                    rq = FMROWS["cq"] + h * 128
                    s.dma("sp", q[:], self.FM[rq:rq + 128, t0:t0 + 512], writes=[q])
                    sz = szs.next()
                    rz = FMROWS["cz"] + h * 128
                    s.dma("sp", sz[:], self.FM[rz:rz + 128, t0:t0 + 512], writes=[sz])
                    s.op("pool", lambda e: e.memset(R[:], 0.0), writes=[R])
                    kbs = list(range(4 * i + 3, -1, -1))
                    for bi, kb in enumerate(kbs):
                        k0 = kb * 128
                        zp = zr.next()
                        s.op("pe", lambda e, zp=zp, k0=k0, q=q, kt=kt: e.matmul(zp[:], kt[:, k0:k0 + 128], q[:], start=True, stop=True),
                             reads=[kt, q], writes=[zp])
                        ee = er.next()
                        s.op("act", lambda e, ee=ee, zp=zp: e.activation(ee[:], zp[:], AF.Exp, scale=SCALE), reads=[zp], writes=[ee])
                        if kb >= 4 * i:
                            s.op("pool", lambda e, ee=ee, k0=k0, t0=t0: e.affine_select(
                                ee[:], ee[:], [[1, 512]], ALU.is_gt, 0.0, base=t0 - k0, channel_multiplier=-1),
                                reads=[ee], writes=[ee])
                        lf = lfr.next()
                        s.op("act", lambda e, lf=lf, ee=ee: e.activation(lf[:], ee[:], AF.Ln, bias=1.0), reads=[ee], writes=[lf])
                        lb = lbr.next()
                        s.op("pool", lambda e, lb=lb, lf=lf: e.tensor_copy(lb[:], lf[:]), reads=[lf], writes=[lb])
                        c1 = c1r.next()
                        c2 = c2r.next()
                        s.op("pe", lambda e, c1=c1, lb=lb: e.matmul(c1[:], self.ustr[:], lb[:], start=True, stop=True),
                             reads=[self.ustr, lb], writes=[c1])
                        s.op("pe", lambda e, c2=c2, lb=lb: e.matmul(c2[:], self.ones_b[:], lb[:], start=True, stop=True),
                             reads=[self.ones_b, lb], writes=[c2])
                        arg = argr.next()
                        s.op("dve", lambda e, arg=arg, c1=c1: e.tensor_tensor(arg[:], c1[:], R[:], ALU.add),
                             reads=[c1, R], writes=[arg])
                        s.op("dve", lambda e, c2=c2: e.tensor_tensor(R[:], c2[:], R[:], ALU.add), reads=[c2, R], writes=[R])
                        s.op("pool", lambda e, arg=arg, lf=lf: e.tensor_tensor(arg[:], arg[:], lf[:], ALU.add),
                             reads=[arg, lf], writes=[arg])
                        w = wr.next()
                        s.op("act", lambda e, w=w, arg=arg: e.activation(w[:], arg[:], AF.Exp, scale=-1.0), reads=[arg], writes=[w])
                        a = ar.next()
                        s.op("pool", lambda e, a=a, ee=ee, w=w: e.tensor_tensor(a[:], ee[:], w[:], ALU.mult),
                             reads=[ee, w], writes=[a])
                        s.op("pe", lambda e, a=a, kb=kb, bi=bi, v=v: e.matmul(po[:], v[:, kb, :], a[:], start=(bi == 0),
                                                                                stop=(bi == len(kbs) - 1)),
                             reads=[v, a], writes=[po])
                    y = yst.next()
                    s.op("dve", lambda e, y=y, sz=sz: e.tensor_tensor(y[:], po[:], sz[:], ALU.mult), reads=[po, sz], writes=[y])
                    ry = YSROW["c"] + h * 128
                    s.dma("sp", self.YS[ry:ry + 128, t0:t0 + 512], y[:], reads=[y])
            s.barrier()

    def phase3_nsa(self, l, g):
        s = self.s
        with ExitStack() as es:
            big = [s.sb(es, "bbig%d" % i, [128, SEQ], BF16) for i in range(4)]
            vs = s.sb(es, "bvs", [128, NKB, 128], BF16)
            vw = s.sb(es, "bvw", [128, NKB, 128], BF16)
            w1 = s.sb(es, "bw1", [128, 32, 128], BF16)
            w2 = s.sb(es, "bw2", [128, 128], BF16)
            pe_sb = s.sb(es, "bpe", [32, 128], F32)
            peT = s.sb(es, "bpeT", [128, 32], BF16)
            c1 = s.sb(es, "bc1", [128, 1], F32)
            hsl = s.sb(es, "bhsl", [128, 256], BF16)
            kcmpT = s.sb(es, "bkcmpT", [128, 256], BF16)
            vcmp = s.sb(es, "bvcmp", [128, 2, 128], BF16)
            qTs = [s.sb(es, "bq%d" % i, [128, 512], BF16) for i in range(4)]
            gts = Ring([s.sb(es, "bgt%d" % i, [128, 3, 512], BF16) for i in range(2)])
            szs = Ring([s.sb(es, "bsz%d" % i, [128, 512], BF16) for i in range(2)])
            pf = [s.sb(es, "bpf%d" % i, [128, 512], F32) for i in range(2)]
            pnb = Ring([s.sb(es, "bpnb%d" % i, [128, 512], BF16) for i in range(2)])
            rs = s.sb(es, "brs", [128, 512], F32)
            ocmp = [s.sb(es, "bocmp%d" % i, [128, 512], F32) for i in range(4)]
            impS = s.sb(es, "bimp", [128, 4, 64], F32)
            m8 = s.sb(es, "bm8", [128, 16], F32)
            wk = s.sb(es, "bwk", [128, 64], F32)
            sel = s.sb(es, "bsel", [128, 64], F32)
            negT = s.sb(es, "bnegT", [64, 512], BF16)
            p_ring = Ring([s.sb(es, "bp%d" % i, [128, 512], BF16) for i in range(3)])
            acc = s.sb(es, "bacc", [128, 512], F32)
            tmp = s.sb(es, "btmp", [128, 512], F32)
            yst = Ring([s.sb(es, "byst%d" % i, [128, 512], BF16) for i in range(2)])
            ps_ring = Ring(self.psum[0:2])
            psum_sum = self.psum[2]
            pimp = self.psum[3]
            psum_o = [self.psum[4]]
            pmisc = self.psum[5]
            pmisc2 = self.psum[6]
            kcT, vcT, ksT, kwT = big
            for t_, nm in ((kcT, "bkc"), (vcT, "bvc"), (ksT, "bks"), (kwT, "bkw")):
                r0 = FMROWS[nm] + g * 128
                s.dma("sp", t_[:], self.FM[r0:r0 + 128, :], writes=[t_])
            for t_, nm in ((vs, "bvs"), (vw, "bvw")):
                vc = VTCOL[nm] + g * 128
                s.dma("sp", t_[:], self.VT[:, vc:vc + 128].rearrange("(kb p) e -> p kb e", p=128), writes=[t_])
            for kv in range(2):
                src = kcT if kv == 0 else vcT
                s.dma("pool", w1[:], self.cmp_w1[kv][l].rearrange("(l d) f -> d l f", d=128), writes=[w1])
                s.dma("pool", w2[:], self.cmp_w2[kv][l], writes=[w2])
                s.dma("sp", pe_sb[:], self.cmp_pe[kv][l], writes=[pe_sb])
                s.op("pe", lambda e: e.transpose(pmisc[:, 0:32], pe_sb[:], self.ident[0:32, 0:32]),
                     reads=[pe_sb, self.ident], writes=[pmisc])
                s.op("dve", lambda e: e.tensor_copy(peT[:], pmisc[:, 0:32]), reads=[pmisc], writes=[peT])
                for li in range(32):
                    s.op("pe", lambda e, li=li: e.matmul(pmisc2[:, 0:1], w1[:, li, :], peT[:, li:li + 1],
                                                         start=(li == 0), stop=(li == 31)),
                         reads=[w1, peT], writes=[pmisc2])
                s.op("dve", lambda e: e.tensor_copy(c1[:], pmisc2[:, 0:1]), reads=[pmisc2], writes=[c1])
                for li in range(32):
                    s.op("pe", lambda e, li=li, src=src: e.matmul(pmisc[:, 0:NCMP], w1[:, li, :],
                                                                   src[:, li:li + 16 * (NCMP - 1) + 1:16],
                                                                   start=(li == 0), stop=(li == 31)),
                         reads=[w1, src], writes=[pmisc])
                s.op("dve", lambda e: e.memset(hsl[:], 0.0), writes=[hsl])
                s.op("act", lambda e: e.activation(hsl[:, 0:NCMP], pmisc[:, 0:NCMP], AF.Silu, bias=c1[:, 0:1]),
                     reads=[pmisc, c1], writes=[hsl])
                if kv == 0:
                    s.op("pe", lambda e: e.matmul(pmisc2[:, 0:256], w2[:], hsl[:], start=True, stop=True),
                         reads=[w2, hsl], writes=[pmisc2])
                    s.op("dve", lambda e: e.tensor_copy(kcmpT[:], pmisc2[:, 0:256]), reads=[pmisc2], writes=[kcmpT])
                else:
                    for nb in range(2):
                        s.op("pe", lambda e, nb=nb: e.matmul(pmisc2[:, nb * 128:(nb + 1) * 128], hsl[:, nb * 128:(nb + 1) * 128],
                                                             w2[:], start=True, stop=True),
                             reads=[w2, hsl], writes=[pmisc2])
                    s.op("dve", lambda e: e.tensor_copy(vcmp[:], pmisc2[:, 0:256].rearrange("p (n d) -> p n d", n=2)),
                         reads=[pmisc2], writes=[vcmp])
            for i in range(NQT):
                t0 = i * 512
                nbs = [nb for nb in range(2) if 16 * nb * 128 + 31 <= t0 + 511]
                for r in range(4):
                    h = g * 4 + r
                    rq = FMROWS["bq"] + h * 128
                    s.dma("sp", qTs[r][:], self.FM[rq:rq + 128, t0:t0 + 512], writes=[qTs[r]])
                for r in range(4):
                    q = qTs[r]
                    for nb in nbs:
                        ps = ps_ring.next()
                        s.op("pe", lambda e, ps=ps, nb=nb, q=q: e.matmul(ps[:], kcmpT[:, nb * 128:(nb + 1) * 128], q[:],
                                                                          start=True, stop=True),
                             reads=[kcmpT, q], writes=[ps])
                        s.op("act", lambda e, ps=ps, nb=nb: e.activation(pf[nb][:], ps[:], AF.Exp, scale=SCALE),
                             reads=[ps], writes=[pf[nb]])
                        s.op("pool", lambda e, nb=nb, t0=t0: e.affine_select(
                            pf[nb][:], pf[nb][:], [[1, 512]], ALU.is_ge, 0.0,
                            base=t0 - 16 * nb * 128 - 31, channel_multiplier=-16), reads=[pf[nb]], writes=[pf[nb]])
                        s.op("pe", lambda e, nb=nb: e.matmul(psum_sum[:], self.ones_f[:], pf[nb][:], start=(nb == nbs[0]),
                                                             stop=(nb == nbs[-1])),
                             reads=[self.ones_f, pf[nb]], writes=[psum_sum])
                    s.op("dve", lambda e: e.tensor_scalar(rs[:], psum_sum[:], 1e-30, None, ALU.max), reads=[psum_sum], writes=[rs])
                    s.op("dve", lambda e: e.reciprocal(rs[:], rs[:]), reads=[rs], writes=[rs])
                    for nb in nbs:
                        s.op("dve", lambda e, nb=nb: e.tensor_tensor(pf[nb][:], pf[nb][:], rs[:], ALU.mult),
                             reads=[pf[nb], rs], writes=[pf[nb]])
                        for tb in range(4):
                            first = (r == 0 and nb == nbs[0])
                            last = (r == 3 and nb == nbs[-1])
                            s.op("pe", lambda e, nb=nb, tb=tb, first=first, last=last: e.matmul(
                                pimp[:, tb * 64:(tb + 1) * 64], pf[nb][:, tb * 128:(tb + 1) * 128], self.ovl[:, nb, :],
                                start=first, stop=last), reads=[pf[nb], self.ovl], writes=[pimp])
                        pb = pnb.next()
                        s.op("act", lambda e, pb=pb, nb=nb: e.copy(pb[:], pf[nb][:]), reads=[pf[nb]], writes=[pb])
                        s.op("pe", lambda e, pb=pb, nb=nb: e.matmul(psum_o[0][:], vcmp[:, nb, :], pb[:], start=(nb == nbs[0]),
                                                                     stop=(nb == nbs[-1])),
                             reads=[vcmp, pb], writes=[psum_o[0]])
                    s.op("dve", lambda e, r=r: e.tensor_copy(ocmp[r][:], psum_o[0][:]), reads=[psum_o[0]], writes=[ocmp[r]])
                s.op("dve", lambda e: e.tensor_copy(impS[:], pimp[:, 0:256].rearrange("p (a b) -> p a b", a=4)),
                     reads=[pimp], writes=[impS])
                for tb in range(4):
                    for hh in range(2):
                        tblk = 8 * i + 2 * tb + hh
                        p0 = hh * 64
                        if tblk < 63:
                            s.op("pool", lambda e, tb=tb, p0=p0, tblk=tblk: e.memset(impS[p0:p0 + 64, tb, tblk + 1:64], -1e30),
                                 reads=[impS], writes=[impS])
                        s.op("pool", lambda e, tb=tb, p0=p0, tblk=tblk: e.memset(impS[p0:p0 + 64, tb, tblk:tblk + 1], 1e6),
                             reads=[impS], writes=[impS])
                        s.op("pool", lambda e, tb=tb, p0=p0: e.memset(impS[p0:p0 + 64, tb, 0:1], 2e6),
                             reads=[impS], writes=[impS])
                for tb in range(4):
                    s.op("dve", lambda e, tb=tb: e.max(out=m8[:, 0:8], in_=impS[:, tb, :]), reads=[impS], writes=[m8])
                    s.op("dve", lambda e, tb=tb: e.match_replace(out=wk[:], in_to_replace=m8[:, 0:8], in_values=impS[:, tb, :],
                                                                 imm_value=-3e30), reads=[impS, m8], writes=[wk])
                    s.op("dve", lambda e: e.max(out=m8[:, 8:16], in_=wk[:]), reads=[wk], writes=[m8])
                    s.op("dve", lambda e, tb=tb: e.tensor_scalar(sel[:], impS[:, tb, :], m8[:, 15:16], None, ALU.is_ge),
                         reads=[impS, m8], writes=[sel])
                    s.op("dve", lambda e: e.tensor_scalar(sel[:], sel[:], -1.0, BIG, ALU.add, ALU.mult), reads=[sel], writes=[sel])
                    s.op("pe", lambda e: e.transpose(pmisc[0:64, 0:128], sel[:], self.ident[:]),
                         reads=[sel, self.ident], writes=[pmisc])
                    s.op("act", lambda e, tb=tb: e.copy(negT[:, tb * 128:(tb + 1) * 128], pmisc[0:64, 0:128]),
                         reads=[pmisc], writes=[negT])
                for r in range(4):
                    h = g * 4 + r
                    q = qTs[r]
                    gt = gts.next()
                    for k3 in range(3):
                        rg = FMROWS["bg"] + h * 3 + k3
                        s.dma("sp", gt[:, k3, :], self.FM[rg:rg + 1, t0:t0 + 512].to_broadcast([128, 512]), writes=[gt])
                    sz = szs.next()
                    rz = FMROWS["bz"] + h * 128
                    s.dma("sp", sz[:], self.FM[rz:rz + 128, t0:t0 + 512], writes=[sz])
                    s.op("pool", lambda e, r=r, gt=gt: e.tensor_tensor(acc[:], ocmp[r][:], gt[:, 0, :], ALU.mult),
                         reads=[ocmp[r], gt], writes=[acc])
                    blocks = []
                    for kb in range(4 * i + 4):
                        k0 = kb * 128
                        blk = dict(kT=(ksT, ksT[:, k0:k0 + 128]), v=[(vs, vs[:, kb, :])],
                                   bias=(self.eall, self.eall[:, k0:k0 + 128], negT, negT[:]))
                        if kb >= 4 * i:
                            blk["mask"] = (lambda p, k0=k0, t0=t0: self._causal(p, k0, t0))
                        blocks.append(blk)
                    self._softmax_attn(q, blocks, p_ring, ps_ring, psum_sum, psum_o)
                    s.op("dve", lambda e: e.reciprocal(rs[:], psum_sum[:]), reads=[psum_sum], writes=[rs])
                    s.op("dve", lambda e: e.tensor_tensor(tmp[:], psum_o[0][:], rs[:], ALU.mult), reads=[psum_o[0], rs], writes=[tmp])
                    s.op("pool", lambda e, gt=gt: e.tensor_tensor(tmp[:], tmp[:], gt[:, 1, :], ALU.mult), reads=[tmp, gt], writes=[tmp])
                    s.op("pool", lambda e: e.tensor_tensor(acc[:], acc[:], tmp[:], ALU.add), reads=[acc, tmp], writes=[acc])
                    blocks = []
                    for kb in range(max(0, 4 * i - 4), 4 * i + 4):
                        k0 = kb * 128
                        blk = dict(kT=(kwT, kwT[:, k0:k0 + 128]), v=[(vw, vw[:, kb, :])])
                        if kb >= 4 * i:
                            blk["mask"] = (lambda p, k0=k0, t0=t0: self._causal(p, k0, t0))
                        else:
                            blk["mask"] = (lambda p, k0=k0, t0=t0: s.op("pool", lambda e: e.affine_select(
                                p[:], p[:], [[-1, 512]], ALU.is_gt, 0.0, base=k0 - t0 + 512, channel_multiplier=1),
                                reads=[p], writes=[p]))
                        blocks.append(blk)
                    self._softmax_attn(q, blocks, p_ring, ps_ring, psum_sum, psum_o)
                    s.op("dve", lambda e: e.reciprocal(rs[:], psum_sum[:]), reads=[psum_sum], writes=[rs])
                    s.op("dve", lambda e: e.tensor_tensor(tmp[:], psum_o[0][:], rs[:], ALU.mult), reads=[psum_o[0], rs], writes=[tmp])
                    s.op("pool", lambda e, gt=gt: e.tensor_tensor(tmp[:], tmp[:], gt[:, 2, :], ALU.mult), reads=[tmp, gt], writes=[tmp])
                    s.op("pool", lambda e: e.tensor_tensor(acc[:], acc[:], tmp[:], ALU.add), reads=[acc, tmp], writes=[acc])
                    y = yst.next()
                    s.op("pool", lambda e, y=y, sz=sz: e.tensor_tensor(y[:], acc[:], sz[:], ALU.mult), reads=[acc, sz], writes=[y])
                    ry = YSROW["b"] + h * 128
                    s.dma("sp", self.YS[ry:ry + 128, t0:t0 + 512], y[:], reads=[y])
            s.barrier()

    def phase4a(self, l):
        s = self.s
        for tp in range(4):
            with ExitStack() as es2:
                ys = [s.sb(es2, "ysb%d" % n, [128, 4, 1024], BF16) for n in range(3)]
                wbr = Ring([s.sb(es2, "wb%d" % i, [128, 4, 512], BF16) for i in range(6)])
                mgr = Ring([s.sb(es2, "mg%d" % i, [128, 1024], BF16) for i in range(4)])
                macc = [s.sb(es2, "macc%d" % i, [128, 512], F32) for i in range(2)]
                tm = Ring([s.sb(es2, "tm%d" % i, [128, 512], F32) for i in range(3)])
                stg = Ring([s.sb(es2, "mstg%d" % i, [128, 512], BF16) for i in range(3)])
                pacc = Ring(self.psum[0:8])
                TP = tp * 1024
                for n in range(3):
                    s.dma("sp", ys[n][:], self.YS[n * 512:(n + 1) * 512, TP:TP + 1024].rearrange("(k p) t -> p k t", p=128),
                          writes=[ys[n]])
                for cg in range(4):
                    wbs = []
                    for n in range(3):
                        wb = wbr.next()
                        s.dma("pool", wb[:], self.w_branch[l, n, :, cg * 512:(cg + 1) * 512].rearrange("(k p) c -> p k c", p=128),
                              writes=[wb])
                        wbs.append(wb)
                    for cb in range(4):
                        cc = cg * 4 + cb
                        for n in range(3):
                            mg = mgr.next()
                            rm = FMROWS["mg"] + n * 2048 + cc * 128
                            s.dma("sp", mg[:], self.FM[rm:rm + 128, TP:TP + 1024], writes=[mg])
                            for tl in range(2):
                                pa = pacc.next()
                                for k in range(4):
                                    s.op("pe", lambda e, pa=pa, n=n, k=k, tl=tl, cb=cb: e.matmul(
                                        pa[:], wbs[n][:, k, cb * 128:(cb + 1) * 128], ys[n][:, k, tl * 512:(tl + 1) * 512],
                                        start=(k == 0), stop=(k == 3)), reads=[wbs[n], ys[n]], writes=[pa])
                                if n == 0:
                                    s.op("dve", lambda e, pa=pa, tl=tl, mg=mg: e.tensor_tensor(
                                        macc[tl][:], pa[:], mg[:, tl * 512:(tl + 1) * 512], ALU.mult),
                                        reads=[pa, mg], writes=[macc[tl]])
                                else:
                                    t_ = tm.next()
                                    s.op("dve", lambda e, pa=pa, tl=tl, mg=mg, t_=t_: e.tensor_tensor(
                                        t_[:], pa[:], mg[:, tl * 512:(tl + 1) * 512], ALU.mult),
                                        reads=[pa, mg], writes=[t_])
                                    if n == 1:
                                        s.op("pool", lambda e, tl=tl, t_=t_: e.tensor_tensor(macc[tl][:], macc[tl][:], t_[:], ALU.add),
                                             reads=[macc[tl], t_], writes=[macc[tl]])
                                    else:
                                        st = stg.next()
                                        s.op("pool", lambda e, tl=tl, t_=t_, st=st: e.tensor_tensor(
                                            st[:], macc[tl][:], t_[:], ALU.add),
                                            reads=[macc[tl], t_], writes=[st])
                                        tt = TP + tl * 512
                                        s.dma("sp", self.MP[cc // 2][(cc % 2) * 128:(cc % 2 + 1) * 128, tt:tt + 512], st[:], reads=[st])
                s.barrier()

    def phase4b(self, l, xsrc, xdst):
        s = self.s
        with ExitStack() as es3:
            wo = s.sb(es3, "wo", [128, 16, 2048], BF16)
            ggr = s.sb(es3, "ggr", [128, 2048], F32)
            m0r = Ring([s.sb(es3, "m0_%d" % i, [128, 16, 512], BF16) for i in range(1)])
            m1r = Ring([s.sb(es3, "m1_%d" % i, [128, 16, 512], BF16) for i in range(1)])
            mTr = Ring([s.sb(es3, "mT%d" % i, [128, 16, 512], BF16) for i in range(2)])
            xr = Ring([s.sb(es3, "xr%d" % i, [128, 2048], F32) for i in range(2)])
            yr = Ring([s.sb(es3, "yr%d" % i, [128, 2048], F32) for i in range(2)])
            junk = s.sb(es3, "junk4", [128, 512], BF16)
            ss = Ring([s.sb(es3, "ss4%d" % i, [128, 4], F32) for i in range(2)])
            rstd = Ring([s.sb(es3, "rstd4%d" % i, [128, 1], F32) for i in range(2)])
            for k4 in range(4):
                s.dma("pool", wo[:, k4 * 4:(k4 + 1) * 4, :],
                      self.w_out[l, k4 * 512:(k4 + 1) * 512, :].rearrange("(k p) c -> p k c", p=128), writes=[wo])
            s.dma("sp", ggr[:], self.GG[l:l + 1, :].to_broadcast([128, 2048]), writes=[ggr])
            for ti in range(NQT):
                t0 = ti * 512
                m0, m1, mT = m0r.next(), m1r.next(), mTr.next()
                for c8 in range(8):
                    for rk, mm in ((0, m0), (1, m1)):
                        s.dma("sp", mm[:, 2 * c8:2 * c8 + 2, :],
                              self.MG[c8][rk * 256:(rk + 1) * 256, t0:t0 + 512].rearrange("(q p) t -> p q t", p=128), writes=[mm])
                for hk in range(2):
                    eng = "pool" if hk == 0 else "dve"
                    s.op(eng, lambda e, hk=hk, m0=m0, m1=m1, mT=mT: e.tensor_tensor(
                        mT[:, hk * 8:(hk + 1) * 8, :], m0[:, hk * 8:(hk + 1) * 8, :], m1[:, hk * 8:(hk + 1) * 8, :], ALU.add),
                        reads=[m0, m1], writes=[mT])
                for bb in range(4):
                    tb = ti * 4 + bb
                    tt = tb * 128
                    x = xr.next()
                    s.dma("sp", x[:], xsrc[tt:tt + 128, :], writes=[x])
                    half_banks = self.psum[0:4] if tb % 2 == 0 else self.psum[4:8]
                    sst = ss.next()
                    for cb in range(4):
                        pa = half_banks[cb]
                        for k in range(16):
                            s.op("pe", lambda e, pa=pa, k=k, bb=bb, cb=cb, mT=mT: e.matmul(
                                pa[:], mT[:, k, bb * 128:(bb + 1) * 128], wo[:, k, cb * 512:(cb + 1) * 512],
                                start=(k == 0), stop=(k == 15)), reads=[mT, wo], writes=[pa])
                        s.op("act", lambda e, pa=pa, cb=cb, sst=sst: e.activation(junk[:], pa[:], AF.Square,
                                                                                 accum_out=sst[:, cb:cb + 1]),
                             reads=[pa], writes=[junk, sst])
                    rt = rstd.next()
                    s.op("dve", lambda e, sst=sst, rt=rt: e.tensor_reduce(rt[:], sst[:], mybir.AxisListType.X, ALU.add),
                         reads=[sst], writes=[rt])
                    s.op("act", lambda e, rt=rt: e.activation(rt[:], rt[:], AF.Ln, scale=1.0 / D, bias=1e-6), reads=[rt], writes=[rt])
                    s.op("act", lambda e, rt=rt: e.activation(rt[:], rt[:], AF.Exp, scale=-0.5), reads=[rt], writes=[rt])
                    y = yr.next()
                    for cb in range(4):
                        pa = half_banks[cb]
                        s.op("dve", lambda e, pa=pa, cb=cb, y=y, rt=rt: e.scalar_tensor_tensor(
                            y[:, cb * 512:(cb + 1) * 512], pa[:], rt[:, 0:1], ggr[:, cb * 512:(cb + 1) * 512], ALU.mult, ALU.mult),
                            reads=[pa, rt, ggr], writes=[y])
                    s.op("pool", lambda e, y=y, x=x: e.tensor_tensor(y[:], y[:], x[:], ALU.add), reads=[y, x], writes=[y])
                    s.dma("sp", xdst[tt:tt + 128, :], y[:], reads=[y])
            s.barrier()

    def build(self, phases=("0", "12", "3a", "3b", "3c", "4")):
        self.declare()
        self.setup()
        if "0" in phases:
            self.phase0()
        for l in range(self.n_layers):
            xsrc = self.x_in if l == 0 else self.XS[(l - 1) % 2]
            xdst = self.out if l == self.n_layers - 1 else self.XS[l % 2]
            if "12" in phases:
                for half in self.halves:
                    self.phase12(l, half, xsrc)
            if "3a" in phases:
                self.phase3_diff(l, 0)
            if "3b" in phases:
                self.phase3_nsa(l, 0)
            if "3c" in phases:
                self.phase3_sb(l, 0)
            if "4" in phases:
                self.phase4a(l)
                for c8 in range(8):
                    self.s.coll("AllGather", [[0, 1], [2, 3], [4, 5], [6, 7]], self.MP[c8], self.MG[c8])
                self.s.barrier()
                self.phase4b(l, xsrc, xdst)
        self.s.barrier()
        self.es.close()
        return self.nc


def make_in_maps(inputs, n_cores=8):
    k = _constants()
    f = lambda a: np.ascontiguousarray(np.asarray(a, dtype=np.float32))
    lam = np.stack([f(inputs["lambda_q1"]), f(inputs["lambda_k1"]), f(inputs["lambda_q2"]), f(inputs["lambda_k2"])], axis=1)
    shared = dict(
        norm_pre_g=f(inputs["norm_pre_g"]), norm_post_g=f(inputs["norm_post_g"]), w_ada=f(inputs["w_ada"]),
        b_ada=f(inputs["b_ada"]), lam=np.ascontiguousarray(lam),
        diff_norm_g=f(inputs["diff_norm_g"]), cmp_pe_k=f(inputs["cmp_pe_k"]), cmp_pe_v=f(inputs["cmp_pe_v"]),
        cmp_w1_k=f(inputs["cmp_w1_k"]), cmp_w1_v=f(inputs["cmp_w1_v"]), cmp_w2_k=f(inputs["cmp_w2_k"]),
        cmp_w2_v=f(inputs["cmp_w2_v"]), w_out=f(inputs["w_out"]),
        k_ident=k["ident"], k_prot=k["prot"], k_ropec=k["ropec"], k_ropes=k["ropes"], k_ovl=k["ovl"], k_eall=k["eall"])
    w_in = f(inputs["w_in"])
    w_br = f(inputs["w_branch"])
    per_hs = []
    for hs in range(2):
        per_hs.append(dict(w_in=np.ascontiguousarray(w_in[:, :, local_cols(hs)]),
                           w_branch=np.ascontiguousarray(w_br[:, :, hs * 512:(hs + 1) * 512, :])))
    x = f(inputs["x"])
    c = f(inputs["c"])
    maps = []
    for core in range(n_cores):
        b, hs = core // 2, core % 2
        m = dict(shared)
        m.update(per_hs[hs])
        m["x"] = np.ascontiguousarray(x[b])
        m["c"] = np.ascontiguousarray(c[b].reshape(16, 128).T)
        maps.append(m)
    return maps


def kernel(**inputs):
    n_cores = 8
    nc = Builder().build()
    maps = make_in_maps(inputs, n_cores)
    res = run_bass_kernel_spmd(nc, maps, core_ids=list(range(n_cores)))
    return np.stack([np.asarray(res.results[2 * b]["out"]) for b in range(4)], axis=0).astype(np.float32)
```

```python
import math
from contextlib import ExitStack

import numpy as np
import concourse.bass as bass
import concourse.mybir as mybir
from concourse.bass_utils import run_bass_kernel_spmd

F32 = mybir.dt.float32
BF16 = mybir.dt.bfloat16
AF = mybir.ActivationFunctionType
ALU = mybir.AluOpType

D = 2048
SEQ = 4096
DEPTH = 4
HD = 128
N_IN = 17944
NQT = SEQ // 512
NKB = SEQ // 128
SCALE = HD ** -0.5
NCMP = 255
BIG = 30000.0

COLG = dict(aq=0, ak=1024, av=2048, az=3072, bq=4096, bkc=5120, bvc=5376, bks=5632, bvs=5888,
            bkw=6144, bvw=6400, bg=6656, bz=6680, cq=7704, ck=8728, cv=9752, cz=10776, mg=11800)
LOCAL = (("aq", 512), ("ak", 512), ("av", 512), ("az", 512), ("bq", 512), ("bkc", 128), ("bvc", 128),
         ("bks", 128), ("bvs", 128), ("bkw", 128), ("bvw", 128), ("bg", 12), ("bz", 512),
         ("cq", 512), ("ck", 512), ("cv", 512), ("cz", 512), ("mg", 6144))
COL = {}
_c = 0
for _n, _w in LOCAL:
    COL[_n] = _c
    _c += _w
NLOC = _c


def local_cols(hs):
    idx = []
    for n, w in LOCAL:
        g0 = COLG[n] + (0 if n == "mg" else hs * w)
        idx.append(np.arange(g0, g0 + w))
    return np.concatenate(idx)


FMROWS = {}
_r = 0
for _n, _w in (("aq", 512), ("ak", 512), ("az", 512), ("bq", 512), ("bkc", 128), ("bvc", 128),
               ("bks", 128), ("bkw", 128), ("bg", 128), ("bz", 512), ("cq", 512), ("ck", 512),
               ("cz", 512), ("mg", 6144)):
    FMROWS[_n] = _r
    _r += _w
NFM = _r
VTCOL = dict(av=0, bvs=512, bvw=640, cv=768)
NVT = 1280
NYS = 1536
YSROW = dict(a=0, b=512, c=1024)
GROUPS = [("aq", 0, 512, "rope"), ("ak", 0, 512, "rope"), ("av", 0, 512, "v"), ("az", 0, 512, "silu"),
          ("bq", 0, 512, "rope"), ("bkc", 0, 128, "rope"), ("bvc", 0, 128, "fm"), ("bks", 0, 128, "rope"),
          ("bvs", 0, 128, "v"), ("bkw", 0, 128, "rope"), ("bvw", 0, 128, "v"), ("bg", 0, 12, "sig"),
          ("bz", 0, 512, "silu"), ("cq", 0, 512, "fm"), ("ck", 0, 512, "fm"), ("cv", 0, 512, "v"),
          ("cz", 0, 512, "silu")]
GROUPS += [("mg", 512 * _i, 512, "sig") for _i in range(12)]


class T:
    def __init__(self, ap, name=""):
        self.ap = ap
        self.name = name
        self.lw = None
        self.rd = []

    def __getitem__(self, idx):
        return self.ap[idx]


class Sched:
    ENG = ("pe", "act", "dve", "pool", "sp")

    def __init__(self, nc, es, n_dma_sems=4):
        self.nc = nc
        self.es = es
        self.eng = {"pe": nc.tensor, "act": nc.scalar, "dve": nc.vector, "pool": nc.gpsimd, "sp": nc.sync}
        self.sem = {}
        self.cnt = {}
        for e in self.ENG:
            self.sem[e] = es.enter_context(nc.semaphore("s_" + e))
            self.cnt[e] = 0
        self.dsem = {}
        for q in ("sp", "pool", "act"):
            lst = []
            for i in range(n_dma_sems):
                k = "d_%s%d" % (q, i)
                self.sem[k] = es.enter_context(nc.semaphore(k))
                self.cnt[k] = 0
                lst.append(k)
            self.dsem[q] = [lst, 0]
        self.waited = {}
        self.n_ins = 0

    def sb(self, es, name, shape, dt):
        self.uid = getattr(self, "uid", 0) + 1
        name = "%s_u%d" % (name, self.uid)
        return T(es.enter_context(self.nc.sbuf_tensor(name, list(shape), dt)), name)

    def ps(self, es, name, shape, dt=F32):
        return T(es.enter_context(self.nc.psum_tensor(name, list(shape), dt)), name)

    def _wait(self, e, key, val):
        if val <= 0 or self.waited.get((e, key), 0) >= val:
            return
        self.eng[e].wait_ge(self.sem[key], val)
        self.waited[(e, key)] = val

    def _deps(self, e, reads, writes):
        for r in reads:
            if r.lw is not None:
                self._wait(e, *r.lw)
        for w in writes:
            if w.lw is not None and w.lw[0] != e:
                self._wait(e, *w.lw)
            for (k, v) in w.rd:
                if k != e:
                    self._wait(e, k, v)

    def _mark(self, key, val, reads, writes):
        for w in writes:
            w.lw = (key, val)
            w.rd = []
        for r in reads:
            if r in writes:
                continue
            r.rd.append((key, val))
            if len(r.rd) > 16:
                d = {}
                for (k, v) in r.rd:
                    d[k] = max(d.get(k, 0), v)
                r.rd = list(d.items())

    def op(self, e, fn, reads=(), writes=()):
        self._deps(e, reads, writes)
        ins = fn(self.eng[e])
        self.cnt[e] += 1
        ins.then_inc(self.sem[e], 1)
        self._mark(e, self.cnt[e], reads, writes)
        self.n_ins += 1
        return ins

    def dma(self, q, out, in_, reads=(), writes=(), **kw):
        lst, i = self.dsem[q]
        k = lst[i % len(lst)]
        self.dsem[q][1] = i + 1
        self._wait(q, k, self.cnt[k])
        self._deps(q, reads, writes)
        ins = self.eng[q].dma_start(out=out, in_=in_, **kw)
        self.cnt[k] += 16
        ins.then_inc(self.sem[k], 16)
        self._mark(k, self.cnt[k], reads, writes)
        self.n_ins += 1
        return ins

    def coll(self, kind, groups, src, dst, op=None):
        if "cc" not in self.sem:
            self.sem["cc"] = self.es.enter_context(self.nc.semaphore("s_cc"))
            self.cnt["cc"] = 0
        ins = self.nc.gpsimd.collective_compute(kind, op if op is not None else ALU.bypass, replica_groups=groups,
                                                ins=[src.opt()], outs=[dst.opt()])
        self.cnt["cc"] += 1
        ins.then_inc(self.sem["cc"])
        self.n_ins += 1
        return ins

    def barrier(self):
        for e in self.ENG:
            for k in self.sem:
                if k != e:
                    self._wait(e, k, self.cnt[k])


class Ring:
    def __init__(self, items):
        self.items = items
        self.i = 0

    def next(self):
        t = self.items[self.i % len(self.items)]
        self.i += 1
        return t


def _constants():
    c = {}
    c["ident"] = np.eye(128, dtype=np.float32)
    prot = np.zeros((128, 128), np.float32)
    for i in range(16):
        prot[i + 16, i] = -1.0
        prot[i, i + 16] = 1.0
    c["prot"] = prot
    pos = np.arange(SEQ, dtype=np.float32)
    inv = (np.float32(500000.0) ** (-np.arange(0, 32, 2, dtype=np.float32) / np.float32(32))).astype(np.float32)
    ang = (pos[None, :] * inv[:, None]).astype(np.float32)
    ct = np.ones((128, SEQ), np.float32)
    st = np.zeros((128, SEQ), np.float32)
    ct[0:16] = np.cos(ang); ct[16:32] = np.cos(ang)
    st[0:16] = np.sin(ang); st[16:32] = np.sin(ang)
    c["ropec"] = ct
    c["ropes"] = st
    n = np.arange(256)[:, None]
    j = np.arange(64)[None, :]
    ov = ((16 * n < 64 * j + 64) & (16 * n + 32 > 64 * j) & (n < NCMP)).astype(np.float32)
    c["ovl"] = ov.reshape(2, 128, 64).transpose(1, 0, 2).copy()
    k = np.arange(SEQ)[None, :]
    c["eall"] = (np.arange(64)[:, None] == (k // 64)).astype(np.float32)
    return c


class Builder:
    def __init__(self, n_layers=DEPTH, halves=(0, 1), headsets=(0, 1), dbg=False):
        self.n_layers = n_layers
        self.halves = halves
        self.headsets = headsets
        self.dbg = dbg
        self.nc = bass.Bass("TRN2", target_bir_lowering=False)
        self.es = ExitStack()
        self.s = Sched(self.nc, self.es)

    def declare(self):
        nc = self.nc
        ein = lambda name, shape: nc.dram_tensor(name, list(shape), F32, kind="ExternalInput").ap()
        self.x_in = ein("x", [SEQ, D])
        self.c_in = ein("c", [128, 16])
        self.norm_pre_g = ein("norm_pre_g", [DEPTH, D])
        self.norm_post_g = ein("norm_post_g", [DEPTH, D])
        self.w_ada = ein("w_ada", [DEPTH, D, 3 * D])
        self.b_ada = ein("b_ada", [DEPTH, 3 * D])
        self.w_in = ein("w_in", [DEPTH, D, NLOC])
        self.lam_in = ein("lam", [DEPTH, 4, 128])
        self.diff_norm_g = ein("diff_norm_g", [DEPTH, 256])
        self.cmp_pe = [ein("cmp_pe_k", [DEPTH, 32, 128]), ein("cmp_pe_v", [DEPTH, 32, 128])]
        self.cmp_w1 = [ein("cmp_w1_k", [DEPTH, 4096, 128]), ein("cmp_w1_v", [DEPTH, 4096, 128])]
        self.cmp_w2 = [ein("cmp_w2_k", [DEPTH, 128, 128]), ein("cmp_w2_v", [DEPTH, 128, 128])]
        self.w_branch = ein("w_branch", [DEPTH, 3, 512, D])
        self.w_out = ein("w_out", [DEPTH, D, D])
        self.k_ident = ein("k_ident", [128, 128])
        self.k_prot = ein("k_prot", [128, 128])
        self.k_ropec = ein("k_ropec", [128, SEQ])
        self.k_ropes = ein("k_ropes", [128, SEQ])
        self.k_ovl = ein("k_ovl", [128, 2, 64])
        self.k_eall = ein("k_eall", [64, SEQ])
        self.out = nc.dram_tensor("out", [SEQ, D], F32, kind="ExternalOutput").ap()
        kind = "ExternalOutput" if self.dbg else "Internal"
        self.FM = nc.dram_tensor("fm", [NFM, SEQ], BF16, kind=kind).ap()
        self.VT = nc.dram_tensor("vt", [SEQ, NVT], BF16, kind=kind).ap()
        self.YS = nc.dram_tensor("ys", [NYS, SEQ], BF16, kind=kind).ap()
        self.MP = [nc.dram_tensor("mp%d" % i, [256, SEQ], BF16, kind="Internal").ap() for i in range(8)]
        self.MG = [nc.dram_tensor("mgath%d" % i, [512, SEQ], BF16, kind="Internal").ap() for i in range(8)]
        self.XS = [nc.dram_tensor("xs%d" % i, [SEQ, D], F32, kind="Internal").ap() for i in range(2)]
        self.GG = nc.dram_tensor("gg", [DEPTH, D], F32, kind="Internal").ap()

    def setup(self):
        s, es = self.s, self.es
        self.ident = s.sb(es, "ident", [128, 128], F32)
        self.prot = s.sb(es, "prot", [128, 128], F32)
        self.ones_b = s.sb(es, "ones_b", [128, 128], BF16)
        self.ones_f = s.sb(es, "ones_f", [128, 128], F32)
        self.ustr = s.sb(es, "ustr", [128, 128], BF16)
        self.ovl = s.sb(es, "ovl", [128, 2, 64], F32)
        self.eall = s.sb(es, "eall", [64, SEQ], BF16)
        self.modp = s.sb(es, "modp", [128, DEPTH, 4, 16], F32)
        self.psum = [s.ps(es, "pb%d" % i, [128, 512], F32) for i in range(8)]
        s.dma("sp", self.ident[:], self.k_ident, writes=[self.ident])
        s.dma("sp", self.prot[:], self.k_prot, writes=[self.prot])
        s.dma("sp", self.ovl[:], self.k_ovl, writes=[self.ovl])
        s.dma("pool", self.eall[:], self.k_eall, writes=[self.eall])
        s.op("dve", lambda e: e.memset(self.ones_b[:], 1.0), writes=[self.ones_b])
        s.op("dve", lambda e: e.memset(self.ones_f[:], 1.0), writes=[self.ones_f])
        s.op("pool", lambda e: e.memset(self.ustr[:], 1.0), writes=[self.ustr])
        s.op("pool", lambda e: e.affine_select(self.ustr[:], self.ustr[:], [[-1, 128]], ALU.is_gt, 0.0,
                                               base=0, channel_multiplier=1),
             reads=[self.ustr], writes=[self.ustr])

    def phase0(self):
        s = self.s
        with ExitStack() as es:
            cs = s.sb(es, "cs", [128, 16], F32)
            wts = Ring([s.sb(es, "wada%d" % i, [128, 16, 512], F32) for i in range(2)])
            tmp = s.sb(es, "p0tmp", [128, 48], F32)
            bada = s.sb(es, "bada", [128, 48], F32)
            gpre = s.sb(es, "gpre", [128, 16], F32)
            gpost = s.sb(es, "gpost", [128, 16], F32)
            s.dma("sp", cs[:], self.c_in, writes=[cs])
            s.op("act", lambda e: e.activation(cs[:], cs[:], AF.Silu), reads=[cs], writes=[cs])
            pm = self.psum[0]
            for l in range(self.n_layers):
                s.dma("sp", bada[:], self.b_ada[l].rearrange("(j p) -> p j", p=128), writes=[bada],
                      allow_slow_non_contiguous=True)
                s.dma("sp", gpre[:], self.norm_pre_g[l].rearrange("(j p) -> p j", p=128), writes=[gpre],
                      allow_slow_non_contiguous=True)
                s.dma("sp", gpost[:], self.norm_post_g[l].rearrange("(j p) -> p j", p=128), writes=[gpost],
                      allow_slow_non_contiguous=True)
                for g in range(12):
                    wt = wts.next()
                    s.dma("sp", wt[:], self.w_ada[l, :, g * 512:(g + 1) * 512].rearrange("(j p) c -> p j c", p=128),
                          writes=[wt])
                    for cb in range(4):
                        col = g * 4 + cb
                        for j in range(16):
                            s.op("pe", lambda e, wt=wt, cb=cb, j=j, col=col: e.matmul(
                                pm[:, col:col + 1], wt[:, j, cb * 128:(cb + 1) * 128], cs[:, j:j + 1],
                                start=(j == 0), stop=(j == 15)), reads=[wt, cs], writes=[pm])
                s.op("dve", lambda e: e.tensor_tensor(tmp[:], pm[:, 0:48], bada[:], ALU.add),
                     reads=[pm, bada], writes=[tmp])
                mp = self.modp
                s.op("dve", lambda e, l=l: e.scalar_tensor_tensor(mp[:, l, 0, :], tmp[:, 16:32], 1.0, gpre[:],
                                                                    ALU.add, ALU.mult),
                     reads=[tmp, gpre], writes=[mp])
                s.op("dve", lambda e, l=l: e.tensor_copy(mp[:, l, 1, :], tmp[:, 0:16]), reads=[tmp], writes=[mp])
                s.op("dve", lambda e, l=l: e.tensor_tensor(mp[:, l, 2, :], tmp[:, 32:48], gpost[:], ALU.mult),
                     reads=[tmp, gpost], writes=[mp])
                s.dma("sp", self.GG[l].rearrange("(j p) -> p j", p=128), mp[:, l, 2, :], reads=[mp],
                      allow_slow_non_contiguous=True)
            s.barrier()

    def phase12(self, l, half, xsrc):
        s = self.s
        T0 = half * 2048
        with ExitStack() as es:
            hT = [s.sb(es, "hT%d" % i, [128, 16, 512], BF16) for i in range(4)]
            xt = s.sb(es, "xt", [128, 4, 2048], F32)
            junk = s.sb(es, "junk", [128, 2048], BF16)
            ss = s.sb(es, "ss", [128, 4], F32)
            rstd = s.sb(es, "rstd", [128, 4], F32)
            ropec = s.sb(es, "ropec", [128, 2048], F32)
            ropes = s.sb(es, "ropes", [128, 2048], F32)
            wts = Ring([s.sb(es, "wt%d" % i, [128, 16, 512], BF16) for i in range(3)])
            qf = Ring([s.sb(es, "qf%d" % i, [128, 512], F32) for i in range(2)])
            t1 = Ring([s.sb(es, "t1%d" % i, [128, 512], F32) for i in range(2)])
            t2 = Ring([s.sb(es, "t2%d" % i, [128, 512], F32) for i in range(2)])
            stg = Ring([s.sb(es, "stg%d" % i, [128, 512], BF16) for i in range(4)])
            pacc = Ring(self.psum[0:4])
            prot_ps = Ring(self.psum[4:6])
            ptr = Ring(self.psum[6:8])
            mp = self.modp
            s.dma("sp", ropec[:], self.k_ropec[:, T0:T0 + 2048], writes=[ropec])
            s.dma("sp", ropes[:], self.k_ropes[:, T0:T0 + 2048], writes=[ropes])
            for ti in range(4):
                t0 = T0 + ti * 512
                s.dma("sp", xt[:], xsrc[t0:t0 + 512, :].rearrange("(b p) d -> p b d", p=128), writes=[xt])
                for b in range(4):
                    s.op("act", lambda e, b=b: e.activation(junk[:], xt[:, b, :], AF.Square,
                                                            accum_out=ss[:, b:b + 1]),
                         reads=[xt], writes=[junk, ss])
                s.op("act", lambda e: e.activation(rstd[:], ss[:], AF.Ln, scale=1.0 / D, bias=1e-6),
                     reads=[ss], writes=[rstd])
                s.op("act", lambda e: e.activation(rstd[:], rstd[:], AF.Exp, scale=-0.5),
                     reads=[rstd], writes=[rstd])
                for b in range(4):
                    s.op("dve", lambda e, b=b: e.tensor_scalar(xt[:, b, :], xt[:, b, :], rstd[:, b:b + 1], None,
                                                               ALU.mult),
                         reads=[xt, rstd], writes=[xt])
                for j in range(16):
                    pt = ptr.next()
                    for b in range(4):
                        s.op("pe", lambda e, b=b, j=j, pt=pt: e.transpose(
                            pt[:, b * 128:(b + 1) * 128], xt[:, b, j * 128:(j + 1) * 128], self.ident[:]),
                            reads=[xt, self.ident], writes=[pt])
                    s.op("dve", lambda e, j=j, pt=pt, ti=ti: e.tensor_scalar(
                        hT[ti][:, j, :], pt[:], mp[:, l, 0, j:j + 1], mp[:, l, 1, j:j + 1], ALU.mult, ALU.add),
                        reads=[pt, mp], writes=[hT[ti]])
            wq = []

            def issue_w(gi):
                if gi < len(GROUPS):
                    name_, off_, gw_, _ = GROUPS[gi]
                    c0_ = COL[name_] + off_
                    wt_ = wts.next()
                    s.dma("pool", wt_[:, :, 0:gw_], self.w_in[l, :, c0_:c0_ + gw_].rearrange("(j p) c -> p j c", p=128),
                          writes=[wt_])
                    wq.append(wt_)

            issue_w(0)
            issue_w(1)
            for gi, (name, off, gw, kind) in enumerate(GROUPS):
                issue_w(gi + 2)
                wt = wq[gi]
                if kind == "v":
                    vc0 = VTCOL[name] + off
                    for tb in range(16):
                        pa = pacc.next()
                        ti, bb = tb // 4, tb % 4
                        for j in range(16):
                            s.op("pe", lambda e, pa=pa, ti=ti, bb=bb, j=j, wt=wt: e.matmul(
                                pa[:, 0:gw], hT[ti][:, j, bb * 128:(bb + 1) * 128], wt[:, j, 0:gw],
                                start=(j == 0), stop=(j == 15)), reads=[hT[ti], wt], writes=[pa])
                        st = stg.next()
                        eng = "act" if tb % 2 == 0 else "dve"
                        if eng == "act":
                            s.op("act", lambda e, st=st, pa=pa: e.copy(st[:, 0:gw], pa[:, 0:gw]),
                                 reads=[pa], writes=[st])
                        else:
                            s.op("dve", lambda e, st=st, pa=pa: e.tensor_copy(st[:, 0:gw], pa[:, 0:gw]),
                                 reads=[pa], writes=[st])
                        tt = T0 + tb * 128
                        s.dma("sp", self.VT[tt:tt + 128, vc0:vc0 + gw], st[:, 0:gw], reads=[st])
                    continue
                nblk = (gw + 127) // 128
                pending = []
                for blk in range(nblk):
                    bw = min(128, gw - blk * 128)
                    r0 = FMROWS[name] + off + blk * 128
                    for ti in range(4):
                        t0 = T0 + ti * 512
                        tl = ti * 512
                        pa = pacc.next()
                        for j in range(16):
                            s.op("pe", lambda e, pa=pa, ti=ti, j=j, wt=wt, blk=blk, bw=bw: e.matmul(
                                pa[0:bw, :], wt[:, j, blk * 128:blk * 128 + bw], hT[ti][:, j, :],
                                start=(j == 0), stop=(j == 15)), reads=[hT[ti], wt], writes=[pa])
                        st = stg.next()
                        if kind == "fm":
                            s.op("dve", lambda e, st=st, pa=pa: e.tensor_copy(st[:], pa[:]), reads=[pa], writes=[st])
                        elif kind == "silu":
                            s.op("act", lambda e, st=st, pa=pa: e.activation(st[:], pa[:], AF.Silu),
                                 reads=[pa], writes=[st])
                        elif kind == "sig":
                            s.op("act", lambda e, st=st, pa=pa, bw=bw: e.activation(st[0:bw, :], pa[0:bw, :], AF.Sigmoid),
                                 reads=[pa], writes=[st])
                        elif kind == "rope":
                            q = qf.next()
                            s.op("act", lambda e, q=q, pa=pa: e.copy(q[:], pa[:]), reads=[pa], writes=[q])

                            def finish(q=q, st=st, tl=tl, r0=r0, t0=t0, bw=bw):
                                pr = prot_ps.next()
                                a1 = t1.next()
                                a2 = t2.next()
                                s.op("pe", lambda e: e.matmul(pr[:], self.prot[:], q[:], start=True, stop=True),
                                     reads=[q, self.prot], writes=[pr])
                                s.op("pool", lambda e: e.tensor_tensor(
                                    a1[:], q[:], ropec[:, tl:tl + 512], ALU.mult), reads=[q, ropec], writes=[a1])
                                s.op("dve", lambda e: e.tensor_tensor(
                                    a2[:], pr[:], ropes[:, tl:tl + 512], ALU.mult), reads=[pr, ropes], writes=[a2])
                                s.op("pool", lambda e: e.tensor_tensor(st[:], a1[:], a2[:], ALU.add),
                                     reads=[a1, a2], writes=[st])
                                s.dma("sp", self.FM[r0:r0 + bw, t0:t0 + 512], st[0:bw, :], reads=[st])

                            if pending:
                                pending.pop()()
                            pending.append(finish)
                            continue
                        s.dma("sp", self.FM[r0:r0 + bw, t0:t0 + 512], st[0:bw, :], reads=[st])
                if pending:
                    pending.pop()()
            s.barrier()

    def _causal(self, t, k0, t0, npart=128):
        self.s.op("pool", lambda e: e.affine_select(t[0:npart, :], t[0:npart, :], [[1, 512]], ALU.is_ge, 0.0,
                                                    base=t0 - k0, channel_multiplier=-1),
                  reads=[t], writes=[t])

    def _softmax_attn(self, qT, blocks, p_ring, ps_ring, psum_sum, psum_o, scale=SCALE):
        s = self.s
        nb = len(blocks)

        def stage_b(bi, blk, p):
            s.op("pe", lambda e: e.matmul(psum_sum[:], self.ones_b[:], p[:], start=(bi == 0), stop=(bi == nb - 1)),
                 reads=[self.ones_b, p], writes=[psum_sum])
            for oi, (v_t, v_ap) in enumerate(blk["v"]):
                po = psum_o[oi]
                s.op("pe", lambda e, po=po, v_ap=v_ap: e.matmul(po[:], v_ap, p[:], start=(bi == 0), stop=(bi == nb - 1)),
                     reads=[v_t, p], writes=[po])

        prev = None
        for bi, blk in enumerate(blocks):
            ps = ps_ring.next()
            kT_t, kT_ap = blk["kT"]
            bias = blk.get("bias")
            s.op("pe", lambda e: e.matmul(ps[:], kT_ap, qT[:], start=True, stop=(bias is None)),
                 reads=[kT_t, qT], writes=[ps])
            if bias is not None:
                bl_t, bl_ap, br_t, br_ap = bias
                s.op("pe", lambda e: e.matmul(ps[:], bl_ap, br_ap, start=False, stop=True),
                     reads=[bl_t, br_t], writes=[ps])
            p = p_ring.next()
            s.op("act", lambda e: e.activation(p[:], ps[:], AF.Exp, scale=scale), reads=[ps], writes=[p])
            if blk.get("mask") is not None:
                blk["mask"](p)
            if prev is not None:
                stage_b(*prev)
            prev = (bi, blk, p)
        stage_b(*prev)

    def phase3_diff(self, l, hs):
        s = self.s
        lam_init = 0.8 - 0.6 * math.exp(-0.3 * l)
        with ExitStack() as es:
            kT = [Ring([s.sb(es, "akT%d_%d" % (c, i), [128, SEQ], BF16) for i in range(2)]) for c in range(2)]
            vv = Ring([s.sb(es, "avv%d" % i, [128, NKB, 256], BF16) for i in range(2)])
            qTs = Ring([s.sb(es, "aq%d" % i, [128, 512], BF16) for i in range(4)])
            szs = Ring([s.sb(es, "asz%d" % i, [128, 512], BF16) for i in range(4)])
            p_ring = Ring([s.sb(es, "ap%d" % i, [128, 512], BF16) for i in range(4)])
            oc = [[s.sb(es, "aoc%d%d" % (c, h), [128, 512], F32) for h in range(2)] for c in range(2)]
            rs = s.sb(es, "ars", [128, 512], F32)
            sq = [s.sb(es, "asq%d" % h, [128, 512], F32) for h in range(2)]
            rstd = s.sb(es, "arstd", [128, 512], F32)
            yst = Ring([s.sb(es, "ayst%d" % i, [128, 512], BF16) for i in range(2)])
            lamt = s.sb(es, "lamt", [128, 4], F32)
            lam2 = s.sb(es, "lam2", [128, 2], F32)
            neglam = s.sb(es, "neglam", [128, 1], F32)
            gco = s.sb(es, "gco", [128, 2], F32)
            ps_ring = Ring([self.psum[0], self.psum[1], self.psum[3]])
            psum_sum = self.psum[2]
            psum_o = self.psum[4:6]
            pmisc = self.psum[6]
            s.dma("sp", lamt[:], self.lam_in[l].rearrange("k p -> p k"), writes=[lamt], allow_slow_non_contiguous=True)
            s.op("dve", lambda e: e.tensor_tensor(lam2[:, 0:1], lamt[:, 0:1], lamt[:, 1:2], ALU.mult), reads=[lamt], writes=[lam2])
            s.op("dve", lambda e: e.tensor_tensor(lam2[:, 1:2], lamt[:, 2:3], lamt[:, 3:4], ALU.mult), reads=[lamt], writes=[lam2])
            s.op("pe", lambda e: e.matmul(pmisc[:, 0:2], self.ones_f[:], lam2[:], start=True, stop=True),
                 reads=[self.ones_f, lam2], writes=[pmisc])
            s.op("act", lambda e: e.activation(lam2[:], pmisc[:, 0:2], AF.Exp), reads=[pmisc], writes=[lam2])
            s.op("dve", lambda e: e.scalar_tensor_tensor(neglam[:], lam2[:, 1:2], -lam_init, lam2[:, 0:1], ALU.add, ALU.subtract),
                 reads=[lam2], writes=[neglam])
            s.dma("sp", gco[:], self.diff_norm_g[l].rearrange("(h p) -> p h", p=128), writes=[gco], allow_slow_non_contiguous=True)
            s.op("dve", lambda e: e.tensor_scalar(gco[:], gco[:], 1.0 - lam_init, None, ALU.mult), reads=[gco], writes=[gco])
            for h in range(2):
                kts = []
                for c in range(2):
                    kt = kT[c].next()
                    r0 = FMROWS["ak"] + h * 256 + c * 128
                    s.dma("sp", kt[:], self.FM[r0:r0 + 128, :], writes=[kt])
                    kts.append(kt)
                v = vv.next()
                vc = VTCOL["av"] + h * 256
                s.dma("sp", v[:], self.VT[:, vc:vc + 256].rearrange("(kb p) e -> p kb e", p=128), writes=[v])
                def load_q(i_, c_):
                    q_ = qTs.next()
                    r0_ = FMROWS["aq"] + h * 256 + c_ * 128
                    s.dma("sp", q_[:], self.FM[r0_:r0_ + 128, i_ * 512:i_ * 512 + 512], writes=[q_])
                    return q_

                steps = [(i_, c_) for i_ in range(NQT) for c_ in range(2)]
                q_next = load_q(*steps[0])
                for i in range(NQT):
                    t0 = i * 512
                    szt = []
                    for hf in range(2):
                        sz = szs.next()
                        rz = FMROWS["az"] + h * 256 + hf * 128
                        s.dma("sp", sz[:], self.FM[rz:rz + 128, t0:t0 + 512], writes=[sz])
                        szt.append(sz)
                    for c in range(2):
                        q = q_next
                        si = steps.index((i, c))
                        if si + 1 < len(steps):
                            q_next = load_q(*steps[si + 1])
                        blocks = []
                        for kb in range(4 * i + 4):
                            k0 = kb * 128
                            blk = dict(kT=(kts[c], kts[c][:, k0:k0 + 128]),
                                       v=[(v, v[:, kb, 0:128]), (v, v[:, kb, 128:256])])
                            if kb >= 4 * i:
                                blk["mask"] = (lambda p, k0=k0, t0=t0: self._causal(p, k0, t0))
                            blocks.append(blk)
                        self._softmax_attn(q, blocks, p_ring, ps_ring, psum_sum, psum_o)
                        s.op("dve", lambda e: e.reciprocal(rs[:], psum_sum[:]), reads=[psum_sum], writes=[rs])
                        for hf in range(2):
                            s.op("dve", lambda e, c=c, hf=hf: e.tensor_tensor(oc[c][hf][:], psum_o[hf][:], rs[:], ALU.mult),
                                 reads=[psum_o[hf], rs], writes=[oc[c][hf]])
                    for hf in range(2):
                        s.op("dve", lambda e, hf=hf: e.scalar_tensor_tensor(
                            oc[0][hf][:], oc[1][hf][:], neglam[:, 0:1], oc[0][hf][:], ALU.mult, ALU.add),
                            reads=[oc[1][hf], neglam, oc[0][hf]], writes=[oc[0][hf]])
                        s.op("pool", lambda e, hf=hf: e.tensor_tensor(sq[hf][:], oc[0][hf][:], oc[0][hf][:], ALU.mult),
                             reads=[oc[0][hf]], writes=[sq[hf]])
                    for hf in range(2):
                        s.op("pe", lambda e, hf=hf: e.matmul(pmisc[:], self.ones_f[:], sq[hf][:], start=(hf == 0), stop=(hf == 1)),
                             reads=[self.ones_f, sq[hf]], writes=[pmisc])
                    s.op("act", lambda e: e.activation(rstd[:], pmisc[:], AF.Ln, scale=1.0 / 256, bias=1e-5),
                         reads=[pmisc], writes=[rstd])
                    s.op("act", lambda e: e.activation(rstd[:], rstd[:], AF.Exp, scale=-0.5), reads=[rstd], writes=[rstd])
                    for hf in range(2):
                        sz = szt[hf]
                        s.op("dve", lambda e, hf=hf: e.scalar_tensor_tensor(
                            oc[0][hf][:], oc[0][hf][:], gco[:, hf:hf + 1], rstd[:], ALU.mult, ALU.mult),
                            reads=[oc[0][hf], gco, rstd], writes=[oc[0][hf]])
                        y = yst.next()
                        s.op("pool", lambda e, hf=hf, y=y, sz=sz: e.tensor_tensor(y[:], oc[0][hf][:], sz[:], ALU.mult),
                             reads=[oc[0][hf], sz], writes=[y])
                        ry = YSROW["a"] + h * 256 + hf * 128
                        s.dma("sp", self.YS[ry:ry + 128, t0:t0 + 512], y[:], reads=[y])
            s.barrier()

    def phase3_sb(self, l, hs):
        s = self.s
        with ExitStack() as es:
            kTr = Ring([s.sb(es, "ckT%d" % i, [128, SEQ], BF16) for i in range(2)])
            vvr = Ring([s.sb(es, "cvv%d" % i, [128, NKB, 128], BF16) for i in range(2)])
            qTs = Ring([s.sb(es, "cq%d" % i, [128, 512], BF16) for i in range(3)])
            szs = Ring([s.sb(es, "csz%d" % i, [128, 512], BF16) for i in range(3)])
            er = Ring([s.sb(es, "ce%d" % i, [128, 512], F32) for i in range(3)])
            lfr = Ring([s.sb(es, "clf%d" % i, [128, 512], F32) for i in range(3)])
            lbr = Ring([s.sb(es, "clb%d" % i, [128, 512], BF16) for i in range(3)])
            argr = Ring([s.sb(es, "carg%d" % i, [128, 512], F32) for i in range(3)])
            wr = Ring([s.sb(es, "cw%d" % i, [128, 512], F32) for i in range(3)])
            ar = Ring([s.sb(es, "ca%d" % i, [128, 512], BF16) for i in range(3)])
            R = s.sb(es, "cR", [128, 512], F32)
            yst = Ring([s.sb(es, "cyst%d" % i, [128, 512], BF16) for i in range(2)])
            zr = Ring(self.psum[0:2])
            c1r = Ring(self.psum[2:4])
            c2r = Ring(self.psum[4:6])
            po = self.psum[6]
            for h in range(4):
                kt = kTr.next()
                r0 = FMROWS["ck"] + h * 128
                s.dma("sp", kt[:], self.FM[r0:r0 + 128, :], writes=[kt])
                v = vvr.next()
                vc = VTCOL["cv"] + h * 128
                s.dma("sp", v[:], self.VT[:, vc:vc + 128].rearrange("(kb p) e -> p kb e", p=128), writes=[v])
                def load_qz(i_):
                    q_ = qTs.next()
                    rq = FMROWS["cq"] + h * 128
                    s.dma("sp", q_[:], self.FM[rq:rq + 128, i_ * 512:i_ * 512 + 512], writes=[q_])
                    sz_ = szs.next()
                    rz = FMROWS["cz"] + h * 128
                    s.dma("sp", sz_[:], self.FM[rz:rz + 128, i_ * 512:i_ * 512 + 512], writes=[sz_])
                    return q_, sz_

                qz_next = load_qz(0)
                for i in range(NQT):
                    t0 = i * 512
                    q, sz = qz_next
                    if i + 1 < NQT:
                        qz_next = load_qz(i + 1)
                    s.op("pool", lambda e: e.memset(R[:], 0.0), writes=[R])
                    kbs = list(range(4 * i + 3, -1, -1))

                    def stage_a(kb):
                        k0 = kb * 128
                        zp = zr.next()
                        s.op("pe", lambda e: e.matmul(zp[:], kt[:, k0:k0 + 128], q[:], start=True, stop=True),
                             reads=[kt, q], writes=[zp])
                        ee = er.next()
                        s.op("act", lambda e: e.activation(ee[:], zp[:], AF.Exp, scale=SCALE), reads=[zp], writes=[ee])
                        if kb >= 4 * i:
                            s.op("pool", lambda e: e.affine_select(
                                ee[:], ee[:], [[1, 512]], ALU.is_gt, 0.0, base=t0 - k0, channel_multiplier=-1),
                                reads=[ee], writes=[ee])
                        lf = lfr.next()
                        s.op("act", lambda e: e.activation(lf[:], ee[:], AF.Ln, bias=1.0), reads=[ee], writes=[lf])
                        lb = lbr.next()
                        s.op("pool", lambda e: e.tensor_copy(lb[:], lf[:]), reads=[lf], writes=[lb])
                        c1 = c1r.next()
                        c2 = c2r.next()
                        s.op("pe", lambda e: e.matmul(c1[:], self.ustr[:], lb[:], start=True, stop=True),
                             reads=[self.ustr, lb], writes=[c1])
                        s.op("pe", lambda e: e.matmul(c2[:], self.ones_b[:], lb[:], start=True, stop=True),
                             reads=[self.ones_b, lb], writes=[c2])
                        return (ee, lf, c1, c2)

                    def stage_b(bi, kb, st):
                        ee, lf, c1, c2 = st
                        arg = argr.next()
                        s.op("dve", lambda e: e.tensor_tensor(arg[:], c1[:], R[:], ALU.add),
                             reads=[c1, R], writes=[arg])
                        s.op("dve", lambda e: e.tensor_tensor(R[:], c2[:], R[:], ALU.add), reads=[c2, R], writes=[R])
                        s.op("dve", lambda e: e.tensor_tensor(arg[:], arg[:], lf[:], ALU.add),
                             reads=[arg, lf], writes=[arg])
                        w = wr.next()
                        s.op("act", lambda e: e.activation(w[:], arg[:], AF.Exp, scale=-1.0), reads=[arg], writes=[w])
                        a = ar.next()
                        s.op("pool", lambda e: e.tensor_tensor(a[:], ee[:], w[:], ALU.mult),
                             reads=[ee, w], writes=[a])
                        s.op("pe", lambda e: e.matmul(po[:], v[:, kb, :], a[:], start=(bi == 0), stop=(bi == len(kbs) - 1)),
                             reads=[v, a], writes=[po])

                    st_next = stage_a(kbs[0])
                    for bi, kb in enumerate(kbs):
                        st_cur = st_next
                        if bi + 1 < len(kbs):
                            st_next = stage_a(kbs[bi + 1])
                        stage_b(bi, kb, st_cur)
                    y = yst.next()
                    s.op("dve", lambda e, y=y, sz=sz: e.tensor_tensor(y[:], po[:], sz[:], ALU.mult), reads=[po, sz], writes=[y])
                    ry = YSROW["c"] + h * 128
                    s.dma("sp", self.YS[ry:ry + 128, t0:t0 + 512], y[:], reads=[y])
            s.barrier()

    def phase3_nsa(self, l, g):
        s = self.s
        with ExitStack() as es:
            big = [s.sb(es, "bbig%d" % i, [128, SEQ], BF16) for i in range(4)]
            vs = s.sb(es, "bvs", [128, NKB, 128], BF16)
            vw = s.sb(es, "bvw", [128, NKB, 128], BF16)
            w1 = s.sb(es, "bw1", [128, 32, 128], BF16)
            w2 = s.sb(es, "bw2", [128, 128], BF16)
            pe_sb = s.sb(es, "bpe", [32, 128], F32)
            peT = s.sb(es, "bpeT", [128, 32], BF16)
            c1 = s.sb(es, "bc1", [128, 1], F32)
            hsl = s.sb(es, "bhsl", [128, 256], BF16)
            kcmpT = s.sb(es, "bkcmpT", [128, 256], BF16)
            vcmp = s.sb(es, "bvcmp", [128, 2, 128], BF16)
            qsets = [[s.sb(es, "bq%d_%d" % (k, i), [128, 512], BF16) for i in range(4)] for k in range(2)]
            gts = Ring([s.sb(es, "bgt%d" % i, [128, 3, 512], BF16) for i in range(2)])
            szs = Ring([s.sb(es, "bsz%d" % i, [128, 512], BF16) for i in range(2)])
            pf = [s.sb(es, "bpf%d" % i, [128, 512], F32) for i in range(2)]
            pnb = Ring([s.sb(es, "bpnb%d" % i, [128, 512], BF16) for i in range(2)])
            rs = s.sb(es, "brs", [128, 512], F32)
            ocmp = [s.sb(es, "bocmp%d" % i, [128, 512], F32) for i in range(4)]
            impS = s.sb(es, "bimp", [128, 4, 64], F32)
            m8 = s.sb(es, "bm8", [128, 16], F32)
            wk = s.sb(es, "bwk", [128, 64], F32)
            sel = s.sb(es, "bsel", [128, 64], F32)
            negT = s.sb(es, "bnegT", [64, 512], BF16)
            p_ring = Ring([s.sb(es, "bp%d" % i, [128, 512], BF16) for i in range(4)])
            acc = s.sb(es, "bacc", [128, 512], F32)
            tmp = s.sb(es, "btmp", [128, 512], F32)
            yst = Ring([s.sb(es, "byst%d" % i, [128, 512], BF16) for i in range(2)])
            ps_ring = Ring([self.psum[0], self.psum[1], self.psum[7]])
            psum_sum = self.psum[2]
            pimp = self.psum[3]
            psum_o = [self.psum[4]]
            pmisc = self.psum[5]
            pmisc2 = self.psum[6]
            kcT, vcT, ksT, kwT = big
            for t_, nm in ((kcT, "bkc"), (vcT, "bvc"), (ksT, "bks"), (kwT, "bkw")):
                r0 = FMROWS[nm] + g * 128
                s.dma("sp", t_[:], self.FM[r0:r0 + 128, :], writes=[t_])
            for t_, nm in ((vs, "bvs"), (vw, "bvw")):
                vc = VTCOL[nm] + g * 128
                s.dma("sp", t_[:], self.VT[:, vc:vc + 128].rearrange("(kb p) e -> p kb e", p=128), writes=[t_])
            for kv in range(2):
                src = kcT if kv == 0 else vcT
                s.dma("pool", w1[:], self.cmp_w1[kv][l].rearrange("(l d) f -> d l f", d=128), writes=[w1])
                s.dma("pool", w2[:], self.cmp_w2[kv][l], writes=[w2])
                s.dma("sp", pe_sb[:], self.cmp_pe[kv][l], writes=[pe_sb])
                s.op("pe", lambda e: e.transpose(pmisc[:, 0:32], pe_sb[:], self.ident[0:32, 0:32]),
                     reads=[pe_sb, self.ident], writes=[pmisc])
                s.op("dve", lambda e: e.tensor_copy(peT[:], pmisc[:, 0:32]), reads=[pmisc], writes=[peT])
                for li in range(32):
                    s.op("pe", lambda e, li=li: e.matmul(pmisc2[:, 0:1], w1[:, li, :], peT[:, li:li + 1],
                                                         start=(li == 0), stop=(li == 31)),
                         reads=[w1, peT], writes=[pmisc2])
                s.op("dve", lambda e: e.tensor_copy(c1[:], pmisc2[:, 0:1]), reads=[pmisc2], writes=[c1])
                for li in range(32):
                    s.op("pe", lambda e, li=li, src=src: e.matmul(pmisc[:, 0:NCMP], w1[:, li, :],
                                                                   src[:, li:li + 16 * (NCMP - 1) + 1:16],
                                                                   start=(li == 0), stop=(li == 31)),
                         reads=[w1, src], writes=[pmisc])
                s.op("dve", lambda e: e.memset(hsl[:], 0.0), writes=[hsl])
                s.op("act", lambda e: e.activation(hsl[:, 0:NCMP], pmisc[:, 0:NCMP], AF.Silu, bias=c1[:, 0:1]),
                     reads=[pmisc, c1], writes=[hsl])
                if kv == 0:
                    s.op("pe", lambda e: e.matmul(pmisc2[:, 0:256], w2[:], hsl[:], start=True, stop=True),
                         reads=[w2, hsl], writes=[pmisc2])
                    s.op("dve", lambda e: e.tensor_copy(kcmpT[:], pmisc2[:, 0:256]), reads=[pmisc2], writes=[kcmpT])
                else:
                    for nb in range(2):
                        s.op("pe", lambda e, nb=nb: e.matmul(pmisc2[:, nb * 128:(nb + 1) * 128], hsl[:, nb * 128:(nb + 1) * 128],
                                                             w2[:], start=True, stop=True),
                             reads=[w2, hsl], writes=[pmisc2])
                    s.op("dve", lambda e: e.tensor_copy(vcmp[:], pmisc2[:, 0:256].rearrange("p (n d) -> p n d", n=2)),
                         reads=[pmisc2], writes=[vcmp])
            def load_qs(i_):
                qs_ = qsets[i_ % 2]
                for r_ in range(4):
                    rq = FMROWS["bq"] + (g * 4 + r_) * 128
                    s.dma("sp", qs_[r_][:], self.FM[rq:rq + 128, i_ * 512:i_ * 512 + 512], writes=[qs_[r_]])
                return qs_

            load_qs(0)
            for i in range(NQT):
                t0 = i * 512
                nbs = [nb for nb in range(2) if 16 * nb * 128 + 31 <= t0 + 511]
                qTs = qsets[i % 2]
                if i + 1 < NQT:
                    load_qs(i + 1)
                for r in range(4):
                    q = qTs[r]
                    for nb in nbs:
                        ps = ps_ring.next()
                        s.op("pe", lambda e, ps=ps, nb=nb, q=q: e.matmul(ps[:], kcmpT[:, nb * 128:(nb + 1) * 128], q[:],
                                                                          start=True, stop=True),
                             reads=[kcmpT, q], writes=[ps])
                        s.op("act", lambda e, ps=ps, nb=nb: e.activation(pf[nb][:], ps[:], AF.Exp, scale=SCALE),
                             reads=[ps], writes=[pf[nb]])
                        s.op("pool", lambda e, nb=nb, t0=t0: e.affine_select(
                            pf[nb][:], pf[nb][:], [[1, 512]], ALU.is_ge, 0.0,
                            base=t0 - 16 * nb * 128 - 31, channel_multiplier=-16), reads=[pf[nb]], writes=[pf[nb]])
                        s.op("pe", lambda e, nb=nb: e.matmul(psum_sum[:], self.ones_f[:], pf[nb][:], start=(nb == nbs[0]),
                                                             stop=(nb == nbs[-1])),
                             reads=[self.ones_f, pf[nb]], writes=[psum_sum])
                    s.op("dve", lambda e: e.tensor_scalar(rs[:], psum_sum[:], 1e-30, None, ALU.max), reads=[psum_sum], writes=[rs])
                    s.op("dve", lambda e: e.reciprocal(rs[:], rs[:]), reads=[rs], writes=[rs])
                    for nb in nbs:
                        s.op("dve", lambda e, nb=nb: e.tensor_tensor(pf[nb][:], pf[nb][:], rs[:], ALU.mult),
                             reads=[pf[nb], rs], writes=[pf[nb]])
                        for tb in range(4):
                            first = (r == 0 and nb == nbs[0])
                            last = (r == 3 and nb == nbs[-1])
                            s.op("pe", lambda e, nb=nb, tb=tb, first=first, last=last: e.matmul(
                                pimp[:, tb * 64:(tb + 1) * 64], pf[nb][:, tb * 128:(tb + 1) * 128], self.ovl[:, nb, :],
                                start=first, stop=last), reads=[pf[nb], self.ovl], writes=[pimp])
                        pb = pnb.next()
                        s.op("act", lambda e, pb=pb, nb=nb: e.copy(pb[:], pf[nb][:]), reads=[pf[nb]], writes=[pb])
                        s.op("pe", lambda e, pb=pb, nb=nb: e.matmul(psum_o[0][:], vcmp[:, nb, :], pb[:], start=(nb == nbs[0]),
                                                                     stop=(nb == nbs[-1])),
                             reads=[vcmp, pb], writes=[psum_o[0]])
                    s.op("dve", lambda e, r=r: e.tensor_copy(ocmp[r][:], psum_o[0][:]), reads=[psum_o[0]], writes=[ocmp[r]])
                s.op("dve", lambda e: e.tensor_copy(impS[:], pimp[:, 0:256].rearrange("p (a b) -> p a b", a=4)),
                     reads=[pimp], writes=[impS])
                for tb in range(4):
                    for hh in range(2):
                        tblk = 8 * i + 2 * tb + hh
                        p0 = hh * 64
                        if tblk < 63:
                            s.op("pool", lambda e, tb=tb, p0=p0, tblk=tblk: e.memset(impS[p0:p0 + 64, tb, tblk + 1:64], -1e30),
                                 reads=[impS], writes=[impS])
                        s.op("pool", lambda e, tb=tb, p0=p0, tblk=tblk: e.memset(impS[p0:p0 + 64, tb, tblk:tblk + 1], 1e6),
                             reads=[impS], writes=[impS])
                        s.op("pool", lambda e, tb=tb, p0=p0: e.memset(impS[p0:p0 + 64, tb, 0:1], 2e6),
                             reads=[impS], writes=[impS])
                for tb in range(4):
                    s.op("dve", lambda e, tb=tb: e.max(out=m8[:, 0:8], in_=impS[:, tb, :]), reads=[impS], writes=[m8])
                    s.op("dve", lambda e, tb=tb: e.match_replace(out=wk[:], in_to_replace=m8[:, 0:8], in_values=impS[:, tb, :],
                                                                 imm_value=-3e30), reads=[impS, m8], writes=[wk])
                    s.op("dve", lambda e: e.max(out=m8[:, 8:16], in_=wk[:]), reads=[wk], writes=[m8])
                    s.op("dve", lambda e, tb=tb: e.tensor_scalar(sel[:], impS[:, tb, :], m8[:, 15:16], None, ALU.is_ge),
                         reads=[impS, m8], writes=[sel])
                    s.op("dve", lambda e: e.tensor_scalar(sel[:], sel[:], -1.0, BIG, ALU.add, ALU.mult), reads=[sel], writes=[sel])
                    s.op("pe", lambda e: e.transpose(pmisc[0:64, 0:128], sel[:], self.ident[:]),
                         reads=[sel, self.ident], writes=[pmisc])
                    s.op("act", lambda e, tb=tb: e.copy(negT[:, tb * 128:(tb + 1) * 128], pmisc[0:64, 0:128]),
                         reads=[pmisc], writes=[negT])
                for r in range(4):
                    h = g * 4 + r
                    q = qTs[r]
                    gt = gts.next()
                    for k3 in range(3):
                        rg = FMROWS["bg"] + h * 3 + k3
                        s.dma("sp", gt[:, k3, :], self.FM[rg:rg + 1, t0:t0 + 512].to_broadcast([128, 512]), writes=[gt])
                    sz = szs.next()
                    rz = FMROWS["bz"] + h * 128
                    s.dma("sp", sz[:], self.FM[rz:rz + 128, t0:t0 + 512], writes=[sz])
                    s.op("pool", lambda e, r=r, gt=gt: e.tensor_tensor(acc[:], ocmp[r][:], gt[:, 0, :], ALU.mult),
                         reads=[ocmp[r], gt], writes=[acc])
                    blocks = []
                    for kb in range(4 * i + 4):
                        k0 = kb * 128
                        blk = dict(kT=(ksT, ksT[:, k0:k0 + 128]), v=[(vs, vs[:, kb, :])],
                                   bias=(self.eall, self.eall[:, k0:k0 + 128], negT, negT[:]))
                        if kb >= 4 * i:
                            blk["mask"] = (lambda p, k0=k0, t0=t0: self._causal(p, k0, t0))
                        blocks.append(blk)
                    self._softmax_attn(q, blocks, p_ring, ps_ring, psum_sum, psum_o)
                    s.op("dve", lambda e: e.reciprocal(rs[:], psum_sum[:]), reads=[psum_sum], writes=[rs])
                    s.op("dve", lambda e: e.tensor_tensor(tmp[:], psum_o[0][:], rs[:], ALU.mult), reads=[psum_o[0], rs], writes=[tmp])
                    s.op("pool", lambda e, gt=gt: e.tensor_tensor(tmp[:], tmp[:], gt[:, 1, :], ALU.mult), reads=[tmp, gt], writes=[tmp])
                    s.op("pool", lambda e: e.tensor_tensor(acc[:], acc[:], tmp[:], ALU.add), reads=[acc, tmp], writes=[acc])
                    blocks = []
                    for kb in range(max(0, 4 * i - 4), 4 * i + 4):
                        k0 = kb * 128
                        blk = dict(kT=(kwT, kwT[:, k0:k0 + 128]), v=[(vw, vw[:, kb, :])])
                        if kb >= 4 * i:
                            blk["mask"] = (lambda p, k0=k0, t0=t0: self._causal(p, k0, t0))
                        else:
                            blk["mask"] = (lambda p, k0=k0, t0=t0: s.op("pool", lambda e: e.affine_select(
                                p[:], p[:], [[-1, 512]], ALU.is_gt, 0.0, base=k0 - t0 + 512, channel_multiplier=1),
                                reads=[p], writes=[p]))
                        blocks.append(blk)
                    self._softmax_attn(q, blocks, p_ring, ps_ring, psum_sum, psum_o)
                    s.op("dve", lambda e: e.reciprocal(rs[:], psum_sum[:]), reads=[psum_sum], writes=[rs])
                    s.op("dve", lambda e: e.tensor_tensor(tmp[:], psum_o[0][:], rs[:], ALU.mult), reads=[psum_o[0], rs], writes=[tmp])
                    s.op("pool", lambda e, gt=gt: e.tensor_tensor(tmp[:], tmp[:], gt[:, 2, :], ALU.mult), reads=[tmp, gt], writes=[tmp])
                    s.op("pool", lambda e: e.tensor_tensor(acc[:], acc[:], tmp[:], ALU.add), reads=[acc, tmp], writes=[acc])
                    y = yst.next()
                    s.op("pool", lambda e, y=y, sz=sz: e.tensor_tensor(y[:], acc[:], sz[:], ALU.mult), reads=[acc, sz], writes=[y])
                    ry = YSROW["b"] + h * 128
                    s.dma("sp", self.YS[ry:ry + 128, t0:t0 + 512], y[:], reads=[y])
            s.barrier()

    def phase4a(self, l):
        s = self.s
        for tp in range(4):
            with ExitStack() as es2:
                ys = [s.sb(es2, "ysb%d" % n, [128, 4, 1024], BF16) for n in range(3)]
                wbr = Ring([s.sb(es2, "wb%d" % i, [128, 4, 512], BF16) for i in range(6)])
                mgr = Ring([s.sb(es2, "mg%d" % i, [128, 1024], BF16) for i in range(4)])
                macc = [s.sb(es2, "macc%d" % i, [128, 512], F32) for i in range(2)]
                tm = Ring([s.sb(es2, "tm%d" % i, [128, 512], F32) for i in range(3)])
                stg = Ring([s.sb(es2, "mstg%d" % i, [128, 512], BF16) for i in range(3)])
                pacc = Ring(self.psum[0:8])
                TP = tp * 1024
                for n in range(3):
                    s.dma("sp", ys[n][:], self.YS[n * 512:(n + 1) * 512, TP:TP + 1024].rearrange("(k p) t -> p k t", p=128),
                          writes=[ys[n]])
                for cg in range(4):
                    wbs = []
                    for n in range(3):
                        wb = wbr.next()
                        s.dma("pool", wb[:], self.w_branch[l, n, :, cg * 512:(cg + 1) * 512].rearrange("(k p) c -> p k c", p=128),
                              writes=[wb])
                        wbs.append(wb)
                    for cb in range(4):
                        cc = cg * 4 + cb
                        for n in range(3):
                            mg = mgr.next()
                            rm = FMROWS["mg"] + n * 2048 + cc * 128
                            s.dma("sp", mg[:], self.FM[rm:rm + 128, TP:TP + 1024], writes=[mg])
                            for tl in range(2):
                                pa = pacc.next()
                                for k in range(4):
                                    s.op("pe", lambda e, pa=pa, n=n, k=k, tl=tl, cb=cb: e.matmul(
                                        pa[:], wbs[n][:, k, cb * 128:(cb + 1) * 128], ys[n][:, k, tl * 512:(tl + 1) * 512],
                                        start=(k == 0), stop=(k == 3)), reads=[wbs[n], ys[n]], writes=[pa])
                                if n == 0:
                                    s.op("dve", lambda e, pa=pa, tl=tl, mg=mg: e.tensor_tensor(
                                        macc[tl][:], pa[:], mg[:, tl * 512:(tl + 1) * 512], ALU.mult),
                                        reads=[pa, mg], writes=[macc[tl]])
                                else:
                                    t_ = tm.next()
                                    s.op("dve", lambda e, pa=pa, tl=tl, mg=mg, t_=t_: e.tensor_tensor(
                                        t_[:], pa[:], mg[:, tl * 512:(tl + 1) * 512], ALU.mult),
                                        reads=[pa, mg], writes=[t_])
                                    if n == 1:
                                        s.op("pool", lambda e, tl=tl, t_=t_: e.tensor_tensor(macc[tl][:], macc[tl][:], t_[:], ALU.add),
                                             reads=[macc[tl], t_], writes=[macc[tl]])
                                    else:
                                        st = stg.next()
                                        s.op("pool", lambda e, tl=tl, t_=t_, st=st: e.tensor_tensor(
                                            st[:], macc[tl][:], t_[:], ALU.add),
                                            reads=[macc[tl], t_], writes=[st])
                                        tt = TP + tl * 512
                                        s.dma("sp", self.MP[cc // 2][(cc % 2) * 128:(cc % 2 + 1) * 128, tt:tt + 512], st[:], reads=[st])
                s.barrier()

    def phase4b(self, l, xsrc, xdst):
        s = self.s
        with ExitStack() as es3:
            wo = s.sb(es3, "wo", [128, 16, 2048], BF16)
            ggr = s.sb(es3, "ggr", [128, 2048], F32)
            m0r = Ring([s.sb(es3, "m0_%d" % i, [128, 16, 512], BF16) for i in range(1)])
            m1r = Ring([s.sb(es3, "m1_%d" % i, [128, 16, 512], BF16) for i in range(1)])
            mTr = Ring([s.sb(es3, "mT%d" % i, [128, 16, 512], BF16) for i in range(2)])
            xr = Ring([s.sb(es3, "xr%d" % i, [128, 2048], F32) for i in range(2)])
            yr = Ring([s.sb(es3, "yr%d" % i, [128, 2048], F32) for i in range(2)])
            junk = s.sb(es3, "junk4", [128, 512], BF16)
            ss = Ring([s.sb(es3, "ss4%d" % i, [128, 4], F32) for i in range(2)])
            rstd = Ring([s.sb(es3, "rstd4%d" % i, [128, 1], F32) for i in range(2)])
            for k4 in range(4):
                s.dma("pool", wo[:, k4 * 4:(k4 + 1) * 4, :],
                      self.w_out[l, k4 * 512:(k4 + 1) * 512, :].rearrange("(k p) c -> p k c", p=128), writes=[wo])
            s.dma("sp", ggr[:], self.GG[l:l + 1, :].to_broadcast([128, 2048]), writes=[ggr])
            for ti in range(NQT):
                t0 = ti * 512
                m0, m1, mT = m0r.next(), m1r.next(), mTr.next()
                for c8 in range(8):
                    for rk, mm in ((0, m0), (1, m1)):
                        s.dma("sp", mm[:, 2 * c8:2 * c8 + 2, :],
                              self.MG[c8][rk * 256:(rk + 1) * 256, t0:t0 + 512].rearrange("(q p) t -> p q t", p=128), writes=[mm])
                for hk in range(2):
                    eng = "pool" if hk == 0 else "dve"
                    s.op(eng, lambda e, hk=hk, m0=m0, m1=m1, mT=mT: e.tensor_tensor(
                        mT[:, hk * 8:(hk + 1) * 8, :], m0[:, hk * 8:(hk + 1) * 8, :], m1[:, hk * 8:(hk + 1) * 8, :], ALU.add),
                        reads=[m0, m1], writes=[mT])
                for bb in range(4):
                    tb = ti * 4 + bb
                    tt = tb * 128
                    x = xr.next()
                    s.dma("sp", x[:], xsrc[tt:tt + 128, :], writes=[x])
                    half_banks = self.psum[0:4] if tb % 2 == 0 else self.psum[4:8]
                    sst = ss.next()
                    for cb in range(4):
                        pa = half_banks[cb]
                        for k in range(16):
                            s.op("pe", lambda e, pa=pa, k=k, bb=bb, cb=cb, mT=mT: e.matmul(
                                pa[:], mT[:, k, bb * 128:(bb + 1) * 128], wo[:, k, cb * 512:(cb + 1) * 512],
                                start=(k == 0), stop=(k == 15)), reads=[mT, wo], writes=[pa])
                        s.op("act", lambda e, pa=pa, cb=cb, sst=sst: e.activation(junk[:], pa[:], AF.Square,
                                                                                 accum_out=sst[:, cb:cb + 1]),
                             reads=[pa], writes=[junk, sst])
                    rt = rstd.next()
                    s.op("dve", lambda e, sst=sst, rt=rt: e.tensor_reduce(rt[:], sst[:], mybir.AxisListType.X, ALU.add),
                         reads=[sst], writes=[rt])
                    s.op("act", lambda e, rt=rt: e.activation(rt[:], rt[:], AF.Ln, scale=1.0 / D, bias=1e-6), reads=[rt], writes=[rt])
                    s.op("act", lambda e, rt=rt: e.activation(rt[:], rt[:], AF.Exp, scale=-0.5), reads=[rt], writes=[rt])
                    y = yr.next()
                    for cb in range(4):
                        pa = half_banks[cb]
                        s.op("dve", lambda e, pa=pa, cb=cb, y=y, rt=rt: e.scalar_tensor_tensor(
                            y[:, cb * 512:(cb + 1) * 512], pa[:], rt[:, 0:1], ggr[:, cb * 512:(cb + 1) * 512], ALU.mult, ALU.mult),
                            reads=[pa, rt, ggr], writes=[y])
                    s.op("pool", lambda e, y=y, x=x: e.tensor_tensor(y[:], y[:], x[:], ALU.add), reads=[y, x], writes=[y])
                    s.dma("sp", xdst[tt:tt + 128, :], y[:], reads=[y])
            s.barrier()

    def build(self, phases=("0", "12", "3a", "3b", "3c", "4")):
        self.declare()
        self.setup()
        if "0" in phases:
            self.phase0()
        for l in range(self.n_layers):
            xsrc = self.x_in if l == 0 else self.XS[(l - 1) % 2]
            xdst = self.out if l == self.n_layers - 1 else self.XS[l % 2]
            if "12" in phases:
                for half in self.halves:
                    self.phase12(l, half, xsrc)
            if "3a" in phases:
                self.phase3_diff(l, 0)
            if "3b" in phases:
                self.phase3_nsa(l, 0)
            if "3c" in phases:
                self.phase3_sb(l, 0)
            if "4" in phases:
                self.phase4a(l)
                for c8 in range(8):
                    self.s.coll("AllGather", [[0, 1], [2, 3], [4, 5], [6, 7]], self.MP[c8], self.MG[c8])
                self.s.barrier()
                self.phase4b(l, xsrc, xdst)
        self.s.barrier()
        self.es.close()
        return self.nc


def make_in_maps(inputs, n_cores=8):
    k = _constants()
    f = lambda a: np.ascontiguousarray(np.asarray(a, dtype=np.float32))
    lam = np.stack([f(inputs["lambda_q1"]), f(inputs["lambda_k1"]), f(inputs["lambda_q2"]), f(inputs["lambda_k2"])], axis=1)
    shared = dict(
        norm_pre_g=f(inputs["norm_pre_g"]), norm_post_g=f(inputs["norm_post_g"]), w_ada=f(inputs["w_ada"]),
        b_ada=f(inputs["b_ada"]), lam=np.ascontiguousarray(lam),
        diff_norm_g=f(inputs["diff_norm_g"]), cmp_pe_k=f(inputs["cmp_pe_k"]), cmp_pe_v=f(inputs["cmp_pe_v"]),
        cmp_w1_k=f(inputs["cmp_w1_k"]), cmp_w1_v=f(inputs["cmp_w1_v"]), cmp_w2_k=f(inputs["cmp_w2_k"]),
        cmp_w2_v=f(inputs["cmp_w2_v"]), w_out=f(inputs["w_out"]),
        k_ident=k["ident"], k_prot=k["prot"], k_ropec=k["ropec"], k_ropes=k["ropes"], k_ovl=k["ovl"], k_eall=k["eall"])
    w_in = f(inputs["w_in"])
    w_br = f(inputs["w_branch"])
    per_hs = []
    for hs in range(2):
        per_hs.append(dict(w_in=np.ascontiguousarray(w_in[:, :, local_cols(hs)]),
                           w_branch=np.ascontiguousarray(w_br[:, :, hs * 512:(hs + 1) * 512, :])))
    x = f(inputs["x"])
    c = f(inputs["c"])
    maps = []
    for core in range(n_cores):
        b, hs = core // 2, core % 2
        m = dict(shared)
        m.update(per_hs[hs])
        m["x"] = np.ascontiguousarray(x[b])
        m["c"] = np.ascontiguousarray(c[b].reshape(16, 128).T)
        maps.append(m)
    return maps


def kernel(**inputs):
    n_cores = 8
    nc = Builder().build()
    maps = make_in_maps(inputs, n_cores)
    res = run_bass_kernel_spmd(nc, maps, core_ids=list(range(n_cores)))
    return np.stack([np.asarray(res.results[2 * b]["out"]) for b in range(4)], axis=0).astype(np.float32)
```

```python
import math
from contextlib import ExitStack

import numpy as np
import concourse.bass as bass
import concourse.mybir as mybir
from concourse.bass_utils import run_bass_kernel_spmd

F32 = mybir.dt.float32
BF16 = mybir.dt.bfloat16
AF = mybir.ActivationFunctionType
ALU = mybir.AluOpType

D = 2048
SEQ = 4096
DEPTH = 4
HD = 128
N_IN = 17944
NQT = SEQ // 512
NKB = SEQ // 128
SCALE = HD ** -0.5
NCMP = 255
BIG = 30000.0

COLG = dict(aq=0, ak=1024, av=2048, az=3072, bq=4096, bkc=5120, bvc=5376, bks=5632, bvs=5888,
            bkw=6144, bvw=6400, bg=6656, bz=6680, cq=7704, ck=8728, cv=9752, cz=10776, mg=11800)
LOCAL = (("aq", 512), ("ak", 512), ("av", 512), ("az", 512), ("bq", 512), ("bkc", 128), ("bvc", 128),
         ("bks", 128), ("bvs", 128), ("bkw", 128), ("bvw", 128), ("bg", 12), ("bz", 512),
         ("cq", 512), ("ck", 512), ("cv", 512), ("cz", 512), ("mg", 6144))
COL = {}
_c = 0
for _n, _w in LOCAL:
    COL[_n] = _c
    _c += _w
NLOC = _c


def local_cols(hs):
    idx = []
    for n, w in LOCAL:
        g0 = COLG[n] + (0 if n == "mg" else hs * w)
        idx.append(np.arange(g0, g0 + w))
    return np.concatenate(idx)


FMROWS = {}
_r = 0
for _n, _w in (("aq", 512), ("ak", 512), ("az", 512), ("bq", 512), ("bkc", 128), ("bvc", 128),
               ("bks", 128), ("bkw", 128), ("bg", 128), ("bz", 512), ("cq", 512), ("ck", 512),
               ("cz", 512), ("mg", 6144)):
    FMROWS[_n] = _r
    _r += _w
NFM = _r
VTCOL = dict(av=0, bvs=512, bvw=640, cv=768)
NVT = 1280
NYS = 1536
YSROW = dict(a=0, b=512, c=1024)
GROUPS = [("aq", 0, 512, "rope"), ("ak", 0, 512, "rope"), ("av", 0, 512, "v"), ("az", 0, 512, "silu"),
          ("bq", 0, 512, "rope"), ("bkc", 0, 128, "rope"), ("bvc", 0, 128, "fm"), ("bks", 0, 128, "rope"),
          ("bvs", 0, 128, "v"), ("bkw", 0, 128, "rope"), ("bvw", 0, 128, "v"), ("bg", 0, 12, "sig"),
          ("bz", 0, 512, "silu"), ("cq", 0, 512, "fm"), ("ck", 0, 512, "fm"), ("cv", 0, 512, "v"),
          ("cz", 0, 512, "silu")]
GROUPS += [("mg", 512 * _i, 512, "sig") for _i in range(12)]


class T:
    def __init__(self, ap, name=""):
        self.ap = ap
        self.name = name
        self.lw = None
        self.rd = []

    def __getitem__(self, idx):
        return self.ap[idx]


class Sched:
    ENG = ("pe", "act", "dve", "pool", "sp")

    def __init__(self, nc, es, n_dma_sems=6):
        self.nc = nc
        self.es = es
        self.eng = {"pe": nc.tensor, "act": nc.scalar, "dve": nc.vector, "pool": nc.gpsimd, "sp": nc.sync}
        self.sem = {}
        self.cnt = {}
        for e in self.ENG:
            self.sem[e] = es.enter_context(nc.semaphore("s_" + e))
            self.cnt[e] = 0
        self.dsem = {}
        for q in ("sp", "pool", "act"):
            lst = []
            for i in range(n_dma_sems):
                k = "d_%s%d" % (q, i)
                self.sem[k] = es.enter_context(nc.semaphore(k))
                self.cnt[k] = 0
                lst.append(k)
            self.dsem[q] = [lst, 0]
        self.waited = {}
        self.n_ins = 0

    def sb(self, es, name, shape, dt):
        self.uid = getattr(self, "uid", 0) + 1
        name = "%s_u%d" % (name, self.uid)
        return T(es.enter_context(self.nc.sbuf_tensor(name, list(shape), dt)), name)

    def ps(self, es, name, shape, dt=F32):
        return T(es.enter_context(self.nc.psum_tensor(name, list(shape), dt)), name)

    def _wait(self, e, key, val):
        if val <= 0 or self.waited.get((e, key), 0) >= val:
            return
        self.eng[e].wait_ge(self.sem[key], val)
        self.waited[(e, key)] = val

    def _deps(self, e, reads, writes):
        for r in reads:
            if r.lw is not None:
                self._wait(e, *r.lw)
        for w in writes:
            if w.lw is not None and w.lw[0] != e:
                self._wait(e, *w.lw)
            for (k, v) in w.rd:
                if k != e:
                    self._wait(e, k, v)

    def _mark(self, key, val, reads, writes):
        for w in writes:
            w.lw = (key, val)
            w.rd = []
        for r in reads:
            if r in writes:
                continue
            r.rd.append((key, val))
            if len(r.rd) > 16:
                d = {}
                for (k, v) in r.rd:
                    d[k] = max(d.get(k, 0), v)
                r.rd = list(d.items())

    def op(self, e, fn, reads=(), writes=()):
        self._deps(e, reads, writes)
        ins = fn(self.eng[e])
        self.cnt[e] += 1
        ins.then_inc(self.sem[e], 1)
        self._mark(e, self.cnt[e], reads, writes)
        self.n_ins += 1
        return ins

    def dma(self, q, out, in_, reads=(), writes=(), **kw):
        lst, i = self.dsem[q]
        k = lst[i % len(lst)]
        self.dsem[q][1] = i + 1
        self._wait(q, k, self.cnt[k])
        self._deps(q, reads, writes)
        ins = self.eng[q].dma_start(out=out, in_=in_, **kw)
        self.cnt[k] += 16
        ins.then_inc(self.sem[k], 16)
        self._mark(k, self.cnt[k], reads, writes)
        self.n_ins += 1
        return ins

    def coll(self, kind, groups, src, dst, op=None):
        if "cc" not in self.sem:
            self.sem["cc"] = self.es.enter_context(self.nc.semaphore("s_cc"))
            self.cnt["cc"] = 0
        ins = self.nc.gpsimd.collective_compute(kind, op if op is not None else ALU.bypass, replica_groups=groups,
                                                ins=[src.opt()], outs=[dst.opt()])
        self.cnt["cc"] += 1
        ins.then_inc(self.sem["cc"])
        self.n_ins += 1
        return ins

    def barrier(self, label=None):
        if not hasattr(self, "marks"):
            self.marks = []
        self.marks.append((label, self.cnt["pe"]))
        for e in self.ENG:
            for k in self.sem:
                if k != e:
                    self._wait(e, k, self.cnt[k])


class Ring:
    def __init__(self, items):
        self.items = items
        self.i = 0

    def next(self):
        t = self.items[self.i % len(self.items)]
        self.i += 1
        return t


def _constants():
    c = {}
    c["ident"] = np.eye(128, dtype=np.float32)
    prot = np.zeros((128, 128), np.float32)
    for i in range(16):
        prot[i + 16, i] = -1.0
        prot[i, i + 16] = 1.0
    c["prot"] = prot
    pos = np.arange(SEQ, dtype=np.float32)
    inv = (np.float32(500000.0) ** (-np.arange(0, 32, 2, dtype=np.float32) / np.float32(32))).astype(np.float32)
    ang = (pos[None, :] * inv[:, None]).astype(np.float32)
    ct = np.ones((128, SEQ), np.float32)
    st = np.zeros((128, SEQ), np.float32)
    ct[0:16] = np.cos(ang); ct[16:32] = np.cos(ang)
    st[0:16] = np.sin(ang); st[16:32] = np.sin(ang)
    c["ropec"] = ct
    c["ropes"] = st
    n = np.arange(256)[:, None]
    j = np.arange(64)[None, :]
    ov = ((16 * n < 64 * j + 64) & (16 * n + 32 > 64 * j) & (n < NCMP)).astype(np.float32)
    c["ovl"] = ov.reshape(2, 128, 64).transpose(1, 0, 2).copy()
    k = np.arange(SEQ)[None, :]
    c["eall"] = (np.arange(64)[:, None] == (k // 64)).astype(np.float32)
    return c


class Builder:
    def __init__(self, n_layers=DEPTH, halves=(0, 1), headsets=(0, 1), dbg=False):
        self.n_layers = n_layers
        self.halves = halves
        self.headsets = headsets
        self.dbg = dbg
        self.nc = bass.Bass("TRN2", target_bir_lowering=False)
        self.es = ExitStack()
        self.s = Sched(self.nc, self.es)

    def declare(self):
        nc = self.nc
        ein = lambda name, shape: nc.dram_tensor(name, list(shape), F32, kind="ExternalInput").ap()
        self.x_in = ein("x", [SEQ, D])
        self.c_in = ein("c", [128, 16])
        self.norm_pre_g = ein("norm_pre_g", [DEPTH, D])
        self.norm_post_g = ein("norm_post_g", [DEPTH, D])
        self.w_ada = ein("w_ada", [DEPTH, D, 3 * D])
        self.b_ada = ein("b_ada", [DEPTH, 3 * D])
        self.w_in = ein("w_in", [DEPTH, D, NLOC])
        self.lam_in = ein("lam", [DEPTH, 4, 128])
        self.diff_norm_g = ein("diff_norm_g", [DEPTH, 256])
        self.cmp_pe = [ein("cmp_pe_k", [DEPTH, 32, 128]), ein("cmp_pe_v", [DEPTH, 32, 128])]
        self.cmp_w1 = [ein("cmp_w1_k", [DEPTH, 4096, 128]), ein("cmp_w1_v", [DEPTH, 4096, 128])]
        self.cmp_w2 = [ein("cmp_w2_k", [DEPTH, 128, 128]), ein("cmp_w2_v", [DEPTH, 128, 128])]
        self.w_branch = ein("w_branch", [DEPTH, 3, 512, D])
        self.w_out = ein("w_out", [DEPTH, D, D])
        self.k_ident = ein("k_ident", [128, 128])
        self.k_prot = ein("k_prot", [128, 128])
        self.k_ropec = ein("k_ropec", [128, SEQ])
        self.k_ropes = ein("k_ropes", [128, SEQ])
        self.k_ovl = ein("k_ovl", [128, 2, 64])
        self.k_eall = ein("k_eall", [64, SEQ])
        self.out = nc.dram_tensor("out", [SEQ, D], F32, kind="ExternalOutput").ap()
        kind = "ExternalOutput" if self.dbg else "Internal"
        self.FM = nc.dram_tensor("fm", [NFM, SEQ], BF16, kind=kind).ap()
        self.VT = nc.dram_tensor("vt", [SEQ, NVT], BF16, kind=kind).ap()
        self.YS = nc.dram_tensor("ys", [NYS, SEQ], BF16, kind=kind).ap()
        self.MP = [nc.dram_tensor("mp%d" % i, [256, SEQ], BF16, kind="Internal").ap() for i in range(8)]
        self.MG = [nc.dram_tensor("mgath%d" % i, [512, SEQ], BF16, kind="Internal").ap() for i in range(8)]
        self.XS = [nc.dram_tensor("xs%d" % i, [SEQ, D], F32, kind="Internal").ap() for i in range(2)]
        self.GG = nc.dram_tensor("gg", [DEPTH, D], F32, kind="Internal").ap()

    def setup(self):
        s, es = self.s, self.es
        self.ident = s.sb(es, "ident", [128, 128], F32)
        self.prot = s.sb(es, "prot", [128, 128], F32)
        self.ones_b = s.sb(es, "ones_b", [128, 128], BF16)
        self.ones_f = s.sb(es, "ones_f", [128, 128], F32)
        self.ustr = s.sb(es, "ustr", [128, 128], BF16)
        self.ovl = s.sb(es, "ovl", [128, 2, 64], F32)
        self.eall = s.sb(es, "eall", [64, SEQ], BF16)
        self.modp = s.sb(es, "modp", [128, DEPTH, 4, 16], F32)
        self.psum = [s.ps(es, "pb%d" % i, [128, 512], F32) for i in range(8)]
        s.dma("sp", self.ident[:], self.k_ident, writes=[self.ident])
        s.dma("sp", self.prot[:], self.k_prot, writes=[self.prot])
        s.dma("sp", self.ovl[:], self.k_ovl, writes=[self.ovl])
        s.dma("pool", self.eall[:], self.k_eall, writes=[self.eall])
        s.op("dve", lambda e: e.memset(self.ones_b[:], 1.0), writes=[self.ones_b])
        s.op("dve", lambda e: e.memset(self.ones_f[:], 1.0), writes=[self.ones_f])
        s.op("pool", lambda e: e.memset(self.ustr[:], 1.0), writes=[self.ustr])
        s.op("pool", lambda e: e.affine_select(self.ustr[:], self.ustr[:], [[-1, 128]], ALU.is_gt, 0.0,
                                               base=0, channel_multiplier=1),
             reads=[self.ustr], writes=[self.ustr])

    def phase0(self):
        s = self.s
        with ExitStack() as es:
            cs = s.sb(es, "cs", [128, 16], F32)
            wts = Ring([s.sb(es, "wada%d" % i, [128, 16, 512], F32) for i in range(2)])
            tmp = s.sb(es, "p0tmp", [128, 48], F32)
            bada = s.sb(es, "bada", [128, 48], F32)
            gpre = s.sb(es, "gpre", [128, 16], F32)
            gpost = s.sb(es, "gpost", [128, 16], F32)
            s.dma("sp", cs[:], self.c_in, writes=[cs])
            s.op("act", lambda e: e.activation(cs[:], cs[:], AF.Silu), reads=[cs], writes=[cs])
            pm = self.psum[0]
            for l in range(self.n_layers):
                s.dma("sp", bada[:], self.b_ada[l].rearrange("(j p) -> p j", p=128), writes=[bada],
                      allow_slow_non_contiguous=True)
                s.dma("sp", gpre[:], self.norm_pre_g[l].rearrange("(j p) -> p j", p=128), writes=[gpre],
                      allow_slow_non_contiguous=True)
                s.dma("sp", gpost[:], self.norm_post_g[l].rearrange("(j p) -> p j", p=128), writes=[gpost],
                      allow_slow_non_contiguous=True)
                for g in range(12):
                    wt = wts.next()
                    s.dma("sp", wt[:], self.w_ada[l, :, g * 512:(g + 1) * 512].rearrange("(j p) c -> p j c", p=128),
                          writes=[wt])
                    for cb in range(4):
                        col = g * 4 + cb
                        for j in range(16):
                            s.op("pe", lambda e, wt=wt, cb=cb, j=j, col=col: e.matmul(
                                pm[:, col:col + 1], wt[:, j, cb * 128:(cb + 1) * 128], cs[:, j:j + 1],
                                start=(j == 0), stop=(j == 15)), reads=[wt, cs], writes=[pm])
                s.op("dve", lambda e: e.tensor_tensor(tmp[:], pm[:, 0:48], bada[:], ALU.add),
                     reads=[pm, bada], writes=[tmp])
                mp = self.modp
                s.op("dve", lambda e, l=l: e.scalar_tensor_tensor(mp[:, l, 0, :], tmp[:, 16:32], 1.0, gpre[:],
                                                                    ALU.add, ALU.mult),
                     reads=[tmp, gpre], writes=[mp])
                s.op("dve", lambda e, l=l: e.tensor_copy(mp[:, l, 1, :], tmp[:, 0:16]), reads=[tmp], writes=[mp])
                s.op("dve", lambda e, l=l: e.tensor_tensor(mp[:, l, 2, :], tmp[:, 32:48], gpost[:], ALU.mult),
                     reads=[tmp, gpost], writes=[mp])
                s.dma("sp", self.GG[l].rearrange("(j p) -> p j", p=128), mp[:, l, 2, :], reads=[mp],
                      allow_slow_non_contiguous=True)
            s.barrier()

    def phase12(self, l, half, xsrc):
        s = self.s
        T0 = half * 2048
        with ExitStack() as es:
            hT = [s.sb(es, "hT%d" % i, [128, 16, 512], BF16) for i in range(4)]
            xt = s.sb(es, "xt", [128, 4, 2048], F32)
            junk = s.sb(es, "junk", [128, 2048], BF16)
            ss = s.sb(es, "ss", [128, 4], F32)
            rstd = s.sb(es, "rstd", [128, 4], F32)
            ropec = s.sb(es, "ropec", [128, 2048], F32)
            ropes = s.sb(es, "ropes", [128, 2048], F32)
            wts = Ring([s.sb(es, "wt%d" % i, [128, 16, 512], BF16) for i in range(3)])
            qf = Ring([s.sb(es, "qf%d" % i, [128, 512], F32) for i in range(2)])
            t1 = Ring([s.sb(es, "t1%d" % i, [128, 512], F32) for i in range(2)])
            t2 = Ring([s.sb(es, "t2%d" % i, [128, 512], F32) for i in range(2)])
            stg = Ring([s.sb(es, "stg%d" % i, [128, 512], BF16) for i in range(4)])
            pacc = Ring(self.psum[0:4])
            prot_ps = Ring(self.psum[4:6])
            ptr = Ring(self.psum[6:8])
            mp = self.modp
            s.dma("sp", ropec[:], self.k_ropec[:, T0:T0 + 2048], writes=[ropec])
            s.dma("sp", ropes[:], self.k_ropes[:, T0:T0 + 2048], writes=[ropes])
            for ti in range(4):
                t0 = T0 + ti * 512
                s.dma("sp", xt[:], xsrc[t0:t0 + 512, :].rearrange("(b p) d -> p b d", p=128), writes=[xt])
                for b in range(4):
                    s.op("act", lambda e, b=b: e.activation(junk[:], xt[:, b, :], AF.Square,
                                                            accum_out=ss[:, b:b + 1]),
                         reads=[xt], writes=[junk, ss])
                s.op("act", lambda e: e.activation(rstd[:], ss[:], AF.Ln, scale=1.0 / D, bias=1e-6),
                     reads=[ss], writes=[rstd])
                s.op("act", lambda e: e.activation(rstd[:], rstd[:], AF.Exp, scale=-0.5),
                     reads=[rstd], writes=[rstd])
                for b in range(4):
                    s.op("dve", lambda e, b=b: e.tensor_scalar(xt[:, b, :], xt[:, b, :], rstd[:, b:b + 1], None,
                                                               ALU.mult),
                         reads=[xt, rstd], writes=[xt])
                for j in range(16):
                    pt = ptr.next()
                    for b in range(4):
                        s.op("pe", lambda e, b=b, j=j, pt=pt: e.transpose(
                            pt[:, b * 128:(b + 1) * 128], xt[:, b, j * 128:(j + 1) * 128], self.ident[:]),
                            reads=[xt, self.ident], writes=[pt])
                    s.op("dve", lambda e, j=j, pt=pt, ti=ti: e.tensor_scalar(
                        hT[ti][:, j, :], pt[:], mp[:, l, 0, j:j + 1], mp[:, l, 1, j:j + 1], ALU.mult, ALU.add),
                        reads=[pt, mp], writes=[hT[ti]])
            wq = []

            def issue_w(gi):
                if gi < len(GROUPS):
                    name_, off_, gw_, _ = GROUPS[gi]
                    c0_ = COL[name_] + off_
                    wt_ = wts.next()
                    s.dma("pool", wt_[:, :, 0:gw_], self.w_in[l, :, c0_:c0_ + gw_].rearrange("(j p) c -> p j c", p=128),
                          writes=[wt_])
                    wq.append(wt_)

            issue_w(0)
            issue_w(1)
            for gi, (name, off, gw, kind) in enumerate(GROUPS):
                issue_w(gi + 2)
                wt = wq[gi]
                if kind == "v":
                    vc0 = VTCOL[name] + off
                    for tb in range(16):
                        pa = pacc.next()
                        ti, bb = tb // 4, tb % 4
                        for j in range(16):
                            s.op("pe", lambda e, pa=pa, ti=ti, bb=bb, j=j, wt=wt: e.matmul(
                                pa[:, 0:gw], hT[ti][:, j, bb * 128:(bb + 1) * 128], wt[:, j, 0:gw],
                                start=(j == 0), stop=(j == 15)), reads=[hT[ti], wt], writes=[pa])
                        st = stg.next()
                        eng = "act" if tb % 2 == 0 else "dve"
                        if eng == "act":
                            s.op("act", lambda e, st=st, pa=pa: e.copy(st[:, 0:gw], pa[:, 0:gw]),
                                 reads=[pa], writes=[st])
                        else:
                            s.op("dve", lambda e, st=st, pa=pa: e.tensor_copy(st[:, 0:gw], pa[:, 0:gw]),
                                 reads=[pa], writes=[st])
                        tt = T0 + tb * 128
                        s.dma("sp", self.VT[tt:tt + 128, vc0:vc0 + gw], st[:, 0:gw], reads=[st])
                    continue
                nblk = (gw + 127) // 128
                pending = []
                for blk in range(nblk):
                    bw = min(128, gw - blk * 128)
                    r0 = FMROWS[name] + off + blk * 128
                    for ti in range(4):
                        t0 = T0 + ti * 512
                        tl = ti * 512
                        pa = pacc.next()
                        for j in range(16):
                            s.op("pe", lambda e, pa=pa, ti=ti, j=j, wt=wt, blk=blk, bw=bw: e.matmul(
                                pa[0:bw, :], wt[:, j, blk * 128:blk * 128 + bw], hT[ti][:, j, :],
                                start=(j == 0), stop=(j == 15)), reads=[hT[ti], wt], writes=[pa])
                        st = stg.next()
                        if kind == "fm":
                            s.op("dve", lambda e, st=st, pa=pa: e.tensor_copy(st[:], pa[:]), reads=[pa], writes=[st])
                        elif kind == "silu":
                            s.op("act", lambda e, st=st, pa=pa: e.activation(st[:], pa[:], AF.Silu),
                                 reads=[pa], writes=[st])
                        elif kind == "sig":
                            s.op("act", lambda e, st=st, pa=pa, bw=bw: e.activation(st[0:bw, :], pa[0:bw, :], AF.Sigmoid),
                                 reads=[pa], writes=[st])
                        elif kind == "rope":
                            q = qf.next()
                            s.op("act", lambda e, q=q, pa=pa: e.copy(q[:], pa[:]), reads=[pa], writes=[q])

                            def finish(q=q, st=st, tl=tl, r0=r0, t0=t0, bw=bw):
                                pr = prot_ps.next()
                                a1 = t1.next()
                                a2 = t2.next()
                                s.op("pe", lambda e: e.matmul(pr[:], self.prot[:], q[:], start=True, stop=True),
                                     reads=[q, self.prot], writes=[pr])
                                s.op("pool", lambda e: e.tensor_tensor(
                                    a1[:], q[:], ropec[:, tl:tl + 512], ALU.mult), reads=[q, ropec], writes=[a1])
                                s.op("dve", lambda e: e.tensor_tensor(
                                    a2[:], pr[:], ropes[:, tl:tl + 512], ALU.mult), reads=[pr, ropes], writes=[a2])
                                s.op("pool", lambda e: e.tensor_tensor(st[:], a1[:], a2[:], ALU.add),
                                     reads=[a1, a2], writes=[st])
                                s.dma("sp", self.FM[r0:r0 + bw, t0:t0 + 512], st[0:bw, :], reads=[st])

                            if pending:
                                pending.pop()()
                            pending.append(finish)
                            continue
                        s.dma("sp", self.FM[r0:r0 + bw, t0:t0 + 512], st[0:bw, :], reads=[st])
                if pending:
                    pending.pop()()
            s.barrier()

    def _causal(self, t, k0, t0, npart=128):
        self.s.op("pool", lambda e: e.affine_select(t[0:npart, :], t[0:npart, :], [[1, 512]], ALU.is_ge, 0.0,
                                                    base=t0 - k0, channel_multiplier=-1),
                  reads=[t], writes=[t])

    def _softmax_attn(self, qT, blocks, p_ring, ps_ring, psum_sum, psum_o, scale=SCALE):
        s = self.s
        nb = len(blocks)

        def stage_b(bi, blk, p):
            s.op("pe", lambda e: e.matmul(psum_sum[:], self.ones_b[:], p[:], start=(bi == 0), stop=(bi == nb - 1)),
                 reads=[self.ones_b, p], writes=[psum_sum])
            for oi, (v_t, v_ap) in enumerate(blk["v"]):
                po = psum_o[oi]
                s.op("pe", lambda e, po=po, v_ap=v_ap: e.matmul(po[:], v_ap, p[:], start=(bi == 0), stop=(bi == nb - 1)),
                     reads=[v_t, p], writes=[po])

        prev = None
        for bi, blk in enumerate(blocks):
            ps = ps_ring.next()
            kT_t, kT_ap = blk["kT"]
            bias = blk.get("bias")
            s.op("pe", lambda e: e.matmul(ps[:], kT_ap, qT[:], start=True, stop=(bias is None)),
                 reads=[kT_t, qT], writes=[ps])
            if bias is not None:
                bl_t, bl_ap, br_t, br_ap = bias
                s.op("pe", lambda e: e.matmul(ps[:], bl_ap, br_ap, start=False, stop=True),
                     reads=[bl_t, br_t], writes=[ps])
            p = p_ring.next()
            s.op("act", lambda e: e.activation(p[:], ps[:], AF.Exp, scale=scale), reads=[ps], writes=[p])
            if blk.get("mask") is not None:
                blk["mask"](p)
            if prev is not None:
                stage_b(*prev)
            prev = (bi, blk, p)
        stage_b(*prev)

    def phase3_diff(self, l, hs):
        s = self.s
        lam_init = 0.8 - 0.6 * math.exp(-0.3 * l)
        with ExitStack() as es:
            kT = [Ring([s.sb(es, "akT%d_%d" % (c, i), [128, SEQ], BF16) for i in range(2)]) for c in range(2)]
            vv = Ring([s.sb(es, "avv%d" % i, [128, NKB, 256], BF16) for i in range(2)])
            qTs = Ring([s.sb(es, "aq%d" % i, [128, 512], BF16) for i in range(4)])
            szs = Ring([s.sb(es, "asz%d" % i, [128, 512], BF16) for i in range(4)])
            p_ring = Ring([s.sb(es, "ap%d" % i, [128, 512], BF16) for i in range(4)])
            oc = [[s.sb(es, "aoc%d%d" % (c, h), [128, 512], F32) for h in range(2)] for c in range(2)]
            rs = s.sb(es, "ars", [128, 512], F32)
            sq = [s.sb(es, "asq%d" % h, [128, 512], F32) for h in range(2)]
            rstd = s.sb(es, "arstd", [128, 512], F32)
            yst = Ring([s.sb(es, "ayst%d" % i, [128, 512], BF16) for i in range(2)])
            lamt = s.sb(es, "lamt", [128, 4], F32)
            lam2 = s.sb(es, "lam2", [128, 2], F32)
            neglam = s.sb(es, "neglam", [128, 1], F32)
            gco = s.sb(es, "gco", [128, 2], F32)
            ps_ring = Ring([self.psum[0], self.psum[1], self.psum[3]])
            psum_sum = self.psum[2]
            psum_o = self.psum[4:6]
            pmisc = self.psum[6]
            s.dma("sp", lamt[:], self.lam_in[l].rearrange("k p -> p k"), writes=[lamt], allow_slow_non_contiguous=True)
            s.op("dve", lambda e: e.tensor_tensor(lam2[:, 0:1], lamt[:, 0:1], lamt[:, 1:2], ALU.mult), reads=[lamt], writes=[lam2])
            s.op("dve", lambda e: e.tensor_tensor(lam2[:, 1:2], lamt[:, 2:3], lamt[:, 3:4], ALU.mult), reads=[lamt], writes=[lam2])
            s.op("pe", lambda e: e.matmul(pmisc[:, 0:2], self.ones_f[:], lam2[:], start=True, stop=True),
                 reads=[self.ones_f, lam2], writes=[pmisc])
            s.op("act", lambda e: e.activation(lam2[:], pmisc[:, 0:2], AF.Exp), reads=[pmisc], writes=[lam2])
            s.op("dve", lambda e: e.scalar_tensor_tensor(neglam[:], lam2[:, 1:2], -lam_init, lam2[:, 0:1], ALU.add, ALU.subtract),
                 reads=[lam2], writes=[neglam])
            s.dma("sp", gco[:], self.diff_norm_g[l].rearrange("(h p) -> p h", p=128), writes=[gco], allow_slow_non_contiguous=True)
            s.op("dve", lambda e: e.tensor_scalar(gco[:], gco[:], 1.0 - lam_init, None, ALU.mult), reads=[gco], writes=[gco])
            for h in range(2):
                kts = []
                for c in range(2):
                    kt = kT[c].next()
                    r0 = FMROWS["ak"] + h * 256 + c * 128
                    s.dma("sp", kt[:], self.FM[r0:r0 + 128, :], writes=[kt])
                    kts.append(kt)
                v = vv.next()
                vc = VTCOL["av"] + h * 256
                s.dma("sp", v[:], self.VT[:, vc:vc + 256].rearrange("(kb p) e -> p kb e", p=128), writes=[v])
                def load_q(i_, c_):
                    q_ = qTs.next()
                    r0_ = FMROWS["aq"] + h * 256 + c_ * 128
                    s.dma("sp", q_[:], self.FM[r0_:r0_ + 128, i_ * 512:i_ * 512 + 512], writes=[q_])
                    return q_

                steps = [(i_, c_) for i_ in range(NQT) for c_ in range(2)]
                q_next = load_q(*steps[0])
                for i in range(NQT):
                    t0 = i * 512
                    szt = []
                    for hf in range(2):
                        sz = szs.next()
                        rz = FMROWS["az"] + h * 256 + hf * 128
                        s.dma("sp", sz[:], self.FM[rz:rz + 128, t0:t0 + 512], writes=[sz])
                        szt.append(sz)
                    for c in range(2):
                        q = q_next
                        si = steps.index((i, c))
                        if si + 1 < len(steps):
                            q_next = load_q(*steps[si + 1])
                        blocks = []
                        for kb in range(4 * i + 4):
                            k0 = kb * 128
                            blk = dict(kT=(kts[c], kts[c][:, k0:k0 + 128]),
                                       v=[(v, v[:, kb, 0:128]), (v, v[:, kb, 128:256])])
                            if kb >= 4 * i:
                                blk["mask"] = (lambda p, k0=k0, t0=t0: self._causal(p, k0, t0))
                            blocks.append(blk)
                        self._softmax_attn(q, blocks, p_ring, ps_ring, psum_sum, psum_o)
                        s.op("dve", lambda e: e.reciprocal(rs[:], psum_sum[:]), reads=[psum_sum], writes=[rs])
                        for hf in range(2):
                            s.op("dve", lambda e, c=c, hf=hf: e.tensor_tensor(oc[c][hf][:], psum_o[hf][:], rs[:], ALU.mult),
                                 reads=[psum_o[hf], rs], writes=[oc[c][hf]])
                    for hf in range(2):
                        s.op("dve", lambda e, hf=hf: e.scalar_tensor_tensor(
                            oc[0][hf][:], oc[1][hf][:], neglam[:, 0:1], oc[0][hf][:], ALU.mult, ALU.add),
                            reads=[oc[1][hf], neglam, oc[0][hf]], writes=[oc[0][hf]])
                        s.op("pool", lambda e, hf=hf: e.tensor_tensor(sq[hf][:], oc[0][hf][:], oc[0][hf][:], ALU.mult),
                             reads=[oc[0][hf]], writes=[sq[hf]])
                    for hf in range(2):
                        s.op("pe", lambda e, hf=hf: e.matmul(pmisc[:], self.ones_f[:], sq[hf][:], start=(hf == 0), stop=(hf == 1)),
                             reads=[self.ones_f, sq[hf]], writes=[pmisc])
                    s.op("act", lambda e: e.activation(rstd[:], pmisc[:], AF.Ln, scale=1.0 / 256, bias=1e-5),
                         reads=[pmisc], writes=[rstd])
                    s.op("act", lambda e: e.activation(rstd[:], rstd[:], AF.Exp, scale=-0.5), reads=[rstd], writes=[rstd])
                    for hf in range(2):
                        sz = szt[hf]
                        s.op("dve", lambda e, hf=hf: e.scalar_tensor_tensor(
                            oc[0][hf][:], oc[0][hf][:], gco[:, hf:hf + 1], rstd[:], ALU.mult, ALU.mult),
                            reads=[oc[0][hf], gco, rstd], writes=[oc[0][hf]])
                        y = yst.next()
                        s.op("pool", lambda e, hf=hf, y=y, sz=sz: e.tensor_tensor(y[:], oc[0][hf][:], sz[:], ALU.mult),
                             reads=[oc[0][hf], sz], writes=[y])
                        ry = YSROW["a"] + h * 256 + hf * 128
                        s.dma("pool", self.YS[ry:ry + 128, t0:t0 + 512], y[:], reads=[y])
            s.barrier()

    def phase3_sb(self, l, hs):
        s = self.s
        with ExitStack() as es:
            kTr = Ring([s.sb(es, "ckT%d" % i, [128, SEQ], BF16) for i in range(2)])
            vvr = Ring([s.sb(es, "cvv%d" % i, [128, NKB, 128], BF16) for i in range(2)])
            qTs = Ring([s.sb(es, "cq%d" % i, [128, 512], BF16) for i in range(3)])
            szs = Ring([s.sb(es, "csz%d" % i, [128, 512], BF16) for i in range(3)])
            er = Ring([s.sb(es, "ce%d" % i, [128, 512], F32) for i in range(4)])
            lfr = Ring([s.sb(es, "clf%d" % i, [128, 512], F32) for i in range(4)])
            lbr = Ring([s.sb(es, "clb%d" % i, [128, 512], BF16) for i in range(4)])
            argr = Ring([s.sb(es, "carg%d" % i, [128, 512], F32) for i in range(4)])
            wr = Ring([s.sb(es, "cw%d" % i, [128, 512], F32) for i in range(4)])
            ar = Ring([s.sb(es, "ca%d" % i, [128, 512], BF16) for i in range(4)])
            R = s.sb(es, "cR", [128, 512], F32)
            yst = Ring([s.sb(es, "cyst%d" % i, [128, 512], BF16) for i in range(2)])
            zr = Ring(self.psum[0:2])
            c1r = Ring(self.psum[2:4])
            c2r = Ring(self.psum[4:6])
            po = self.psum[6]
            for h in range(4):
                kt = kTr.next()
                r0 = FMROWS["ck"] + h * 128
                s.dma("sp", kt[:], self.FM[r0:r0 + 128, :], writes=[kt])
                v = vvr.next()
                vc = VTCOL["cv"] + h * 128
                s.dma("sp", v[:], self.VT[:, vc:vc + 128].rearrange("(kb p) e -> p kb e", p=128), writes=[v])
                def load_qz(i_):
                    q_ = qTs.next()
                    rq = FMROWS["cq"] + h * 128
                    s.dma("sp", q_[:], self.FM[rq:rq + 128, i_ * 512:i_ * 512 + 512], writes=[q_])
                    sz_ = szs.next()
                    rz = FMROWS["cz"] + h * 128
                    s.dma("sp", sz_[:], self.FM[rz:rz + 128, i_ * 512:i_ * 512 + 512], writes=[sz_])
                    return q_, sz_

                qz_next = load_qz(0)
                for i in range(NQT):
                    t0 = i * 512
                    q, sz = qz_next
                    if i + 1 < NQT:
                        qz_next = load_qz(i + 1)
                    s.op("pool", lambda e: e.memset(R[:], 0.0), writes=[R])
                    kbs = list(range(4 * i + 3, -1, -1))

                    def stage_a1(kb):
                        k0 = kb * 128
                        zp = zr.next()
                        s.op("pe", lambda e: e.matmul(zp[:], kt[:, k0:k0 + 128], q[:], start=True, stop=True),
                             reads=[kt, q], writes=[zp])
                        ee = er.next()
                        s.op("act", lambda e: e.activation(ee[:], zp[:], AF.Exp, scale=SCALE), reads=[zp], writes=[ee])
                        if kb >= 4 * i:
                            s.op("pool", lambda e: e.affine_select(
                                ee[:], ee[:], [[1, 512]], ALU.is_gt, 0.0, base=t0 - k0, channel_multiplier=-1),
                                reads=[ee], writes=[ee])
                        lf = lfr.next()
                        s.op("act", lambda e: e.activation(lf[:], ee[:], AF.Ln, bias=1.0), reads=[ee], writes=[lf])
                        lb = lbr.next()
                        s.op("pool", lambda e: e.tensor_copy(lb[:], lf[:]), reads=[lf], writes=[lb])
                        return [ee, lf, lb]

                    def stage_a2(st):
                        lb = st[2]
                        c1 = c1r.next()
                        c2 = c2r.next()
                        s.op("pe", lambda e: e.matmul(c1[:], self.ustr[:], lb[:], start=True, stop=True),
                             reads=[self.ustr, lb], writes=[c1])
                        s.op("pe", lambda e: e.matmul(c2[:], self.ones_b[:], lb[:], start=True, stop=True),
                             reads=[self.ones_b, lb], writes=[c2])
                        st += [c1, c2]

                    def stage_b(bi, kb, st):
                        ee, lf, lb, c1, c2 = st
                        arg = argr.next()
                        s.op("dve", lambda e: e.tensor_tensor(arg[:], c1[:], R[:], ALU.add),
                             reads=[c1, R], writes=[arg])
                        s.op("dve", lambda e: e.tensor_tensor(R[:], c2[:], R[:], ALU.add), reads=[c2, R], writes=[R])
                        s.op("dve", lambda e: e.tensor_tensor(arg[:], arg[:], lf[:], ALU.add),
                             reads=[arg, lf], writes=[arg])
                        w = wr.next()
                        s.op("act", lambda e: e.activation(w[:], arg[:], AF.Exp, scale=-1.0), reads=[arg], writes=[w])
                        a = ar.next()
                        s.op("pool", lambda e: e.tensor_tensor(a[:], ee[:], w[:], ALU.mult),
                             reads=[ee, w], writes=[a])
                        s.op("pe", lambda e: e.matmul(po[:], v[:, kb, :], a[:], start=(bi == 0), stop=(bi == len(kbs) - 1)),
                             reads=[v, a], writes=[po])

                    nk = len(kbs)
                    sts = {0: stage_a1(kbs[0])}
                    if nk > 1:
                        sts[1] = stage_a1(kbs[1])
                    stage_a2(sts[0])
                    for bi, kb in enumerate(kbs):
                        if bi + 2 < nk:
                            sts[bi + 2] = stage_a1(kbs[bi + 2])
                        if bi + 1 < nk:
                            stage_a2(sts[bi + 1])
                        stage_b(bi, kb, sts.pop(bi))
                    y = yst.next()
                    s.op("dve", lambda e, y=y, sz=sz: e.tensor_tensor(y[:], po[:], sz[:], ALU.mult), reads=[po, sz], writes=[y])
                    ry = YSROW["c"] + h * 128
                    s.dma("sp", self.YS[ry:ry + 128, t0:t0 + 512], y[:], reads=[y])
            s.barrier()

    def phase3_nsa(self, l, g):
        s = self.s
        with ExitStack() as es:
            big = [s.sb(es, "bbig%d" % i, [128, SEQ], BF16) for i in range(4)]
            vs = s.sb(es, "bvs", [128, NKB, 128], BF16)
            vw = s.sb(es, "bvw", [128, NKB, 128], BF16)
            w1 = s.sb(es, "bw1", [128, 32, 128], BF16)
            w2 = s.sb(es, "bw2", [128, 128], BF16)
            pe_sb = s.sb(es, "bpe", [32, 128], F32)
            peT = s.sb(es, "bpeT", [128, 32], BF16)
            c1 = s.sb(es, "bc1", [128, 1], F32)
            hsl = s.sb(es, "bhsl", [128, 256], BF16)
            kcmpT = s.sb(es, "bkcmpT", [128, 256], BF16)
            vcmp = s.sb(es, "bvcmp", [128, 2, 128], BF16)
            qsets = [[s.sb(es, "bq%d_%d" % (k, i), [128, 512], BF16) for i in range(4)] for k in range(2)]
            gts = Ring([s.sb(es, "bgt%d" % i, [128, 3, 512], BF16) for i in range(2)])
            szs = Ring([s.sb(es, "bsz%d" % i, [128, 512], BF16) for i in range(2)])
            pf = [s.sb(es, "bpf%d" % i, [128, 512], F32) for i in range(2)]
            pnb = Ring([s.sb(es, "bpnb%d" % i, [128, 512], BF16) for i in range(2)])
            rs = s.sb(es, "brs", [128, 512], F32)
            ocmp = [s.sb(es, "bocmp%d" % i, [128, 512], F32) for i in range(4)]
            impS = s.sb(es, "bimp", [128, 4, 64], F32)
            m8 = s.sb(es, "bm8", [128, 16], F32)
            wk = s.sb(es, "bwk", [128, 64], F32)
            sel = s.sb(es, "bsel", [128, 64], F32)
            negT = s.sb(es, "bnegT", [64, 512], BF16)
            p_ring = Ring([s.sb(es, "bp%d" % i, [128, 512], BF16) for i in range(4)])
            acc = s.sb(es, "bacc", [128, 512], F32)
            tmp = s.sb(es, "btmp", [128, 512], F32)
            yst = Ring([s.sb(es, "byst%d" % i, [128, 512], BF16) for i in range(2)])
            ps_ring = Ring([self.psum[0], self.psum[1], self.psum[7]])
            psum_sum = self.psum[2]
            pimp = self.psum[3]
            psum_o = [self.psum[4]]
            pmisc = self.psum[5]
            pmisc2 = self.psum[6]
            kcT, vcT, ksT, kwT = big
            for t_, nm in ((kcT, "bkc"), (vcT, "bvc"), (ksT, "bks"), (kwT, "bkw")):
                r0 = FMROWS[nm] + g * 128
                s.dma("sp", t_[:], self.FM[r0:r0 + 128, :], writes=[t_])
            for t_, nm in ((vs, "bvs"), (vw, "bvw")):
                vc = VTCOL[nm] + g * 128
                s.dma("sp", t_[:], self.VT[:, vc:vc + 128].rearrange("(kb p) e -> p kb e", p=128), writes=[t_])
            for kv in range(2):
                src = kcT if kv == 0 else vcT
                s.dma("pool", w1[:], self.cmp_w1[kv][l].rearrange("(l d) f -> d l f", d=128), writes=[w1])
                s.dma("pool", w2[:], self.cmp_w2[kv][l], writes=[w2])
                s.dma("sp", pe_sb[:], self.cmp_pe[kv][l], writes=[pe_sb])
                s.op("pe", lambda e: e.transpose(pmisc[:, 0:32], pe_sb[:], self.ident[0:32, 0:32]),
                     reads=[pe_sb, self.ident], writes=[pmisc])
                s.op("dve", lambda e: e.tensor_copy(peT[:], pmisc[:, 0:32]), reads=[pmisc], writes=[peT])
                for li in range(32):
                    s.op("pe", lambda e, li=li: e.matmul(pmisc2[:, 0:1], w1[:, li, :], peT[:, li:li + 1],
                                                         start=(li == 0), stop=(li == 31)),
                         reads=[w1, peT], writes=[pmisc2])
                s.op("dve", lambda e: e.tensor_copy(c1[:], pmisc2[:, 0:1]), reads=[pmisc2], writes=[c1])
                for li in range(32):
                    s.op("pe", lambda e, li=li, src=src: e.matmul(pmisc[:, 0:NCMP], w1[:, li, :],
                                                                   src[:, li:li + 16 * (NCMP - 1) + 1:16],
                                                                   start=(li == 0), stop=(li == 31)),
                         reads=[w1, src], writes=[pmisc])
                s.op("dve", lambda e: e.memset(hsl[:], 0.0), writes=[hsl])
                s.op("act", lambda e: e.activation(hsl[:, 0:NCMP], pmisc[:, 0:NCMP], AF.Silu, bias=c1[:, 0:1]),
                     reads=[pmisc, c1], writes=[hsl])
                if kv == 0:
                    s.op("pe", lambda e: e.matmul(pmisc2[:, 0:256], w2[:], hsl[:], start=True, stop=True),
                         reads=[w2, hsl], writes=[pmisc2])
                    s.op("dve", lambda e: e.tensor_copy(kcmpT[:], pmisc2[:, 0:256]), reads=[pmisc2], writes=[kcmpT])
                else:
                    for nb in range(2):
                        s.op("pe", lambda e, nb=nb: e.matmul(pmisc2[:, nb * 128:(nb + 1) * 128], hsl[:, nb * 128:(nb + 1) * 128],
                                                             w2[:], start=True, stop=True),
                             reads=[w2, hsl], writes=[pmisc2])
                    s.op("dve", lambda e: e.tensor_copy(vcmp[:], pmisc2[:, 0:256].rearrange("p (n d) -> p n d", n=2)),
                         reads=[pmisc2], writes=[vcmp])
            def load_qs(i_):
                qs_ = qsets[i_ % 2]
                for r_ in range(4):
                    rq = FMROWS["bq"] + (g * 4 + r_) * 128
                    s.dma("sp", qs_[r_][:], self.FM[rq:rq + 128, i_ * 512:i_ * 512 + 512], writes=[qs_[r_]])
                return qs_

            load_qs(0)
            for i in range(NQT):
                t0 = i * 512
                nbs = [nb for nb in range(2) if 16 * nb * 128 + 31 <= t0 + 511]
                qTs = qsets[i % 2]
                if i + 1 < NQT:
                    load_qs(i + 1)
                for r in range(4):
                    q = qTs[r]
                    for nb in nbs:
                        ps = ps_ring.next()
                        s.op("pe", lambda e, ps=ps, nb=nb, q=q: e.matmul(ps[:], kcmpT[:, nb * 128:(nb + 1) * 128], q[:],
                                                                          start=True, stop=True),
                             reads=[kcmpT, q], writes=[ps])
                        s.op("act", lambda e, ps=ps, nb=nb: e.activation(pf[nb][:], ps[:], AF.Exp, scale=SCALE),
                             reads=[ps], writes=[pf[nb]])
                        s.op("pool", lambda e, nb=nb, t0=t0: e.affine_select(
                            pf[nb][:], pf[nb][:], [[1, 512]], ALU.is_ge, 0.0,
                            base=t0 - 16 * nb * 128 - 31, channel_multiplier=-16), reads=[pf[nb]], writes=[pf[nb]])
                        s.op("pe", lambda e, nb=nb: e.matmul(psum_sum[:], self.ones_f[:], pf[nb][:], start=(nb == nbs[0]),
                                                             stop=(nb == nbs[-1])),
                             reads=[self.ones_f, pf[nb]], writes=[psum_sum])
                    s.op("dve", lambda e: e.tensor_scalar(rs[:], psum_sum[:], 1e-30, None, ALU.max), reads=[psum_sum], writes=[rs])
                    s.op("dve", lambda e: e.reciprocal(rs[:], rs[:]), reads=[rs], writes=[rs])
                    for nb in nbs:
                        s.op("dve", lambda e, nb=nb: e.tensor_tensor(pf[nb][:], pf[nb][:], rs[:], ALU.mult),
                             reads=[pf[nb], rs], writes=[pf[nb]])
                        for tb in range(4):
                            first = (r == 0 and nb == nbs[0])
                            last = (r == 3 and nb == nbs[-1])
                            s.op("pe", lambda e, nb=nb, tb=tb, first=first, last=last: e.matmul(
                                pimp[:, tb * 64:(tb + 1) * 64], pf[nb][:, tb * 128:(tb + 1) * 128], self.ovl[:, nb, :],
                                start=first, stop=last), reads=[pf[nb], self.ovl], writes=[pimp])
                        pb = pnb.next()
                        s.op("act", lambda e, pb=pb, nb=nb: e.copy(pb[:], pf[nb][:]), reads=[pf[nb]], writes=[pb])
                        s.op("pe", lambda e, pb=pb, nb=nb: e.matmul(psum_o[0][:], vcmp[:, nb, :], pb[:], start=(nb == nbs[0]),
                                                                     stop=(nb == nbs[-1])),
                             reads=[vcmp, pb], writes=[psum_o[0]])
                    s.op("dve", lambda e, r=r: e.tensor_copy(ocmp[r][:], psum_o[0][:]), reads=[psum_o[0]], writes=[ocmp[r]])
                s.op("dve", lambda e: e.tensor_copy(impS[:], pimp[:, 0:256].rearrange("p (a b) -> p a b", a=4)),
                     reads=[pimp], writes=[impS])
                for tb in range(4):
                    for hh in range(2):
                        tblk = 8 * i + 2 * tb + hh
                        p0 = hh * 64
                        if tblk < 63:
                            s.op("pool", lambda e, tb=tb, p0=p0, tblk=tblk: e.memset(impS[p0:p0 + 64, tb, tblk + 1:64], -1e30),
                                 reads=[impS], writes=[impS])
                        s.op("pool", lambda e, tb=tb, p0=p0, tblk=tblk: e.memset(impS[p0:p0 + 64, tb, tblk:tblk + 1], 1e6),
                             reads=[impS], writes=[impS])
                        s.op("pool", lambda e, tb=tb, p0=p0: e.memset(impS[p0:p0 + 64, tb, 0:1], 2e6),
                             reads=[impS], writes=[impS])
                for tb in range(4):
                    s.op("dve", lambda e, tb=tb: e.max(out=m8[:, 0:8], in_=impS[:, tb, :]), reads=[impS], writes=[m8])
                    s.op("dve", lambda e, tb=tb: e.match_replace(out=wk[:], in_to_replace=m8[:, 0:8], in_values=impS[:, tb, :],
                                                                 imm_value=-3e30), reads=[impS, m8], writes=[wk])
                    s.op("dve", lambda e: e.max(out=m8[:, 8:16], in_=wk[:]), reads=[wk], writes=[m8])
                    s.op("dve", lambda e, tb=tb: e.tensor_scalar(sel[:], impS[:, tb, :], m8[:, 15:16], None, ALU.is_ge),
                         reads=[impS, m8], writes=[sel])
                    s.op("dve", lambda e: e.tensor_scalar(sel[:], sel[:], -1.0, BIG, ALU.add, ALU.mult), reads=[sel], writes=[sel])
                    s.op("pe", lambda e: e.transpose(pmisc[0:64, 0:128], sel[:], self.ident[:]),
                         reads=[sel, self.ident], writes=[pmisc])
                    s.op("act", lambda e, tb=tb: e.copy(negT[:, tb * 128:(tb + 1) * 128], pmisc[0:64, 0:128]),
                         reads=[pmisc], writes=[negT])
                for r in range(4):
                    h = g * 4 + r
                    q = qTs[r]
                    gt = gts.next()
                    for k3 in range(3):
                        rg = FMROWS["bg"] + h * 3 + k3
                        s.dma("sp", gt[:, k3, :], self.FM[rg:rg + 1, t0:t0 + 512].to_broadcast([128, 512]), writes=[gt])
                    sz = szs.next()
                    rz = FMROWS["bz"] + h * 128
                    s.dma("sp", sz[:], self.FM[rz:rz + 128, t0:t0 + 512], writes=[sz])
                    s.op("pool", lambda e, r=r, gt=gt: e.tensor_tensor(acc[:], ocmp[r][:], gt[:, 0, :], ALU.mult),
                         reads=[ocmp[r], gt], writes=[acc])
                    blocks = []
                    for kb in range(4 * i + 4):
                        k0 = kb * 128
                        blk = dict(kT=(ksT, ksT[:, k0:k0 + 128]), v=[(vs, vs[:, kb, :])],
                                   bias=(self.eall, self.eall[:, k0:k0 + 128], negT, negT[:]))
                        if kb >= 4 * i:
                            blk["mask"] = (lambda p, k0=k0, t0=t0: self._causal(p, k0, t0))
                        blocks.append(blk)
                    self._softmax_attn(q, blocks, p_ring, ps_ring, psum_sum, psum_o)
                    s.op("dve", lambda e: e.reciprocal(rs[:], psum_sum[:]), reads=[psum_sum], writes=[rs])
                    s.op("dve", lambda e: e.tensor_tensor(tmp[:], psum_o[0][:], rs[:], ALU.mult), reads=[psum_o[0], rs], writes=[tmp])
                    s.op("pool", lambda e, gt=gt: e.tensor_tensor(tmp[:], tmp[:], gt[:, 1, :], ALU.mult), reads=[tmp, gt], writes=[tmp])
                    s.op("pool", lambda e: e.tensor_tensor(acc[:], acc[:], tmp[:], ALU.add), reads=[acc, tmp], writes=[acc])
                    blocks = []
                    for kb in range(max(0, 4 * i - 4), 4 * i + 4):
                        k0 = kb * 128
                        blk = dict(kT=(kwT, kwT[:, k0:k0 + 128]), v=[(vw, vw[:, kb, :])])
                        if kb >= 4 * i:
                            blk["mask"] = (lambda p, k0=k0, t0=t0: self._causal(p, k0, t0))
                        else:
                            blk["mask"] = (lambda p, k0=k0, t0=t0: s.op("pool", lambda e: e.affine_select(
                                p[:], p[:], [[-1, 512]], ALU.is_gt, 0.0, base=k0 - t0 + 512, channel_multiplier=1),
                                reads=[p], writes=[p]))
                        blocks.append(blk)
                    self._softmax_attn(q, blocks, p_ring, ps_ring, psum_sum, psum_o)
                    s.op("dve", lambda e: e.reciprocal(rs[:], psum_sum[:]), reads=[psum_sum], writes=[rs])
                    s.op("dve", lambda e: e.tensor_tensor(tmp[:], psum_o[0][:], rs[:], ALU.mult), reads=[psum_o[0], rs], writes=[tmp])
                    s.op("pool", lambda e, gt=gt: e.tensor_tensor(tmp[:], tmp[:], gt[:, 2, :], ALU.mult), reads=[tmp, gt], writes=[tmp])
                    s.op("pool", lambda e: e.tensor_tensor(acc[:], acc[:], tmp[:], ALU.add), reads=[acc, tmp], writes=[acc])
                    y = yst.next()
                    s.op("pool", lambda e, y=y, sz=sz: e.tensor_tensor(y[:], acc[:], sz[:], ALU.mult), reads=[acc, sz], writes=[y])
                    ry = YSROW["b"] + h * 128
                    s.dma("pool", self.YS[ry:ry + 128, t0:t0 + 512], y[:], reads=[y])
            s.barrier()

    def phase4a(self, l):
        s = self.s
        for tp in range(4):
            with ExitStack() as es2:
                ys = [s.sb(es2, "ysb%d" % n, [128, 4, 1024], BF16) for n in range(3)]
                wbr = Ring([s.sb(es2, "wb%d" % i, [128, 4, 512], BF16) for i in range(6)])
                mgr = Ring([s.sb(es2, "mg%d" % i, [128, 1024], BF16) for i in range(4)])
                macc = [s.sb(es2, "macc%d" % i, [128, 512], F32) for i in range(2)]
                tm = Ring([s.sb(es2, "tm%d" % i, [128, 512], F32) for i in range(3)])
                stg = Ring([s.sb(es2, "mstg%d" % i, [128, 512], BF16) for i in range(3)])
                pacc = Ring(self.psum[0:8])
                TP = tp * 1024
                for n in range(3):
                    s.dma("sp", ys[n][:], self.YS[n * 512:(n + 1) * 512, TP:TP + 1024].rearrange("(k p) t -> p k t", p=128),
                          writes=[ys[n]])
                for cg in range(4):
                    wbs = []
                    for n in range(3):
                        wb = wbr.next()
                        s.dma("pool", wb[:], self.w_branch[l, n, :, cg * 512:(cg + 1) * 512].rearrange("(k p) c -> p k c", p=128),
                              writes=[wb])
                        wbs.append(wb)
                    for cb in range(4):
                        cc = cg * 4 + cb
                        for n in range(3):
                            mg = mgr.next()
                            rm = FMROWS["mg"] + n * 2048 + cc * 128
                            s.dma("sp", mg[:], self.FM[rm:rm + 128, TP:TP + 1024], writes=[mg])
                            for tl in range(2):
                                pa = pacc.next()
                                for k in range(4):
                                    s.op("pe", lambda e, pa=pa, n=n, k=k, tl=tl, cb=cb: e.matmul(
                                        pa[:], wbs[n][:, k, cb * 128:(cb + 1) * 128], ys[n][:, k, tl * 512:(tl + 1) * 512],
                                        start=(k == 0), stop=(k == 3)), reads=[wbs[n], ys[n]], writes=[pa])
                                if n == 0:
                                    s.op("dve", lambda e, pa=pa, tl=tl, mg=mg: e.tensor_tensor(
                                        macc[tl][:], pa[:], mg[:, tl * 512:(tl + 1) * 512], ALU.mult),
                                        reads=[pa, mg], writes=[macc[tl]])
                                else:
                                    t_ = tm.next()
                                    s.op("dve", lambda e, pa=pa, tl=tl, mg=mg, t_=t_: e.tensor_tensor(
                                        t_[:], pa[:], mg[:, tl * 512:(tl + 1) * 512], ALU.mult),
                                        reads=[pa, mg], writes=[t_])
                                    if n == 1:
                                        s.op("pool", lambda e, tl=tl, t_=t_: e.tensor_tensor(macc[tl][:], macc[tl][:], t_[:], ALU.add),
                                             reads=[macc[tl], t_], writes=[macc[tl]])
                                    else:
                                        st = stg.next()
                                        s.op("pool", lambda e, tl=tl, t_=t_, st=st: e.tensor_tensor(
                                            st[:], macc[tl][:], t_[:], ALU.add),
                                            reads=[macc[tl], t_], writes=[st])
                                        tt = TP + tl * 512
                                        s.dma("pool", self.MP[cc // 2][(cc % 2) * 128:(cc % 2 + 1) * 128, tt:tt + 512], st[:], reads=[st])
                s.barrier()

    def load_wo(self, es, l):
        s = self.s
        wo = s.sb(es, "wo", [128, 16, 2048], BF16)
        for k4 in range(4):
            s.dma("pool", wo[:, k4 * 4:(k4 + 1) * 4, :],
                  self.w_out[l, k4 * 512:(k4 + 1) * 512, :].rearrange("(k p) c -> p k c", p=128), writes=[wo])
        return wo

    def phase4b(self, l, xsrc, xdst, wo):
        s = self.s
        with ExitStack() as es3:
            ggr = s.sb(es3, "ggr", [128, 2048], F32)
            m0 = s.sb(es3, "m0", [128, 16, 512], BF16)
            m1 = s.sb(es3, "m1", [128, 16, 512], BF16)
            mTr = Ring([s.sb(es3, "mT%d" % i, [128, 16, 512], BF16) for i in range(2)])
            xr = Ring([s.sb(es3, "xr%d" % i, [128, 2048], F32) for i in range(3)])
            yr = Ring([s.sb(es3, "yr%d" % i, [128, 2048], F32) for i in range(2)])
            junk = s.sb(es3, "junk4", [128, 512], BF16)
            ss = Ring([s.sb(es3, "ss4%d" % i, [128, 4], F32) for i in range(2)])
            rstd = Ring([s.sb(es3, "rstd4%d" % i, [128, 1], F32) for i in range(2)])
            s.dma("sp", ggr[:], self.GG[l:l + 1, :].to_broadcast([128, 2048]), writes=[ggr])

            def load_m(ti_):
                t0_ = ti_ * 512
                for c8 in range(8):
                    for rk, mm in ((0, m0), (1, m1)):
                        s.dma("sp", mm[:, 2 * c8:2 * c8 + 2, :],
                              self.MG[c8][rk * 256:(rk + 1) * 256, t0_:t0_ + 512].rearrange("(q p) t -> p q t", p=128), writes=[mm])

            def load_x(tb_):
                x_ = xr.next()
                s.dma("sp", x_[:], xsrc[tb_ * 128:(tb_ + 1) * 128, :], writes=[x_])
                return x_

            load_m(0)
            x_next = load_x(0)
            for ti in range(NQT):
                mT = mTr.next()
                for hk in range(2):
                    eng = "pool" if hk == 0 else "dve"
                    s.op(eng, lambda e, hk=hk, mT=mT: e.tensor_tensor(
                        mT[:, hk * 8:(hk + 1) * 8, :], m0[:, hk * 8:(hk + 1) * 8, :], m1[:, hk * 8:(hk + 1) * 8, :], ALU.add),
                        reads=[m0, m1], writes=[mT])
                if ti + 1 < NQT:
                    load_m(ti + 1)
                for bb in range(4):
                    tb = ti * 4 + bb
                    tt = tb * 128
                    x = x_next
                    if tb + 1 < 4 * NQT:
                        x_next = load_x(tb + 1)
                    half_banks = self.psum[0:4] if tb % 2 == 0 else self.psum[4:8]
                    sst = ss.next()
                    for cb in range(4):
                        pa = half_banks[cb]
                        for k in range(16):
                            s.op("pe", lambda e, pa=pa, k=k, bb=bb, cb=cb, mT=mT: e.matmul(
                                pa[:], mT[:, k, bb * 128:(bb + 1) * 128], wo[:, k, cb * 512:(cb + 1) * 512],
                                start=(k == 0), stop=(k == 15)), reads=[mT, wo], writes=[pa])
                        s.op("act", lambda e, pa=pa, cb=cb, sst=sst: e.activation(junk[:], pa[:], AF.Square,
                                                                                 accum_out=sst[:, cb:cb + 1]),
                             reads=[pa], writes=[junk, sst])
                    rt = rstd.next()
                    s.op("dve", lambda e, sst=sst, rt=rt: e.tensor_reduce(rt[:], sst[:], mybir.AxisListType.X, ALU.add),
                         reads=[sst], writes=[rt])
                    s.op("act", lambda e, rt=rt: e.activation(rt[:], rt[:], AF.Ln, scale=1.0 / D, bias=1e-6), reads=[rt], writes=[rt])
                    s.op("act", lambda e, rt=rt: e.activation(rt[:], rt[:], AF.Exp, scale=-0.5), reads=[rt], writes=[rt])
                    y = yr.next()
                    for cb in range(4):
                        pa = half_banks[cb]
                        s.op("dve", lambda e, pa=pa, cb=cb, y=y, rt=rt: e.scalar_tensor_tensor(
                            y[:, cb * 512:(cb + 1) * 512], pa[:], rt[:, 0:1], ggr[:, cb * 512:(cb + 1) * 512], ALU.mult, ALU.mult),
                            reads=[pa, rt, ggr], writes=[y])
                    s.op("pool", lambda e, y=y, x=x: e.tensor_tensor(y[:], y[:], x[:], ALU.add), reads=[y, x], writes=[y])
                    s.dma("pool", xdst[tt:tt + 128, :], y[:], reads=[y])
            s.barrier()

    def build(self, phases=("0", "12", "3a", "3b", "3c", "4")):
        self.declare()
        self.setup()
        if "0" in phases:
            self.phase0()
        for l in range(self.n_layers):
            xsrc = self.x_in if l == 0 else self.XS[(l - 1) % 2]
            xdst = self.out if l == self.n_layers - 1 else self.XS[l % 2]
            if "12" in phases:
                for half in self.halves:
                    self.phase12(l, half, xsrc)
            if "3a" in phases:
                self.phase3_diff(l, 0)
            if "3b" in phases:
                self.phase3_nsa(l, 0)
            if "3c" in phases:
                self.phase3_sb(l, 0)
            if "4" in phases:
                with ExitStack() as es4:
                    wo = self.load_wo(es4, l)
                    self.phase4a(l)
                    for c8 in range(8):
                        self.s.coll("AllGather", [[0, 1], [2, 3], [4, 5], [6, 7]], self.MP[c8], self.MG[c8])
                    self.s.barrier()
                    self.phase4b(l, xsrc, xdst, wo)
        self.s.barrier()
        self.es.close()
        return self.nc


def make_in_maps(inputs, n_cores=8):
    k = _constants()
    f = lambda a: np.ascontiguousarray(np.asarray(a, dtype=np.float32))
    lam = np.stack([f(inputs["lambda_q1"]), f(inputs["lambda_k1"]), f(inputs["lambda_q2"]), f(inputs["lambda_k2"])], axis=1)
    shared = dict(
        norm_pre_g=f(inputs["norm_pre_g"]), norm_post_g=f(inputs["norm_post_g"]), w_ada=f(inputs["w_ada"]),
        b_ada=f(inputs["b_ada"]), lam=np.ascontiguousarray(lam),
        diff_norm_g=f(inputs["diff_norm_g"]), cmp_pe_k=f(inputs["cmp_pe_k"]), cmp_pe_v=f(inputs["cmp_pe_v"]),
        cmp_w1_k=f(inputs["cmp_w1_k"]), cmp_w1_v=f(inputs["cmp_w1_v"]), cmp_w2_k=f(inputs["cmp_w2_k"]),
        cmp_w2_v=f(inputs["cmp_w2_v"]), w_out=f(inputs["w_out"]),
        k_ident=k["ident"], k_prot=k["prot"], k_ropec=k["ropec"], k_ropes=k["ropes"], k_ovl=k["ovl"], k_eall=k["eall"])
    w_in = f(inputs["w_in"])
    w_br = f(inputs["w_branch"])
    per_hs = []
    for hs in range(2):
        per_hs.append(dict(w_in=np.ascontiguousarray(w_in[:, :, local_cols(hs)]),
                           w_branch=np.ascontiguousarray(w_br[:, :, hs * 512:(hs + 1) * 512, :])))
    x = f(inputs["x"])
    c = f(inputs["c"])
    maps = []
    for core in range(n_cores):
        b, hs = core // 2, core % 2
        m = dict(shared)
        m.update(per_hs[hs])
        m["x"] = np.ascontiguousarray(x[b])
        m["c"] = np.ascontiguousarray(c[b].reshape(16, 128).T)
        maps.append(m)
    return maps


def kernel(**inputs):
    n_cores = 8
    nc = Builder().build()
    maps = make_in_maps(inputs, n_cores)
    res = run_bass_kernel_spmd(nc, maps, core_ids=list(range(n_cores)))
    return np.stack([np.asarray(res.results[2 * b]["out"]) for b in range(4)], axis=0).astype(np.float32)
```

```python
import math
from contextlib import ExitStack

import numpy as np
import concourse.bass as bass
import concourse.mybir as mybir
from concourse.bass_utils import run_bass_kernel_spmd

F32 = mybir.dt.float32
BF16 = mybir.dt.bfloat16
AF = mybir.ActivationFunctionType
ALU = mybir.AluOpType

D = 2048
SEQ = 4096
DEPTH = 4
HD = 128
N_IN = 17944
NQT = SEQ // 512
NKB = SEQ // 128
SCALE = HD ** -0.5
NCMP = 255
BIG = 30000.0

COLG = dict(aq=0, ak=1024, av=2048, az=3072, bq=4096, bkc=5120, bvc=5376, bks=5632, bvs=5888,
            bkw=6144, bvw=6400, bg=6656, bz=6680, cq=7704, ck=8728, cv=9752, cz=10776, mg=11800)
LOCAL = (("aq", 512), ("ak", 512), ("av", 512), ("az", 512), ("bq", 512), ("bkc", 128), ("bvc", 128),
         ("bks", 128), ("bvs", 128), ("bkw", 128), ("bvw", 128), ("bg", 12), ("bz", 512),
         ("cq", 512), ("ck", 512), ("cv", 512), ("cz", 512), ("mg", 6144))
COL = {}
_c = 0
for _n, _w in LOCAL:
    COL[_n] = _c
    _c += _w
NLOC = _c


def local_cols(hs):
    idx = []
    for n, w in LOCAL:
        g0 = COLG[n] + (0 if n == "mg" else hs * w)
        idx.append(np.arange(g0, g0 + w))
    return np.concatenate(idx)


FMROWS = {}
_r = 0
for _n, _w in (("aq", 512), ("ak", 512), ("az", 512), ("bq", 512), ("bkc", 128), ("bvc", 128),
               ("bks", 128), ("bkw", 128), ("bg", 128), ("bz", 512), ("cq", 512), ("ck", 512),
               ("cz", 512), ("mg", 6144)):
    FMROWS[_n] = _r
    _r += _w
NFM = _r
VTCOL = dict(av=0, bvs=512, bvw=640, cv=768)
NVT = 1280
NYS = 1536
YSROW = dict(a=0, b=512, c=1024)
GROUPS = [("aq", 0, 512, "rope"), ("ak", 0, 512, "rope"), ("av", 0, 512, "v"), ("az", 0, 512, "silu"),
          ("bq", 0, 512, "rope"), ("bkc", 0, 128, "rope"), ("bvc", 0, 128, "fm"), ("bks", 0, 128, "rope"),
          ("bvs", 0, 128, "v"), ("bkw", 0, 128, "rope"), ("bvw", 0, 128, "v"), ("bg", 0, 12, "sig"),
          ("bz", 0, 512, "silu"), ("cq", 0, 512, "fm"), ("ck", 0, 512, "fm"), ("cv", 0, 512, "v"),
          ("cz", 0, 512, "silu")]
GROUPS += [("mg", 512 * _i, 512, "sig") for _i in range(12)]


class T:
    def __init__(self, ap, name=""):
        self.ap = ap
        self.name = name
        self.lw = None
        self.rd = []

    def __getitem__(self, idx):
        return self.ap[idx]


class Sched:
    ENG = ("pe", "act", "dve", "pool", "sp")

    def __init__(self, nc, es, n_dma_sems=6):
        self.nc = nc
        self.es = es
        self.eng = {"pe": nc.tensor, "act": nc.scalar, "dve": nc.vector, "pool": nc.gpsimd, "sp": nc.sync}
        self.sem = {}
        self.cnt = {}
        for e in self.ENG:
            self.sem[e] = es.enter_context(nc.semaphore("s_" + e))
            self.cnt[e] = 0
        self.dsem = {}
        for q in ("sp", "pool", "act"):
            lst = []
            for i in range(n_dma_sems):
                k = "d_%s%d" % (q, i)
                self.sem[k] = es.enter_context(nc.semaphore(k))
                self.cnt[k] = 0
                lst.append(k)
            self.dsem[q] = [lst, 0]
        self.waited = {}
        self.n_ins = 0

    def sb(self, es, name, shape, dt):
        self.uid = getattr(self, "uid", 0) + 1
        name = "%s_u%d" % (name, self.uid)
        return T(es.enter_context(self.nc.sbuf_tensor(name, list(shape), dt)), name)

    def ps(self, es, name, shape, dt=F32):
        return T(es.enter_context(self.nc.psum_tensor(name, list(shape), dt)), name)

    def _wait(self, e, key, val):
        if val <= 0 or self.waited.get((e, key), 0) >= val:
            return
        self.eng[e].wait_ge(self.sem[key], val)
        self.waited[(e, key)] = val

    def _deps(self, e, reads, writes):
        for r in reads:
            if r.lw is not None:
                self._wait(e, *r.lw)
        for w in writes:
            if w.lw is not None and w.lw[0] != e:
                self._wait(e, *w.lw)
            for (k, v) in w.rd:
                if k != e:
                    self._wait(e, k, v)

    def _mark(self, key, val, reads, writes):
        for w in writes:
            w.lw = (key, val)
            w.rd = []
        for r in reads:
            if r in writes:
                continue
            r.rd.append((key, val))
            if len(r.rd) > 16:
                d = {}
                for (k, v) in r.rd:
                    d[k] = max(d.get(k, 0), v)
                r.rd = list(d.items())

    def op(self, e, fn, reads=(), writes=()):
        self._deps(e, reads, writes)
        ins = fn(self.eng[e])
        self.cnt[e] += 1
        ins.then_inc(self.sem[e], 1)
        self._mark(e, self.cnt[e], reads, writes)
        self.n_ins += 1
        return ins

    def dma(self, q, out, in_, reads=(), writes=(), **kw):
        lst, i = self.dsem[q]
        k = lst[i % len(lst)]
        self.dsem[q][1] = i + 1
        self._wait(q, k, self.cnt[k])
        self._deps(q, reads, writes)
        ins = self.eng[q].dma_start(out=out, in_=in_, **kw)
        self.cnt[k] += 16
        ins.then_inc(self.sem[k], 16)
        self._mark(k, self.cnt[k], reads, writes)
        self.n_ins += 1
        return ins

    def coll(self, kind, groups, src, dst, op=None):
        if "cc" not in self.sem:
            self.sem["cc"] = self.es.enter_context(self.nc.semaphore("s_cc"))
            self.cnt["cc"] = 0
        ins = self.nc.gpsimd.collective_compute(kind, op if op is not None else ALU.bypass, replica_groups=groups,
                                                ins=[src.opt()], outs=[dst.opt()])
        self.cnt["cc"] += 1
        ins.then_inc(self.sem["cc"])
        self.n_ins += 1
        return ins

    def barrier(self, label=None):
        if not hasattr(self, "marks"):
            self.marks = []
        self.marks.append((label, self.cnt["pe"]))
        for e in self.ENG:
            for k in self.sem:
                if k != e:
                    self._wait(e, k, self.cnt[k])


class Ring:
    def __init__(self, items):
        self.items = items
        self.i = 0

    def next(self):
        t = self.items[self.i % len(self.items)]
        self.i += 1
        return t


def _constants():
    c = {}
    c["ident"] = np.eye(128, dtype=np.float32)
    prot = np.zeros((128, 128), np.float32)
    for i in range(16):
        prot[i + 16, i] = -1.0
        prot[i, i + 16] = 1.0
    c["prot"] = prot
    pos = np.arange(SEQ, dtype=np.float32)
    inv = (np.float32(500000.0) ** (-np.arange(0, 32, 2, dtype=np.float32) / np.float32(32))).astype(np.float32)
    ang = (pos[None, :] * inv[:, None]).astype(np.float32)
    ct = np.ones((128, SEQ), np.float32)
    st = np.zeros((128, SEQ), np.float32)
    ct[0:16] = np.cos(ang); ct[16:32] = np.cos(ang)
    st[0:16] = np.sin(ang); st[16:32] = np.sin(ang)
    c["ropec"] = ct
    c["ropes"] = st
    n = np.arange(256)[:, None]
    j = np.arange(64)[None, :]
    ov = ((16 * n < 64 * j + 64) & (16 * n + 32 > 64 * j) & (n < NCMP)).astype(np.float32)
    c["ovl"] = ov.reshape(2, 128, 64).transpose(1, 0, 2).copy()
    k = np.arange(SEQ)[None, :]
    c["eall"] = (np.arange(64)[:, None] == (k // 64)).astype(np.float32)
    return c


class Builder:
    def __init__(self, n_layers=DEPTH, halves=(0, 1), headsets=(0, 1), dbg=False):
        self.n_layers = n_layers
        self.halves = halves
        self.headsets = headsets
        self.dbg = dbg
        self.nc = bass.Bass("TRN2", target_bir_lowering=False)
        self.es = ExitStack()
        self.s = Sched(self.nc, self.es)

    def declare(self):
        nc = self.nc
        ein = lambda name, shape: nc.dram_tensor(name, list(shape), F32, kind="ExternalInput").ap()
        self.x_in = ein("x", [SEQ, D])
        self.c_in = ein("c", [128, 16])
        self.norm_pre_g = ein("norm_pre_g", [DEPTH, D])
        self.norm_post_g = ein("norm_post_g", [DEPTH, D])
        self.w_ada = ein("w_ada", [DEPTH, D, 3 * D])
        self.b_ada = ein("b_ada", [DEPTH, 3 * D])
        self.w_in = ein("w_in", [DEPTH, D, NLOC])
        self.lam_in = ein("lam", [DEPTH, 4, 128])
        self.diff_norm_g = ein("diff_norm_g", [DEPTH, 256])
        self.cmp_pe = [ein("cmp_pe_k", [DEPTH, 32, 128]), ein("cmp_pe_v", [DEPTH, 32, 128])]
        self.cmp_w1 = [ein("cmp_w1_k", [DEPTH, 4096, 128]), ein("cmp_w1_v", [DEPTH, 4096, 128])]
        self.cmp_w2 = [ein("cmp_w2_k", [DEPTH, 128, 128]), ein("cmp_w2_v", [DEPTH, 128, 128])]
        self.w_branch = ein("w_branch", [DEPTH, 3, 512, D])
        self.w_out = ein("w_out", [DEPTH, D, D])
        self.k_ident = ein("k_ident", [128, 128])
        self.k_prot = ein("k_prot", [128, 128])
        self.k_ropec = ein("k_ropec", [128, SEQ])
        self.k_ropes = ein("k_ropes", [128, SEQ])
        self.k_ovl = ein("k_ovl", [128, 2, 64])
        self.k_eall = ein("k_eall", [64, SEQ])
        self.out = nc.dram_tensor("out", [SEQ, D], F32, kind="ExternalOutput").ap()
        kind = "ExternalOutput" if self.dbg else "Internal"
        self.FM = nc.dram_tensor("fm", [NFM, SEQ], BF16, kind=kind).ap()
        self.VT = nc.dram_tensor("vt", [SEQ, NVT], BF16, kind=kind).ap()
        self.YS = nc.dram_tensor("ys", [NYS, SEQ], BF16, kind=kind).ap()
        self.MP = [nc.dram_tensor("mp%d" % i, [256, SEQ], BF16, kind="Internal").ap() for i in range(8)]
        self.MG = [nc.dram_tensor("mgath%d" % i, [512, SEQ], BF16, kind="Internal").ap() for i in range(8)]
        self.XS = [nc.dram_tensor("xs%d" % i, [SEQ, D], F32, kind="Internal").ap() for i in range(2)]
        self.GG = nc.dram_tensor("gg", [DEPTH, D], F32, kind="Internal").ap()

    def setup(self):
        s, es = self.s, self.es
        self.ident = s.sb(es, "ident", [128, 128], F32)
        self.prot = s.sb(es, "prot", [128, 128], F32)
        self.ones_b = s.sb(es, "ones_b", [128, 128], BF16)
        self.ones_f = s.sb(es, "ones_f", [128, 128], F32)
        self.ustr = s.sb(es, "ustr", [128, 128], BF16)
        self.uinc = s.sb(es, "uinc", [128, 128], BF16)
        self.ovl = s.sb(es, "ovl", [128, 2, 64], F32)
        self.eall = s.sb(es, "eall", [64, SEQ], BF16)
        self.modp = s.sb(es, "modp", [128, DEPTH, 4, 16], F32)
        self.psum = [s.ps(es, "pb%d" % i, [128, 512], F32) for i in range(8)]
        s.dma("sp", self.ident[:], self.k_ident, writes=[self.ident])
        s.dma("sp", self.prot[:], self.k_prot, writes=[self.prot])
        s.dma("sp", self.ovl[:], self.k_ovl, writes=[self.ovl])
        s.dma("pool", self.eall[:], self.k_eall, writes=[self.eall])
        s.op("dve", lambda e: e.memset(self.ones_b[:], 1.0), writes=[self.ones_b])
        s.op("dve", lambda e: e.memset(self.ones_f[:], 1.0), writes=[self.ones_f])
        s.op("pool", lambda e: e.memset(self.ustr[:], 1.0), writes=[self.ustr])
        s.op("pool", lambda e: e.memset(self.uinc[:], 1.0), writes=[self.uinc])
        s.op("pool", lambda e: e.affine_select(self.uinc[:], self.uinc[:], [[-1, 128]], ALU.is_ge, 0.0,
                                               base=0, channel_multiplier=1),
             reads=[self.uinc], writes=[self.uinc])
        s.op("pool", lambda e: e.affine_select(self.ustr[:], self.ustr[:], [[-1, 128]], ALU.is_gt, 0.0,
                                               base=0, channel_multiplier=1),
             reads=[self.ustr], writes=[self.ustr])

    def phase0(self):
        s = self.s
        with ExitStack() as es:
            cs = s.sb(es, "cs", [128, 16], F32)
            wts = Ring([s.sb(es, "wada%d" % i, [128, 16, 512], F32) for i in range(2)])
            tmp = s.sb(es, "p0tmp", [128, 48], F32)
            bada = s.sb(es, "bada", [128, 48], F32)
            gpre = s.sb(es, "gpre", [128, 16], F32)
            gpost = s.sb(es, "gpost", [128, 16], F32)
            s.dma("sp", cs[:], self.c_in, writes=[cs])
            s.op("act", lambda e: e.activation(cs[:], cs[:], AF.Silu), reads=[cs], writes=[cs])
            pm = self.psum[0]
            for l in range(self.n_layers):
                s.dma("sp", bada[:], self.b_ada[l].rearrange("(j p) -> p j", p=128), writes=[bada],
                      allow_slow_non_contiguous=True)
                s.dma("sp", gpre[:], self.norm_pre_g[l].rearrange("(j p) -> p j", p=128), writes=[gpre],
                      allow_slow_non_contiguous=True)
                s.dma("sp", gpost[:], self.norm_post_g[l].rearrange("(j p) -> p j", p=128), writes=[gpost],
                      allow_slow_non_contiguous=True)
                for g in range(12):
                    wt = wts.next()
                    s.dma("sp", wt[:], self.w_ada[l, :, g * 512:(g + 1) * 512].rearrange("(j p) c -> p j c", p=128),
                          writes=[wt])
                    for cb in range(4):
                        col = g * 4 + cb
                        for j in range(16):
                            s.op("pe", lambda e, wt=wt, cb=cb, j=j, col=col: e.matmul(
                                pm[:, col:col + 1], wt[:, j, cb * 128:(cb + 1) * 128], cs[:, j:j + 1],
                                start=(j == 0), stop=(j == 15)), reads=[wt, cs], writes=[pm])
                s.op("dve", lambda e: e.tensor_tensor(tmp[:], pm[:, 0:48], bada[:], ALU.add),
                     reads=[pm, bada], writes=[tmp])
                mp = self.modp
                s.op("dve", lambda e, l=l: e.scalar_tensor_tensor(mp[:, l, 0, :], tmp[:, 16:32], 1.0, gpre[:],
                                                                    ALU.add, ALU.mult),
                     reads=[tmp, gpre], writes=[mp])
                s.op("dve", lambda e, l=l: e.tensor_copy(mp[:, l, 1, :], tmp[:, 0:16]), reads=[tmp], writes=[mp])
                s.op("dve", lambda e, l=l: e.tensor_tensor(mp[:, l, 2, :], tmp[:, 32:48], gpost[:], ALU.mult),
                     reads=[tmp, gpost], writes=[mp])
                s.dma("sp", self.GG[l].rearrange("(j p) -> p j", p=128), mp[:, l, 2, :], reads=[mp],
                      allow_slow_non_contiguous=True)
            s.barrier()

    def phase12(self, l, half, xsrc):
        s = self.s
        T0 = half * 2048
        with ExitStack() as es:
            hT = [s.sb(es, "hT%d" % i, [128, 16, 512], BF16) for i in range(4)]
            xt = s.sb(es, "xt", [128, 4, 2048], F32)
            junk = s.sb(es, "junk", [128, 2048], BF16)
            ss = s.sb(es, "ss", [128, 4], F32)
            rstd = s.sb(es, "rstd", [128, 4], F32)
            ropec = s.sb(es, "ropec", [128, 2048], F32)
            ropes = s.sb(es, "ropes", [128, 2048], F32)
            wts = Ring([s.sb(es, "wt%d" % i, [128, 16, 512], BF16) for i in range(3)])
            qf = Ring([s.sb(es, "qf%d" % i, [128, 512], F32) for i in range(2)])
            t1 = Ring([s.sb(es, "t1%d" % i, [128, 512], F32) for i in range(2)])
            t2 = Ring([s.sb(es, "t2%d" % i, [128, 512], F32) for i in range(2)])
            stg = Ring([s.sb(es, "stg%d" % i, [128, 512], BF16) for i in range(4)])
            pacc = Ring(self.psum[0:4])
            prot_ps = Ring(self.psum[4:6])
            ptr = Ring(self.psum[6:8])
            mp = self.modp
            s.dma("sp", ropec[:], self.k_ropec[:, T0:T0 + 2048], writes=[ropec])
            s.dma("sp", ropes[:], self.k_ropes[:, T0:T0 + 2048], writes=[ropes])
            for ti in range(4):
                t0 = T0 + ti * 512
                s.dma("sp", xt[:], xsrc[t0:t0 + 512, :].rearrange("(b p) d -> p b d", p=128), writes=[xt])
                for b in range(4):
                    s.op("act", lambda e, b=b: e.activation(junk[:], xt[:, b, :], AF.Square,
                                                            accum_out=ss[:, b:b + 1]),
                         reads=[xt], writes=[junk, ss])
                s.op("act", lambda e: e.activation(rstd[:], ss[:], AF.Ln, scale=1.0 / D, bias=1e-6),
                     reads=[ss], writes=[rstd])
                s.op("act", lambda e: e.activation(rstd[:], rstd[:], AF.Exp, scale=-0.5),
                     reads=[rstd], writes=[rstd])
                for b in range(4):
                    s.op("dve", lambda e, b=b: e.tensor_scalar(xt[:, b, :], xt[:, b, :], rstd[:, b:b + 1], None,
                                                               ALU.mult),
                         reads=[xt, rstd], writes=[xt])
                for j in range(16):
                    pt = ptr.next()
                    for b in range(4):
                        s.op("pe", lambda e, b=b, j=j, pt=pt: e.transpose(
                            pt[:, b * 128:(b + 1) * 128], xt[:, b, j * 128:(j + 1) * 128], self.ident[:]),
                            reads=[xt, self.ident], writes=[pt])
                    s.op("dve", lambda e, j=j, pt=pt, ti=ti: e.tensor_scalar(
                        hT[ti][:, j, :], pt[:], mp[:, l, 0, j:j + 1], mp[:, l, 1, j:j + 1], ALU.mult, ALU.add),
                        reads=[pt, mp], writes=[hT[ti]])
            wq = []

            def issue_w(gi):
                if gi < len(GROUPS):
                    name_, off_, gw_, _ = GROUPS[gi]
                    c0_ = COL[name_] + off_
                    wt_ = wts.next()
                    s.dma("pool", wt_[:, :, 0:gw_], self.w_in[l, :, c0_:c0_ + gw_].rearrange("(j p) c -> p j c", p=128),
                          writes=[wt_])
                    wq.append(wt_)

            issue_w(0)
            issue_w(1)
            for gi, (name, off, gw, kind) in enumerate(GROUPS):
                issue_w(gi + 2)
                wt = wq[gi]
                if kind == "v":
                    vc0 = VTCOL[name] + off
                    for tb in range(16):
                        pa = pacc.next()
                        ti, bb = tb // 4, tb % 4
                        for j in range(16):
                            s.op("pe", lambda e, pa=pa, ti=ti, bb=bb, j=j, wt=wt: e.matmul(
                                pa[:, 0:gw], hT[ti][:, j, bb * 128:(bb + 1) * 128], wt[:, j, 0:gw],
                                start=(j == 0), stop=(j == 15)), reads=[hT[ti], wt], writes=[pa])
                        st = stg.next()
                        eng = "act" if tb % 2 == 0 else "dve"
                        if eng == "act":
                            s.op("act", lambda e, st=st, pa=pa: e.copy(st[:, 0:gw], pa[:, 0:gw]),
                                 reads=[pa], writes=[st])
                        else:
                            s.op("dve", lambda e, st=st, pa=pa: e.tensor_copy(st[:, 0:gw], pa[:, 0:gw]),
                                 reads=[pa], writes=[st])
                        tt = T0 + tb * 128
                        s.dma("sp", self.VT[tt:tt + 128, vc0:vc0 + gw], st[:, 0:gw], reads=[st])
                    continue
                nblk = (gw + 127) // 128
                pending = []
                for blk in range(nblk):
                    bw = min(128, gw - blk * 128)
                    r0 = FMROWS[name] + off + blk * 128
                    for ti in range(4):
                        t0 = T0 + ti * 512
                        tl = ti * 512
                        pa = pacc.next()
                        for j in range(16):
                            s.op("pe", lambda e, pa=pa, ti=ti, j=j, wt=wt, blk=blk, bw=bw: e.matmul(
                                pa[0:bw, :], wt[:, j, blk * 128:blk * 128 + bw], hT[ti][:, j, :],
                                start=(j == 0), stop=(j == 15)), reads=[hT[ti], wt], writes=[pa])
                        st = stg.next()
                        if kind == "fm":
                            s.op("dve", lambda e, st=st, pa=pa: e.tensor_copy(st[:], pa[:]), reads=[pa], writes=[st])
                        elif kind == "silu":
                            s.op("act", lambda e, st=st, pa=pa: e.activation(st[:], pa[:], AF.Silu),
                                 reads=[pa], writes=[st])
                        elif kind == "sig":
                            s.op("act", lambda e, st=st, pa=pa, bw=bw: e.activation(st[0:bw, :], pa[0:bw, :], AF.Sigmoid),
                                 reads=[pa], writes=[st])
                        elif kind == "rope":
                            q = qf.next()
                            s.op("act", lambda e, q=q, pa=pa: e.copy(q[:], pa[:]), reads=[pa], writes=[q])

                            def finish(q=q, st=st, tl=tl, r0=r0, t0=t0, bw=bw):
                                pr = prot_ps.next()
                                a1 = t1.next()
                                a2 = t2.next()
                                s.op("pe", lambda e: e.matmul(pr[:], self.prot[:], q[:], start=True, stop=True),
                                     reads=[q, self.prot], writes=[pr])
                                s.op("dve", lambda e: e.tensor_tensor(
                                    a1[:], q[:], ropec[:, tl:tl + 512], ALU.mult), reads=[q, ropec], writes=[a1])
                                s.op("dve", lambda e: e.tensor_tensor(
                                    a2[:], pr[:], ropes[:, tl:tl + 512], ALU.mult), reads=[pr, ropes], writes=[a2])
                                s.op("dve", lambda e: e.tensor_tensor(st[:], a1[:], a2[:], ALU.add),
                                     reads=[a1, a2], writes=[st])
                                s.dma("sp", self.FM[r0:r0 + bw, t0:t0 + 512], st[0:bw, :], reads=[st])

                            if pending:
                                pending.pop()()
                            pending.append(finish)
                            continue
                        s.dma("sp", self.FM[r0:r0 + bw, t0:t0 + 512], st[0:bw, :], reads=[st])
                if pending:
                    pending.pop()()
            s.barrier()

    def _causal(self, t, k0, t0, npart=128):
        self.s.op("pool", lambda e: e.affine_select(t[0:npart, :], t[0:npart, :], [[1, 512]], ALU.is_ge, 0.0,
                                                    base=t0 - k0, channel_multiplier=-1),
                  reads=[t], writes=[t])

    def _softmax_attn(self, qT, blocks, p_ring, ps_ring, psum_sum, psum_o, scale=SCALE):
        s = self.s
        nb = len(blocks)

        def stage_b(bi, blk, p):
            s.op("pe", lambda e: e.matmul(psum_sum[:], self.ones_b[:], p[:], start=(bi == 0), stop=(bi == nb - 1)),
                 reads=[self.ones_b, p], writes=[psum_sum])
            for oi, (v_t, v_ap) in enumerate(blk["v"]):
                po = psum_o[oi]
                s.op("pe", lambda e, po=po, v_ap=v_ap: e.matmul(po[:], v_ap, p[:], start=(bi == 0), stop=(bi == nb - 1)),
                     reads=[v_t, p], writes=[po])

        prev = None
        for bi, blk in enumerate(blocks):
            ps = ps_ring.next()
            kT_t, kT_ap = blk["kT"]
            bias = blk.get("bias")
            s.op("pe", lambda e: e.matmul(ps[:], kT_ap, qT[:], start=True, stop=(bias is None)),
                 reads=[kT_t, qT], writes=[ps])
            if bias is not None:
                bl_t, bl_ap, br_t, br_ap = bias
                s.op("pe", lambda e: e.matmul(ps[:], bl_ap, br_ap, start=False, stop=True),
                     reads=[bl_t, br_t], writes=[ps])
            p = p_ring.next()
            s.op("act", lambda e: e.activation(p[:], ps[:], AF.Exp, scale=scale), reads=[ps], writes=[p])
            if blk.get("mask") is not None:
                blk["mask"](p)
            if prev is not None:
                stage_b(*prev)
            prev = (bi, blk, p)
        stage_b(*prev)

    def phase3_diff(self, l, hs):
        s = self.s
        lam_init = 0.8 - 0.6 * math.exp(-0.3 * l)
        with ExitStack() as es:
            kT = [Ring([s.sb(es, "akT%d_%d" % (c, i), [128, SEQ], BF16) for i in range(2)]) for c in range(2)]
            vv = Ring([s.sb(es, "avv%d" % i, [128, NKB, 256], BF16) for i in range(2)])
            qTs = Ring([s.sb(es, "aq%d" % i, [128, 512], BF16) for i in range(4)])
            szs = Ring([s.sb(es, "asz%d" % i, [128, 512], BF16) for i in range(4)])
            p_ring = Ring([s.sb(es, "ap%d" % i, [128, 512], BF16) for i in range(4)])
            oc = [[s.sb(es, "aoc%d%d" % (c, h), [128, 512], F32) for h in range(2)] for c in range(2)]
            rs = s.sb(es, "ars", [128, 512], F32)
            sq = [s.sb(es, "asq%d" % h, [128, 512], F32) for h in range(2)]
            rstd = s.sb(es, "arstd", [128, 512], F32)
            yst = Ring([s.sb(es, "ayst%d" % i, [128, 512], BF16) for i in range(2)])
            lamt = s.sb(es, "lamt", [128, 4], F32)
            lam2 = s.sb(es, "lam2", [128, 2], F32)
            neglam = s.sb(es, "neglam", [128, 1], F32)
            gco = s.sb(es, "gco", [128, 2], F32)
            ps_ring = Ring([self.psum[0], self.psum[1], self.psum[3]])
            psum_sum = self.psum[2]
            psum_o = self.psum[4:6]
            pmisc = self.psum[6]
            s.dma("sp", lamt[:], self.lam_in[l].rearrange("k p -> p k"), writes=[lamt], allow_slow_non_contiguous=True)
            s.op("dve", lambda e: e.tensor_tensor(lam2[:, 0:1], lamt[:, 0:1], lamt[:, 1:2], ALU.mult), reads=[lamt], writes=[lam2])
            s.op("dve", lambda e: e.tensor_tensor(lam2[:, 1:2], lamt[:, 2:3], lamt[:, 3:4], ALU.mult), reads=[lamt], writes=[lam2])
            s.op("pe", lambda e: e.matmul(pmisc[:, 0:2], self.ones_f[:], lam2[:], start=True, stop=True),
                 reads=[self.ones_f, lam2], writes=[pmisc])
            s.op("act", lambda e: e.activation(lam2[:], pmisc[:, 0:2], AF.Exp), reads=[pmisc], writes=[lam2])
            s.op("dve", lambda e: e.scalar_tensor_tensor(neglam[:], lam2[:, 1:2], -lam_init, lam2[:, 0:1], ALU.add, ALU.subtract),
                 reads=[lam2], writes=[neglam])
            s.dma("sp", gco[:], self.diff_norm_g[l].rearrange("(h p) -> p h", p=128), writes=[gco], allow_slow_non_contiguous=True)
            s.op("dve", lambda e: e.tensor_scalar(gco[:], gco[:], 1.0 - lam_init, None, ALU.mult), reads=[gco], writes=[gco])
            for h in range(2):
                kts = []
                for c in range(2):
                    kt = kT[c].next()
                    r0 = FMROWS["ak"] + h * 256 + c * 128
                    s.dma("sp", kt[:], self.FM[r0:r0 + 128, :], writes=[kt])
                    kts.append(kt)
                v = vv.next()
                vc = VTCOL["av"] + h * 256
                s.dma("sp", v[:], self.VT[:, vc:vc + 256].rearrange("(kb p) e -> p kb e", p=128), writes=[v])
                def load_q(i_, c_):
                    q_ = qTs.next()
                    r0_ = FMROWS["aq"] + h * 256 + c_ * 128
                    s.dma("sp", q_[:], self.FM[r0_:r0_ + 128, i_ * 512:i_ * 512 + 512], writes=[q_])
                    return q_

                steps = [(i_, c_) for i_ in range(NQT) for c_ in range(2)]
                q_next = load_q(*steps[0])
                for i in range(NQT):
                    t0 = i * 512
                    szt = []
                    for hf in range(2):
                        sz = szs.next()
                        rz = FMROWS["az"] + h * 256 + hf * 128
                        s.dma("sp", sz[:], self.FM[rz:rz + 128, t0:t0 + 512], writes=[sz])
                        szt.append(sz)
                    for c in range(2):
                        q = q_next
                        si = steps.index((i, c))
                        if si + 1 < len(steps):
                            q_next = load_q(*steps[si + 1])
                        blocks = []
                        for kb in range(4 * i + 4):
                            k0 = kb * 128
                            blk = dict(kT=(kts[c], kts[c][:, k0:k0 + 128]),
                                       v=[(v, v[:, kb, 0:128]), (v, v[:, kb, 128:256])])
                            if kb >= 4 * i:
                                blk["mask"] = (lambda p, k0=k0, t0=t0: self._causal(p, k0, t0))
                            blocks.append(blk)
                        self._softmax_attn(q, blocks, p_ring, ps_ring, psum_sum, psum_o)
                        s.op("dve", lambda e: e.reciprocal(rs[:], psum_sum[:]), reads=[psum_sum], writes=[rs])
                        for hf in range(2):
                            s.op("dve", lambda e, c=c, hf=hf: e.tensor_tensor(oc[c][hf][:], psum_o[hf][:], rs[:], ALU.mult),
                                 reads=[psum_o[hf], rs], writes=[oc[c][hf]])
                    for hf in range(2):
                        s.op("dve", lambda e, hf=hf: e.scalar_tensor_tensor(
                            oc[0][hf][:], oc[1][hf][:], neglam[:, 0:1], oc[0][hf][:], ALU.mult, ALU.add),
                            reads=[oc[1][hf], neglam, oc[0][hf]], writes=[oc[0][hf]])
                        s.op("act", lambda e, hf=hf: e.activation(sq[hf][:], oc[0][hf][:], AF.Square),
                             reads=[oc[0][hf]], writes=[sq[hf]])
                    for hf in range(2):
                        s.op("pe", lambda e, hf=hf: e.matmul(pmisc[:], self.ones_f[:], sq[hf][:], start=(hf == 0), stop=(hf == 1)),
                             reads=[self.ones_f, sq[hf]], writes=[pmisc])
                    s.op("act", lambda e: e.activation(rstd[:], pmisc[:], AF.Ln, scale=1.0 / 256, bias=1e-5),
                         reads=[pmisc], writes=[rstd])
                    s.op("act", lambda e: e.activation(rstd[:], rstd[:], AF.Exp, scale=-0.5), reads=[rstd], writes=[rstd])
                    for hf in range(2):
                        sz = szt[hf]
                        s.op("dve", lambda e, hf=hf: e.scalar_tensor_tensor(
                            oc[0][hf][:], oc[0][hf][:], gco[:, hf:hf + 1], rstd[:], ALU.mult, ALU.mult),
                            reads=[oc[0][hf], gco, rstd], writes=[oc[0][hf]])
                        y = yst.next()
                        s.op("dve", lambda e, hf=hf, y=y, sz=sz: e.tensor_tensor(y[:], oc[0][hf][:], sz[:], ALU.mult),
                             reads=[oc[0][hf], sz], writes=[y])
                        ry = YSROW["a"] + h * 256 + hf * 128
                        s.dma("pool", self.YS[ry:ry + 128, t0:t0 + 512], y[:], reads=[y])
            s.barrier()

    def phase3_sb(self, l, hs):
        s = self.s
        with ExitStack() as es:
            kTr = Ring([s.sb(es, "ckT%d" % i, [128, SEQ], BF16) for i in range(2)])
            knr = Ring([s.sb(es, "ckn%d" % i, [128, SEQ], BF16) for i in range(2)])
            vvr = Ring([s.sb(es, "cvv%d" % i, [128, NKB, 128], BF16) for i in range(2)])
            qTs = Ring([s.sb(es, "cq%d" % i, [128, 512], BF16) for i in range(3)])
            szs = Ring([s.sb(es, "csz%d" % i, [128, 512], BF16) for i in range(3)])
            er = Ring([s.sb(es, "ce%d" % i, [128, 512], F32) for i in range(4)])
            lbr = Ring([s.sb(es, "clb%d" % i, [128, 512], BF16) for i in range(5)])
            lsr = Ring([s.sb(es, "cls%d" % i, [128, 512], BF16) for i in range(3)])
            ar = Ring([s.sb(es, "ca%d" % i, [128, 512], BF16) for i in range(4)])
            yst = Ring([s.sb(es, "cyst%d" % i, [128, 512], BF16) for i in range(2)])
            zr = Ring(self.psum[0:2])
            cr = Ring(self.psum[2:5])
            po = self.psum[6]
            for h in range(4):
                kt = kTr.next()
                r0 = FMROWS["ck"] + h * 128
                s.dma("sp", kt[:], self.FM[r0:r0 + 128, :], writes=[kt])
                kn = knr.next()
                s.op("act", lambda e: e.mul(kn[:], kt[:], -SCALE), reads=[kt], writes=[kn])
                v = vvr.next()
                vc = VTCOL["cv"] + h * 128
                s.dma("sp", v[:], self.VT[:, vc:vc + 128].rearrange("(kb p) e -> p kb e", p=128), writes=[v])

                def load_qz(i_):
                    q_ = qTs.next()
                    rq = FMROWS["cq"] + h * 128
                    s.dma("sp", q_[:], self.FM[rq:rq + 128, i_ * 512:i_ * 512 + 512], writes=[q_])
                    sz_ = szs.next()
                    rz = FMROWS["cz"] + h * 128
                    s.dma("sp", sz_[:], self.FM[rz:rz + 128, i_ * 512:i_ * 512 + 512], writes=[sz_])
                    return q_, sz_

                qz_next = load_qz(0)
                for i in range(NQT):
                    t0 = i * 512
                    q, sz = qz_next
                    if i + 1 < NQT:
                        qz_next = load_qz(i + 1)
                    kbs = list(range(4 * i + 3, -1, -1))
                    nk = len(kbs)
                    state = {"ls": None}

                    def stage_a1(kb):
                        k0 = kb * 128
                        zp = zr.next()
                        s.op("pe", lambda e: e.matmul(zp[:], kt[:, k0:k0 + 128], q[:], start=True, stop=True),
                             reads=[kt, q], writes=[zp])
                        ee = er.next()
                        s.op("act", lambda e: e.activation(ee[:], zp[:], AF.Exp, scale=SCALE), reads=[zp], writes=[ee])
                        if kb >= 4 * i:
                            s.op("pool", lambda e: e.affine_select(
                                ee[:], ee[:], [[1, 512]], ALU.is_gt, 0.0, base=t0 - k0, channel_multiplier=-1),
                                reads=[ee], writes=[ee])
                        lb = lbr.next()
                        s.op("act", lambda e: e.activation(lb[:], ee[:], AF.Ln, bias=1.0), reads=[ee], writes=[lb])
                        return lb

                    def stage_a2(bi, kb, lb):
                        k0 = kb * 128
                        cp = cr.next()
                        ls = state["ls"]
                        s.op("pe", lambda e: e.matmul(cp[:], kn[:, k0:k0 + 128], q[:], start=True, stop=False),
                             reads=[kn, q], writes=[cp])
                        s.op("pe", lambda e: e.matmul(cp[:], self.uinc[:], lb[:], start=False, stop=(ls is None)),
                             reads=[self.uinc, lb], writes=[cp])
                        if ls is not None:
                            s.op("pe", lambda e: e.matmul(cp[:], self.ones_b[:], ls[:], start=False, stop=True),
                                 reads=[self.ones_b, ls], writes=[cp])
                        if bi + 1 < nk:
                            if ls is None:
                                state["ls"] = lb
                            else:
                                ln_ = lsr.next()
                                s.op("dve", lambda e: e.tensor_tensor(ln_[:], ls[:], lb[:], ALU.add), reads=[ls, lb], writes=[ln_])
                                state["ls"] = ln_
                        return cp

                    def stage_b(bi, kb, cp):
                        k0 = kb * 128
                        a = ar.next()
                        s.op("act", lambda e: e.activation(a[:], cp[:], AF.Exp, scale=-1.0), reads=[cp], writes=[a])
                        if kb >= 4 * i:
                            s.op("pool", lambda e: e.affine_select(
                                a[:], a[:], [[1, 512]], ALU.is_gt, 0.0, base=t0 - k0, channel_multiplier=-1),
                                reads=[a], writes=[a])
                        s.op("pe", lambda e: e.matmul(po[:], v[:, kb, :], a[:], start=(bi == 0), stop=(bi == nk - 1)),
                             reads=[v, a], writes=[po])

                    lbs = {0: stage_a1(kbs[0])}
                    if nk > 1:
                        lbs[1] = stage_a1(kbs[1])
                    cps = {0: stage_a2(0, kbs[0], lbs[0])}
                    for bi, kb in enumerate(kbs):
                        if bi + 2 < nk:
                            lbs[bi + 2] = stage_a1(kbs[bi + 2])
                        if bi + 1 < nk:
                            cps[bi + 1] = stage_a2(bi + 1, kbs[bi + 1], lbs[bi + 1])
                        stage_b(bi, kb, cps.pop(bi))
                    y = yst.next()
                    s.op("dve", lambda e, y=y, sz=sz: e.tensor_tensor(y[:], po[:], sz[:], ALU.mult), reads=[po, sz], writes=[y])
                    ry = YSROW["c"] + h * 128
                    s.dma("sp", self.YS[ry:ry + 128, t0:t0 + 512], y[:], reads=[y])
            s.barrier()

    def phase3_nsa(self, l, g):
        s = self.s
        with ExitStack() as es:
            big = [s.sb(es, "bbig%d" % i, [128, SEQ], BF16) for i in range(4)]
            vs = s.sb(es, "bvs", [128, NKB, 128], BF16)
            vw = s.sb(es, "bvw", [128, NKB, 128], BF16)
            w1 = s.sb(es, "bw1", [128, 32, 128], BF16)
            w2 = s.sb(es, "bw2", [128, 128], BF16)
            pe_sb = s.sb(es, "bpe", [32, 128], F32)
            peT = s.sb(es, "bpeT", [128, 32], BF16)
            c1 = s.sb(es, "bc1", [128, 1], F32)
            hsl = s.sb(es, "bhsl", [128, 256], BF16)
            kcmpT = s.sb(es, "bkcmpT", [128, 256], BF16)
            vcmp = s.sb(es, "bvcmp", [128, 2, 128], BF16)
            qsets = [[s.sb(es, "bq%d_%d" % (k, i), [128, 512], BF16) for i in range(4)] for k in range(2)]
            gts = Ring([s.sb(es, "bgt%d" % i, [128, 3, 512], BF16) for i in range(2)])
            szs = Ring([s.sb(es, "bsz%d" % i, [128, 512], BF16) for i in range(2)])
            pf = [s.sb(es, "bpf%d" % i, [128, 512], F32) for i in range(2)]
            pnb = Ring([s.sb(es, "bpnb%d" % i, [128, 512], BF16) for i in range(2)])
            rs = s.sb(es, "brs", [128, 512], F32)
            ocmp = [s.sb(es, "bocmp%d" % i, [128, 512], F32) for i in range(4)]
            impS = s.sb(es, "bimp", [128, 4, 64], F32)
            m8 = s.sb(es, "bm8", [128, 16], F32)
            wk = s.sb(es, "bwk", [128, 64], F32)
            sel = s.sb(es, "bsel", [128, 64], F32)
            negT = s.sb(es, "bnegT", [64, 512], BF16)
            p_ring = Ring([s.sb(es, "bp%d" % i, [128, 512], BF16) for i in range(4)])
            acc = s.sb(es, "bacc", [128, 512], F32)
            tmp = s.sb(es, "btmp", [128, 512], F32)
            yst = Ring([s.sb(es, "byst%d" % i, [128, 512], BF16) for i in range(2)])
            ps_ring = Ring([self.psum[0], self.psum[1], self.psum[7]])
            psum_sum = self.psum[2]
            pimp = self.psum[3]
            psum_o = [self.psum[4]]
            pmisc = self.psum[5]
            pmisc2 = self.psum[6]
            kcT, vcT, ksT, kwT = big
            for t_, nm in ((kcT, "bkc"), (vcT, "bvc"), (ksT, "bks"), (kwT, "bkw")):
                r0 = FMROWS[nm] + g * 128
                s.dma("sp", t_[:], self.FM[r0:r0 + 128, :], writes=[t_])
            for t_, nm in ((vs, "bvs"), (vw, "bvw")):
                vc = VTCOL[nm] + g * 128
                s.dma("sp", t_[:], self.VT[:, vc:vc + 128].rearrange("(kb p) e -> p kb e", p=128), writes=[t_])
            for kv in range(2):
                src = kcT if kv == 0 else vcT
                s.dma("pool", w1[:], self.cmp_w1[kv][l].rearrange("(l d) f -> d l f", d=128), writes=[w1])
                s.dma("pool", w2[:], self.cmp_w2[kv][l], writes=[w2])
                s.dma("sp", pe_sb[:], self.cmp_pe[kv][l], writes=[pe_sb])
                s.op("pe", lambda e: e.transpose(pmisc[:, 0:32], pe_sb[:], self.ident[0:32, 0:32]),
                     reads=[pe_sb, self.ident], writes=[pmisc])
                s.op("dve", lambda e: e.tensor_copy(peT[:], pmisc[:, 0:32]), reads=[pmisc], writes=[peT])
                for li in range(32):
                    s.op("pe", lambda e, li=li: e.matmul(pmisc2[:, 0:1], w1[:, li, :], peT[:, li:li + 1],
                                                         start=(li == 0), stop=(li == 31)),
                         reads=[w1, peT], writes=[pmisc2])
                s.op("dve", lambda e: e.tensor_copy(c1[:], pmisc2[:, 0:1]), reads=[pmisc2], writes=[c1])
                for li in range(32):
                    s.op("pe", lambda e, li=li, src=src: e.matmul(pmisc[:, 0:NCMP], w1[:, li, :],
                                                                   src[:, li:li + 16 * (NCMP - 1) + 1:16],
                                                                   start=(li == 0), stop=(li == 31)),
                         reads=[w1, src], writes=[pmisc])
                s.op("dve", lambda e: e.memset(hsl[:], 0.0), writes=[hsl])
                s.op("act", lambda e: e.activation(hsl[:, 0:NCMP], pmisc[:, 0:NCMP], AF.Silu, bias=c1[:, 0:1]),
                     reads=[pmisc, c1], writes=[hsl])
                if kv == 0:
                    s.op("pe", lambda e: e.matmul(pmisc2[:, 0:256], w2[:], hsl[:], start=True, stop=True),
                         reads=[w2, hsl], writes=[pmisc2])
                    s.op("dve", lambda e: e.tensor_copy(kcmpT[:], pmisc2[:, 0:256]), reads=[pmisc2], writes=[kcmpT])
                else:
                    for nb in range(2):
                        s.op("pe", lambda e, nb=nb: e.matmul(pmisc2[:, nb * 128:(nb + 1) * 128], hsl[:, nb * 128:(nb + 1) * 128],
                                                             w2[:], start=True, stop=True),
                             reads=[w2, hsl], writes=[pmisc2])
                    s.op("dve", lambda e: e.tensor_copy(vcmp[:], pmisc2[:, 0:256].rearrange("p (n d) -> p n d", n=2)),
                         reads=[pmisc2], writes=[vcmp])
            def load_qs(i_):
                qs_ = qsets[i_ % 2]
                for r_ in range(4):
                    rq = FMROWS["bq"] + (g * 4 + r_) * 128
                    s.dma("sp", qs_[r_][:], self.FM[rq:rq + 128, i_ * 512:i_ * 512 + 512], writes=[qs_[r_]])
                return qs_

            load_qs(0)
            for i in range(NQT):
                t0 = i * 512
                nbs = [nb for nb in range(2) if 16 * nb * 128 + 31 <= t0 + 511]
                qTs = qsets[i % 2]
                if i + 1 < NQT:
                    load_qs(i + 1)
                for r in range(4):
                    q = qTs[r]
                    for nb in nbs:
                        ps = ps_ring.next()
                        s.op("pe", lambda e, ps=ps, nb=nb, q=q: e.matmul(ps[:], kcmpT[:, nb * 128:(nb + 1) * 128], q[:],
                                                                          start=True, stop=True),
                             reads=[kcmpT, q], writes=[ps])
                        s.op("act", lambda e, ps=ps, nb=nb: e.activation(pf[nb][:], ps[:], AF.Exp, scale=SCALE),
                             reads=[ps], writes=[pf[nb]])
                        s.op("pool", lambda e, nb=nb, t0=t0: e.affine_select(
                            pf[nb][:], pf[nb][:], [[1, 512]], ALU.is_ge, 0.0,
                            base=t0 - 16 * nb * 128 - 31, channel_multiplier=-16), reads=[pf[nb]], writes=[pf[nb]])
                        s.op("pe", lambda e, nb=nb: e.matmul(psum_sum[:], self.ones_f[:], pf[nb][:], start=(nb == nbs[0]),
                                                             stop=(nb == nbs[-1])),
                             reads=[self.ones_f, pf[nb]], writes=[psum_sum])
                    s.op("dve", lambda e: e.tensor_scalar(rs[:], psum_sum[:], 1e-30, None, ALU.max), reads=[psum_sum], writes=[rs])
                    s.op("dve", lambda e: e.reciprocal(rs[:], rs[:]), reads=[rs], writes=[rs])
                    for nb in nbs:
                        s.op("dve", lambda e, nb=nb: e.tensor_tensor(pf[nb][:], pf[nb][:], rs[:], ALU.mult),
                             reads=[pf[nb], rs], writes=[pf[nb]])
                        for tb in range(4):
                            first = (r == 0 and nb == nbs[0])
                            last = (r == 3 and nb == nbs[-1])
                            s.op("pe", lambda e, nb=nb, tb=tb, first=first, last=last: e.matmul(
                                pimp[:, tb * 64:(tb + 1) * 64], pf[nb][:, tb * 128:(tb + 1) * 128], self.ovl[:, nb, :],
                                start=first, stop=last), reads=[pf[nb], self.ovl], writes=[pimp])
                        pb = pnb.next()
                        s.op("act", lambda e, pb=pb, nb=nb: e.copy(pb[:], pf[nb][:]), reads=[pf[nb]], writes=[pb])
                        s.op("pe", lambda e, pb=pb, nb=nb: e.matmul(psum_o[0][:], vcmp[:, nb, :], pb[:], start=(nb == nbs[0]),
                                                                     stop=(nb == nbs[-1])),
                             reads=[vcmp, pb], writes=[psum_o[0]])
                    s.op("dve", lambda e, r=r: e.tensor_copy(ocmp[r][:], psum_o[0][:]), reads=[psum_o[0]], writes=[ocmp[r]])
                s.op("dve", lambda e: e.tensor_copy(impS[:], pimp[:, 0:256].rearrange("p (a b) -> p a b", a=4)),
                     reads=[pimp], writes=[impS])
                for tb in range(4):
                    for hh in range(2):
                        tblk = 8 * i + 2 * tb + hh
                        p0 = hh * 64
                        if tblk < 63:
                            s.op("pool", lambda e, tb=tb, p0=p0, tblk=tblk: e.memset(impS[p0:p0 + 64, tb, tblk + 1:64], -1e30),
                                 reads=[impS], writes=[impS])
                        s.op("pool", lambda e, tb=tb, p0=p0, tblk=tblk: e.memset(impS[p0:p0 + 64, tb, tblk:tblk + 1], 1e6),
                             reads=[impS], writes=[impS])
                        s.op("pool", lambda e, tb=tb, p0=p0: e.memset(impS[p0:p0 + 64, tb, 0:1], 2e6),
                             reads=[impS], writes=[impS])
                for tb in range(4):
                    s.op("dve", lambda e, tb=tb: e.max(out=m8[:, 0:8], in_=impS[:, tb, :]), reads=[impS], writes=[m8])
                    s.op("dve", lambda e, tb=tb: e.match_replace(out=wk[:], in_to_replace=m8[:, 0:8], in_values=impS[:, tb, :],
                                                                 imm_value=-3e30), reads=[impS, m8], writes=[wk])
                    s.op("dve", lambda e: e.max(out=m8[:, 8:16], in_=wk[:]), reads=[wk], writes=[m8])
                    s.op("dve", lambda e, tb=tb: e.tensor_scalar(sel[:], impS[:, tb, :], m8[:, 15:16], None, ALU.is_ge),
                         reads=[impS, m8], writes=[sel])
                    s.op("dve", lambda e: e.tensor_scalar(sel[:], sel[:], -1.0, BIG, ALU.add, ALU.mult), reads=[sel], writes=[sel])
                    s.op("pe", lambda e: e.transpose(pmisc[0:64, 0:128], sel[:], self.ident[:]),
                         reads=[sel, self.ident], writes=[pmisc])
                    s.op("act", lambda e, tb=tb: e.copy(negT[:, tb * 128:(tb + 1) * 128], pmisc[0:64, 0:128]),
                         reads=[pmisc], writes=[negT])
                for r in range(4):
                    h = g * 4 + r
                    q = qTs[r]
                    gt = gts.next()
                    for k3 in range(3):
                        rg = FMROWS["bg"] + h * 3 + k3
                        s.dma("sp", gt[:, k3, :], self.FM[rg:rg + 1, t0:t0 + 512].to_broadcast([128, 512]), writes=[gt])
                    sz = szs.next()
                    rz = FMROWS["bz"] + h * 128
                    s.dma("sp", sz[:], self.FM[rz:rz + 128, t0:t0 + 512], writes=[sz])
                    s.op("dve", lambda e, r=r, gt=gt: e.tensor_tensor(acc[:], ocmp[r][:], gt[:, 0, :], ALU.mult),
                         reads=[ocmp[r], gt], writes=[acc])
                    blocks = []
                    for kb in range(4 * i + 4):
                        k0 = kb * 128
                        blk = dict(kT=(ksT, ksT[:, k0:k0 + 128]), v=[(vs, vs[:, kb, :])],
                                   bias=(self.eall, self.eall[:, k0:k0 + 128], negT, negT[:]))
                        if kb >= 4 * i:
                            blk["mask"] = (lambda p, k0=k0, t0=t0: self._causal(p, k0, t0))
                        blocks.append(blk)
                    self._softmax_attn(q, blocks, p_ring, ps_ring, psum_sum, psum_o)
                    s.op("dve", lambda e: e.reciprocal(rs[:], psum_sum[:]), reads=[psum_sum], writes=[rs])
                    s.op("dve", lambda e: e.tensor_tensor(tmp[:], psum_o[0][:], rs[:], ALU.mult), reads=[psum_o[0], rs], writes=[tmp])
                    s.op("dve", lambda e, gt=gt: e.tensor_tensor(tmp[:], tmp[:], gt[:, 1, :], ALU.mult), reads=[tmp, gt], writes=[tmp])
                    s.op("dve", lambda e: e.tensor_tensor(acc[:], acc[:], tmp[:], ALU.add), reads=[acc, tmp], writes=[acc])
                    blocks = []
                    for kb in range(max(0, 4 * i - 4), 4 * i + 4):
                        k0 = kb * 128
                        blk = dict(kT=(kwT, kwT[:, k0:k0 + 128]), v=[(vw, vw[:, kb, :])])
                        if kb >= 4 * i:
                            blk["mask"] = (lambda p, k0=k0, t0=t0: self._causal(p, k0, t0))
                        else:
                            blk["mask"] = (lambda p, k0=k0, t0=t0: s.op("pool", lambda e: e.affine_select(
                                p[:], p[:], [[-1, 512]], ALU.is_gt, 0.0, base=k0 - t0 + 512, channel_multiplier=1),
                                reads=[p], writes=[p]))
                        blocks.append(blk)
                    self._softmax_attn(q, blocks, p_ring, ps_ring, psum_sum, psum_o)
                    s.op("dve", lambda e: e.reciprocal(rs[:], psum_sum[:]), reads=[psum_sum], writes=[rs])
                    s.op("dve", lambda e: e.tensor_tensor(tmp[:], psum_o[0][:], rs[:], ALU.mult), reads=[psum_o[0], rs], writes=[tmp])
                    s.op("dve", lambda e, gt=gt: e.tensor_tensor(tmp[:], tmp[:], gt[:, 2, :], ALU.mult), reads=[tmp, gt], writes=[tmp])
                    s.op("dve", lambda e: e.tensor_tensor(acc[:], acc[:], tmp[:], ALU.add), reads=[acc, tmp], writes=[acc])
                    y = yst.next()
                    s.op("dve", lambda e, y=y, sz=sz: e.tensor_tensor(y[:], acc[:], sz[:], ALU.mult), reads=[acc, sz], writes=[y])
                    ry = YSROW["b"] + h * 128
                    s.dma("pool", self.YS[ry:ry + 128, t0:t0 + 512], y[:], reads=[y])
            s.barrier()

    def phase4a(self, l):
        s = self.s
        for tp in range(4):
            with ExitStack() as es2:
                ys = [s.sb(es2, "ysb%d" % n, [128, 4, 1024], BF16) for n in range(3)]
                wbr = Ring([s.sb(es2, "wb%d" % i, [128, 4, 512], BF16) for i in range(6)])
                mgr = Ring([s.sb(es2, "mg%d" % i, [128, 1024], BF16) for i in range(4)])
                macc = [s.sb(es2, "macc%d" % i, [128, 512], F32) for i in range(2)]
                tm = Ring([s.sb(es2, "tm%d" % i, [128, 512], F32) for i in range(3)])
                stg = Ring([s.sb(es2, "mstg%d" % i, [128, 512], BF16) for i in range(3)])
                pacc = Ring(self.psum[0:8])
                TP = tp * 1024
                for n in range(3):
                    s.dma("sp", ys[n][:], self.YS[n * 512:(n + 1) * 512, TP:TP + 1024].rearrange("(k p) t -> p k t", p=128),
                          writes=[ys[n]])
                for cg in range(4):
                    wbs = []
                    for n in range(3):
                        wb = wbr.next()
                        s.dma("pool", wb[:], self.w_branch[l, n, :, cg * 512:(cg + 1) * 512].rearrange("(k p) c -> p k c", p=128),
                              writes=[wb])
                        wbs.append(wb)
                    for cb in range(4):
                        cc = cg * 4 + cb
                        for n in range(3):
                            mg = mgr.next()
                            rm = FMROWS["mg"] + n * 2048 + cc * 128
                            s.dma("sp", mg[:], self.FM[rm:rm + 128, TP:TP + 1024], writes=[mg])
                            for tl in range(2):
                                pa = pacc.next()
                                for k in range(4):
                                    s.op("pe", lambda e, pa=pa, n=n, k=k, tl=tl, cb=cb: e.matmul(
                                        pa[:], wbs[n][:, k, cb * 128:(cb + 1) * 128], ys[n][:, k, tl * 512:(tl + 1) * 512],
                                        start=(k == 0), stop=(k == 3)), reads=[wbs[n], ys[n]], writes=[pa])
                                if n == 0:
                                    s.op("dve", lambda e, pa=pa, tl=tl, mg=mg: e.tensor_tensor(
                                        macc[tl][:], pa[:], mg[:, tl * 512:(tl + 1) * 512], ALU.mult),
                                        reads=[pa, mg], writes=[macc[tl]])
                                else:
                                    t_ = tm.next()
                                    s.op("dve", lambda e, pa=pa, tl=tl, mg=mg, t_=t_: e.tensor_tensor(
                                        t_[:], pa[:], mg[:, tl * 512:(tl + 1) * 512], ALU.mult),
                                        reads=[pa, mg], writes=[t_])
                                    if n == 1:
                                        s.op("dve", lambda e, tl=tl, t_=t_: e.tensor_tensor(macc[tl][:], macc[tl][:], t_[:], ALU.add),
                                             reads=[macc[tl], t_], writes=[macc[tl]])
                                    else:
                                        st = stg.next()
                                        s.op("dve", lambda e, tl=tl, t_=t_, st=st: e.tensor_tensor(
                                            st[:], macc[tl][:], t_[:], ALU.add),
                                            reads=[macc[tl], t_], writes=[st])
                                        tt = TP + tl * 512
                                        s.dma("pool", self.MP[cc // 2][(cc % 2) * 128:(cc % 2 + 1) * 128, tt:tt + 512], st[:], reads=[st])
                s.barrier()

    def load_wo(self, es, l):
        s = self.s
        wo = s.sb(es, "wo", [128, 16, 2048], BF16)
        for k4 in range(4):
            s.dma("pool", wo[:, k4 * 4:(k4 + 1) * 4, :],
                  self.w_out[l, k4 * 512:(k4 + 1) * 512, :].rearrange("(k p) c -> p k c", p=128), writes=[wo])
        return wo

    def phase4b(self, l, xsrc, xdst, wo):
        s = self.s
        with ExitStack() as es3:
            ggr = s.sb(es3, "ggr", [128, 2048], F32)
            m0 = s.sb(es3, "m0", [128, 16, 512], BF16)
            m1 = s.sb(es3, "m1", [128, 16, 512], BF16)
            mTr = Ring([s.sb(es3, "mT%d" % i, [128, 16, 512], BF16) for i in range(2)])
            xr = Ring([s.sb(es3, "xr%d" % i, [128, 2048], F32) for i in range(3)])
            yr = Ring([s.sb(es3, "yr%d" % i, [128, 2048], F32) for i in range(2)])
            junk = s.sb(es3, "junk4", [128, 512], BF16)
            ss = Ring([s.sb(es3, "ss4%d" % i, [128, 4], F32) for i in range(2)])
            rstd = Ring([s.sb(es3, "rstd4%d" % i, [128, 1], F32) for i in range(2)])
            s.dma("sp", ggr[:], self.GG[l:l + 1, :].to_broadcast([128, 2048]), writes=[ggr])

            def load_m(ti_):
                t0_ = ti_ * 512
                for c8 in range(8):
                    for rk, mm in ((0, m0), (1, m1)):
                        s.dma("sp", mm[:, 2 * c8:2 * c8 + 2, :],
                              self.MG[c8][rk * 256:(rk + 1) * 256, t0_:t0_ + 512].rearrange("(q p) t -> p q t", p=128), writes=[mm])

            def load_x(tb_):
                x_ = xr.next()
                s.dma("sp", x_[:], xsrc[tb_ * 128:(tb_ + 1) * 128, :], writes=[x_])
                return x_

            load_m(0)
            x_next = load_x(0)
            for ti in range(NQT):
                mT = mTr.next()
                for hk in range(2):
                    eng = "dve"
                    s.op(eng, lambda e, hk=hk, mT=mT: e.tensor_tensor(
                        mT[:, hk * 8:(hk + 1) * 8, :], m0[:, hk * 8:(hk + 1) * 8, :], m1[:, hk * 8:(hk + 1) * 8, :], ALU.add),
                        reads=[m0, m1], writes=[mT])
                if ti + 1 < NQT:
                    load_m(ti + 1)
                for bb in range(4):
                    tb = ti * 4 + bb
                    tt = tb * 128
                    x = x_next
                    if tb + 1 < 4 * NQT:
                        x_next = load_x(tb + 1)
                    half_banks = self.psum[0:4] if tb % 2 == 0 else self.psum[4:8]
                    sst = ss.next()
                    for cb in range(4):
                        pa = half_banks[cb]
                        for k in range(16):
                            s.op("pe", lambda e, pa=pa, k=k, bb=bb, cb=cb, mT=mT: e.matmul(
                                pa[:], mT[:, k, bb * 128:(bb + 1) * 128], wo[:, k, cb * 512:(cb + 1) * 512],
                                start=(k == 0), stop=(k == 15)), reads=[mT, wo], writes=[pa])
                        s.op("act", lambda e, pa=pa, cb=cb, sst=sst: e.activation(junk[:], pa[:], AF.Square,
                                                                                 accum_out=sst[:, cb:cb + 1]),
                             reads=[pa], writes=[junk, sst])
                    rt = rstd.next()
                    s.op("dve", lambda e, sst=sst, rt=rt: e.tensor_reduce(rt[:], sst[:], mybir.AxisListType.X, ALU.add),
                         reads=[sst], writes=[rt])
                    s.op("act", lambda e, rt=rt: e.activation(rt[:], rt[:], AF.Ln, scale=1.0 / D, bias=1e-6), reads=[rt], writes=[rt])
                    s.op("act", lambda e, rt=rt: e.activation(rt[:], rt[:], AF.Exp, scale=-0.5), reads=[rt], writes=[rt])
                    y = yr.next()
                    for cb in range(4):
                        pa = half_banks[cb]
                        s.op("dve", lambda e, pa=pa, cb=cb, y=y, rt=rt: e.scalar_tensor_tensor(
                            y[:, cb * 512:(cb + 1) * 512], pa[:], rt[:, 0:1], ggr[:, cb * 512:(cb + 1) * 512], ALU.mult, ALU.mult),
                            reads=[pa, rt, ggr], writes=[y])
                    s.op("dve", lambda e, y=y, x=x: e.tensor_tensor(y[:], y[:], x[:], ALU.add), reads=[y, x], writes=[y])
                    s.dma("pool", xdst[tt:tt + 128, :], y[:], reads=[y])
            s.barrier()

    def build(self, phases=("0", "12", "3a", "3b", "3c", "4")):
        self.declare()
        self.setup()
        if "0" in phases:
            self.phase0()
        for l in range(self.n_layers):
            xsrc = self.x_in if l == 0 else self.XS[(l - 1) % 2]
            xdst = self.out if l == self.n_layers - 1 else self.XS[l % 2]
            if "12" in phases:
                for half in self.halves:
                    self.phase12(l, half, xsrc)
            if "3a" in phases:
                self.phase3_diff(l, 0)
            if "3b" in phases:
                self.phase3_nsa(l, 0)
            if "3c" in phases:
                self.phase3_sb(l, 0)
            if "4" in phases:
                with ExitStack() as es4:
                    wo = self.load_wo(es4, l)
                    self.phase4a(l)
                    for c8 in range(8):
                        self.s.coll("AllGather", [[0, 1], [2, 3], [4, 5], [6, 7]], self.MP[c8], self.MG[c8])
                    self.s.barrier()
                    self.phase4b(l, xsrc, xdst, wo)
        self.s.barrier()
        self.es.close()
        return self.nc


def make_in_maps(inputs, n_cores=8):
    k = _constants()
    f = lambda a: np.ascontiguousarray(np.asarray(a, dtype=np.float32))
    lam = np.stack([f(inputs["lambda_q1"]), f(inputs["lambda_k1"]), f(inputs["lambda_q2"]), f(inputs["lambda_k2"])], axis=1)
    shared = dict(
        norm_pre_g=f(inputs["norm_pre_g"]), norm_post_g=f(inputs["norm_post_g"]), w_ada=f(inputs["w_ada"]),
        b_ada=f(inputs["b_ada"]), lam=np.ascontiguousarray(lam),
        diff_norm_g=f(inputs["diff_norm_g"]), cmp_pe_k=f(inputs["cmp_pe_k"]), cmp_pe_v=f(inputs["cmp_pe_v"]),
        cmp_w1_k=f(inputs["cmp_w1_k"]), cmp_w1_v=f(inputs["cmp_w1_v"]), cmp_w2_k=f(inputs["cmp_w2_k"]),
        cmp_w2_v=f(inputs["cmp_w2_v"]), w_out=f(inputs["w_out"]),
        k_ident=k["ident"], k_prot=k["prot"], k_ropec=k["ropec"], k_ropes=k["ropes"], k_ovl=k["ovl"], k_eall=k["eall"])
    w_in = f(inputs["w_in"])
    w_br = f(inputs["w_branch"])
    per_hs = []
    for hs in range(2):
        per_hs.append(dict(w_in=np.ascontiguousarray(w_in[:, :, local_cols(hs)]),
                           w_branch=np.ascontiguousarray(w_br[:, :, hs * 512:(hs + 1) * 512, :])))
    x = f(inputs["x"])
    c = f(inputs["c"])
    maps = []
    for core in range(n_cores):
        b, hs = core // 2, core % 2
        m = dict(shared)
        m.update(per_hs[hs])
        m["x"] = np.ascontiguousarray(x[b])
        m["c"] = np.ascontiguousarray(c[b].reshape(16, 128).T)
        maps.append(m)
    return maps


def kernel(**inputs):
    n_cores = 8
    nc = Builder().build()
    maps = make_in_maps(inputs, n_cores)
    res = run_bass_kernel_spmd(nc, maps, core_ids=list(range(n_cores)))
    return np.stack([np.asarray(res.results[2 * b]["out"]) for b in range(4)], axis=0).astype(np.float32)
```

```python
import math
from contextlib import ExitStack

import numpy as np
import concourse.bass as bass
import concourse.mybir as mybir
from concourse.bass_utils import run_bass_kernel_spmd

F32 = mybir.dt.float32
BF16 = mybir.dt.bfloat16
AF = mybir.ActivationFunctionType
ALU = mybir.AluOpType

D = 2048
SEQ = 4096
DEPTH = 4
HD = 128
N_IN = 17944
NQT = SEQ // 512
NKB = SEQ // 128
SCALE = HD ** -0.5
NCMP = 255
BIG = 30000.0

COLG = dict(aq=0, ak=1024, av=2048, az=3072, bq=4096, bkc=5120, bvc=5376, bks=5632, bvs=5888,
            bkw=6144, bvw=6400, bg=6656, bz=6680, cq=7704, ck=8728, cv=9752, cz=10776, mg=11800)
LOCAL = (("aq", 512), ("ak", 512), ("av", 512), ("az", 512), ("bq", 512), ("bkc", 128), ("bvc", 128),
         ("bks", 128), ("bvs", 128), ("bkw", 128), ("bvw", 128), ("bg", 12), ("bz", 512),
         ("cq", 512), ("ck", 512), ("cv", 512), ("cz", 512), ("mg", 6144))
COL = {}
_c = 0
for _n, _w in LOCAL:
    COL[_n] = _c
    _c += _w
NLOC = _c


def local_cols(hs):
    idx = []
    for n, w in LOCAL:
        g0 = COLG[n] + (0 if n == "mg" else hs * w)
        idx.append(np.arange(g0, g0 + w))
    return np.concatenate(idx)


FMROWS = {}
_r = 0
for _n, _w in (("aq", 512), ("ak", 512), ("az", 512), ("bq", 512), ("bkc", 128), ("bvc", 128),
               ("bks", 128), ("bkw", 128), ("bg", 128), ("bz", 512), ("cq", 512), ("ck", 512),
               ("cz", 512), ("mg", 6144)):
    FMROWS[_n] = _r
    _r += _w
NFM = _r
VTCOL = dict(av=0, bvs=512, bvw=640, cv=768)
NVT = 1280
NYS = 1536
YSROW = dict(a=0, b=512, c=1024)
GROUPS = [("aq", 0, 512, "rope"), ("ak", 0, 512, "rope"), ("av", 0, 512, "v"), ("az", 0, 512, "silu"),
          ("bq", 0, 512, "rope"), ("bkc", 0, 128, "rope"), ("bvc", 0, 128, "fm"), ("bks", 0, 128, "rope"),
          ("bvs", 0, 128, "v"), ("bkw", 0, 128, "rope"), ("bvw", 0, 128, "v"), ("bg", 0, 12, "sig"),
          ("bz", 0, 512, "silu"), ("cq", 0, 512, "fm"), ("ck", 0, 512, "fm"), ("cv", 0, 512, "v"),
          ("cz", 0, 512, "silu")]
GROUPS += [("mg", 512 * _i, 512, "sig") for _i in range(12)]


class T:
    def __init__(self, ap, name=""):
        self.ap = ap
        self.name = name
        self.lw = None
        self.rd = []

    def __getitem__(self, idx):
        return self.ap[idx]


class Sched:
    ENG = ("pe", "act", "dve", "pool", "sp")

    def __init__(self, nc, es, n_dma_sems=6):
        self.nc = nc
        self.es = es
        self.eng = {"pe": nc.tensor, "act": nc.scalar, "dve": nc.vector, "pool": nc.gpsimd, "sp": nc.sync}
        self.sem = {}
        self.cnt = {}
        for e in self.ENG:
            self.sem[e] = es.enter_context(nc.semaphore("s_" + e))
            self.cnt[e] = 0
        self.dsem = {}
        for q in ("sp", "pool", "act"):
            lst = []
            for i in range(n_dma_sems):
                k = "d_%s%d" % (q, i)
                self.sem[k] = es.enter_context(nc.semaphore(k))
                self.cnt[k] = 0
                lst.append(k)
            self.dsem[q] = [lst, 0]
        self.waited = {}
        self.n_ins = 0

    def sb(self, es, name, shape, dt):
        self.uid = getattr(self, "uid", 0) + 1
        name = "%s_u%d" % (name, self.uid)
        return T(es.enter_context(self.nc.sbuf_tensor(name, list(shape), dt)), name)

    def ps(self, es, name, shape, dt=F32):
        return T(es.enter_context(self.nc.psum_tensor(name, list(shape), dt)), name)

    def _wait(self, e, key, val):
        if val <= 0 or self.waited.get((e, key), 0) >= val:
            return
        self.eng[e].wait_ge(self.sem[key], val)
        self.waited[(e, key)] = val

    def _deps(self, e, reads, writes):
        for r in reads:
            if r.lw is not None:
                self._wait(e, *r.lw)
        for w in writes:
            if w.lw is not None and w.lw[0] != e:
                self._wait(e, *w.lw)
            for (k, v) in w.rd:
                if k != e:
                    self._wait(e, k, v)

    def _mark(self, key, val, reads, writes):
        for w in writes:
            w.lw = (key, val)
            w.rd = []
        for r in reads:
            if r in writes:
                continue
            r.rd.append((key, val))
            if len(r.rd) > 16:
                d = {}
                for (k, v) in r.rd:
                    d[k] = max(d.get(k, 0), v)
                r.rd = list(d.items())

    def op(self, e, fn, reads=(), writes=()):
        self._deps(e, reads, writes)
        ins = fn(self.eng[e])
        self.cnt[e] += 1
        ins.then_inc(self.sem[e], 1)
        self._mark(e, self.cnt[e], reads, writes)
        self.n_ins += 1
        return ins

    def dma(self, q, out, in_, reads=(), writes=(), **kw):
        lst, i = self.dsem[q]
        k = lst[i % len(lst)]
        self.dsem[q][1] = i + 1
        self._wait(q, k, self.cnt[k])
        self._deps(q, reads, writes)
        ins = self.eng[q].dma_start(out=out, in_=in_, **kw)
        self.cnt[k] += 16
        ins.then_inc(self.sem[k], 16)
        self._mark(k, self.cnt[k], reads, writes)
        self.n_ins += 1
        return ins

    def coll(self, kind, groups, src, dst, op=None):
        if "cc" not in self.sem:
            self.sem["cc"] = self.es.enter_context(self.nc.semaphore("s_cc"))
            self.cnt["cc"] = 0
        ins = self.nc.gpsimd.collective_compute(kind, op if op is not None else ALU.bypass, replica_groups=groups,
                                                ins=[src.opt()], outs=[dst.opt()])
        self.cnt["cc"] += 1
        ins.then_inc(self.sem["cc"])
        self._wait("pool", "cc", self.cnt["cc"])
        self.n_ins += 1
        return ins

    def barrier(self, label=None):
        if not hasattr(self, "marks"):
            self.marks = []
        self.marks.append((label, self.cnt["pe"]))
        for e in self.ENG:
            for k in self.sem:
                if k != e:
                    self._wait(e, k, self.cnt[k])


class Ring:
    def __init__(self, items):
        self.items = items
        self.i = 0

    def next(self):
        t = self.items[self.i % len(self.items)]
        self.i += 1
        return t


def _constants():
    c = {}
    c["ident"] = np.eye(128, dtype=np.float32)
    prot = np.zeros((128, 128), np.float32)
    for i in range(16):
        prot[i + 16, i] = -1.0
        prot[i, i + 16] = 1.0
    c["prot"] = prot
    pos = np.arange(SEQ, dtype=np.float32)
    inv = (np.float32(500000.0) ** (-np.arange(0, 32, 2, dtype=np.float32) / np.float32(32))).astype(np.float32)
    ang = (pos[None, :] * inv[:, None]).astype(np.float32)
    ct = np.ones((128, SEQ), np.float32)
    st = np.zeros((128, SEQ), np.float32)
    ct[0:16] = np.cos(ang); ct[16:32] = np.cos(ang)
    st[0:16] = np.sin(ang); st[16:32] = np.sin(ang)
    c["ropec"] = ct
    c["ropes"] = st
    n = np.arange(256)[:, None]
    j = np.arange(64)[None, :]
    ov = ((16 * n < 64 * j + 64) & (16 * n + 32 > 64 * j) & (n < NCMP)).astype(np.float32)
    c["ovl"] = ov.reshape(2, 128, 64).transpose(1, 0, 2).copy()
    k = np.arange(SEQ)[None, :]
    c["eall"] = (np.arange(64)[:, None] == (k // 64)).astype(np.float32)
    return c


class Builder:
    def __init__(self, n_layers=DEPTH, halves=(0, 1), headsets=(0, 1), dbg=False):
        self.n_layers = n_layers
        self.halves = halves
        self.headsets = headsets
        self.dbg = dbg
        self.nc = bass.Bass("TRN2", target_bir_lowering=False)
        self.es = ExitStack()
        self.s = Sched(self.nc, self.es)

    def declare(self):
        nc = self.nc
        ein = lambda name, shape: nc.dram_tensor(name, list(shape), F32, kind="ExternalInput").ap()
        self.x_in = ein("x", [SEQ, D])
        self.c_in = ein("c", [128, 16])
        self.norm_pre_g = ein("norm_pre_g", [DEPTH, D])
        self.norm_post_g = ein("norm_post_g", [DEPTH, D])
        self.w_ada = ein("w_ada", [DEPTH, D, 3 * D])
        self.b_ada = ein("b_ada", [DEPTH, 3 * D])
        self.w_in = ein("w_in", [DEPTH, D, NLOC])
        self.lam_in = ein("lam", [DEPTH, 4, 128])
        self.diff_norm_g = ein("diff_norm_g", [DEPTH, 256])
        self.cmp_pe = [ein("cmp_pe_k", [DEPTH, 32, 128]), ein("cmp_pe_v", [DEPTH, 32, 128])]
        self.cmp_w1 = [ein("cmp_w1_k", [DEPTH, 4096, 128]), ein("cmp_w1_v", [DEPTH, 4096, 128])]
        self.cmp_w2 = [ein("cmp_w2_k", [DEPTH, 128, 128]), ein("cmp_w2_v", [DEPTH, 128, 128])]
        self.w_branch = ein("w_branch", [DEPTH, 3, 512, D])
        self.w_out = ein("w_out", [DEPTH, D, D])
        self.k_ident = ein("k_ident", [128, 128])
        self.k_prot = ein("k_prot", [128, 128])
        self.k_ropec = ein("k_ropec", [128, SEQ])
        self.k_ropes = ein("k_ropes", [128, SEQ])
        self.k_ovl = ein("k_ovl", [128, 2, 64])
        self.k_eall = ein("k_eall", [64, SEQ])
        self.out = nc.dram_tensor("out", [SEQ, D], F32, kind="ExternalOutput").ap()
        kind = "ExternalOutput" if self.dbg else "Internal"
        self.FM = nc.dram_tensor("fm", [NFM, SEQ], BF16, kind=kind).ap()
        self.VT = nc.dram_tensor("vt", [SEQ, NVT], BF16, kind=kind).ap()
        self.YS = nc.dram_tensor("ys", [NYS, SEQ], BF16, kind=kind).ap()
        self.MP = [nc.dram_tensor("mp%d" % i, [256, SEQ], BF16, kind="Internal").ap() for i in range(8)]
        self.MG = [nc.dram_tensor("mgath%d" % i, [512, SEQ], BF16, kind="Internal").ap() for i in range(8)]
        self.XS = [nc.dram_tensor("xs%d" % i, [SEQ, D], F32, kind="Internal").ap() for i in range(2)]
        self.GG = nc.dram_tensor("gg", [DEPTH, D], F32, kind="Internal").ap()
        self.MODROW = nc.dram_tensor("modrow", [DEPTH, 3 * D], F32, kind="Internal").ap()

    def setup(self):
        s, es = self.s, self.es
        self.ident = s.sb(es, "ident", [128, 128], F32)
        self.prot = s.sb(es, "prot", [128, 128], F32)
        self.ones_b = s.sb(es, "ones_b", [128, 128], BF16)
        self.ones_f = s.sb(es, "ones_f", [128, 128], F32)
        self.ustr = s.sb(es, "ustr", [128, 128], BF16)
        self.uinc = s.sb(es, "uinc", [128, 128], BF16)
        self.ovl = s.sb(es, "ovl", [128, 2, 64], F32)
        self.eall = s.sb(es, "eall", [64, SEQ], BF16)
        self.modp = s.sb(es, "modp", [128, DEPTH, 4, 16], F32)
        self.psum = [s.ps(es, "pb%d" % i, [128, 512], F32) for i in range(8)]
        s.dma("sp", self.ident[:], self.k_ident, writes=[self.ident])
        s.dma("sp", self.prot[:], self.k_prot, writes=[self.prot])
        s.dma("sp", self.ovl[:], self.k_ovl, writes=[self.ovl])
        s.dma("pool", self.eall[:], self.k_eall, writes=[self.eall])
        s.op("dve", lambda e: e.memset(self.ones_b[:], 1.0), writes=[self.ones_b])
        s.op("dve", lambda e: e.memset(self.ones_f[:], 1.0), writes=[self.ones_f])
        s.op("pool", lambda e: e.memset(self.ustr[:], 1.0), writes=[self.ustr])
        s.op("pool", lambda e: e.memset(self.uinc[:], 1.0), writes=[self.uinc])
        s.op("pool", lambda e: e.affine_select(self.uinc[:], self.uinc[:], [[-1, 128]], ALU.is_ge, 0.0,
                                               base=0, channel_multiplier=1),
             reads=[self.uinc], writes=[self.uinc])
        s.op("pool", lambda e: e.affine_select(self.ustr[:], self.ustr[:], [[-1, 128]], ALU.is_gt, 0.0,
                                               base=0, channel_multiplier=1),
             reads=[self.ustr], writes=[self.ustr])

    def phase0(self):
        s = self.s
        with ExitStack() as es:
            cs = s.sb(es, "cs", [128, 16], F32)
            wts = Ring([s.sb(es, "wada%d" % i, [128, 16, 512], F32) for i in range(2)])
            tmp = s.sb(es, "p0tmp", [128, 48], F32)
            bada = s.sb(es, "bada", [128, 48], F32)
            gpre = s.sb(es, "gpre", [128, 16], F32)
            gpost = s.sb(es, "gpost", [128, 16], F32)
            s.dma("sp", cs[:], self.c_in, writes=[cs])
            s.op("act", lambda e: e.activation(cs[:], cs[:], AF.Silu), reads=[cs], writes=[cs])
            prow = Ring(self.psum[0:4])
            rowbuf = s.sb(es, "rowbuf", [1, 3 * D], F32)
            pmod = s.sb(es, "pmod", [128, 48], F32)
            modrow_t = T(self.MODROW, "modrow")
            for l in range(self.n_layers):
                s.dma("sp", bada[:], self.b_ada[l].rearrange("(j p) -> p j", p=128), writes=[bada],
                      allow_slow_non_contiguous=True)
                s.dma("sp", gpre[:], self.norm_pre_g[l].rearrange("(j p) -> p j", p=128), writes=[gpre],
                      allow_slow_non_contiguous=True)
                s.dma("sp", gpost[:], self.norm_post_g[l].rearrange("(j p) -> p j", p=128), writes=[gpost],
                      allow_slow_non_contiguous=True)
                for g in range(12):
                    wt = wts.next()
                    s.dma("sp", wt[:], self.w_ada[l, :, g * 512:(g + 1) * 512].rearrange("(j p) c -> p j c", p=128),
                          writes=[wt])
                    pg = prow.next()
                    for j in range(16):
                        s.op("pe", lambda e, wt=wt, j=j, pg=pg: e.matmul(
                            pg[0:1, :], cs[:, j:j + 1], wt[:, j, :], start=(j == 0), stop=(j == 15)),
                            reads=[wt, cs], writes=[pg])
                    s.op("act", lambda e, pg=pg, g=g: e.copy(rowbuf[0:1, g * 512:(g + 1) * 512], pg[0:1, :]),
                         reads=[pg], writes=[rowbuf])
                s.dma("sp", self.MODROW[l:l + 1, :], rowbuf[0:1, :], reads=[rowbuf], writes=[modrow_t])
                s.dma("sp", pmod[:], self.MODROW[l].rearrange("(j p) -> p j", p=128), reads=[modrow_t], writes=[pmod],
                      allow_slow_non_contiguous=True)
                s.op("dve", lambda e: e.tensor_tensor(tmp[:], pmod[:], bada[:], ALU.add),
                     reads=[pmod, bada], writes=[tmp])
                mp = self.modp
                s.op("dve", lambda e, l=l: e.scalar_tensor_tensor(mp[:, l, 0, :], tmp[:, 16:32], 1.0, gpre[:],
                                                                    ALU.add, ALU.mult),
                     reads=[tmp, gpre], writes=[mp])
                s.op("dve", lambda e, l=l: e.tensor_copy(mp[:, l, 1, :], tmp[:, 0:16]), reads=[tmp], writes=[mp])
                s.op("dve", lambda e, l=l: e.tensor_tensor(mp[:, l, 2, :], tmp[:, 32:48], gpost[:], ALU.mult),
                     reads=[tmp, gpost], writes=[mp])
                s.dma("sp", self.GG[l].rearrange("(j p) -> p j", p=128), mp[:, l, 2, :], reads=[mp],
                      allow_slow_non_contiguous=True)
            s.barrier()

    def phase12(self, l, half, xsrc):
        s = self.s
        T0 = half * 2048
        with ExitStack() as es:
            hT = [s.sb(es, "hT%d" % i, [128, 16, 512], BF16) for i in range(4)]
            xt = s.sb(es, "xt", [128, 4, 2048], F32)
            junk = s.sb(es, "junk", [128, 2048], BF16)
            ss = s.sb(es, "ss", [128, 4], F32)
            rstd = s.sb(es, "rstd", [128, 4], F32)
            ropec = s.sb(es, "ropec", [128, 2048], F32)
            ropes = s.sb(es, "ropes", [128, 2048], F32)
            wts = Ring([s.sb(es, "wt%d" % i, [128, 16, 512], BF16) for i in range(3)])
            qf = Ring([s.sb(es, "qf%d" % i, [128, 512], F32) for i in range(2)])
            t1 = Ring([s.sb(es, "t1%d" % i, [128, 512], F32) for i in range(2)])
            t2 = Ring([s.sb(es, "t2%d" % i, [128, 512], F32) for i in range(2)])
            stg = Ring([s.sb(es, "stg%d" % i, [128, 512], BF16) for i in range(4)])
            pacc = Ring(self.psum[0:4])
            prot_ps = Ring(self.psum[4:6])
            ptr = Ring(self.psum[6:8])
            mp = self.modp
            s.dma("sp", ropec[:], self.k_ropec[:, T0:T0 + 2048], writes=[ropec])
            s.dma("sp", ropes[:], self.k_ropes[:, T0:T0 + 2048], writes=[ropes])
            for ti in range(4):
                t0 = T0 + ti * 512
                s.dma("sp", xt[:], xsrc[t0:t0 + 512, :].rearrange("(b p) d -> p b d", p=128), writes=[xt])
                for b in range(4):
                    s.op("act", lambda e, b=b: e.activation(junk[:], xt[:, b, :], AF.Square,
                                                            accum_out=ss[:, b:b + 1]),
                         reads=[xt], writes=[junk, ss])
                s.op("act", lambda e: e.activation(rstd[:], ss[:], AF.Ln, scale=1.0 / D, bias=1e-6),
                     reads=[ss], writes=[rstd])
                s.op("act", lambda e: e.activation(rstd[:], rstd[:], AF.Exp, scale=-0.5),
                     reads=[rstd], writes=[rstd])
                for b in range(4):
                    s.op("dve", lambda e, b=b: e.tensor_scalar(xt[:, b, :], xt[:, b, :], rstd[:, b:b + 1], None,
                                                               ALU.mult),
                         reads=[xt, rstd], writes=[xt])
                for j in range(16):
                    pt = ptr.next()
                    for b in range(4):
                        s.op("pe", lambda e, b=b, j=j, pt=pt: e.transpose(
                            pt[:, b * 128:(b + 1) * 128], xt[:, b, j * 128:(j + 1) * 128], self.ident[:]),
                            reads=[xt, self.ident], writes=[pt])
                    s.op("dve", lambda e, j=j, pt=pt, ti=ti: e.tensor_scalar(
                        hT[ti][:, j, :], pt[:], mp[:, l, 0, j:j + 1], mp[:, l, 1, j:j + 1], ALU.mult, ALU.add),
                        reads=[pt, mp], writes=[hT[ti]])
            wq = []

            def issue_w(gi):
                if gi < len(GROUPS):
                    name_, off_, gw_, _ = GROUPS[gi]
                    c0_ = COL[name_] + off_
                    wt_ = wts.next()
                    s.dma("pool", wt_[:, :, 0:gw_], self.w_in[l, :, c0_:c0_ + gw_].rearrange("(j p) c -> p j c", p=128),
                          writes=[wt_])
                    wq.append(wt_)

            issue_w(0)
            issue_w(1)
            for gi, (name, off, gw, kind) in enumerate(GROUPS):
                issue_w(gi + 2)
                wt = wq[gi]
                if kind == "v":
                    vc0 = VTCOL[name] + off
                    for tb in range(16):
                        pa = pacc.next()
                        ti, bb = tb // 4, tb % 4
                        for j in range(16):
                            s.op("pe", lambda e, pa=pa, ti=ti, bb=bb, j=j, wt=wt: e.matmul(
                                pa[:, 0:gw], hT[ti][:, j, bb * 128:(bb + 1) * 128], wt[:, j, 0:gw],
                                start=(j == 0), stop=(j == 15)), reads=[hT[ti], wt], writes=[pa])
                        st = stg.next()
                        eng = "act" if tb % 2 == 0 else "dve"
                        if eng == "act":
                            s.op("act", lambda e, st=st, pa=pa: e.copy(st[:, 0:gw], pa[:, 0:gw]),
                                 reads=[pa], writes=[st])
                        else:
                            s.op("dve", lambda e, st=st, pa=pa: e.tensor_copy(st[:, 0:gw], pa[:, 0:gw]),
                                 reads=[pa], writes=[st])
                        tt = T0 + tb * 128
                        s.dma("sp", self.VT[tt:tt + 128, vc0:vc0 + gw], st[:, 0:gw], reads=[st])
                    continue
                nblk = (gw + 127) // 128
                pending = []
                for blk in range(nblk):
                    bw = min(128, gw - blk * 128)
                    r0 = FMROWS[name] + off + blk * 128
                    for ti in range(4):
                        t0 = T0 + ti * 512
                        tl = ti * 512
                        pa = pacc.next()
                        for j in range(16):
                            s.op("pe", lambda e, pa=pa, ti=ti, j=j, wt=wt, blk=blk, bw=bw: e.matmul(
                                pa[0:bw, :], wt[:, j, blk * 128:blk * 128 + bw], hT[ti][:, j, :],
                                start=(j == 0), stop=(j == 15)), reads=[hT[ti], wt], writes=[pa])
                        st = stg.next()
                        if kind == "fm":
                            s.op("dve", lambda e, st=st, pa=pa: e.tensor_copy(st[:], pa[:]), reads=[pa], writes=[st])
                        elif kind == "silu":
                            s.op("act", lambda e, st=st, pa=pa: e.activation(st[:], pa[:], AF.Silu),
                                 reads=[pa], writes=[st])
                        elif kind == "sig":
                            s.op("act", lambda e, st=st, pa=pa, bw=bw: e.activation(st[0:bw, :], pa[0:bw, :], AF.Sigmoid),
                                 reads=[pa], writes=[st])
                        elif kind == "rope":
                            q = qf.next()
                            s.op("act", lambda e, q=q, pa=pa: e.copy(q[:], pa[:]), reads=[pa], writes=[q])

                            def finish(q=q, st=st, tl=tl, r0=r0, t0=t0, bw=bw):
                                pr = prot_ps.next()
                                a1 = t1.next()
                                a2 = t2.next()
                                s.op("pe", lambda e: e.matmul(pr[:], self.prot[:], q[:], start=True, stop=True),
                                     reads=[q, self.prot], writes=[pr])
                                s.op("dve", lambda e: e.tensor_tensor(
                                    a1[:], q[:], ropec[:, tl:tl + 512], ALU.mult), reads=[q, ropec], writes=[a1])
                                s.op("dve", lambda e: e.tensor_tensor(
                                    a2[:], pr[:], ropes[:, tl:tl + 512], ALU.mult), reads=[pr, ropes], writes=[a2])
                                s.op("dve", lambda e: e.tensor_tensor(st[:], a1[:], a2[:], ALU.add),
                                     reads=[a1, a2], writes=[st])
                                s.dma("sp", self.FM[r0:r0 + bw, t0:t0 + 512], st[0:bw, :], reads=[st])

                            if pending:
                                pending.pop()()
                            pending.append(finish)
                            continue
                        s.dma("sp", self.FM[r0:r0 + bw, t0:t0 + 512], st[0:bw, :], reads=[st])
                if pending:
                    pending.pop()()
            s.barrier()

    def _causal(self, t, k0, t0, npart=128):
        self.s.op("pool", lambda e: e.affine_select(t[0:npart, :], t[0:npart, :], [[1, 512]], ALU.is_ge, 0.0,
                                                    base=t0 - k0, channel_multiplier=-1),
                  reads=[t], writes=[t])

    def _softmax_attn(self, qT, blocks, p_ring, ps_ring, psum_sum, psum_o, scale=SCALE):
        s = self.s
        nb = len(blocks)

        def stage_b(bi, blk, p):
            s.op("pe", lambda e: e.matmul(psum_sum[:], self.ones_b[:], p[:], start=(bi == 0), stop=(bi == nb - 1)),
                 reads=[self.ones_b, p], writes=[psum_sum])
            for oi, (v_t, v_ap) in enumerate(blk["v"]):
                po = psum_o[oi]
                s.op("pe", lambda e, po=po, v_ap=v_ap: e.matmul(po[:], v_ap, p[:], start=(bi == 0), stop=(bi == nb - 1)),
                     reads=[v_t, p], writes=[po])

        prev = None
        for bi, blk in enumerate(blocks):
            ps = ps_ring.next()
            kT_t, kT_ap = blk["kT"]
            bias = blk.get("bias")
            s.op("pe", lambda e: e.matmul(ps[:], kT_ap, qT[:], start=True, stop=(bias is None)),
                 reads=[kT_t, qT], writes=[ps])
            if bias is not None:
                bl_t, bl_ap, br_t, br_ap = bias
                s.op("pe", lambda e: e.matmul(ps[:], bl_ap, br_ap, start=False, stop=True),
                     reads=[bl_t, br_t], writes=[ps])
            p = p_ring.next()
            s.op("act", lambda e: e.activation(p[:], ps[:], AF.Exp, scale=scale), reads=[ps], writes=[p])
            if blk.get("mask") is not None:
                blk["mask"](p)
            if prev is not None:
                stage_b(*prev)
            prev = (bi, blk, p)
        stage_b(*prev)

    def phase3_diff(self, l, hs):
        s = self.s
        lam_init = 0.8 - 0.6 * math.exp(-0.3 * l)
        with ExitStack() as es:
            kT = [Ring([s.sb(es, "akT%d_%d" % (c, i), [128, SEQ], BF16) for i in range(2)]) for c in range(2)]
            vv = Ring([s.sb(es, "avv%d" % i, [128, NKB, 256], BF16) for i in range(2)])
            qTs = Ring([s.sb(es, "aq%d" % i, [128, 512], BF16) for i in range(4)])
            szs = Ring([s.sb(es, "asz%d" % i, [128, 512], BF16) for i in range(4)])
            p_ring = Ring([s.sb(es, "ap%d" % i, [128, 512], BF16) for i in range(4)])
            oc = [[s.sb(es, "aoc%d%d" % (c, h), [128, 512], F32) for h in range(2)] for c in range(2)]
            rs = s.sb(es, "ars", [128, 512], F32)
            sq = [s.sb(es, "asq%d" % h, [128, 512], F32) for h in range(2)]
            rstd = s.sb(es, "arstd", [128, 512], F32)
            yst = Ring([s.sb(es, "ayst%d" % i, [128, 512], BF16) for i in range(2)])
            lamt = s.sb(es, "lamt", [128, 4], F32)
            lam2 = s.sb(es, "lam2", [128, 2], F32)
            neglam = s.sb(es, "neglam", [128, 1], F32)
            gco = s.sb(es, "gco", [128, 2], F32)
            ps_ring = Ring([self.psum[0], self.psum[1], self.psum[3]])
            psum_sum = self.psum[2]
            psum_o = self.psum[4:6]
            pmisc = self.psum[6]
            s.dma("sp", lamt[:], self.lam_in[l].rearrange("k p -> p k"), writes=[lamt], allow_slow_non_contiguous=True)
            s.op("dve", lambda e: e.tensor_tensor(lam2[:, 0:1], lamt[:, 0:1], lamt[:, 1:2], ALU.mult), reads=[lamt], writes=[lam2])
            s.op("dve", lambda e: e.tensor_tensor(lam2[:, 1:2], lamt[:, 2:3], lamt[:, 3:4], ALU.mult), reads=[lamt], writes=[lam2])
            s.op("pe", lambda e: e.matmul(pmisc[:, 0:2], self.ones_f[:], lam2[:], start=True, stop=True),
                 reads=[self.ones_f, lam2], writes=[pmisc])
            s.op("act", lambda e: e.activation(lam2[:], pmisc[:, 0:2], AF.Exp), reads=[pmisc], writes=[lam2])
            s.op("dve", lambda e: e.scalar_tensor_tensor(neglam[:], lam2[:, 1:2], -lam_init, lam2[:, 0:1], ALU.add, ALU.subtract),
                 reads=[lam2], writes=[neglam])
            s.dma("sp", gco[:], self.diff_norm_g[l].rearrange("(h p) -> p h", p=128), writes=[gco], allow_slow_non_contiguous=True)
            s.op("dve", lambda e: e.tensor_scalar(gco[:], gco[:], 1.0 - lam_init, None, ALU.mult), reads=[gco], writes=[gco])
            for h in range(2):
                kts = []
                for c in range(2):
                    kt = kT[c].next()
                    r0 = FMROWS["ak"] + h * 256 + c * 128
                    s.dma("sp", kt[:], self.FM[r0:r0 + 128, :], writes=[kt])
                    kts.append(kt)
                v = vv.next()
                vc = VTCOL["av"] + h * 256
                s.dma("sp", v[:], self.VT[:, vc:vc + 256].rearrange("(kb p) e -> p kb e", p=128), writes=[v])
                def load_q(i_, c_):
                    q_ = qTs.next()
                    r0_ = FMROWS["aq"] + h * 256 + c_ * 128
                    s.dma("sp", q_[:], self.FM[r0_:r0_ + 128, i_ * 512:i_ * 512 + 512], writes=[q_])
                    return q_

                steps = [(i_, c_) for i_ in range(NQT) for c_ in range(2)]
                q_next = load_q(*steps[0])
                for i in range(NQT):
                    t0 = i * 512
                    szt = []
                    for hf in range(2):
                        sz = szs.next()
                        rz = FMROWS["az"] + h * 256 + hf * 128
                        s.dma("sp", sz[:], self.FM[rz:rz + 128, t0:t0 + 512], writes=[sz])
                        szt.append(sz)
                    for c in range(2):
                        q = q_next
                        si = steps.index((i, c))
                        if si + 1 < len(steps):
                            q_next = load_q(*steps[si + 1])
                        blocks = []
                        for kb in range(4 * i + 4):
                            k0 = kb * 128
                            blk = dict(kT=(kts[c], kts[c][:, k0:k0 + 128]),
                                       v=[(v, v[:, kb, 0:128]), (v, v[:, kb, 128:256])])
                            if kb >= 4 * i:
                                blk["mask"] = (lambda p, k0=k0, t0=t0: self._causal(p, k0, t0))
                            blocks.append(blk)
                        self._softmax_attn(q, blocks, p_ring, ps_ring, psum_sum, psum_o)
                        s.op("dve", lambda e: e.reciprocal(rs[:], psum_sum[:]), reads=[psum_sum], writes=[rs])
                        for hf in range(2):
                            s.op("dve", lambda e, c=c, hf=hf: e.tensor_tensor(oc[c][hf][:], psum_o[hf][:], rs[:], ALU.mult),
                                 reads=[psum_o[hf], rs], writes=[oc[c][hf]])
                    for hf in range(2):
                        s.op("dve", lambda e, hf=hf: e.scalar_tensor_tensor(
                            oc[0][hf][:], oc[1][hf][:], neglam[:, 0:1], oc[0][hf][:], ALU.mult, ALU.add),
                            reads=[oc[1][hf], neglam, oc[0][hf]], writes=[oc[0][hf]])
                        s.op("act", lambda e, hf=hf: e.activation(sq[hf][:], oc[0][hf][:], AF.Square),
                             reads=[oc[0][hf]], writes=[sq[hf]])
                    for hf in range(2):
                        s.op("pe", lambda e, hf=hf: e.matmul(pmisc[:], self.ones_f[:], sq[hf][:], start=(hf == 0), stop=(hf == 1)),
                             reads=[self.ones_f, sq[hf]], writes=[pmisc])
                    s.op("act", lambda e: e.activation(rstd[:], pmisc[:], AF.Ln, scale=1.0 / 256, bias=1e-5),
                         reads=[pmisc], writes=[rstd])
                    s.op("act", lambda e: e.activation(rstd[:], rstd[:], AF.Exp, scale=-0.5), reads=[rstd], writes=[rstd])
                    for hf in range(2):
                        sz = szt[hf]
                        s.op("dve", lambda e, hf=hf: e.scalar_tensor_tensor(
                            oc[0][hf][:], oc[0][hf][:], gco[:, hf:hf + 1], rstd[:], ALU.mult, ALU.mult),
                            reads=[oc[0][hf], gco, rstd], writes=[oc[0][hf]])
                        y = yst.next()
                        s.op("dve", lambda e, hf=hf, y=y, sz=sz: e.tensor_tensor(y[:], oc[0][hf][:], sz[:], ALU.mult),
                             reads=[oc[0][hf], sz], writes=[y])
                        ry = YSROW["a"] + h * 256 + hf * 128
                        s.dma("pool", self.YS[ry:ry + 128, t0:t0 + 512], y[:], reads=[y])
            s.barrier()

    def phase3_sb(self, l, hs):
        s = self.s
        with ExitStack() as es:
            kTr = Ring([s.sb(es, "ckT%d" % i, [128, SEQ], BF16) for i in range(2)])
            knr = Ring([s.sb(es, "ckn%d" % i, [128, SEQ], BF16) for i in range(2)])
            vvr = Ring([s.sb(es, "cvv%d" % i, [128, NKB, 128], BF16) for i in range(2)])
            qTs = Ring([s.sb(es, "cq%d" % i, [128, 512], BF16) for i in range(3)])
            szs = Ring([s.sb(es, "csz%d" % i, [128, 512], BF16) for i in range(3)])
            er = Ring([s.sb(es, "ce%d" % i, [128, 512], F32) for i in range(4)])
            lbr = Ring([s.sb(es, "clb%d" % i, [128, 512], BF16) for i in range(5)])
            lsr = Ring([s.sb(es, "cls%d" % i, [128, 512], BF16) for i in range(3)])
            ar = Ring([s.sb(es, "ca%d" % i, [128, 512], BF16) for i in range(4)])
            yst = Ring([s.sb(es, "cyst%d" % i, [128, 512], BF16) for i in range(2)])
            zr = Ring(self.psum[0:2])
            cr = Ring(self.psum[2:5])
            po = self.psum[6]
            for h in range(4):
                kt = kTr.next()
                r0 = FMROWS["ck"] + h * 128
                s.dma("sp", kt[:], self.FM[r0:r0 + 128, :], writes=[kt])
                kn = knr.next()
                s.op("act", lambda e: e.mul(kn[:], kt[:], -SCALE), reads=[kt], writes=[kn])
                v = vvr.next()
                vc = VTCOL["cv"] + h * 128
                s.dma("sp", v[:], self.VT[:, vc:vc + 128].rearrange("(kb p) e -> p kb e", p=128), writes=[v])

                def load_qz(i_):
                    q_ = qTs.next()
                    rq = FMROWS["cq"] + h * 128
                    s.dma("sp", q_[:], self.FM[rq:rq + 128, i_ * 512:i_ * 512 + 512], writes=[q_])
                    sz_ = szs.next()
                    rz = FMROWS["cz"] + h * 128
                    s.dma("sp", sz_[:], self.FM[rz:rz + 128, i_ * 512:i_ * 512 + 512], writes=[sz_])
                    return q_, sz_

                qz_next = load_qz(0)
                for i in range(NQT):
                    t0 = i * 512
                    q, sz = qz_next
                    if i + 1 < NQT:
                        qz_next = load_qz(i + 1)
                    kbs = list(range(4 * i + 3, -1, -1))
                    nk = len(kbs)
                    state = {"ls": None}

                    def z_mm(kb):
                        k0 = kb * 128
                        zp = zr.next()
                        s.op("pe", lambda e: e.matmul(zp[:], kt[:, k0:k0 + 128], q[:], start=True, stop=True),
                             reads=[kt, q], writes=[zp])
                        return zp

                    def softplus(kb, zp):
                        k0 = kb * 128
                        ee = er.next()
                        s.op("act", lambda e: e.activation(ee[:], zp[:], AF.Exp, scale=SCALE), reads=[zp], writes=[ee])
                        if kb >= 4 * i:
                            s.op("pool", lambda e: e.affine_select(
                                ee[:], ee[:], [[1, 512]], ALU.is_gt, 0.0, base=t0 - k0, channel_multiplier=-1),
                                reads=[ee], writes=[ee])
                        lb = lbr.next()
                        s.op("act", lambda e: e.activation(lb[:], ee[:], AF.Ln, bias=1.0), reads=[ee], writes=[lb])
                        return lb

                    def cum_bank(bi, kb, lb):
                        k0 = kb * 128
                        cp = cr.next()
                        ls = state["ls"]
                        s.op("pe", lambda e: e.matmul(cp[:], kn[:, k0:k0 + 128], q[:], start=True, stop=False),
                             reads=[kn, q], writes=[cp])
                        s.op("pe", lambda e: e.matmul(cp[:], self.uinc[:], lb[:], start=False, stop=(ls is None)),
                             reads=[self.uinc, lb], writes=[cp])
                        if ls is not None:
                            s.op("pe", lambda e: e.matmul(cp[:], self.ones_b[:], ls[:], start=False, stop=True),
                                 reads=[self.ones_b, ls], writes=[cp])
                        if bi + 1 < nk:
                            if ls is None:
                                state["ls"] = lb
                            else:
                                ln_ = lsr.next()
                                s.op("dve", lambda e: e.tensor_tensor(ln_[:], ls[:], lb[:], ALU.add), reads=[ls, lb], writes=[ln_])
                                state["ls"] = ln_
                        return cp

                    def final_exp(kb, cp):
                        k0 = kb * 128
                        a = ar.next()
                        s.op("act", lambda e: e.activation(a[:], cp[:], AF.Exp, scale=-1.0), reads=[cp], writes=[a])
                        if kb >= 4 * i:
                            s.op("pool", lambda e: e.affine_select(
                                a[:], a[:], [[1, 512]], ALU.is_gt, 0.0, base=t0 - k0, channel_multiplier=-1),
                                reads=[a], writes=[a])
                        return a

                    def pv(bi, kb, a):
                        s.op("pe", lambda e: e.matmul(po[:], v[:, kb, :], a[:], start=(bi == 0), stop=(bi == nk - 1)),
                             reads=[v, a], writes=[po])

                    lbs, cps, avs = {}, {}, {}
                    for j in range(min(2, nk)):
                        lbs[j] = softplus(kbs[j], z_mm(kbs[j]))
                    cps[0] = cum_bank(0, kbs[0], lbs[0])
                    for bi, kb in enumerate(kbs):
                        avs[bi] = final_exp(kb, cps.pop(bi))
                        if bi >= 1:
                            pv(bi - 1, kbs[bi - 1], avs.pop(bi - 1))
                        if bi + 2 < nk:
                            lbs[bi + 2] = softplus(kbs[bi + 2], z_mm(kbs[bi + 2]))
                        if bi + 1 < nk:
                            cps[bi + 1] = cum_bank(bi + 1, kbs[bi + 1], lbs[bi + 1])
                    pv(nk - 1, kbs[nk - 1], avs.pop(nk - 1))
                    y = yst.next()
                    s.op("dve", lambda e, y=y, sz=sz: e.tensor_tensor(y[:], po[:], sz[:], ALU.mult), reads=[po, sz], writes=[y])
                    ry = YSROW["c"] + h * 128
                    s.dma("sp", self.YS[ry:ry + 128, t0:t0 + 512], y[:], reads=[y])
            s.barrier()

    def phase3_nsa(self, l, g):
        s = self.s
        with ExitStack() as es:
            big = [s.sb(es, "bbig%d" % i, [128, SEQ], BF16) for i in range(4)]
            vs = s.sb(es, "bvs", [128, NKB, 128], BF16)
            vw = s.sb(es, "bvw", [128, NKB, 128], BF16)
            w1 = s.sb(es, "bw1", [128, 32, 128], BF16)
            w2 = s.sb(es, "bw2", [128, 128], BF16)
            pe_sb = s.sb(es, "bpe", [32, 128], F32)
            peT = s.sb(es, "bpeT", [128, 32], BF16)
            c1 = s.sb(es, "bc1", [128, 1], F32)
            hsl = s.sb(es, "bhsl", [128, 256], BF16)
            kcmpT = s.sb(es, "bkcmpT", [128, 256], BF16)
            vcmp = s.sb(es, "bvcmp", [128, 2, 128], BF16)
            qsets = [[s.sb(es, "bq%d_%d" % (k, i), [128, 512], BF16) for i in range(4)] for k in range(2)]
            gts = Ring([s.sb(es, "bgt%d" % i, [128, 3, 512], BF16) for i in range(2)])
            szs = Ring([s.sb(es, "bsz%d" % i, [128, 512], BF16) for i in range(2)])
            pf = [s.sb(es, "bpf%d" % i, [128, 512], F32) for i in range(2)]
            pnb = Ring([s.sb(es, "bpnb%d" % i, [128, 512], BF16) for i in range(2)])
            rs = s.sb(es, "brs", [128, 512], F32)
            ocmp = [s.sb(es, "bocmp%d" % i, [128, 512], F32) for i in range(4)]
            impS = s.sb(es, "bimp", [128, 4, 64], F32)
            m8 = s.sb(es, "bm8", [128, 16], F32)
            wk = s.sb(es, "bwk", [128, 64], F32)
            sel = s.sb(es, "bsel", [128, 64], F32)
            negT = s.sb(es, "bnegT", [64, 512], BF16)
            p_ring = Ring([s.sb(es, "bp%d" % i, [128, 512], BF16) for i in range(4)])
            acc = s.sb(es, "bacc", [128, 512], F32)
            tmp = s.sb(es, "btmp", [128, 512], F32)
            yst = Ring([s.sb(es, "byst%d" % i, [128, 512], BF16) for i in range(2)])
            ps_ring = Ring([self.psum[0], self.psum[1], self.psum[7]])
            psum_sum = self.psum[2]
            pimp = self.psum[3]
            psum_o = [self.psum[4]]
            pmisc = self.psum[5]
            pmisc2 = self.psum[6]
            kcT, vcT, ksT, kwT = big
            for t_, nm in ((kcT, "bkc"), (vcT, "bvc"), (ksT, "bks"), (kwT, "bkw")):
                r0 = FMROWS[nm] + g * 128
                s.dma("sp", t_[:], self.FM[r0:r0 + 128, :], writes=[t_])
            for t_, nm in ((vs, "bvs"), (vw, "bvw")):
                vc = VTCOL[nm] + g * 128
                s.dma("sp", t_[:], self.VT[:, vc:vc + 128].rearrange("(kb p) e -> p kb e", p=128), writes=[t_])
            for kv in range(2):
                src = kcT if kv == 0 else vcT
                s.dma("pool", w1[:], self.cmp_w1[kv][l].rearrange("(l d) f -> d l f", d=128), writes=[w1])
                s.dma("pool", w2[:], self.cmp_w2[kv][l], writes=[w2])
                s.dma("sp", pe_sb[:], self.cmp_pe[kv][l], writes=[pe_sb])
                s.op("pe", lambda e: e.transpose(pmisc[:, 0:32], pe_sb[:], self.ident[0:32, 0:32]),
                     reads=[pe_sb, self.ident], writes=[pmisc])
                s.op("dve", lambda e: e.tensor_copy(peT[:], pmisc[:, 0:32]), reads=[pmisc], writes=[peT])
                for li in range(32):
                    s.op("pe", lambda e, li=li: e.matmul(pmisc2[:, 0:1], w1[:, li, :], peT[:, li:li + 1],
                                                         start=(li == 0), stop=(li == 31)),
                         reads=[w1, peT], writes=[pmisc2])
                s.op("dve", lambda e: e.tensor_copy(c1[:], pmisc2[:, 0:1]), reads=[pmisc2], writes=[c1])
                for li in range(32):
                    s.op("pe", lambda e, li=li, src=src: e.matmul(pmisc[:, 0:NCMP], w1[:, li, :],
                                                                   src[:, li:li + 16 * (NCMP - 1) + 1:16],
                                                                   start=(li == 0), stop=(li == 31)),
                         reads=[w1, src], writes=[pmisc])
                s.op("dve", lambda e: e.memset(hsl[:], 0.0), writes=[hsl])
                s.op("act", lambda e: e.activation(hsl[:, 0:NCMP], pmisc[:, 0:NCMP], AF.Silu, bias=c1[:, 0:1]),
                     reads=[pmisc, c1], writes=[hsl])
                if kv == 0:
                    s.op("pe", lambda e: e.matmul(pmisc2[:, 0:256], w2[:], hsl[:], start=True, stop=True),
                         reads=[w2, hsl], writes=[pmisc2])
                    s.op("dve", lambda e: e.tensor_copy(kcmpT[:], pmisc2[:, 0:256]), reads=[pmisc2], writes=[kcmpT])
                else:
                    for nb in range(2):
                        s.op("pe", lambda e, nb=nb: e.matmul(pmisc2[:, nb * 128:(nb + 1) * 128], hsl[:, nb * 128:(nb + 1) * 128],
                                                             w2[:], start=True, stop=True),
                             reads=[w2, hsl], writes=[pmisc2])
                    s.op("dve", lambda e: e.tensor_copy(vcmp[:], pmisc2[:, 0:256].rearrange("p (n d) -> p n d", n=2)),
                         reads=[pmisc2], writes=[vcmp])
            def load_qs(i_):
                qs_ = qsets[i_ % 2]
                for r_ in range(4):
                    rq = FMROWS["bq"] + (g * 4 + r_) * 128
                    s.dma("sp", qs_[r_][:], self.FM[rq:rq + 128, i_ * 512:i_ * 512 + 512], writes=[qs_[r_]])
                return qs_

            load_qs(0)
            for i in range(NQT):
                t0 = i * 512
                nbs = [nb for nb in range(2) if 16 * nb * 128 + 31 <= t0 + 511]
                qTs = qsets[i % 2]
                if i + 1 < NQT:
                    load_qs(i + 1)
                for r in range(4):
                    q = qTs[r]
                    for nb in nbs:
                        ps = ps_ring.next()
                        s.op("pe", lambda e, ps=ps, nb=nb, q=q: e.matmul(ps[:], kcmpT[:, nb * 128:(nb + 1) * 128], q[:],
                                                                          start=True, stop=True),
                             reads=[kcmpT, q], writes=[ps])
                        s.op("act", lambda e, ps=ps, nb=nb: e.activation(pf[nb][:], ps[:], AF.Exp, scale=SCALE),
                             reads=[ps], writes=[pf[nb]])
                        s.op("pool", lambda e, nb=nb, t0=t0: e.affine_select(
                            pf[nb][:], pf[nb][:], [[1, 512]], ALU.is_ge, 0.0,
                            base=t0 - 16 * nb * 128 - 31, channel_multiplier=-16), reads=[pf[nb]], writes=[pf[nb]])
                        s.op("pe", lambda e, nb=nb: e.matmul(psum_sum[:], self.ones_f[:], pf[nb][:], start=(nb == nbs[0]),
                                                             stop=(nb == nbs[-1])),
                             reads=[self.ones_f, pf[nb]], writes=[psum_sum])
                    s.op("dve", lambda e: e.tensor_scalar(rs[:], psum_sum[:], 1e-30, None, ALU.max), reads=[psum_sum], writes=[rs])
                    s.op("dve", lambda e: e.reciprocal(rs[:], rs[:]), reads=[rs], writes=[rs])
                    for nb in nbs:
                        s.op("dve", lambda e, nb=nb: e.tensor_tensor(pf[nb][:], pf[nb][:], rs[:], ALU.mult),
                             reads=[pf[nb], rs], writes=[pf[nb]])
                        for tb in range(4):
                            first = (r == 0 and nb == nbs[0])
                            last = (r == 3 and nb == nbs[-1])
                            s.op("pe", lambda e, nb=nb, tb=tb, first=first, last=last: e.matmul(
                                pimp[:, tb * 64:(tb + 1) * 64], pf[nb][:, tb * 128:(tb + 1) * 128], self.ovl[:, nb, :],
                                start=first, stop=last), reads=[pf[nb], self.ovl], writes=[pimp])
                        pb = pnb.next()
                        s.op("act", lambda e, pb=pb, nb=nb: e.copy(pb[:], pf[nb][:]), reads=[pf[nb]], writes=[pb])
                        s.op("pe", lambda e, pb=pb, nb=nb: e.matmul(psum_o[0][:], vcmp[:, nb, :], pb[:], start=(nb == nbs[0]),
                                                                     stop=(nb == nbs[-1])),
                             reads=[vcmp, pb], writes=[psum_o[0]])
                    s.op("dve", lambda e, r=r: e.tensor_copy(ocmp[r][:], psum_o[0][:]), reads=[psum_o[0]], writes=[ocmp[r]])
                s.op("dve", lambda e: e.tensor_copy(impS[:], pimp[:, 0:256].rearrange("p (a b) -> p a b", a=4)),
                     reads=[pimp], writes=[impS])
                for tb in range(4):
                    for hh in range(2):
                        tblk = 8 * i + 2 * tb + hh
                        p0 = hh * 64
                        if tblk < 63:
                            s.op("pool", lambda e, tb=tb, p0=p0, tblk=tblk: e.memset(impS[p0:p0 + 64, tb, tblk + 1:64], -1e30),
                                 reads=[impS], writes=[impS])
                        s.op("pool", lambda e, tb=tb, p0=p0, tblk=tblk: e.memset(impS[p0:p0 + 64, tb, tblk:tblk + 1], 1e6),
                             reads=[impS], writes=[impS])
                        s.op("pool", lambda e, tb=tb, p0=p0: e.memset(impS[p0:p0 + 64, tb, 0:1], 2e6),
                             reads=[impS], writes=[impS])
                for tb in range(4):
                    s.op("dve", lambda e, tb=tb: e.max(out=m8[:, 0:8], in_=impS[:, tb, :]), reads=[impS], writes=[m8])
                    s.op("dve", lambda e, tb=tb: e.match_replace(out=wk[:], in_to_replace=m8[:, 0:8], in_values=impS[:, tb, :],
                                                                 imm_value=-3e30), reads=[impS, m8], writes=[wk])
                    s.op("dve", lambda e: e.max(out=m8[:, 8:16], in_=wk[:]), reads=[wk], writes=[m8])
                    s.op("dve", lambda e, tb=tb: e.tensor_scalar(sel[:], impS[:, tb, :], m8[:, 15:16], None, ALU.is_ge),
                         reads=[impS, m8], writes=[sel])
                    s.op("dve", lambda e: e.tensor_scalar(sel[:], sel[:], -1.0, BIG, ALU.add, ALU.mult), reads=[sel], writes=[sel])
                    s.op("pe", lambda e: e.transpose(pmisc[0:64, 0:128], sel[:], self.ident[:]),
                         reads=[sel, self.ident], writes=[pmisc])
                    s.op("act", lambda e, tb=tb: e.copy(negT[:, tb * 128:(tb + 1) * 128], pmisc[0:64, 0:128]),
                         reads=[pmisc], writes=[negT])
                for r in range(4):
                    h = g * 4 + r
                    q = qTs[r]
                    gt = gts.next()
                    for k3 in range(3):
                        rg = FMROWS["bg"] + h * 3 + k3
                        s.dma("sp", gt[:, k3, :], self.FM[rg:rg + 1, t0:t0 + 512].to_broadcast([128, 512]), writes=[gt])
                    sz = szs.next()
                    rz = FMROWS["bz"] + h * 128
                    s.dma("sp", sz[:], self.FM[rz:rz + 128, t0:t0 + 512], writes=[sz])
                    s.op("dve", lambda e, r=r, gt=gt: e.tensor_tensor(acc[:], ocmp[r][:], gt[:, 0, :], ALU.mult),
                         reads=[ocmp[r], gt], writes=[acc])
                    blocks = []
                    for kb in range(4 * i + 4):
                        k0 = kb * 128
                        blk = dict(kT=(ksT, ksT[:, k0:k0 + 128]), v=[(vs, vs[:, kb, :])],
                                   bias=(self.eall, self.eall[:, k0:k0 + 128], negT, negT[:]))
                        if kb >= 4 * i:
                            blk["mask"] = (lambda p, k0=k0, t0=t0: self._causal(p, k0, t0))
                        blocks.append(blk)
                    self._softmax_attn(q, blocks, p_ring, ps_ring, psum_sum, psum_o)
                    s.op("dve", lambda e: e.reciprocal(rs[:], psum_sum[:]), reads=[psum_sum], writes=[rs])
                    s.op("dve", lambda e: e.tensor_tensor(tmp[:], psum_o[0][:], rs[:], ALU.mult), reads=[psum_o[0], rs], writes=[tmp])
                    s.op("dve", lambda e, gt=gt: e.tensor_tensor(tmp[:], tmp[:], gt[:, 1, :], ALU.mult), reads=[tmp, gt], writes=[tmp])
                    s.op("dve", lambda e: e.tensor_tensor(acc[:], acc[:], tmp[:], ALU.add), reads=[acc, tmp], writes=[acc])
                    blocks = []
                    for kb in range(max(0, 4 * i - 4), 4 * i + 4):
                        k0 = kb * 128
                        blk = dict(kT=(kwT, kwT[:, k0:k0 + 128]), v=[(vw, vw[:, kb, :])])
                        if kb >= 4 * i:
                            blk["mask"] = (lambda p, k0=k0, t0=t0: self._causal(p, k0, t0))
                        else:
                            blk["mask"] = (lambda p, k0=k0, t0=t0: s.op("pool", lambda e: e.affine_select(
                                p[:], p[:], [[-1, 512]], ALU.is_gt, 0.0, base=k0 - t0 + 512, channel_multiplier=1),
                                reads=[p], writes=[p]))
                        blocks.append(blk)
                    self._softmax_attn(q, blocks, p_ring, ps_ring, psum_sum, psum_o)
                    s.op("dve", lambda e: e.reciprocal(rs[:], psum_sum[:]), reads=[psum_sum], writes=[rs])
                    s.op("dve", lambda e: e.tensor_tensor(tmp[:], psum_o[0][:], rs[:], ALU.mult), reads=[psum_o[0], rs], writes=[tmp])
                    s.op("dve", lambda e, gt=gt: e.tensor_tensor(tmp[:], tmp[:], gt[:, 2, :], ALU.mult), reads=[tmp, gt], writes=[tmp])
                    s.op("dve", lambda e: e.tensor_tensor(acc[:], acc[:], tmp[:], ALU.add), reads=[acc, tmp], writes=[acc])
                    y = yst.next()
                    s.op("dve", lambda e, y=y, sz=sz: e.tensor_tensor(y[:], acc[:], sz[:], ALU.mult), reads=[acc, sz], writes=[y])
                    ry = YSROW["b"] + h * 128
                    s.dma("pool", self.YS[ry:ry + 128, t0:t0 + 512], y[:], reads=[y])
            s.barrier()

    def phase4a(self, l):
        s = self.s
        for tp in range(4):
            with ExitStack() as es2:
                ys = [s.sb(es2, "ysb%d" % n, [128, 4, 1024], BF16) for n in range(3)]
                wbr = Ring([s.sb(es2, "wb%d" % i, [128, 4, 512], BF16) for i in range(6)])
                mgr = Ring([s.sb(es2, "mg%d" % i, [128, 1024], BF16) for i in range(4)])
                macc = [s.sb(es2, "macc%d" % i, [128, 512], F32) for i in range(2)]
                tm = Ring([s.sb(es2, "tm%d" % i, [128, 512], F32) for i in range(3)])
                stg = Ring([s.sb(es2, "mstg%d" % i, [128, 512], BF16) for i in range(3)])
                pacc = Ring(self.psum[0:8])
                TP = tp * 1024
                for n in range(3):
                    s.dma("sp", ys[n][:], self.YS[n * 512:(n + 1) * 512, TP:TP + 1024].rearrange("(k p) t -> p k t", p=128),
                          writes=[ys[n]])
                for cg in range(4):
                    wbs = []
                    for n in range(3):
                        wb = wbr.next()
                        s.dma("pool", wb[:], self.w_branch[l, n, :, cg * 512:(cg + 1) * 512].rearrange("(k p) c -> p k c", p=128),
                              writes=[wb])
                        wbs.append(wb)
                    for cb in range(4):
                        cc = cg * 4 + cb
                        for n in range(3):
                            mg = mgr.next()
                            rm = FMROWS["mg"] + n * 2048 + cc * 128
                            s.dma("sp", mg[:], self.FM[rm:rm + 128, TP:TP + 1024], writes=[mg])
                            for tl in range(2):
                                pa = pacc.next()
                                for k in range(4):
                                    s.op("pe", lambda e, pa=pa, n=n, k=k, tl=tl, cb=cb: e.matmul(
                                        pa[:], wbs[n][:, k, cb * 128:(cb + 1) * 128], ys[n][:, k, tl * 512:(tl + 1) * 512],
                                        start=(k == 0), stop=(k == 3)), reads=[wbs[n], ys[n]], writes=[pa])
                                if n == 0:
                                    s.op("dve", lambda e, pa=pa, tl=tl, mg=mg: e.tensor_tensor(
                                        macc[tl][:], pa[:], mg[:, tl * 512:(tl + 1) * 512], ALU.mult),
                                        reads=[pa, mg], writes=[macc[tl]])
                                else:
                                    t_ = tm.next()
                                    s.op("dve", lambda e, pa=pa, tl=tl, mg=mg, t_=t_: e.tensor_tensor(
                                        t_[:], pa[:], mg[:, tl * 512:(tl + 1) * 512], ALU.mult),
                                        reads=[pa, mg], writes=[t_])
                                    if n == 1:
                                        s.op("dve", lambda e, tl=tl, t_=t_: e.tensor_tensor(macc[tl][:], macc[tl][:], t_[:], ALU.add),
                                             reads=[macc[tl], t_], writes=[macc[tl]])
                                    else:
                                        st = stg.next()
                                        s.op("dve", lambda e, tl=tl, t_=t_, st=st: e.tensor_tensor(
                                            st[:], macc[tl][:], t_[:], ALU.add),
                                            reads=[macc[tl], t_], writes=[st])
                                        tt = TP + tl * 512
                                        s.dma("pool", self.MP[cc // 2][(cc % 2) * 128:(cc % 2 + 1) * 128, tt:tt + 512], st[:], reads=[st])
                s.barrier()

    def load_wo(self, es, l):
        s = self.s
        wo = s.sb(es, "wo", [128, 16, 2048], BF16)
        for k4 in range(4):
            s.dma("pool", wo[:, k4 * 4:(k4 + 1) * 4, :],
                  self.w_out[l, k4 * 512:(k4 + 1) * 512, :].rearrange("(k p) c -> p k c", p=128), writes=[wo])
        return wo

    def phase4b(self, l, xsrc, xdst, wo):
        s = self.s
        with ExitStack() as es3:
            ggr = s.sb(es3, "ggr", [128, 2048], F32)
            m0 = s.sb(es3, "m0", [128, 16, 512], BF16)
            m1 = s.sb(es3, "m1", [128, 16, 512], BF16)
            mTr = Ring([s.sb(es3, "mT%d" % i, [128, 16, 512], BF16) for i in range(2)])
            xr = Ring([s.sb(es3, "xr%d" % i, [128, 2048], F32) for i in range(3)])
            yr = Ring([s.sb(es3, "yr%d" % i, [128, 2048], F32) for i in range(2)])
            junk = s.sb(es3, "junk4", [128, 512], BF16)
            ss = Ring([s.sb(es3, "ss4%d" % i, [128, 4], F32) for i in range(2)])
            rstd = Ring([s.sb(es3, "rstd4%d" % i, [128, 1], F32) for i in range(2)])
            s.dma("sp", ggr[:], self.GG[l:l + 1, :].to_broadcast([128, 2048]), writes=[ggr])

            def load_m(ti_):
                t0_ = ti_ * 512
                for c8 in range(8):
                    for rk, mm in ((0, m0), (1, m1)):
                        s.dma("sp", mm[:, 2 * c8:2 * c8 + 2, :],
                              self.MG[c8][rk * 256:(rk + 1) * 256, t0_:t0_ + 512].rearrange("(q p) t -> p q t", p=128), writes=[mm])

            def load_x(tb_):
                x_ = xr.next()
                s.dma("sp", x_[:], xsrc[tb_ * 128:(tb_ + 1) * 128, :], writes=[x_])
                return x_

            load_m(0)
            x_next = load_x(0)
            for ti in range(NQT):
                mT = mTr.next()
                for hk in range(2):
                    eng = "dve"
                    s.op(eng, lambda e, hk=hk, mT=mT: e.tensor_tensor(
                        mT[:, hk * 8:(hk + 1) * 8, :], m0[:, hk * 8:(hk + 1) * 8, :], m1[:, hk * 8:(hk + 1) * 8, :], ALU.add),
                        reads=[m0, m1], writes=[mT])
                if ti + 1 < NQT:
                    load_m(ti + 1)
                for bb in range(4):
                    tb = ti * 4 + bb
                    tt = tb * 128
                    x = x_next
                    if tb + 1 < 4 * NQT:
                        x_next = load_x(tb + 1)
                    half_banks = self.psum[0:4] if tb % 2 == 0 else self.psum[4:8]
                    sst = ss.next()
                    for cb in range(4):
                        pa = half_banks[cb]
                        for k in range(16):
                            s.op("pe", lambda e, pa=pa, k=k, bb=bb, cb=cb, mT=mT: e.matmul(
                                pa[:], mT[:, k, bb * 128:(bb + 1) * 128], wo[:, k, cb * 512:(cb + 1) * 512],
                                start=(k == 0), stop=(k == 15)), reads=[mT, wo], writes=[pa])
                        s.op("act", lambda e, pa=pa, cb=cb, sst=sst: e.activation(junk[:], pa[:], AF.Square,
                                                                                 accum_out=sst[:, cb:cb + 1]),
                             reads=[pa], writes=[junk, sst])
                    rt = rstd.next()
                    s.op("dve", lambda e, sst=sst, rt=rt: e.tensor_reduce(rt[:], sst[:], mybir.AxisListType.X, ALU.add),
                         reads=[sst], writes=[rt])
                    s.op("act", lambda e, rt=rt: e.activation(rt[:], rt[:], AF.Ln, scale=1.0 / D, bias=1e-6), reads=[rt], writes=[rt])
                    s.op("act", lambda e, rt=rt: e.activation(rt[:], rt[:], AF.Exp, scale=-0.5), reads=[rt], writes=[rt])
                    y = yr.next()
                    for cb in range(4):
                        pa = half_banks[cb]
                        s.op("dve", lambda e, pa=pa, cb=cb, y=y, rt=rt: e.scalar_tensor_tensor(
                            y[:, cb * 512:(cb + 1) * 512], pa[:], rt[:, 0:1], ggr[:, cb * 512:(cb + 1) * 512], ALU.mult, ALU.mult),
                            reads=[pa, rt, ggr], writes=[y])
                    s.op("dve", lambda e, y=y, x=x: e.tensor_tensor(y[:], y[:], x[:], ALU.add), reads=[y, x], writes=[y])
                    s.dma("pool", xdst[tt:tt + 128, :], y[:], reads=[y])
            s.barrier()

    def build(self, phases=("0", "12", "3a", "3b", "3c", "4")):
        self.declare()
        self.setup()
        if "0" in phases:
            self.phase0()
        for l in range(self.n_layers):
            xsrc = self.x_in if l == 0 else self.XS[(l - 1) % 2]
            xdst = self.out if l == self.n_layers - 1 else self.XS[l % 2]
            if "12" in phases:
                for half in self.halves:
                    self.phase12(l, half, xsrc)
            if "3a" in phases:
                self.phase3_diff(l, 0)
            if "3b" in phases:
                self.phase3_nsa(l, 0)
            if "3c" in phases:
                self.phase3_sb(l, 0)
            if "4" in phases:
                with ExitStack() as es4:
                    wo = self.load_wo(es4, l)
                    self.phase4a(l)
                    for c8 in range(8):
                        self.s.coll("AllGather", [[0, 1], [2, 3], [4, 5], [6, 7]], self.MP[c8], self.MG[c8])
                    self.s.barrier()
                    self.phase4b(l, xsrc, xdst, wo)
        self.s.barrier()
        self.es.close()
        return self.nc


def make_in_maps(inputs, n_cores=8):
    k = _constants()
    f = lambda a: np.ascontiguousarray(np.asarray(a, dtype=np.float32))
    lam = np.stack([f(inputs["lambda_q1"]), f(inputs["lambda_k1"]), f(inputs["lambda_q2"]), f(inputs["lambda_k2"])], axis=1)
    shared = dict(
        norm_pre_g=f(inputs["norm_pre_g"]), norm_post_g=f(inputs["norm_post_g"]), w_ada=f(inputs["w_ada"]),
        b_ada=f(inputs["b_ada"]), lam=np.ascontiguousarray(lam),
        diff_norm_g=f(inputs["diff_norm_g"]), cmp_pe_k=f(inputs["cmp_pe_k"]), cmp_pe_v=f(inputs["cmp_pe_v"]),
        cmp_w1_k=f(inputs["cmp_w1_k"]), cmp_w1_v=f(inputs["cmp_w1_v"]), cmp_w2_k=f(inputs["cmp_w2_k"]),
        cmp_w2_v=f(inputs["cmp_w2_v"]), w_out=f(inputs["w_out"]),
        k_ident=k["ident"], k_prot=k["prot"], k_ropec=k["ropec"], k_ropes=k["ropes"], k_ovl=k["ovl"], k_eall=k["eall"])
    w_in = f(inputs["w_in"])
    w_br = f(inputs["w_branch"])
    per_hs = []
    for hs in range(2):
        per_hs.append(dict(w_in=np.ascontiguousarray(w_in[:, :, local_cols(hs)]),
                           w_branch=np.ascontiguousarray(w_br[:, :, hs * 512:(hs + 1) * 512, :])))
    x = f(inputs["x"])
    c = f(inputs["c"])
    maps = []
    for core in range(n_cores):
        b, hs = core // 2, core % 2
        m = dict(shared)
        m.update(per_hs[hs])
        m["x"] = np.ascontiguousarray(x[b])
        m["c"] = np.ascontiguousarray(c[b].reshape(16, 128).T)
        maps.append(m)
    return maps


def kernel(**inputs):
    n_cores = 8
    nc = Builder().build()
    maps = make_in_maps(inputs, n_cores)
    res = run_bass_kernel_spmd(nc, maps, core_ids=list(range(n_cores)))
    return np.stack([np.asarray(res.results[2 * b]["out"]) for b in range(4)], axis=0).astype(np.float32)
```

```python
import math
from contextlib import ExitStack

import numpy as np
import concourse.bass as bass
import concourse.mybir as mybir
from concourse.bass_utils import run_bass_kernel_spmd

F32 = mybir.dt.float32
BF16 = mybir.dt.bfloat16
AF = mybir.ActivationFunctionType
ALU = mybir.AluOpType

D = 2048
SEQ = 4096
DEPTH = 4
HD = 128
N_IN = 17944
NQT = SEQ // 512
NKB = SEQ // 128
SCALE = HD ** -0.5
NCMP = 255
BIG = 30000.0

COLG = dict(aq=0, ak=1024, av=2048, az=3072, bq=4096, bkc=5120, bvc=5376, bks=5632, bvs=5888,
            bkw=6144, bvw=6400, bg=6656, bz=6680, cq=7704, ck=8728, cv=9752, cz=10776, mg=11800)
LOCAL = (("aq", 512), ("ak", 512), ("av", 512), ("az", 512), ("bq", 512), ("bkc", 128), ("bvc", 128),
         ("bks", 128), ("bvs", 128), ("bkw", 128), ("bvw", 128), ("bg", 12), ("bz", 512),
         ("cq", 512), ("ck", 512), ("cv", 512), ("cz", 512), ("mg", 6144))
COL = {}
_c = 0
for _n, _w in LOCAL:
    COL[_n] = _c
    _c += _w
NLOC = _c


def local_cols(hs):
    idx = []
    for n, w in LOCAL:
        g0 = COLG[n] + (0 if n == "mg" else hs * w)
        idx.append(np.arange(g0, g0 + w))
    return np.concatenate(idx)


FMROWS = {}
_r = 0
for _n, _w in (("aq", 512), ("ak", 512), ("az", 512), ("bq", 512), ("bkc", 128), ("bvc", 128),
               ("bks", 128), ("bkw", 128), ("bg", 128), ("bz", 512), ("cq", 512), ("ck", 512),
               ("cz", 512), ("mg", 6144)):
    FMROWS[_n] = _r
    _r += _w
NFM = _r
VTCOL = dict(av=0, bvs=512, bvw=640, cv=768)
NVT = 1280
NYS = 1536
YSROW = dict(a=0, b=512, c=1024)
GROUPS = [("aq", 0, 512, "rope"), ("ak", 0, 512, "rope"), ("av", 0, 512, "v"), ("az", 0, 512, "silu"),
          ("bq", 0, 512, "rope"), ("bkc", 0, 128, "rope"), ("bvc", 0, 128, "fm"), ("bks", 0, 128, "rope"),
          ("bvs", 0, 128, "v"), ("bkw", 0, 128, "rope"), ("bvw", 0, 128, "v"), ("bg", 0, 12, "sig"),
          ("bz", 0, 512, "silu"), ("cq", 0, 512, "fm"), ("ck", 0, 512, "fm"), ("cv", 0, 512, "v"),
          ("cz", 0, 512, "silu")]
GROUPS += [("mg", 512 * _i, 512, "sig") for _i in range(12)]


class T:
    def __init__(self, ap, name=""):
        self.ap = ap
        self.name = name
        self.lw = None
        self.rd = []

    def __getitem__(self, idx):
        return self.ap[idx]


class Sched:
    ENG = ("pe", "act", "dve", "pool", "sp")

    def __init__(self, nc, es, n_dma_sems=6):
        self.nc = nc
        self.es = es
        self.eng = {"pe": nc.tensor, "act": nc.scalar, "dve": nc.vector, "pool": nc.gpsimd, "sp": nc.sync}
        self.sem = {}
        self.cnt = {}
        for e in self.ENG:
            self.sem[e] = es.enter_context(nc.semaphore("s_" + e))
            self.cnt[e] = 0
        self.dsem = {}
        for q in ("sp", "pool", "act"):
            lst = []
            for i in range(n_dma_sems):
                k = "d_%s%d" % (q, i)
                self.sem[k] = es.enter_context(nc.semaphore(k))
                self.cnt[k] = 0
                lst.append(k)
            self.dsem[q] = [lst, 0]
        self.waited = {}
        self.n_ins = 0

    def sb(self, es, name, shape, dt):
        self.uid = getattr(self, "uid", 0) + 1
        name = "%s_u%d" % (name, self.uid)
        return T(es.enter_context(self.nc.sbuf_tensor(name, list(shape), dt)), name)

    def ps(self, es, name, shape, dt=F32):
        return T(es.enter_context(self.nc.psum_tensor(name, list(shape), dt)), name)

    def _wait(self, e, key, val):
        if val <= 0 or self.waited.get((e, key), 0) >= val:
            return
        self.eng[e].wait_ge(self.sem[key], val)
        self.waited[(e, key)] = val

    def _deps(self, e, reads, writes):
        for r in reads:
            if r.lw is not None:
                self._wait(e, *r.lw)
        for w in writes:
            if w.lw is not None and w.lw[0] != e:
                self._wait(e, *w.lw)
            for (k, v) in w.rd:
                if k != e:
                    self._wait(e, k, v)

    def _mark(self, key, val, reads, writes):
        for w in writes:
            w.lw = (key, val)
            w.rd = []
        for r in reads:
            if r in writes:
                continue
            r.rd.append((key, val))
            if len(r.rd) > 16:
                d = {}
                for (k, v) in r.rd:
                    d[k] = max(d.get(k, 0), v)
                r.rd = list(d.items())

    def op(self, e, fn, reads=(), writes=()):
        self._deps(e, reads, writes)
        ins = fn(self.eng[e])
        self.cnt[e] += 1
        ins.then_inc(self.sem[e], 1)
        self._mark(e, self.cnt[e], reads, writes)
        self.n_ins += 1
        return ins

    def dma(self, q, out, in_, reads=(), writes=(), **kw):
        lst, i = self.dsem[q]
        k = lst[i % len(lst)]
        self.dsem[q][1] = i + 1
        self._wait(q, k, self.cnt[k])
        self._deps(q, reads, writes)
        ins = self.eng[q].dma_start(out=out, in_=in_, **kw)
        self.cnt[k] += 16
        ins.then_inc(self.sem[k], 16)
        self._mark(k, self.cnt[k], reads, writes)
        self.n_ins += 1
        return ins

    def coll(self, kind, groups, src, dst, op=None):
        if "cc" not in self.sem:
            self.sem["cc"] = self.es.enter_context(self.nc.semaphore("s_cc"))
            self.cnt["cc"] = 0
        ins = self.nc.gpsimd.collective_compute(kind, op if op is not None else ALU.bypass, replica_groups=groups,
                                                ins=[src.opt()], outs=[dst.opt()])
        self.cnt["cc"] += 1
        ins.then_inc(self.sem["cc"])
        self._wait("pool", "cc", self.cnt["cc"])
        self.n_ins += 1
        return ins

    def barrier(self, label=None):
        if not hasattr(self, "marks"):
            self.marks = []
        self.marks.append((label, self.cnt["pe"]))
        for e in self.ENG:
            for k in self.sem:
                if k != e:
                    self._wait(e, k, self.cnt[k])


class Ring:
    def __init__(self, items):
        self.items = items
        self.i = 0

    def next(self):
        t = self.items[self.i % len(self.items)]
        self.i += 1
        return t


def _constants():
    c = {}
    c["ident"] = np.eye(128, dtype=np.float32)
    prot = np.zeros((128, 128), np.float32)
    for i in range(16):
        prot[i + 16, i] = -1.0
        prot[i, i + 16] = 1.0
    c["prot"] = prot
    pos = np.arange(SEQ, dtype=np.float32)
    inv = (np.float32(500000.0) ** (-np.arange(0, 32, 2, dtype=np.float32) / np.float32(32))).astype(np.float32)
    ang = (pos[None, :] * inv[:, None]).astype(np.float32)
    ct = np.ones((128, SEQ), np.float32)
    st = np.zeros((128, SEQ), np.float32)
    ct[0:16] = np.cos(ang); ct[16:32] = np.cos(ang)
    st[0:16] = np.sin(ang); st[16:32] = np.sin(ang)
    c["ropec"] = ct
    c["ropes"] = st
    n = np.arange(256)[:, None]
    j = np.arange(64)[None, :]
    ov = ((16 * n < 64 * j + 64) & (16 * n + 32 > 64 * j) & (n < NCMP)).astype(np.float32)
    c["ovl"] = ov.reshape(2, 128, 64).transpose(1, 0, 2).copy()
    k = np.arange(SEQ)[None, :]
    c["eall"] = (np.arange(64)[:, None] == (k // 64)).astype(np.float32)
    return c


class Builder:
    def __init__(self, n_layers=DEPTH, halves=(0, 1), headsets=(0, 1), dbg=False):
        self.n_layers = n_layers
        self.halves = halves
        self.headsets = headsets
        self.dbg = dbg
        self.nc = bass.Bass("TRN2", target_bir_lowering=False)
        self.es = ExitStack()
        self.s = Sched(self.nc, self.es)

    def declare(self):
        nc = self.nc
        ein = lambda name, shape: nc.dram_tensor(name, list(shape), F32, kind="ExternalInput").ap()
        self.x_in = ein("x", [SEQ, D])
        self.c_in = ein("c", [128, 16])
        self.norm_pre_g = ein("norm_pre_g", [DEPTH, D])
        self.norm_post_g = ein("norm_post_g", [DEPTH, D])
        self.w_ada = ein("w_ada", [DEPTH, D, 3 * D])
        self.b_ada = ein("b_ada", [DEPTH, 3 * D])
        self.w_in = ein("w_in", [DEPTH, D, NLOC])
        self.lam_in = ein("lam", [DEPTH, 4, 128])
        self.diff_norm_g = ein("diff_norm_g", [DEPTH, 256])
        self.cmp_pe = [ein("cmp_pe_k", [DEPTH, 32, 128]), ein("cmp_pe_v", [DEPTH, 32, 128])]
        self.cmp_w1 = [ein("cmp_w1_k", [DEPTH, 4096, 128]), ein("cmp_w1_v", [DEPTH, 4096, 128])]
        self.cmp_w2 = [ein("cmp_w2_k", [DEPTH, 128, 128]), ein("cmp_w2_v", [DEPTH, 128, 128])]
        self.w_branch = ein("w_branch", [DEPTH, 3, 512, D])
        self.w_out = ein("w_out", [DEPTH, D, D])
        self.k_ident = ein("k_ident", [128, 128])
        self.k_prot = ein("k_prot", [128, 128])
        self.k_ropec = ein("k_ropec", [128, SEQ])
        self.k_ropes = ein("k_ropes", [128, SEQ])
        self.k_ovl = ein("k_ovl", [128, 2, 64])
        self.k_eall = ein("k_eall", [64, SEQ])
        self.out = nc.dram_tensor("out", [SEQ, D], F32, kind="ExternalOutput").ap()
        kind = "ExternalOutput" if self.dbg else "Internal"
        self.FM = nc.dram_tensor("fm", [NFM, SEQ], BF16, kind=kind).ap()
        self.VT = nc.dram_tensor("vt", [SEQ, NVT], BF16, kind=kind).ap()
        self.YS = nc.dram_tensor("ys", [NYS, SEQ], BF16, kind=kind).ap()
        self.MP = [nc.dram_tensor("mp%d" % i, [256, SEQ], BF16, kind="Internal").ap() for i in range(8)]
        self.MG = [nc.dram_tensor("mgath%d" % i, [512, SEQ], BF16, kind="Internal").ap() for i in range(8)]
        self.XS = [nc.dram_tensor("xs%d" % i, [SEQ, D], F32, kind="Internal").ap() for i in range(2)]
        self.GG = nc.dram_tensor("gg", [DEPTH, D], F32, kind="Internal").ap()
        self.MODROW = nc.dram_tensor("modrow", [DEPTH, 3 * D], F32, kind="Internal").ap()

    def setup(self):
        s, es = self.s, self.es
        self.ident = s.sb(es, "ident", [128, 128], F32)
        self.prot = s.sb(es, "prot", [128, 128], F32)
        self.ones_b = s.sb(es, "ones_b", [128, 128], BF16)
        self.ones_f = s.sb(es, "ones_f", [128, 128], F32)
        self.ustr = s.sb(es, "ustr", [128, 128], BF16)
        self.uinc = s.sb(es, "uinc", [128, 128], BF16)
        self.ovl = s.sb(es, "ovl", [128, 2, 64], F32)
        self.eall = s.sb(es, "eall", [64, SEQ], BF16)
        self.modp = s.sb(es, "modp", [128, DEPTH, 4, 16], F32)
        self.psum = [s.ps(es, "pb%d" % i, [128, 512], F32) for i in range(8)]
        s.dma("sp", self.ident[:], self.k_ident, writes=[self.ident])
        s.dma("sp", self.prot[:], self.k_prot, writes=[self.prot])
        s.dma("sp", self.ovl[:], self.k_ovl, writes=[self.ovl])
        s.dma("pool", self.eall[:], self.k_eall, writes=[self.eall])
        s.op("dve", lambda e: e.memset(self.ones_b[:], 1.0), writes=[self.ones_b])
        s.op("dve", lambda e: e.memset(self.ones_f[:], 1.0), writes=[self.ones_f])
        s.op("pool", lambda e: e.memset(self.ustr[:], 1.0), writes=[self.ustr])
        s.op("pool", lambda e: e.memset(self.uinc[:], 1.0), writes=[self.uinc])
        s.op("pool", lambda e: e.affine_select(self.uinc[:], self.uinc[:], [[-1, 128]], ALU.is_ge, 0.0,
                                               base=0, channel_multiplier=1),
             reads=[self.uinc], writes=[self.uinc])
        s.op("pool", lambda e: e.affine_select(self.ustr[:], self.ustr[:], [[-1, 128]], ALU.is_gt, 0.0,
                                               base=0, channel_multiplier=1),
             reads=[self.ustr], writes=[self.ustr])

    def phase0(self):
        s = self.s
        with ExitStack() as es:
            cs = s.sb(es, "cs", [128, 16], F32)
            wts = Ring([s.sb(es, "wada%d" % i, [128, 16, 512], F32) for i in range(2)])
            tmp = s.sb(es, "p0tmp", [128, 48], F32)
            bada = s.sb(es, "bada", [128, 48], F32)
            gpre = s.sb(es, "gpre", [128, 16], F32)
            gpost = s.sb(es, "gpost", [128, 16], F32)
            s.dma("sp", cs[:], self.c_in, writes=[cs])
            s.op("act", lambda e: e.activation(cs[:], cs[:], AF.Silu), reads=[cs], writes=[cs])
            prow = Ring(self.psum[0:4])
            rowbuf = s.sb(es, "rowbuf", [1, 3 * D], F32)
            pmod = s.sb(es, "pmod", [128, 48], F32)
            modrow_t = T(self.MODROW, "modrow")
            for l in range(self.n_layers):
                s.dma("sp", bada[:], self.b_ada[l].rearrange("(j p) -> p j", p=128), writes=[bada],
                      allow_slow_non_contiguous=True)
                s.dma("sp", gpre[:], self.norm_pre_g[l].rearrange("(j p) -> p j", p=128), writes=[gpre],
                      allow_slow_non_contiguous=True)
                s.dma("sp", gpost[:], self.norm_post_g[l].rearrange("(j p) -> p j", p=128), writes=[gpost],
                      allow_slow_non_contiguous=True)
                for g in range(12):
                    wt = wts.next()
                    s.dma("sp", wt[:], self.w_ada[l, :, g * 512:(g + 1) * 512].rearrange("(j p) c -> p j c", p=128),
                          writes=[wt])
                    pg = prow.next()
                    for j in range(16):
                        s.op("pe", lambda e, wt=wt, j=j, pg=pg: e.matmul(
                            pg[0:1, :], cs[:, j:j + 1], wt[:, j, :], start=(j == 0), stop=(j == 15)),
                            reads=[wt, cs], writes=[pg])
                    s.op("act", lambda e, pg=pg, g=g: e.copy(rowbuf[0:1, g * 512:(g + 1) * 512], pg[0:1, :]),
                         reads=[pg], writes=[rowbuf])
                s.dma("sp", self.MODROW[l:l + 1, :], rowbuf[0:1, :], reads=[rowbuf], writes=[modrow_t])
                s.dma("sp", pmod[:], self.MODROW[l].rearrange("(j p) -> p j", p=128), reads=[modrow_t], writes=[pmod],
                      allow_slow_non_contiguous=True)
                s.op("dve", lambda e: e.tensor_tensor(tmp[:], pmod[:], bada[:], ALU.add),
                     reads=[pmod, bada], writes=[tmp])
                mp = self.modp
                s.op("dve", lambda e, l=l: e.scalar_tensor_tensor(mp[:, l, 0, :], tmp[:, 16:32], 1.0, gpre[:],
                                                                    ALU.add, ALU.mult),
                     reads=[tmp, gpre], writes=[mp])
                s.op("dve", lambda e, l=l: e.tensor_copy(mp[:, l, 1, :], tmp[:, 0:16]), reads=[tmp], writes=[mp])
                s.op("dve", lambda e, l=l: e.tensor_tensor(mp[:, l, 2, :], tmp[:, 32:48], gpost[:], ALU.mult),
                     reads=[tmp, gpost], writes=[mp])
                s.dma("sp", self.GG[l].rearrange("(j p) -> p j", p=128), mp[:, l, 2, :], reads=[mp],
                      allow_slow_non_contiguous=True)
            s.barrier()

    def phase12(self, l, half, xsrc):
        s = self.s
        T0 = half * 2048
        with ExitStack() as es:
            hT = [s.sb(es, "hT%d" % i, [128, 16, 512], BF16) for i in range(4)]
            xt = s.sb(es, "xt", [128, 4, 2048], F32)
            junk = s.sb(es, "junk", [128, 2048], BF16)
            ss = s.sb(es, "ss", [128, 4], F32)
            rstd = s.sb(es, "rstd", [128, 4], F32)
            ropec = s.sb(es, "ropec", [128, 2048], F32)
            ropes = s.sb(es, "ropes", [128, 2048], F32)
            wts = Ring([s.sb(es, "wt%d" % i, [128, 16, 512], BF16) for i in range(3)])
            qf = Ring([s.sb(es, "qf%d" % i, [128, 512], F32) for i in range(2)])
            t1 = Ring([s.sb(es, "t1%d" % i, [128, 512], F32) for i in range(2)])
            t2 = Ring([s.sb(es, "t2%d" % i, [128, 512], F32) for i in range(2)])
            stg = Ring([s.sb(es, "stg%d" % i, [128, 512], BF16) for i in range(4)])
            pacc = Ring(self.psum[0:4])
            prot_ps = Ring(self.psum[4:6])
            ptr = Ring(self.psum[6:8])
            mp = self.modp
            s.dma("sp", ropec[:], self.k_ropec[:, T0:T0 + 2048], writes=[ropec])
            s.dma("sp", ropes[:], self.k_ropes[:, T0:T0 + 2048], writes=[ropes])
            for ti in range(4):
                t0 = T0 + ti * 512
                s.dma("sp", xt[:], xsrc[t0:t0 + 512, :].rearrange("(b p) d -> p b d", p=128), writes=[xt])
                for b in range(4):
                    s.op("act", lambda e, b=b: e.activation(junk[:], xt[:, b, :], AF.Square,
                                                            accum_out=ss[:, b:b + 1]),
                         reads=[xt], writes=[junk, ss])
                s.op("act", lambda e: e.activation(rstd[:], ss[:], AF.Ln, scale=1.0 / D, bias=1e-6),
                     reads=[ss], writes=[rstd])
                s.op("act", lambda e: e.activation(rstd[:], rstd[:], AF.Exp, scale=-0.5),
                     reads=[rstd], writes=[rstd])
                for b in range(4):
                    s.op("dve", lambda e, b=b: e.tensor_scalar(xt[:, b, :], xt[:, b, :], rstd[:, b:b + 1], None,
                                                               ALU.mult),
                         reads=[xt, rstd], writes=[xt])
                for j in range(16):
                    pt = ptr.next()
                    for b in range(4):
                        s.op("pe", lambda e, b=b, j=j, pt=pt: e.transpose(
                            pt[:, b * 128:(b + 1) * 128], xt[:, b, j * 128:(j + 1) * 128], self.ident[:]),
                            reads=[xt, self.ident], writes=[pt])
                    s.op("dve", lambda e, j=j, pt=pt, ti=ti: e.tensor_scalar(
                        hT[ti][:, j, :], pt[:], mp[:, l, 0, j:j + 1], mp[:, l, 1, j:j + 1], ALU.mult, ALU.add),
                        reads=[pt, mp], writes=[hT[ti]])
            wq = []

            def issue_w(gi):
                if gi < len(GROUPS):
                    name_, off_, gw_, _ = GROUPS[gi]
                    c0_ = COL[name_] + off_
                    wt_ = wts.next()
                    s.dma("pool", wt_[:, :, 0:gw_], self.w_in[l, :, c0_:c0_ + gw_].rearrange("(j p) c -> p j c", p=128),
                          writes=[wt_])
                    wq.append(wt_)

            issue_w(0)
            issue_w(1)
            for gi, (name, off, gw, kind) in enumerate(GROUPS):
                issue_w(gi + 2)
                wt = wq[gi]
                if kind == "v":
                    vc0 = VTCOL[name] + off
                    for tb in range(16):
                        pa = pacc.next()
                        ti, bb = tb // 4, tb % 4
                        for j in range(16):
                            s.op("pe", lambda e, pa=pa, ti=ti, bb=bb, j=j, wt=wt: e.matmul(
                                pa[:, 0:gw], hT[ti][:, j, bb * 128:(bb + 1) * 128], wt[:, j, 0:gw],
                                start=(j == 0), stop=(j == 15)), reads=[hT[ti], wt], writes=[pa])
                        st = stg.next()
                        eng = "act" if tb % 2 == 0 else "dve"
                        if eng == "act":
                            s.op("act", lambda e, st=st, pa=pa: e.copy(st[:, 0:gw], pa[:, 0:gw]),
                                 reads=[pa], writes=[st])
                        else:
                            s.op("dve", lambda e, st=st, pa=pa: e.tensor_copy(st[:, 0:gw], pa[:, 0:gw]),
                                 reads=[pa], writes=[st])
                        tt = T0 + tb * 128
                        s.dma("sp", self.VT[tt:tt + 128, vc0:vc0 + gw], st[:, 0:gw], reads=[st])
                    continue
                nblk = (gw + 127) // 128
                pending = []
                for blk in range(nblk):
                    bw = min(128, gw - blk * 128)
                    r0 = FMROWS[name] + off + blk * 128
                    for ti in range(4):
                        t0 = T0 + ti * 512
                        tl = ti * 512
                        pa = pacc.next()
                        for j in range(16):
                            s.op("pe", lambda e, pa=pa, ti=ti, j=j, wt=wt, blk=blk, bw=bw: e.matmul(
                                pa[0:bw, :], wt[:, j, blk * 128:blk * 128 + bw], hT[ti][:, j, :],
                                start=(j == 0), stop=(j == 15)), reads=[hT[ti], wt], writes=[pa])
                        st = stg.next()
                        if kind == "fm":
                            s.op("dve", lambda e, st=st, pa=pa: e.tensor_copy(st[:], pa[:]), reads=[pa], writes=[st])
                        elif kind == "silu":
                            s.op("act", lambda e, st=st, pa=pa: e.activation(st[:], pa[:], AF.Silu),
                                 reads=[pa], writes=[st])
                        elif kind == "sig":
                            s.op("act", lambda e, st=st, pa=pa, bw=bw: e.activation(st[0:bw, :], pa[0:bw, :], AF.Sigmoid),
                                 reads=[pa], writes=[st])
                        elif kind == "rope":
                            q = qf.next()
                            s.op("act", lambda e, q=q, pa=pa: e.copy(q[:], pa[:]), reads=[pa], writes=[q])

                            def finish(q=q, st=st, tl=tl, r0=r0, t0=t0, bw=bw):
                                pr = prot_ps.next()
                                a1 = t1.next()
                                a2 = t2.next()
                                s.op("pe", lambda e: e.matmul(pr[:], self.prot[:], q[:], start=True, stop=True),
                                     reads=[q, self.prot], writes=[pr])
                                s.op("dve", lambda e: e.tensor_tensor(
                                    a1[:], q[:], ropec[:, tl:tl + 512], ALU.mult), reads=[q, ropec], writes=[a1])
                                s.op("dve", lambda e: e.tensor_tensor(
                                    a2[:], pr[:], ropes[:, tl:tl + 512], ALU.mult), reads=[pr, ropes], writes=[a2])
                                s.op("dve", lambda e: e.tensor_tensor(st[:], a1[:], a2[:], ALU.add),
                                     reads=[a1, a2], writes=[st])
                                s.dma("sp", self.FM[r0:r0 + bw, t0:t0 + 512], st[0:bw, :], reads=[st])

                            if pending:
                                pending.pop()()
                            pending.append(finish)
                            continue
                        s.dma("sp", self.FM[r0:r0 + bw, t0:t0 + 512], st[0:bw, :], reads=[st])
                if pending:
                    pending.pop()()
            s.barrier()

    def _causal(self, t, k0, t0, npart=128):
        self.s.op("pool", lambda e: e.affine_select(t[0:npart, :], t[0:npart, :], [[1, 512]], ALU.is_ge, 0.0,
                                                    base=t0 - k0, channel_multiplier=-1),
                  reads=[t], writes=[t])

    def _softmax_attn(self, qT, blocks, p_ring, ps_ring, psum_sum, psum_o, scale=SCALE):
        s = self.s
        nb = len(blocks)

        def stage_b(bi, blk, p):
            s.op("pe", lambda e: e.matmul(psum_sum[:], self.ones_b[:], p[:], start=(bi == 0), stop=(bi == nb - 1)),
                 reads=[self.ones_b, p], writes=[psum_sum])
            for oi, (v_t, v_ap) in enumerate(blk["v"]):
                po = psum_o[oi]
                s.op("pe", lambda e, po=po, v_ap=v_ap: e.matmul(po[:], v_ap, p[:], start=(bi == 0), stop=(bi == nb - 1)),
                     reads=[v_t, p], writes=[po])

        prev = None
        for bi, blk in enumerate(blocks):
            ps = ps_ring.next()
            kT_t, kT_ap = blk["kT"]
            bias = blk.get("bias")
            s.op("pe", lambda e: e.matmul(ps[:], kT_ap, qT[:], start=True, stop=(bias is None)),
                 reads=[kT_t, qT], writes=[ps])
            if bias is not None:
                bl_t, bl_ap, br_t, br_ap = bias
                s.op("pe", lambda e: e.matmul(ps[:], bl_ap, br_ap, start=False, stop=True),
                     reads=[bl_t, br_t], writes=[ps])
            p = p_ring.next()
            s.op("act", lambda e: e.activation(p[:], ps[:], AF.Exp, scale=scale), reads=[ps], writes=[p])
            if blk.get("mask") is not None:
                blk["mask"](p)
            if prev is not None:
                stage_b(*prev)
            prev = (bi, blk, p)
        stage_b(*prev)

    def phase3_diff(self, l, hs):
        s = self.s
        lam_init = 0.8 - 0.6 * math.exp(-0.3 * l)
        with ExitStack() as es:
            kT = [Ring([s.sb(es, "akT%d_%d" % (c, i), [128, SEQ], BF16) for i in range(2)]) for c in range(2)]
            vv = Ring([s.sb(es, "avv%d" % i, [128, NKB, 256], BF16) for i in range(2)])
            qTs = Ring([s.sb(es, "aq%d" % i, [128, 512], BF16) for i in range(4)])
            szs = Ring([s.sb(es, "asz%d" % i, [128, 512], BF16) for i in range(4)])
            p_ring = Ring([s.sb(es, "ap%d" % i, [128, 512], BF16) for i in range(4)])
            oc = [[s.sb(es, "aoc%d%d" % (c, h), [128, 512], F32) for h in range(2)] for c in range(2)]
            rs = s.sb(es, "ars", [128, 512], F32)
            sq = [s.sb(es, "asq%d" % h, [128, 512], F32) for h in range(2)]
            rstd = s.sb(es, "arstd", [128, 512], F32)
            yst = Ring([s.sb(es, "ayst%d" % i, [128, 512], BF16) for i in range(2)])
            lamt = s.sb(es, "lamt", [128, 4], F32)
            lam2 = s.sb(es, "lam2", [128, 2], F32)
            neglam = s.sb(es, "neglam", [128, 1], F32)
            gco = s.sb(es, "gco", [128, 2], F32)
            ps_ring = Ring([self.psum[0], self.psum[1], self.psum[3]])
            sum_ring = Ring([self.psum[2], self.psum[7]])
            psum_o = self.psum[4:6]
            pmisc = self.psum[6]
            s.dma("sp", lamt[:], self.lam_in[l].rearrange("k p -> p k"), writes=[lamt], allow_slow_non_contiguous=True)
            s.op("dve", lambda e: e.tensor_tensor(lam2[:, 0:1], lamt[:, 0:1], lamt[:, 1:2], ALU.mult), reads=[lamt], writes=[lam2])
            s.op("dve", lambda e: e.tensor_tensor(lam2[:, 1:2], lamt[:, 2:3], lamt[:, 3:4], ALU.mult), reads=[lamt], writes=[lam2])
            s.op("pe", lambda e: e.matmul(pmisc[:, 0:2], self.ones_f[:], lam2[:], start=True, stop=True),
                 reads=[self.ones_f, lam2], writes=[pmisc])
            s.op("act", lambda e: e.activation(lam2[:], pmisc[:, 0:2], AF.Exp), reads=[pmisc], writes=[lam2])
            s.op("dve", lambda e: e.scalar_tensor_tensor(neglam[:], lam2[:, 1:2], -lam_init, lam2[:, 0:1], ALU.add, ALU.subtract),
                 reads=[lam2], writes=[neglam])
            s.dma("sp", gco[:], self.diff_norm_g[l].rearrange("(h p) -> p h", p=128), writes=[gco], allow_slow_non_contiguous=True)
            s.op("dve", lambda e: e.tensor_scalar(gco[:], gco[:], 1.0 - lam_init, None, ALU.mult), reads=[gco], writes=[gco])
            for h in range(2):
                kts = []
                for c in range(2):
                    kt = kT[c].next()
                    r0 = FMROWS["ak"] + h * 256 + c * 128
                    s.dma("sp", kt[:], self.FM[r0:r0 + 128, :], writes=[kt])
                    kts.append(kt)
                v = vv.next()
                vc = VTCOL["av"] + h * 256
                s.dma("sp", v[:], self.VT[:, vc:vc + 256].rearrange("(kb p) e -> p kb e", p=128), writes=[v])
                def load_q(i_, c_):
                    q_ = qTs.next()
                    r0_ = FMROWS["aq"] + h * 256 + c_ * 128
                    s.dma("sp", q_[:], self.FM[r0_:r0_ + 128, i_ * 512:i_ * 512 + 512], writes=[q_])
                    return q_

                steps = [(i_, c_) for i_ in range(NQT) for c_ in range(2)]
                q_next = load_q(*steps[0])
                for i in range(NQT):
                    t0 = i * 512
                    szt = []
                    for hf in range(2):
                        sz = szs.next()
                        rz = FMROWS["az"] + h * 256 + hf * 128
                        s.dma("sp", sz[:], self.FM[rz:rz + 128, t0:t0 + 512], writes=[sz])
                        szt.append(sz)
                    for c in range(2):
                        q = q_next
                        si = steps.index((i, c))
                        if si + 1 < len(steps):
                            q_next = load_q(*steps[si + 1])
                        blocks = []
                        for kb in range(4 * i + 4):
                            k0 = kb * 128
                            blk = dict(kT=(kts[c], kts[c][:, k0:k0 + 128]),
                                       v=[(v, v[:, kb, 0:128]), (v, v[:, kb, 128:256])])
                            if kb >= 4 * i:
                                blk["mask"] = (lambda p, k0=k0, t0=t0: self._causal(p, k0, t0))
                            blocks.append(blk)
                        psum_sum = sum_ring.next()
                        self._softmax_attn(q, blocks, p_ring, ps_ring, psum_sum, psum_o)
                        s.op("dve", lambda e, psum_sum=psum_sum: e.reciprocal(rs[:], psum_sum[:]), reads=[psum_sum], writes=[rs])
                        for hf in range(2):
                            s.op("dve", lambda e, c=c, hf=hf: e.tensor_tensor(oc[c][hf][:], psum_o[hf][:], rs[:], ALU.mult),
                                 reads=[psum_o[hf], rs], writes=[oc[c][hf]])
                    for hf in range(2):
                        s.op("dve", lambda e, hf=hf: e.scalar_tensor_tensor(
                            oc[0][hf][:], oc[1][hf][:], neglam[:, 0:1], oc[0][hf][:], ALU.mult, ALU.add),
                            reads=[oc[1][hf], neglam, oc[0][hf]], writes=[oc[0][hf]])
                        s.op("act", lambda e, hf=hf: e.activation(sq[hf][:], oc[0][hf][:], AF.Square),
                             reads=[oc[0][hf]], writes=[sq[hf]])
                    for hf in range(2):
                        s.op("pe", lambda e, hf=hf: e.matmul(pmisc[:], self.ones_f[:], sq[hf][:], start=(hf == 0), stop=(hf == 1)),
                             reads=[self.ones_f, sq[hf]], writes=[pmisc])
                    s.op("act", lambda e: e.activation(rstd[:], pmisc[:], AF.Ln, scale=1.0 / 256, bias=1e-5),
                         reads=[pmisc], writes=[rstd])
                    s.op("act", lambda e: e.activation(rstd[:], rstd[:], AF.Exp, scale=-0.5), reads=[rstd], writes=[rstd])
                    for hf in range(2):
                        sz = szt[hf]
                        s.op("dve", lambda e, hf=hf: e.scalar_tensor_tensor(
                            oc[0][hf][:], oc[0][hf][:], gco[:, hf:hf + 1], rstd[:], ALU.mult, ALU.mult),
                            reads=[oc[0][hf], gco, rstd], writes=[oc[0][hf]])
                        y = yst.next()
                        s.op("dve", lambda e, hf=hf, y=y, sz=sz: e.tensor_tensor(y[:], oc[0][hf][:], sz[:], ALU.mult),
                             reads=[oc[0][hf], sz], writes=[y])
                        ry = YSROW["a"] + h * 256 + hf * 128
                        s.dma("pool", self.YS[ry:ry + 128, t0:t0 + 512], y[:], reads=[y])
            s.barrier()

    def phase3_sb(self, l, hs):
        s = self.s
        with ExitStack() as es:
            kTr = Ring([s.sb(es, "ckT%d" % i, [128, SEQ], BF16) for i in range(2)])
            knr = Ring([s.sb(es, "ckn%d" % i, [128, SEQ], BF16) for i in range(2)])
            vvr = Ring([s.sb(es, "cvv%d" % i, [128, NKB, 128], BF16) for i in range(2)])
            qTs = Ring([s.sb(es, "cq%d" % i, [128, 512], BF16) for i in range(3)])
            szs = Ring([s.sb(es, "csz%d" % i, [128, 512], BF16) for i in range(3)])
            er = Ring([s.sb(es, "ce%d" % i, [128, 512], F32) for i in range(4)])
            lbr = Ring([s.sb(es, "clb%d" % i, [128, 512], BF16) for i in range(5)])
            lsr = Ring([s.sb(es, "cls%d" % i, [128, 512], BF16) for i in range(3)])
            ar = Ring([s.sb(es, "ca%d" % i, [128, 512], BF16) for i in range(4)])
            yst = Ring([s.sb(es, "cyst%d" % i, [128, 512], BF16) for i in range(2)])
            zr = Ring(self.psum[0:2])
            cr = Ring(self.psum[2:5])
            po = self.psum[6]
            for h in range(4):
                kt = kTr.next()
                r0 = FMROWS["ck"] + h * 128
                s.dma("sp", kt[:], self.FM[r0:r0 + 128, :], writes=[kt])
                kn = knr.next()
                s.op("act", lambda e: e.mul(kn[:], kt[:], -SCALE), reads=[kt], writes=[kn])
                v = vvr.next()
                vc = VTCOL["cv"] + h * 128
                s.dma("sp", v[:], self.VT[:, vc:vc + 128].rearrange("(kb p) e -> p kb e", p=128), writes=[v])

                def load_qz(i_):
                    q_ = qTs.next()
                    rq = FMROWS["cq"] + h * 128
                    s.dma("sp", q_[:], self.FM[rq:rq + 128, i_ * 512:i_ * 512 + 512], writes=[q_])
                    sz_ = szs.next()
                    rz = FMROWS["cz"] + h * 128
                    s.dma("sp", sz_[:], self.FM[rz:rz + 128, i_ * 512:i_ * 512 + 512], writes=[sz_])
                    return q_, sz_

                qz_next = load_qz(0)
                for i in range(NQT):
                    t0 = i * 512
                    q, sz = qz_next
                    if i + 1 < NQT:
                        qz_next = load_qz(i + 1)
                    kbs = list(range(4 * i + 3, -1, -1))
                    nk = len(kbs)
                    state = {"ls": None}

                    def z_mm(kb):
                        k0 = kb * 128
                        zp = zr.next()
                        s.op("pe", lambda e: e.matmul(zp[:], kt[:, k0:k0 + 128], q[:], start=True, stop=True),
                             reads=[kt, q], writes=[zp])
                        return zp

                    def softplus(kb, zp):
                        k0 = kb * 128
                        ee = er.next()
                        s.op("act", lambda e: e.activation(ee[:], zp[:], AF.Exp, scale=SCALE), reads=[zp], writes=[ee])
                        if kb >= 4 * i:
                            s.op("pool", lambda e: e.affine_select(
                                ee[:], ee[:], [[1, 512]], ALU.is_gt, 0.0, base=t0 - k0, channel_multiplier=-1),
                                reads=[ee], writes=[ee])
                        lb = lbr.next()
                        s.op("act", lambda e: e.activation(lb[:], ee[:], AF.Ln, bias=1.0), reads=[ee], writes=[lb])
                        return lb

                    def cum_bank(bi, kb, lb):
                        k0 = kb * 128
                        cp = cr.next()
                        ls = state["ls"]
                        s.op("pe", lambda e: e.matmul(cp[:], kn[:, k0:k0 + 128], q[:], start=True, stop=False),
                             reads=[kn, q], writes=[cp])
                        s.op("pe", lambda e: e.matmul(cp[:], self.uinc[:], lb[:], start=False, stop=(ls is None)),
                             reads=[self.uinc, lb], writes=[cp])
                        if ls is not None:
                            s.op("pe", lambda e: e.matmul(cp[:], self.ones_b[:], ls[:], start=False, stop=True),
                                 reads=[self.ones_b, ls], writes=[cp])
                        if bi + 1 < nk:
                            if ls is None:
                                state["ls"] = lb
                            else:
                                ln_ = lsr.next()
                                s.op("dve", lambda e: e.tensor_tensor(ln_[:], ls[:], lb[:], ALU.add), reads=[ls, lb], writes=[ln_])
                                state["ls"] = ln_
                        return cp

                    def final_exp(kb, cp):
                        k0 = kb * 128
                        a = ar.next()
                        s.op("act", lambda e: e.activation(a[:], cp[:], AF.Exp, scale=-1.0), reads=[cp], writes=[a])
                        if kb >= 4 * i:
                            s.op("pool", lambda e: e.affine_select(
                                a[:], a[:], [[1, 512]], ALU.is_gt, 0.0, base=t0 - k0, channel_multiplier=-1),
                                reads=[a], writes=[a])
                        return a

                    def pv(bi, kb, a):
                        s.op("pe", lambda e: e.matmul(po[:], v[:, kb, :], a[:], start=(bi == 0), stop=(bi == nk - 1)),
                             reads=[v, a], writes=[po])

                    lbs, cps, avs = {}, {}, {}
                    for j in range(min(2, nk)):
                        lbs[j] = softplus(kbs[j], z_mm(kbs[j]))
                    cps[0] = cum_bank(0, kbs[0], lbs[0])
                    for bi, kb in enumerate(kbs):
                        avs[bi] = final_exp(kb, cps.pop(bi))
                        if bi >= 1:
                            pv(bi - 1, kbs[bi - 1], avs.pop(bi - 1))
                        if bi + 2 < nk:
                            lbs[bi + 2] = softplus(kbs[bi + 2], z_mm(kbs[bi + 2]))
                        if bi + 1 < nk:
                            cps[bi + 1] = cum_bank(bi + 1, kbs[bi + 1], lbs[bi + 1])
                    pv(nk - 1, kbs[nk - 1], avs.pop(nk - 1))
                    y = yst.next()
                    s.op("dve", lambda e, y=y, sz=sz: e.tensor_tensor(y[:], po[:], sz[:], ALU.mult), reads=[po, sz], writes=[y])
                    ry = YSROW["c"] + h * 128
                    s.dma("sp", self.YS[ry:ry + 128, t0:t0 + 512], y[:], reads=[y])
            s.barrier()

    def phase3_nsa(self, l, g):
        s = self.s
        with ExitStack() as es:
            big = [s.sb(es, "bbig%d" % i, [128, SEQ], BF16) for i in range(4)]
            vs = s.sb(es, "bvs", [128, NKB, 128], BF16)
            vw = s.sb(es, "bvw", [128, NKB, 128], BF16)
            w1 = s.sb(es, "bw1", [128, 32, 128], BF16)
            w2 = s.sb(es, "bw2", [128, 128], BF16)
            pe_sb = s.sb(es, "bpe", [32, 128], F32)
            peT = s.sb(es, "bpeT", [128, 32], BF16)
            c1 = s.sb(es, "bc1", [128, 1], F32)
            hsl = s.sb(es, "bhsl", [128, 256], BF16)
            kcmpT = s.sb(es, "bkcmpT", [128, 256], BF16)
            vcmp = s.sb(es, "bvcmp", [128, 2, 128], BF16)
            qsets = [[s.sb(es, "bq%d_%d" % (k, i), [128, 512], BF16) for i in range(4)] for k in range(2)]
            gts = Ring([s.sb(es, "bgt%d" % i, [128, 3, 512], BF16) for i in range(2)])
            szs = Ring([s.sb(es, "bsz%d" % i, [128, 512], BF16) for i in range(2)])
            pf = [s.sb(es, "bpf%d" % i, [128, 512], F32) for i in range(2)]
            pnb = Ring([s.sb(es, "bpnb%d" % i, [128, 512], BF16) for i in range(2)])
            rs = s.sb(es, "brs", [128, 512], F32)
            ocmp = [s.sb(es, "bocmp%d" % i, [128, 512], F32) for i in range(4)]
            impS = s.sb(es, "bimp", [128, 4, 64], F32)
            m8 = s.sb(es, "bm8", [128, 16], F32)
            wk = s.sb(es, "bwk", [128, 64], F32)
            sel = s.sb(es, "bsel", [128, 64], F32)
            negT = s.sb(es, "bnegT", [64, 512], BF16)
            p_ring = Ring([s.sb(es, "bp%d" % i, [128, 512], BF16) for i in range(4)])
            acc = s.sb(es, "bacc", [128, 512], F32)
            tmp = s.sb(es, "btmp", [128, 512], F32)
            yst = Ring([s.sb(es, "byst%d" % i, [128, 512], BF16) for i in range(2)])
            ps_ring = Ring([self.psum[0], self.psum[1], self.psum[7]])
            psum_sum = self.psum[2]
            pimp = self.psum[3]
            psum_o = [self.psum[4]]
            pmisc = self.psum[5]
            pmisc2 = self.psum[6]
            kcT, vcT, ksT, kwT = big
            for t_, nm in ((kcT, "bkc"), (vcT, "bvc"), (ksT, "bks"), (kwT, "bkw")):
                r0 = FMROWS[nm] + g * 128
                s.dma("sp", t_[:], self.FM[r0:r0 + 128, :], writes=[t_])
            for t_, nm in ((vs, "bvs"), (vw, "bvw")):
                vc = VTCOL[nm] + g * 128
                s.dma("sp", t_[:], self.VT[:, vc:vc + 128].rearrange("(kb p) e -> p kb e", p=128), writes=[t_])
            for kv in range(2):
                src = kcT if kv == 0 else vcT
                s.dma("pool", w1[:], self.cmp_w1[kv][l].rearrange("(l d) f -> d l f", d=128), writes=[w1])
                s.dma("pool", w2[:], self.cmp_w2[kv][l], writes=[w2])
                s.dma("sp", pe_sb[:], self.cmp_pe[kv][l], writes=[pe_sb])
                s.op("pe", lambda e: e.transpose(pmisc[:, 0:32], pe_sb[:], self.ident[0:32, 0:32]),
                     reads=[pe_sb, self.ident], writes=[pmisc])
                s.op("dve", lambda e: e.tensor_copy(peT[:], pmisc[:, 0:32]), reads=[pmisc], writes=[peT])
                for li in range(32):
                    s.op("pe", lambda e, li=li: e.matmul(pmisc2[:, 0:1], w1[:, li, :], peT[:, li:li + 1],
                                                         start=(li == 0), stop=(li == 31)),
                         reads=[w1, peT], writes=[pmisc2])
                s.op("dve", lambda e: e.tensor_copy(c1[:], pmisc2[:, 0:1]), reads=[pmisc2], writes=[c1])
                for li in range(32):
                    s.op("pe", lambda e, li=li, src=src: e.matmul(pmisc[:, 0:NCMP], w1[:, li, :],
                                                                   src[:, li:li + 16 * (NCMP - 1) + 1:16],
                                                                   start=(li == 0), stop=(li == 31)),
                         reads=[w1, src], writes=[pmisc])
                s.op("dve", lambda e: e.memset(hsl[:], 0.0), writes=[hsl])
                s.op("act", lambda e: e.activation(hsl[:, 0:NCMP], pmisc[:, 0:NCMP], AF.Silu, bias=c1[:, 0:1]),
                     reads=[pmisc, c1], writes=[hsl])
                if kv == 0:
                    s.op("pe", lambda e: e.matmul(pmisc2[:, 0:256], w2[:], hsl[:], start=True, stop=True),
                         reads=[w2, hsl], writes=[pmisc2])
                    s.op("dve", lambda e: e.tensor_copy(kcmpT[:], pmisc2[:, 0:256]), reads=[pmisc2], writes=[kcmpT])
                else:
                    for nb in range(2):
                        s.op("pe", lambda e, nb=nb: e.matmul(pmisc2[:, nb * 128:(nb + 1) * 128], hsl[:, nb * 128:(nb + 1) * 128],
                                                             w2[:], start=True, stop=True),
                             reads=[w2, hsl], writes=[pmisc2])
                    s.op("dve", lambda e: e.tensor_copy(vcmp[:], pmisc2[:, 0:256].rearrange("p (n d) -> p n d", n=2)),
                         reads=[pmisc2], writes=[vcmp])
            def load_qs(i_):
                qs_ = qsets[i_ % 2]
                for r_ in range(4):
                    rq = FMROWS["bq"] + (g * 4 + r_) * 128
                    s.dma("sp", qs_[r_][:], self.FM[rq:rq + 128, i_ * 512:i_ * 512 + 512], writes=[qs_[r_]])
                return qs_

            load_qs(0)
            for i in range(NQT):
                t0 = i * 512
                nbs = [nb for nb in range(2) if 16 * nb * 128 + 31 <= t0 + 511]
                qTs = qsets[i % 2]
                if i + 1 < NQT:
                    load_qs(i + 1)
                for r in range(4):
                    q = qTs[r]
                    for nb in nbs:
                        ps = ps_ring.next()
                        s.op("pe", lambda e, ps=ps, nb=nb, q=q: e.matmul(ps[:], kcmpT[:, nb * 128:(nb + 1) * 128], q[:],
                                                                          start=True, stop=True),
                             reads=[kcmpT, q], writes=[ps])
                        s.op("act", lambda e, ps=ps, nb=nb: e.activation(pf[nb][:], ps[:], AF.Exp, scale=SCALE),
                             reads=[ps], writes=[pf[nb]])
                        s.op("pool", lambda e, nb=nb, t0=t0: e.affine_select(
                            pf[nb][:], pf[nb][:], [[1, 512]], ALU.is_ge, 0.0,
                            base=t0 - 16 * nb * 128 - 31, channel_multiplier=-16), reads=[pf[nb]], writes=[pf[nb]])
                        s.op("pe", lambda e, nb=nb: e.matmul(psum_sum[:], self.ones_f[:], pf[nb][:], start=(nb == nbs[0]),
                                                             stop=(nb == nbs[-1])),
                             reads=[self.ones_f, pf[nb]], writes=[psum_sum])
                    s.op("dve", lambda e: e.tensor_scalar(rs[:], psum_sum[:], 1e-30, None, ALU.max), reads=[psum_sum], writes=[rs])
                    s.op("dve", lambda e: e.reciprocal(rs[:], rs[:]), reads=[rs], writes=[rs])
                    for nb in nbs:
                        s.op("dve", lambda e, nb=nb: e.tensor_tensor(pf[nb][:], pf[nb][:], rs[:], ALU.mult),
                             reads=[pf[nb], rs], writes=[pf[nb]])
                        for tb in range(4):
                            first = (r == 0 and nb == nbs[0])
                            last = (r == 3 and nb == nbs[-1])
                            s.op("pe", lambda e, nb=nb, tb=tb, first=first, last=last: e.matmul(
                                pimp[:, tb * 64:(tb + 1) * 64], pf[nb][:, tb * 128:(tb + 1) * 128], self.ovl[:, nb, :],
                                start=first, stop=last), reads=[pf[nb], self.ovl], writes=[pimp])
                        pb = pnb.next()
                        s.op("act", lambda e, pb=pb, nb=nb: e.copy(pb[:], pf[nb][:]), reads=[pf[nb]], writes=[pb])
                        s.op("pe", lambda e, pb=pb, nb=nb: e.matmul(psum_o[0][:], vcmp[:, nb, :], pb[:], start=(nb == nbs[0]),
                                                                     stop=(nb == nbs[-1])),
                             reads=[vcmp, pb], writes=[psum_o[0]])
                    s.op("dve", lambda e, r=r: e.tensor_copy(ocmp[r][:], psum_o[0][:]), reads=[psum_o[0]], writes=[ocmp[r]])
                s.op("dve", lambda e: e.tensor_copy(impS[:], pimp[:, 0:256].rearrange("p (a b) -> p a b", a=4)),
                     reads=[pimp], writes=[impS])
                for tb in range(4):
                    for hh in range(2):
                        tblk = 8 * i + 2 * tb + hh
                        p0 = hh * 64
                        if tblk < 63:
                            s.op("pool", lambda e, tb=tb, p0=p0, tblk=tblk: e.memset(impS[p0:p0 + 64, tb, tblk + 1:64], -1e30),
                                 reads=[impS], writes=[impS])
                        s.op("pool", lambda e, tb=tb, p0=p0, tblk=tblk: e.memset(impS[p0:p0 + 64, tb, tblk:tblk + 1], 1e6),
                             reads=[impS], writes=[impS])
                        s.op("pool", lambda e, tb=tb, p0=p0: e.memset(impS[p0:p0 + 64, tb, 0:1], 2e6),
                             reads=[impS], writes=[impS])
                for tb in range(4):
                    s.op("dve", lambda e, tb=tb: e.max(out=m8[:, 0:8], in_=impS[:, tb, :]), reads=[impS], writes=[m8])
                    s.op("dve", lambda e, tb=tb: e.match_replace(out=wk[:], in_to_replace=m8[:, 0:8], in_values=impS[:, tb, :],
                                                                 imm_value=-3e30), reads=[impS, m8], writes=[wk])
                    s.op("dve", lambda e: e.max(out=m8[:, 8:16], in_=wk[:]), reads=[wk], writes=[m8])
                    s.op("dve", lambda e, tb=tb: e.tensor_scalar(sel[:], impS[:, tb, :], m8[:, 15:16], None, ALU.is_ge),
                         reads=[impS, m8], writes=[sel])
                    s.op("dve", lambda e: e.tensor_scalar(sel[:], sel[:], -1.0, BIG, ALU.add, ALU.mult), reads=[sel], writes=[sel])
                    s.op("pe", lambda e: e.transpose(pmisc[0:64, 0:128], sel[:], self.ident[:]),
                         reads=[sel, self.ident], writes=[pmisc])
                    s.op("act", lambda e, tb=tb: e.copy(negT[:, tb * 128:(tb + 1) * 128], pmisc[0:64, 0:128]),
                         reads=[pmisc], writes=[negT])
                for r in range(4):
                    h = g * 4 + r
                    q = qTs[r]
                    gt = gts.next()
                    for k3 in range(3):
                        rg = FMROWS["bg"] + h * 3 + k3
                        s.dma("sp", gt[:, k3, :], self.FM[rg:rg + 1, t0:t0 + 512].to_broadcast([128, 512]), writes=[gt])
                    sz = szs.next()
                    rz = FMROWS["bz"] + h * 128
                    s.dma("sp", sz[:], self.FM[rz:rz + 128, t0:t0 + 512], writes=[sz])
                    s.op("dve", lambda e, r=r, gt=gt: e.tensor_tensor(acc[:], ocmp[r][:], gt[:, 0, :], ALU.mult),
                         reads=[ocmp[r], gt], writes=[acc])
                    blocks = []
                    for kb in range(4 * i + 4):
                        k0 = kb * 128
                        blk = dict(kT=(ksT, ksT[:, k0:k0 + 128]), v=[(vs, vs[:, kb, :])],
                                   bias=(self.eall, self.eall[:, k0:k0 + 128], negT, negT[:]))
                        if kb >= 4 * i:
                            blk["mask"] = (lambda p, k0=k0, t0=t0: self._causal(p, k0, t0))
                        blocks.append(blk)
                    self._softmax_attn(q, blocks, p_ring, ps_ring, psum_sum, psum_o)
                    s.op("dve", lambda e: e.reciprocal(rs[:], psum_sum[:]), reads=[psum_sum], writes=[rs])
                    s.op("dve", lambda e: e.tensor_tensor(tmp[:], psum_o[0][:], rs[:], ALU.mult), reads=[psum_o[0], rs], writes=[tmp])
                    s.op("dve", lambda e, gt=gt: e.tensor_tensor(tmp[:], tmp[:], gt[:, 1, :], ALU.mult), reads=[tmp, gt], writes=[tmp])
                    s.op("dve", lambda e: e.tensor_tensor(acc[:], acc[:], tmp[:], ALU.add), reads=[acc, tmp], writes=[acc])
                    blocks = []
                    for kb in range(max(0, 4 * i - 4), 4 * i + 4):
                        k0 = kb * 128
                        blk = dict(kT=(kwT, kwT[:, k0:k0 + 128]), v=[(vw, vw[:, kb, :])])
                        if kb >= 4 * i:
                            blk["mask"] = (lambda p, k0=k0, t0=t0: self._causal(p, k0, t0))
                        else:
                            blk["mask"] = (lambda p, k0=k0, t0=t0: s.op("pool", lambda e: e.affine_select(
                                p[:], p[:], [[-1, 512]], ALU.is_gt, 0.0, base=k0 - t0 + 512, channel_multiplier=1),
                                reads=[p], writes=[p]))
                        blocks.append(blk)
                    self._softmax_attn(q, blocks, p_ring, ps_ring, psum_sum, psum_o)
                    s.op("dve", lambda e: e.reciprocal(rs[:], psum_sum[:]), reads=[psum_sum], writes=[rs])
                    s.op("dve", lambda e: e.tensor_tensor(tmp[:], psum_o[0][:], rs[:], ALU.mult), reads=[psum_o[0], rs], writes=[tmp])
                    s.op("dve", lambda e, gt=gt: e.tensor_tensor(tmp[:], tmp[:], gt[:, 2, :], ALU.mult), reads=[tmp, gt], writes=[tmp])
                    s.op("dve", lambda e: e.tensor_tensor(acc[:], acc[:], tmp[:], ALU.add), reads=[acc, tmp], writes=[acc])
                    y = yst.next()
                    s.op("dve", lambda e, y=y, sz=sz: e.tensor_tensor(y[:], acc[:], sz[:], ALU.mult), reads=[acc, sz], writes=[y])
                    ry = YSROW["b"] + h * 128
                    s.dma("pool", self.YS[ry:ry + 128, t0:t0 + 512], y[:], reads=[y])
            s.barrier()

    def phase4a(self, l):
        s = self.s
        for tp in range(4):
            with ExitStack() as es2:
                ys = [s.sb(es2, "ysb%d" % n, [128, 4, 1024], BF16) for n in range(3)]
                wbr = Ring([s.sb(es2, "wb%d" % i, [128, 4, 512], BF16) for i in range(6)])
                mgr = Ring([s.sb(es2, "mg%d" % i, [128, 1024], BF16) for i in range(4)])
                macc = [s.sb(es2, "macc%d" % i, [128, 512], F32) for i in range(2)]
                tm = Ring([s.sb(es2, "tm%d" % i, [128, 512], F32) for i in range(3)])
                stg = Ring([s.sb(es2, "mstg%d" % i, [128, 512], BF16) for i in range(3)])
                pacc = Ring(self.psum[0:8])
                TP = tp * 1024
                for n in range(3):
                    s.dma("sp", ys[n][:], self.YS[n * 512:(n + 1) * 512, TP:TP + 1024].rearrange("(k p) t -> p k t", p=128),
                          writes=[ys[n]])
                for cg in range(4):
                    wbs = []
                    for n in range(3):
                        wb = wbr.next()
                        s.dma("pool", wb[:], self.w_branch[l, n, :, cg * 512:(cg + 1) * 512].rearrange("(k p) c -> p k c", p=128),
                              writes=[wb])
                        wbs.append(wb)
                    for cb in range(4):
                        cc = cg * 4 + cb
                        for n in range(3):
                            mg = mgr.next()
                            rm = FMROWS["mg"] + n * 2048 + cc * 128
                            s.dma("sp", mg[:], self.FM[rm:rm + 128, TP:TP + 1024], writes=[mg])
                            for tl in range(2):
                                pa = pacc.next()
                                for k in range(4):
                                    s.op("pe", lambda e, pa=pa, n=n, k=k, tl=tl, cb=cb: e.matmul(
                                        pa[:], wbs[n][:, k, cb * 128:(cb + 1) * 128], ys[n][:, k, tl * 512:(tl + 1) * 512],
                                        start=(k == 0), stop=(k == 3)), reads=[wbs[n], ys[n]], writes=[pa])
                                if n == 0:
                                    s.op("dve", lambda e, pa=pa, tl=tl, mg=mg: e.tensor_tensor(
                                        macc[tl][:], pa[:], mg[:, tl * 512:(tl + 1) * 512], ALU.mult),
                                        reads=[pa, mg], writes=[macc[tl]])
                                else:
                                    t_ = tm.next()
                                    s.op("dve", lambda e, pa=pa, tl=tl, mg=mg, t_=t_: e.tensor_tensor(
                                        t_[:], pa[:], mg[:, tl * 512:(tl + 1) * 512], ALU.mult),
                                        reads=[pa, mg], writes=[t_])
                                    if n == 1:
                                        s.op("dve", lambda e, tl=tl, t_=t_: e.tensor_tensor(macc[tl][:], macc[tl][:], t_[:], ALU.add),
                                             reads=[macc[tl], t_], writes=[macc[tl]])
                                    else:
                                        st = stg.next()
                                        s.op("dve", lambda e, tl=tl, t_=t_, st=st: e.tensor_tensor(
                                            st[:], macc[tl][:], t_[:], ALU.add),
                                            reads=[macc[tl], t_], writes=[st])
                                        tt = TP + tl * 512
                                        s.dma("pool", self.MP[cc // 2][(cc % 2) * 128:(cc % 2 + 1) * 128, tt:tt + 512], st[:], reads=[st])
                s.barrier()

    def load_wo(self, es, l):
        s = self.s
        wo = s.sb(es, "wo", [128, 16, 2048], BF16)
        for k4 in range(4):
            s.dma("pool", wo[:, k4 * 4:(k4 + 1) * 4, :],
                  self.w_out[l, k4 * 512:(k4 + 1) * 512, :].rearrange("(k p) c -> p k c", p=128), writes=[wo])
        return wo

    def phase4b(self, l, xsrc, xdst, wo):
        s = self.s
        with ExitStack() as es3:
            ggr = s.sb(es3, "ggr", [128, 2048], F32)
            m0 = s.sb(es3, "m0", [128, 16, 512], BF16)
            m1 = s.sb(es3, "m1", [128, 16, 512], BF16)
            mTr = Ring([s.sb(es3, "mT%d" % i, [128, 16, 512], BF16) for i in range(2)])
            xr = Ring([s.sb(es3, "xr%d" % i, [128, 2048], F32) for i in range(3)])
            yr = Ring([s.sb(es3, "yr%d" % i, [128, 2048], F32) for i in range(2)])
            junk = s.sb(es3, "junk4", [128, 512], BF16)
            ss = Ring([s.sb(es3, "ss4%d" % i, [128, 4], F32) for i in range(2)])
            rstd = Ring([s.sb(es3, "rstd4%d" % i, [128, 1], F32) for i in range(2)])
            s.dma("sp", ggr[:], self.GG[l:l + 1, :].to_broadcast([128, 2048]), writes=[ggr])

            def load_m(ti_):
                t0_ = ti_ * 512
                for c8 in range(8):
                    for rk, mm in ((0, m0), (1, m1)):
                        s.dma("sp", mm[:, 2 * c8:2 * c8 + 2, :],
                              self.MG[c8][rk * 256:(rk + 1) * 256, t0_:t0_ + 512].rearrange("(q p) t -> p q t", p=128), writes=[mm])

            def load_x(tb_):
                x_ = xr.next()
                s.dma("sp", x_[:], xsrc[tb_ * 128:(tb_ + 1) * 128, :], writes=[x_])
                return x_

            load_m(0)
            x_next = load_x(0)
            for ti in range(NQT):
                mT = mTr.next()
                for hk in range(2):
                    eng = "dve"
                    s.op(eng, lambda e, hk=hk, mT=mT: e.tensor_tensor(
                        mT[:, hk * 8:(hk + 1) * 8, :], m0[:, hk * 8:(hk + 1) * 8, :], m1[:, hk * 8:(hk + 1) * 8, :], ALU.add),
                        reads=[m0, m1], writes=[mT])
                if ti + 1 < NQT:
                    load_m(ti + 1)
                for bb in range(4):
                    tb = ti * 4 + bb
                    tt = tb * 128
                    x = x_next
                    if tb + 1 < 4 * NQT:
                        x_next = load_x(tb + 1)
                    half_banks = self.psum[0:4] if tb % 2 == 0 else self.psum[4:8]
                    sst = ss.next()
                    for cb in range(4):
                        pa = half_banks[cb]
                        for k in range(16):
                            s.op("pe", lambda e, pa=pa, k=k, bb=bb, cb=cb, mT=mT: e.matmul(
                                pa[:], mT[:, k, bb * 128:(bb + 1) * 128], wo[:, k, cb * 512:(cb + 1) * 512],
                                start=(k == 0), stop=(k == 15)), reads=[mT, wo], writes=[pa])
                        s.op("act", lambda e, pa=pa, cb=cb, sst=sst: e.activation(junk[:], pa[:], AF.Square,
                                                                                 accum_out=sst[:, cb:cb + 1]),
                             reads=[pa], writes=[junk, sst])
                    rt = rstd.next()
                    s.op("dve", lambda e, sst=sst, rt=rt: e.tensor_reduce(rt[:], sst[:], mybir.AxisListType.X, ALU.add),
                         reads=[sst], writes=[rt])
                    s.op("act", lambda e, rt=rt: e.activation(rt[:], rt[:], AF.Ln, scale=1.0 / D, bias=1e-6), reads=[rt], writes=[rt])
                    s.op("act", lambda e, rt=rt: e.activation(rt[:], rt[:], AF.Exp, scale=-0.5), reads=[rt], writes=[rt])
                    y = yr.next()
                    for cb in range(4):
                        pa = half_banks[cb]
                        s.op("dve", lambda e, pa=pa, cb=cb, y=y, rt=rt: e.scalar_tensor_tensor(
                            y[:, cb * 512:(cb + 1) * 512], pa[:], rt[:, 0:1], ggr[:, cb * 512:(cb + 1) * 512], ALU.mult, ALU.mult),
                            reads=[pa, rt, ggr], writes=[y])
                    s.op("dve", lambda e, y=y, x=x: e.tensor_tensor(y[:], y[:], x[:], ALU.add), reads=[y, x], writes=[y])
                    s.dma("pool", xdst[tt:tt + 128, :], y[:], reads=[y])
            s.barrier()

    def build(self, phases=("0", "12", "3a", "3b", "3c", "4")):
        self.declare()
        self.setup()
        if "0" in phases:
            self.phase0()
        for l in range(self.n_layers):
            xsrc = self.x_in if l == 0 else self.XS[(l - 1) % 2]
            xdst = self.out if l == self.n_layers - 1 else self.XS[l % 2]
            if "12" in phases:
                for half in self.halves:
                    self.phase12(l, half, xsrc)
            if "3a" in phases:
                self.phase3_diff(l, 0)
            if "3b" in phases:
                self.phase3_nsa(l, 0)
            if "3c" in phases:
                self.phase3_sb(l, 0)
            if "4" in phases:
                with ExitStack() as es4:
                    wo = self.load_wo(es4, l)
                    self.phase4a(l)
                    for c8 in range(8):
                        self.s.coll("AllGather", [[0, 1], [2, 3], [4, 5], [6, 7]], self.MP[c8], self.MG[c8])
                    self.s.barrier()
                    self.phase4b(l, xsrc, xdst, wo)
        self.s.barrier()
        self.es.close()
        return self.nc


def make_in_maps(inputs, n_cores=8):
    k = _constants()
    f = lambda a: np.ascontiguousarray(np.asarray(a, dtype=np.float32))
    lam = np.stack([f(inputs["lambda_q1"]), f(inputs["lambda_k1"]), f(inputs["lambda_q2"]), f(inputs["lambda_k2"])], axis=1)
    shared = dict(
        norm_pre_g=f(inputs["norm_pre_g"]), norm_post_g=f(inputs["norm_post_g"]), w_ada=f(inputs["w_ada"]),
        b_ada=f(inputs["b_ada"]), lam=np.ascontiguousarray(lam),
        diff_norm_g=f(inputs["diff_norm_g"]), cmp_pe_k=f(inputs["cmp_pe_k"]), cmp_pe_v=f(inputs["cmp_pe_v"]),
        cmp_w1_k=f(inputs["cmp_w1_k"]), cmp_w1_v=f(inputs["cmp_w1_v"]), cmp_w2_k=f(inputs["cmp_w2_k"]),
        cmp_w2_v=f(inputs["cmp_w2_v"]), w_out=f(inputs["w_out"]),
        k_ident=k["ident"], k_prot=k["prot"], k_ropec=k["ropec"], k_ropes=k["ropes"], k_ovl=k["ovl"], k_eall=k["eall"])
    w_in = f(inputs["w_in"])
    w_br = f(inputs["w_branch"])
    per_hs = []
    for hs in range(2):
        per_hs.append(dict(w_in=np.ascontiguousarray(w_in[:, :, local_cols(hs)]),
                           w_branch=np.ascontiguousarray(w_br[:, :, hs * 512:(hs + 1) * 512, :])))
    x = f(inputs["x"])
    c = f(inputs["c"])
    maps = []
    for core in range(n_cores):
        b, hs = core // 2, core % 2
        m = dict(shared)
        m.update(per_hs[hs])
        m["x"] = np.ascontiguousarray(x[b])
        m["c"] = np.ascontiguousarray(c[b].reshape(16, 128).T)
        maps.append(m)
    return maps


def kernel(**inputs):
    n_cores = 8
    nc = Builder().build()
    maps = make_in_maps(inputs, n_cores)
    res = run_bass_kernel_spmd(nc, maps, core_ids=list(range(n_cores)))
    return np.stack([np.asarray(res.results[2 * b]["out"]) for b in range(4)], axis=0).astype(np.float32)
```
